# Optimizing a Trainium2 kernel written in Bass

```python
import math
import jax
import jax.numpy as jnp
from jax import lax
import numpy as np

D_MODEL = 1024
BATCH = 4
SEQ = 8192
DEPTH = 2

GRID_W = 64
CTX_LEN = 256
EPS = 1e-6
D_HYENA = D_MODEL // 2
HYENA_ORDER = 2
SHORT_CONV = 3
FILTER_EMB = 33
FILTER_WIDTH = 64
FILTER_OUT_STD = 0.005
DECAY_FAST = 0.3
DECAY_SLOW = 1.5
DECAY_TARGET = 1e-2
DECAY_SHIFT = 0.05
DA_HEADS = 4
DA_HEAD_DIM = 64
D_DIFF = DA_HEADS * 2 * DA_HEAD_DIM
ROPE_BASE = 10000.0
Q_BLOCK = 128
N_HQ = 3 * D_HYENA + D_DIFF
D_SGU = D_MODEL
SGU_GROUPS = 8
CHUNK = 128
D_FF = 2816
FFN_CONV = 3

kernel_name = "hybrid_hyena_diffattn_sgu_dit"


def rmsnorm(x, g):
    xf = x.astype(jnp.float32)
    y = xf * lax.rsqrt(jnp.mean(xf * xf, axis=-1, keepdims=True) + EPS)
    return (y * g.astype(jnp.float32)).astype(x.dtype)


def layernorm(x, g, b):
    xf = x.astype(jnp.float32)
    mu = jnp.mean(xf, axis=-1, keepdims=True)
    xc = xf - mu
    y = xc * lax.rsqrt(jnp.mean(xc * xc, axis=-1, keepdims=True) + EPS)
    return (y * g.astype(jnp.float32) + b.astype(jnp.float32)).astype(x.dtype)


def modulate(h, shift, scale):
    return h * (1 + scale) + shift


def dwconv3(u, w, b):
    up = jnp.pad(u, ((0, 0), (1, 1), (0, 0)))
    return up[:, :-2] * w[0] + up[:, 1:-1] * w[1] + up[:, 2:] * w[2] + b


def hyena_filters(L, w1, b1, freq, w2, b2, w3):
    f32 = jnp.float32
    bands = (FILTER_EMB - 1) // 2
    pos = jnp.arange(L, dtype=f32)
    t = jnp.linspace(0.0, 1.0, L, dtype=f32)[:, None]
    ang = (2.0 * math.pi / L) * pos[:, None] * jnp.linspace(1e-4, bands - 1, bands, dtype=f32)[None, :]
    z = jnp.concatenate([t, jnp.cos(ang), -jnp.sin(ang)], axis=-1)
    h = jnp.sin(freq[0].astype(f32) * (z @ w1.astype(f32) + b1.astype(f32)))
    h = jnp.sin(freq[1].astype(f32) * (h @ w2.astype(f32) + b2.astype(f32)))
    h = h @ w3.astype(f32)
    deltas = jnp.abs(jnp.linspace(math.log(DECAY_TARGET) / DECAY_SLOW, math.log(DECAY_TARGET) / DECAY_FAST, D_HYENA, dtype=f32))
    window = jnp.exp(-t * deltas[None, :]) + DECAY_SHIFT
    h = h.reshape(L, HYENA_ORDER, 2, D_HYENA) * window[:, None, None, :]
    fwd, bwd = h[:, :, 0], h[:, :, 1]
    return jnp.concatenate([fwd, jnp.zeros_like(fwd[:1]), bwd[:0:-1]], axis=0)


def fftconv_bidir(u, h_full, d_skip):
    L = u.shape[1]
    uf32 = u.astype(jnp.float32)
    uf = jnp.fft.rfft(uf32, n=2 * L, axis=1)
    hf = jnp.fft.rfft(h_full, n=2 * L, axis=0)
    y = jnp.fft.irfft(uf * hf[None], n=2 * L, axis=1)[:, :L]
    return (y + uf32 * d_skip.astype(jnp.float32)).astype(u.dtype)


def hyena_mix(proj, conv_w, conv_b, w1, b1, freq, w2, b2, w3, d_skip):
    L = proj.shape[1]
    proj = dwconv3(proj, conv_w, conv_b)
    v, x1, x2 = jnp.split(proj, 3, axis=-1)
    filt = hyena_filters(L, w1, b1, freq, w2, b2, w3)
    z = v
    for o, gate in enumerate((x1, x2)):
        z = gate * fftconv_bidir(z, filt[:, o], d_skip[o])
    return z


def rope_1d(x, pos):
    m = x.shape[-1] // 2
    inv = ROPE_BASE ** (-jnp.arange(m, dtype=jnp.float32) / m)
    ang = pos.astype(jnp.float32)[:, None] * inv[None, :]
    cos, sin = jnp.cos(ang).astype(x.dtype), jnp.sin(ang).astype(x.dtype)
    x1, x2 = x[..., :m], x[..., m:]
    return jnp.concatenate([x1 * cos - x2 * sin, x1 * sin + x2 * cos], axis=-1)


def rope_2d(x, row, col):
    half = x.shape[-1] // 2
    return jnp.concatenate([rope_1d(x[..., :half], row), rope_1d(x[..., half:], col)], axis=-1)


def to_qk_heads(t):
    B, L, _ = t.shape
    return t.reshape(B, L, DA_HEADS, 2, DA_HEAD_DIM).transpose(0, 2, 3, 1, 4)


def to_v_heads(t):
    B, L, _ = t.shape
    return t.reshape(B, L, DA_HEADS, 2 * DA_HEAD_DIM).transpose(0, 2, 1, 3)


def diff_attention(q, k, v, lam):
    B, H, _, Lq, dh = q.shape
    nb = Lq // Q_BLOCK
    qb = q.reshape(B, H, 2, nb, Q_BLOCK, dh).transpose(3, 0, 1, 2, 4, 5)
    scale = dh ** -0.5

    def block(qi):
        s = jnp.einsum('bhmqd,bhmkd->bhmqk', qi, k).astype(jnp.float32) * scale
        p = jax.nn.softmax(s, axis=-1)
        a = (p[:, :, 0] - lam * p[:, :, 1]).astype(v.dtype)
        return jnp.einsum('bhqk,bhkd->bhqd', a, v)

    o = lax.map(block, qb)
    return o.transpose(1, 2, 0, 3, 4).reshape(B, H, Lq, 2 * dh)


def ab_mixer(h, w_in, w_out, hy, lam, lam_init, subln_g, kc, vc, pos):
    B, L, _ = h.shape
    if pos is None:
        proj = h @ w_in[:, :N_HQ]
        k, v = kc, vc
    else:
        proj = h @ w_in
        kv = proj[..., N_HQ:]
        k = jnp.concatenate([kc, rope_2d(to_qk_heads(kv[..., :D_DIFF]), *pos)], axis=3)
        v = jnp.concatenate([vc, to_v_heads(kv[..., D_DIFF:])], axis=2)
    y_a = hyena_mix(proj[..., :3 * D_HYENA], *hy)
    q = to_qk_heads(proj[..., 3 * D_HYENA:N_HQ])
    if pos is not None:
        q = rope_2d(q, *pos)
    o = diff_attention(q, k, v, lam)
    o = rmsnorm(o, subln_g) * (1.0 - lam_init)
    y_b = o.transpose(0, 2, 1, 3).reshape(B, L, D_DIFF)
    return jnp.concatenate([y_a, y_b], axis=-1) @ w_out


def sgu_mixer(h, w_in, b_in, ln_g, ln_b, w_s, b_s, w_out):
    B, L, _ = h.shape
    u, v = jnp.split(jax.nn.gelu(h @ w_in + b_in, approximate=False), 2, axis=-1)
    v = layernorm(v, ln_g, ln_b)
    n = L // CHUNK
    vb = v.reshape(B, n, CHUNK, SGU_GROUPS, D_SGU // SGU_GROUPS)
    s = jnp.einsum('gts,bnsgc->bntgc', w_s, vb) + b_s.T[:, :, None]
    return (u * s.reshape(B, L, D_SGU)) @ w_out


def conv_ffn(h, w_up, conv_w, conv_b, w_down):
    a = dwconv3(h @ w_up, conv_w, conv_b)
    g, u = jnp.split(a, 2, axis=-1)
    return (jax.nn.silu(g) * u) @ w_down


def setup_inputs(seed: int = 0) -> dict:
    key = jax.random.key(seed)
    ks = iter(jax.random.split(key, 40))
    D = D_MODEL
    n_even = (DEPTH + 1) // 2
    n_odd = DEPTH // 2

    def nrm(shape, scale):
        return jax.random.normal(next(ks), shape, jnp.float32) * scale

    def gain(shape):
        return 1.0 + nrm(shape, 0.02)

    return {
        "x": nrm((BATCH, SEQ, D), 1.0),
        "c": nrm((BATCH, D), 1.0),
        "ctx": nrm((BATCH, CTX_LEN, D), 1.0),
        "c_ctx": nrm((D,), 1.0),
        "mod_w": nrm((DEPTH, D, 6 * D), D ** -0.5),
        "mod_b": nrm((DEPTH, 6 * D), 0.02),
        "norm_mix_g": gain((DEPTH, D)),
        "norm_ffn_g": gain((DEPTH, D)),
        "ffn_w_up": nrm((DEPTH, D, 2 * D_FF), D ** -0.5),
        "ffn_conv_w": nrm((DEPTH, FFN_CONV, 2 * D_FF), FFN_CONV ** -0.5),
        "ffn_conv_b": nrm((DEPTH, 2 * D_FF), 0.02),
        "ffn_w_down": nrm((DEPTH, D_FF, D), D_FF ** -0.5),
        "ab_w_in": nrm((n_even, D, 3 * D_HYENA + 3 * D_DIFF), D ** -0.5),
        "hy_conv_w": nrm((n_even, SHORT_CONV, 3 * D_HYENA), SHORT_CONV ** -0.5),
        "hy_conv_b": nrm((n_even, 3 * D_HYENA), 0.02),
        "hy_w1": nrm((n_even, FILTER_EMB, FILTER_WIDTH), FILTER_EMB ** -0.5),
        "hy_b1": nrm((n_even, FILTER_WIDTH), 0.1),
        "hy_freq": gain((n_even, 2, FILTER_WIDTH)),
        "hy_w2": nrm((n_even, FILTER_WIDTH, FILTER_WIDTH), FILTER_WIDTH ** -0.5),
        "hy_b2": nrm((n_even, FILTER_WIDTH), 0.1),
        "hy_w3": nrm((n_even, FILTER_WIDTH, HYENA_ORDER * 2 * D_HYENA), FILTER_OUT_STD),
        "hy_bias": nrm((n_even, HYENA_ORDER, D_HYENA), 1.0),
        "da_lambda": nrm((n_even, 4, DA_HEAD_DIM), 0.1),
        "da_subln_g": gain((n_even, 2 * DA_HEAD_DIM)),
        "ab_w_out": nrm((n_even, D_HYENA + D_DIFF, D), (D_HYENA + D_DIFF) ** -0.5),
        "sgu_w_in": nrm((n_odd, D, 2 * D_SGU), D ** -0.5),
        "sgu_b_in": nrm((n_odd, 2 * D_SGU), 0.02),
        "sgu_ln_g": gain((n_odd, D_SGU)),
        "sgu_ln_b": nrm((n_odd, D_SGU), 0.02),
        "sgu_w_s": nrm((n_odd, SGU_GROUPS, CHUNK, CHUNK), CHUNK ** -0.5),
        "sgu_b_s": gain((n_odd, SGU_GROUPS, CHUNK)),
        "sgu_w_out": nrm((n_odd, D_SGU, D), D_SGU ** -0.5),
        "final_norm_g": gain((D,)),
    }


def reference(x, c, ctx, c_ctx, mod_w, mod_b, norm_mix_g, norm_ffn_g, ffn_w_up, ffn_conv_w, ffn_conv_b,
              ffn_w_down, ab_w_in, hy_conv_w, hy_conv_b, hy_w1, hy_b1, hy_freq, hy_w2, hy_b2, hy_w3, hy_bias,
              da_lambda, da_subln_g, ab_w_out, sgu_w_in, sgu_b_in, sgu_ln_g, sgu_ln_b, sgu_w_s, sgu_b_s,
              sgu_w_out, final_norm_g):
    D = D_MODEL
    S = x.shape[1]
    rows = S // GRID_W
    row = jnp.repeat(jnp.arange(rows, dtype=jnp.int32), GRID_W)
    col = jnp.tile(jnp.arange(GRID_W, dtype=jnp.int32), rows)
    for i in range(DEPTH):
        ctx_live = any(j % 2 == 0 for j in range(i + 1, DEPTH))
        need_ctx = (i % 2 == 0) or ctx_live
        sh1, sc1, g1, sh2, sc2, g2 = (m[:, None, :] for m in jnp.split(jax.nn.silu(c) @ mod_w[i] + mod_b[i], 6, axis=-1))
        hx = modulate(rmsnorm(x, norm_mix_g[i]), sh1, sc1)
        if need_ctx:
            n_mod = 6 if ctx_live else 2
            cmod = jnp.split(jax.nn.silu(c_ctx) @ mod_w[i][:, :n_mod * D] + mod_b[i][:n_mod * D], n_mod)
            hc = modulate(rmsnorm(ctx, norm_mix_g[i]), cmod[0], cmod[1])
        if i % 2 == 0:
            e = i // 2
            lam_init = 0.8 - 0.6 * math.exp(-0.3 * i)
            lq1, lk1, lq2, lk2 = da_lambda[e].astype(jnp.float32)
            lam = jnp.exp(jnp.sum(lq1 * lk1)) - jnp.exp(jnp.sum(lq2 * lk2)) + lam_init
            hy = (hy_conv_w[e], hy_conv_b[e], hy_w1[e], hy_b1[e], hy_freq[e], hy_w2[e], hy_b2[e], hy_w3[e], hy_bias[e])
            kvc = hc @ ab_w_in[e][:, N_HQ:]
            kc, vc = to_qk_heads(kvc[..., :D_DIFF]), to_v_heads(kvc[..., D_DIFF:])
            mix_x = ab_mixer(hx, ab_w_in[e], ab_w_out[e], hy, lam, lam_init, da_subln_g[e], kc, vc, (row, col))
            if ctx_live:
                mix_c = ab_mixer(hc, ab_w_in[e], ab_w_out[e], hy, lam, lam_init, da_subln_g[e], kc, vc, None)
        else:
            o = i // 2
            sgu = (sgu_w_in[o], sgu_b_in[o], sgu_ln_g[o], sgu_ln_b[o], sgu_w_s[o], sgu_b_s[o], sgu_w_out[o])
            mix_x = sgu_mixer(hx, *sgu)
            if ctx_live:
                mix_c = sgu_mixer(hc, *sgu)
        ffn = (ffn_w_up[i], ffn_conv_w[i], ffn_conv_b[i], ffn_w_down[i])
        x = x + g1 * mix_x
        x = x + g2 * conv_ffn(modulate(rmsnorm(x, norm_ffn_g[i]), sh2, sc2), *ffn)
        if ctx_live:
            ctx = ctx + cmod[2] * mix_c
            ctx = ctx + cmod[5] * conv_ffn(modulate(rmsnorm(ctx, norm_ffn_g[i]), cmod[3], cmod[4]), *ffn)
    return rmsnorm(x, final_norm_g)
```

```python
import math
import os
import numpy as np
import ml_dtypes
import concourse.bass as bass
import concourse.mybir as mybir
from concourse.bass_utils import run_bass_kernel_spmd

F32 = mybir.dt.float32
BF16 = mybir.dt.bfloat16
U8 = mybir.dt.uint8
AF = mybir.ActivationFunctionType
ALU = mybir.AluOpType

ENGS = ("pe", "act", "dve", "pool", "sp")
SEM_ROT = 16000
L = 8192
EXT = 4352
NEXT_T = EXT // 128
D = 1024
DFF = 2816
EPS = 1e-6
GS = 256


class Res:
    __slots__ = ("name", "w", "r", "dsem", "dcnt", "excl", "lk")

    def __init__(self, name, excl=False):
        self.name = name
        self.lk = None
        self.excl = excl
        self.w = {}
        self.r = {}
        self.dsem = None
        self.dcnt = 0


class Prog:
    def __init__(self, nc):
        self.nc = nc
        self.ops = {e: [] for e in ENGS}
        self.cnt = {e: 0 for e in ENGS}
        self.epoch = {e: 0 for e in ENGS}
        self.sems = {}
        self.known = {e: {} for e in ENGS}
        self.dres = []
        self.meta = {e: [] for e in ENGS}
        self.free_sems = []
        self.nd = 0
        for e in ENGS:
            if e != "sp":
                self._engsem(e)

    def _engsem(self, e):
        k = ("E", e, self.epoch[e])
        if k not in self.sems:
            self.sems[k] = self.nc.alloc_semaphore(name="s_%s_%d" % (e, self.epoch[e]))
        return k

    def _semname(self, sem):
        for k, v in self.sems.items():
            if v is sem:
                return k
        return None

    def check_deadlock(self):
        val = {}
        pc = {e: 0 for e in ENGS}
        prog = True
        while prog:
            prog = False
            for e in ENGS:
                while pc[e] < len(self.meta[e]):
                    waits, inc, desc = self.meta[e][pc[e]]
                    if all(val.get(k, 0) >= v for k, v in waits):
                        if inc is not None:
                            val[inc[0]] = val.get(inc[0], 0) + inc[1]
                        pc[e] += 1
                        prog = True
                    else:
                        break
        bad = False
        for e in ENGS:
            if pc[e] < len(self.meta[e]):
                bad = True
                waits, inc, desc = self.meta[e][pc[e]]
                print("DEADLOCK", e, pc[e], len(self.meta[e]), desc, [(k, v, val.get(k, 0)) for k, v in waits if val.get(k, 0) < v])
        return not bad

    def _deps(self, reads, writes):
        deps = {}
        for r in reads:
            for k, v in r.w.items():
                if deps.get(k, 0) < v:
                    deps[k] = v
        for w in writes:
            for d in (w.w, w.r):
                for k, v in d.items():
                    if deps.get(k, 0) < v:
                        deps[k] = v
        return deps

    def _waits(self, eng, deps):
        kn = self.known[eng]
        out = []
        for k, v in deps.items():
            if kn.get(k, 0) < v:
                kn[k] = v
                out.append((self.sems[k], v))
        return out

    def op(self, eng, fn, reads=(), writes=(), signal=True):
        deps = self._deps(reads, writes)
        k = self._engsem(eng)
        for r in reads:
            if r.excl:
                for kk, vv in r.r.items():
                    if not (kk[0] == "E" and kk[1] == eng) and deps.get(kk, 0) < vv:
                        deps[kk] = vv
        if eng == "pe":
            deps = {kk: vv for kk, vv in deps.items() if kk[1] != "pe" or kk[0] != "E"}
        else:
            deps = {kk: vv for kk, vv in deps.items() if not (kk == k and vv > self.cnt[eng])}
        waits = self._waits(eng, deps)
        if signal:
            self.cnt[eng] += 1
            tok = (k, self.cnt[eng])
            sem = self.sems[k]
        else:
            tok = (k, self.cnt[eng] + 1)
            sem = None

        def emit(h, fn=fn, waits=waits, sem=sem):
            for s, v in waits:
                h.wait_ge(s, v)
            ins = fn(h)
            if sem is not None:
                ins.then_inc(sem, 1)

        self.ops[eng].append(emit)
        self.meta[eng].append(([(self._semname(s_), v_) for s_, v_ in waits], (k, 1) if signal else None,
                               "op r=%s w=%s" % ([r.name for r in reads], [w.name for w in writes])))
        kk, vv = tok
        for w in writes:
            w.w = {kk: vv}
            w.r = {}
        for r in reads:
            if r.r.get(kk, 0) < vv:
                r.r[kk] = vv
        if signal and self.cnt[eng] >= SEM_ROT:
            self.epoch[eng] += 1
            self.cnt[eng] = 0
            self._engsem(eng)

    def dma(self, eng, out, in_, reads=(), writes=(), store=False):
        assert len(writes) == 1
        wres = writes[0]
        owner = reads[0] if store else wres
        deps = {}
        for r in reads:
            for k, v in r.w.items():
                if deps.get(k, 0) < v:
                    deps[k] = v
        for d in ((wres.r,) if store else (wres.w, wres.r)):
            for k, v in d.items():
                if deps.get(k, 0) < v:
                    deps[k] = v
        if owner.dsem is None:
            if eng == "pool":
                self.nd += 1
                owner.dsem = ("S", self.nd)
                owner.dcnt = 0
                self.sems[owner.dsem] = self.nc.alloc_semaphore(name="sw_%d" % self.nd)
            elif self.free_sems:
                owner.dsem, owner.dcnt = self.free_sems.pop()
            else:
                self.nd += 1
                owner.dsem = ("D", self.nd)
                owner.dcnt = 0
                self.sems[owner.dsem] = self.nc.alloc_semaphore(name="d_%d" % self.nd)
            self.dres.append(owner)
        if (not store) and owner.lk == "load" and not wres.r:
            deps.pop(owner.dsem, None)
        owner.lk = "store" if store else "load"
        waits = self._waits(eng, deps)
        owner.dcnt += 1
        k, v = owner.dsem, 16 * owner.dcnt
        sem = self.sems[k]

        def emit(h, waits=waits, sem=sem, out=out, in_=in_):
            for s, vv in waits:
                h.wait_ge(s, vv)
            h.dma_start(out=out, in_=in_).then_inc(sem, 16)

        self.ops[eng].append(emit)
        self.meta[eng].append(([(self._semname(s_), v_) for s_, v_ in waits], (k, 16),
                               "dma r=%s w=%s" % ([r.name for r in reads], [w.name for w in writes])))
        if store:
            wres.w[k] = v
        else:
            wres.w = {k: v}
            wres.r = {}
        for r in reads:
            if r.r.get(k, 0) < v:
                r.r[k] = v

    def barrier(self):
        toks = {}
        for e in ENGS:
            if e == "sp":
                continue
            k = self._engsem(e)
            if self.cnt[e] > 0:
                toks[k] = self.cnt[e]
            if self.epoch[e] > 0:
                toks[("E", e, self.epoch[e] - 1)] = SEM_ROT
        for r in self.dres:
            toks[r.dsem] = 16 * r.dcnt
        for e in ENGS:
            waits = self._waits(e, dict(toks))

            def emit(h, waits=waits):
                for s, v in waits:
                    h.wait_ge(s, v)

            self.ops[e].append(emit)
            self.meta[e].append(([(self._semname(s_), v_) for s_, v_ in waits], None, "barrier"))
        keep = []
        for r in self.dres:
            if r.dsem[0] == "S":
                keep.append(r)
                continue
            self.free_sems.append((r.dsem, r.dcnt))
            r.dsem = None
        self.dres = keep

    def build(self):
        nc = self.nc
        with nc.Block() as block:
            @block.tensor
            def _(h):
                for f in self.ops["pe"]:
                    f(h)

            @block.scalar
            def _(h):
                for f in self.ops["act"]:
                    f(h)

            @block.vector
            def _(h):
                for f in self.ops["dve"]:
                    f(h)

            @block.gpsimd
            def _(h):
                for f in self.ops["pool"]:
                    f(h)

            @block.sync
            def _(h):
                for f in self.ops["sp"]:
                    f(h)


class Buf:
    __slots__ = ("t", "r")

    def __init__(self, t, r):
        self.t = t
        self.r = r


def _dtsize(dt):
    return 4 if dt == F32 else (2 if dt == BF16 else 1)


class KB:
    def __init__(self, stop_after=None):
        nc = bass.Bass("TRN2", target_bir_lowering=False)
        self.nc = nc
        self.P = Prog(nc)
        self.stop_after = stop_after
        self.ARENA = 207 * 1024
        self.arena = nc.alloc_sbuf_tensor("arena", [128, self.ARENA], U8)
        self.top = 0
        ps = nc.alloc_psum_tensor("psum", [128, 4096], F32)
        self.ps = ps
        self.pb = [Buf(ps[:, 512 * i:512 * (i + 1)], Res("pb%d" % i, excl=True)) for i in range(8)]
        self.dram = {}
        self.dram_names = set()
        self.ins = {}
        self.outs = {}

    def alloc(self, name, shape, dt):
        n = 1
        for s in shape[1:]:
            n *= s
        nb = n * _dtsize(dt)
        nb = (nb + 31) // 32 * 32
        off = self.top
        self.top += nb
        assert self.top <= self.ARENA, "SBUF arena overflow %s %d" % (name, self.top)
        v = self.arena[:shape[0], off:off + n * _dtsize(dt)].bitcast(dt)
        if len(shape) == 3:
            v = v.rearrange("p (a b) -> p a b", a=shape[1])
        elif len(shape) == 4:
            v = v.rearrange("p (a b c) -> p a b c", a=shape[1], b=shape[2])
        return Buf(v, Res(name))

    def inp(self, name, shape, dt=F32):
        t = self.nc.dram_tensor(name, list(shape), dt, kind="ExternalInput").ap()
        b = Buf(t, Res(name))
        self.ins[name] = b
        return b

    def outp(self, name, shape, dt=F32):
        t = self.nc.dram_tensor(name, list(shape), dt, kind="ExternalOutput").ap()
        self.dram_names.add(name)
        b = Buf(t, Res(name))
        self.outs[name] = b
        return b

    def scratch(self, name, shape, dt, debug=False):
        if debug:
            return self.outp(name, shape, dt)
        t = self.nc.dram_tensor(name, list(shape), dt).ap()
        self.dram_names.add(name)
        return Buf(t, Res(name))

    def mm(self, out, lhsT, rhs, start, stop, reads, writes):
        self.P.op("pe", lambda h: h.matmul(out, lhsT, rhs, start=start, stop=stop),
                  reads=reads, writes=writes, signal=bool(stop))

    def tr(self, out, in_, ident, reads, writes, signal=True):
        self.P.op("pe", lambda h: h.transpose(out, in_, ident), reads=reads, writes=writes, signal=signal)

    def act(self, out, in_, func, reads, writes, bias=None, scale=None, accum=None):
        kw = {}
        if bias is not None:
            kw["bias"] = bias
        if scale is not None:
            kw["scale"] = scale
        if accum is not None:
            kw["accum_out"] = accum
        self.P.op("act", lambda h: h.activation(out=out, in_=in_, func=func, **kw), reads=reads, writes=writes)

    def tt(self, eng, out, a, b, op, reads, writes):
        self.P.op(eng, lambda h: h.tensor_tensor(out=out, in0=a, in1=b, op=op), reads=reads, writes=writes)

    def ts(self, eng, out, a, s1, s2, op0, op1, reads, writes):
        if op1 is None:
            s2, op1 = 0.0, ALU.add
        self.P.op(eng, lambda h: h.tensor_scalar(out, a, s1, s2, op0, op1), reads=reads, writes=writes)

    def stt(self, eng, out, in0, scalar, in1, op0, op1, reads, writes):
        self.P.op(eng, lambda h: h.scalar_tensor_tensor(out=out, in0=in0, scalar=scalar, in1=in1, op0=op0, op1=op1),
                  reads=reads, writes=writes)

    def cp(self, eng, out, in_, reads, writes):
        if eng == "act":
            self.P.op("act", lambda h: h.activation(out=out, in_=in_, func=AF.Copy), reads=reads, writes=writes)
        else:
            self.P.op(eng, lambda h: h.tensor_copy(out, in_), reads=reads, writes=writes)

    def recip(self, out, in_, reads, writes):
        self.P.op("dve", lambda h: h.reciprocal(out, in_), reads=reads, writes=writes)

    def memset(self, eng, out, val, writes):
        self.P.op(eng, lambda h: h.memset(out, val), writes=writes)

    def dma(self, q, out, in_, reads, writes, store=None):
        if store is None:
            store = writes[0].name in self.dram_names
        self.P.dma(q, out, in_, reads=reads, writes=writes, store=store)

    def ld(self, q, dst, src_ap, src=None):
        self.P.dma(q, dst.t if isinstance(dst, Buf) else dst[0], src_ap,
                   reads=[src.r] if src is not None else [], writes=[dst.r if isinstance(dst, Buf) else dst[1]])

    def consts(self):
        self.ident = self.alloc("ident", [128, 128], BF16)
        self.ones = self.alloc("ones", [128, 128], BF16)
        self.epsb = self.alloc("epsb", [128, 1], F32)
        idin = self.inp("c_ident", [128, 128], BF16)
        self.dma("sp", self.ident.t, idin.t, [], [self.ident.r])
        self.memset("pool", self.ones.t, 1.0, [self.ones.r])
        self.memset("pool", self.epsb.t, EPS, [self.epsb.r])
        self.cmark = self.top

    def norm_T(self, xt, A, SH, tmp, xn, junk, stat, hT, col0, ntok=128, plain_g=None):
        ss = stat.t[:, 0:1]
        rs = stat.t[:, 1:2]
        self.act(junk.t, xt.t, AF.Square, [xt.r], [junk.r, stat.r], accum=ss)
        self.act(rs, ss, AF.Sqrt, [stat.r, self.epsb.r], [stat.r], bias=self.epsb.t[:, 0:1], scale=1.0 / D)
        self.recip(rs, rs, [stat.r], [stat.r])
        self.stt("dve", tmp.t, xt.t, rs, A.t, ALU.mult, ALU.mult, [xt.r, stat.r, A.r], [tmp.r])
        self.tt("pool", xn.t, tmp.t, SH.t, ALU.add, [tmp.r, SH.r], [xn.r])
        pT = self.pb[7]
        pTv = pT.t.bitcast(BF16)
        for kc in range(8):
            self.tr(pTv[:, kc * 128:kc * 128 + ntok], xn.t[:ntok, kc * 128:(kc + 1) * 128], self.ident.t[:ntok, :ntok],
                    [xn.r, self.ident.r], [pT.r], signal=(kc == 7))
        src = pTv.rearrange("p (a b) -> p a b", a=8)[:, :, :ntok]
        self.cp("act", hT.t[:, :, col0:col0 + ntok], src, [pT.r], [hT.r])

    def pool_of(self, name, n, shape, dt):
        return {"b": [self.alloc("%s%d" % (name, i), shape, dt) for i in range(n)], "i": 0}

    def nxt(self, pool):
        b = pool["b"][pool["i"] % len(pool["b"])]
        pool["i"] += 1
        return b

    def bank(self):
        b = self.pb[self._bk % 6]
        self._bk += 1
        return b

    def phase_mod(self):
        ccol = self.inp("ccol", [128, 8, 2])
        modw = self.inp("mod_w", [2, 1024, 6144])
        modb = self.inp("mod_b", [2, 6144])
        self.modv = self.scratch("modv", [2, 2, 6144], F32, debug=self.dbg)
        sc = self.alloc("scol", [128, 8, 2], F32)
        self.dma("sp", sc.t, ccol.t, [], [sc.r])
        self.act(sc.t, sc.t, AF.Silu, [sc.r], [sc.r])
        mrow = self.alloc("mrow", [2, 6144], F32)
        mb2 = self.alloc("mb2", [2, 6144], F32)
        wb = [self.alloc("mwb%d" % i, [128, 8, 512], F32) for i in range(2)]
        for i in range(2):
            self.dma("sp", mb2.t, modb.t[i].partition_broadcast(2), [], [mb2.r])
            for n in range(12):
                w = wb[n % 2]
                self.dma("sp", w.t, modw.t[i, :, n * 512:(n + 1) * 512].rearrange("(kc p) f -> p kc f", p=128), [], [w.r])
                pbk = self.pb[n % 2]
                for kc in range(8):
                    self.mm(pbk.t[0:2, :], sc.t[:, kc, :], w.t[:, kc, :], kc == 0, kc == 7, [sc.r, w.r], [pbk.r])
                self.tt("dve", mrow.t[:, n * 512:(n + 1) * 512], pbk.t[0:2, :], mb2.t[:, n * 512:(n + 1) * 512], ALU.add,
                        [pbk.r, mb2.r], [mrow.r])
            self.dma("sp", self.modv.t[i], mrow.t, [mrow.r], [self.modv.r])

    def mod_tiles(self, layer, which, i_sh, i_sc, g_ap):
        A = self.alloc("modA", [128, 1024], F32)
        SH = self.alloc("modSH", [128, 1024], F32)
        G = self.alloc("modG", [128, 1024], F32)
        mv = self.modv.t[layer, which]
        self.dma("sp", A.t, mv[i_sc * D:(i_sc + 1) * D].partition_broadcast(128), [self.modv.r], [A.r])
        self.dma("sp", SH.t, mv[i_sh * D:(i_sh + 1) * D].partition_broadcast(128), [self.modv.r], [SH.r])
        self.dma("sp", G.t, g_ap.partition_broadcast(128), [], [G.r])
        self.stt("dve", A.t, A.t, 1.0, G.t, ALU.add, ALU.mult, [A.r, G.r], [A.r])
        return A, SH

    def load_w_bf16(self, dst, src_ap, kcn):
        for kc in range(kcn):
            self.dma("pool", dst.t[:, kc, :], src_ap[kc * 128:(kc + 1) * 128, :], [], [dst.r])

    def norm_bufs(self):
        nb = {}
        nb["x"] = self.pool_of("nx", 2, [128, 1024], F32)
        nb["tmp"] = self.alloc("ntmp", [128, 1024], F32)
        nb["xn"] = self.pool_of("nxn", 2, [128, 1024], BF16)
        nb["junk"] = self.alloc("njunk", [128, 1024], BF16)
        nb["stat"] = self.pool_of("nstat", 2, [128, 2], F32)
        return nb

    def proj_pass(self, xin, T, A, SH, w, fm, tm, nb, post=None):
        skip = os.environ.get("KSKIP", "")
        fm = [sp for sp in fm if sp["kind"] not in skip.split(",")]
        ng = T // GS
        hT = [self.alloc("hT%d" % i, [128, 8, GS + 32], BF16) for i in range(2)]
        for hb in hT:
            self.memset("pool", hb.t, 0.0, [hb.r])
        tmpc = self.pool_of("tmpc", 2, [128, GS], F32)
        tmpd = self.pool_of("tmpd", 2, [128, GS], F32)
        obf = self.pool_of("obf", 4, [128, GS], BF16)
        tst = self.pool_of("tst", 2, [128, 512], BF16) if tm else None
        cs = self.pool_of("cs", 2, [128, 2, GS], F32) if any(sp_["kind"] == "rope" for sp_ in fm) else None
        for g in range(ng + 1):
            if g < ng:
                h = hT[g % 2]
                for j in range(GS // 128):
                    xt = self.nxt(nb["x"])
                    t0 = g * GS + j * 128
                    self.dma("sp", xt.t, xin.t[t0:t0 + 128, :], [xin.r], [xt.r])
                    self.norm_T(xt, A, SH, nb["tmp"], self.nxt(nb["xn"]), nb["junk"], self.nxt(nb["stat"]), h, 16 + j * 128)
                if g == 0:
                    self.memset("pool", h.t[:, :, 15:16], 0.0, [h.r])
                else:
                    self.cp("pool", h.t[:, :, 15:16], hT[(g - 1) % 2].t[:, :, GS + 15:GS + 16], [hT[(g - 1) % 2].r], [h.r])
            if g == 0:
                continue
            gg = g - 1
            h = hT[gg % 2]
            if g == ng:
                self.memset("pool", h.t[:, :, GS + 16:GS + 17], 0.0, [h.r])
            else:
                self.cp("pool", h.t[:, :, GS + 16:GS + 17], hT[g % 2].t[:, :, 16:17], [hT[g % 2].r], [h.r])
            g0 = gg * GS
            for sp in fm:
                kind = sp["kind"]
                if kind == "rope":
                    c = self.nxt(cs)
                    if "nodma" in os.environ.get("ROPEVAR", ""):
                        self.memset("pool", c.t, 1.0, [c.r])
                    else:
                        self.dma("sp", c.t[:, 0, :], sp["cos"].t[:, g0:g0 + GS], [], [c.r])
                        self.dma("sp", c.t[:, 1, :], sp["sin"].t[:, g0:g0 + GS], [], [c.r])
                for ci in sp.get("order", range(sp["n"])):
                    pbk = self.bank()
                    col = sp["col0"] + ci * 128
                    for kc in range(8):
                        self.mm(pbk.t[:, 0:GS + 4], w.t[:, kc, col:col + 128], h.t[:, kc, 14:GS + 18], kc == 0, kc == 7,
                                [w.r, h.r], [pbk.r])
                    if kind == "conv":
                        cw = sp["cw"]
                        k = sp["cwi0"] + ci
                        t1 = self.nxt(tmpc)
                        ob = self.nxt(obf)
                        self.act(t1.t, pbk.t[:, 2:GS + 2], AF.Identity, [pbk.r, cw.r], [t1.r],
                                 bias=cw.t[:, k, 3:4], scale=cw.t[:, k, 1:2])
                        self.stt("dve", t1.t, pbk.t[:, 1:GS + 1], cw.t[:, k, 0:1], t1.t, ALU.mult, ALU.add,
                                 [pbk.r, cw.r, t1.r], [t1.r])
                        if "emit" in sp:
                            t3 = self.nxt(tmpd)
                            self.stt("dve", t3.t, pbk.t[:, 3:GS + 3], cw.t[:, k, 2:3], t1.t, ALU.mult, ALU.add,
                                     [pbk.r, cw.r, t1.r], [t3.r])
                            sp["emit"](ci, t3, g0)
                        else:
                            self.stt("dve", ob.t, pbk.t[:, 3:GS + 3], cw.t[:, k, 2:3], t1.t, ALU.mult, ALU.add,
                                     [pbk.r, cw.r, t1.r], [ob.r])
                            r0 = sp["row0"] + ci * 128
                            self.dma("sp", sp["out"].t[r0:r0 + 128, g0:g0 + GS], ob.t, [ob.r], [sp["out"].r])
                    elif kind == "rope":
                        kb = self.nxt(obf)
                        ob = self.nxt(obf)
                        t1 = self.nxt(tmpc)
                        t2 = self.nxt(tmpd)
                        p2 = self.pb[6]
                        self.cp("act", kb.t, pbk.t[:, 2:GS + 2], [pbk.r], [kb.r])
                        if os.environ.get("ROPEMM", "1") == "1":
                            self.mm(p2.t[:, 0:GS], self.rperm.t, kb.t, True, True, [self.rperm.r, kb.r], [p2.r])
                        else:
                            p2 = pbk
                        if "nott" in os.environ.get("ROPEVAR", ""):
                            self.cp("dve", t1.t, pbk.t[:, 2:GS + 2], [pbk.r, c.r], [t1.r])
                            self.cp("dve", t2.t, p2.t[:, 0:GS], [p2.r, c.r], [t2.r])
                        else:
                            self.tt("dve", t1.t, pbk.t[:, 2:GS + 2], c.t[:, 0, :], ALU.mult, [pbk.r, c.r, kb.r], [t1.r])
                            self.tt("dve", t2.t, p2.t[:, 0:GS], c.t[:, 1, :], ALU.mult, [p2.r, c.r], [t2.r])
                        self.tt(os.environ.get("ROPEADD", "pool"), ob.t, t1.t, t2.t, ALU.add, [t1.r, t2.r], [ob.r])
                        to = sp["toff"] + g0
                        self.dma("sp", sp["out"].t[ci, :, to:to + GS], ob.t, [ob.r], [sp["out"].r])
                    else:
                        ob = self.nxt(obf)
                        self.cp("act", ob.t, pbk.t[:, 2:GS + 2], [pbk.r], [ob.r])
                        to = sp["toff"] + g0
                        self.dma("sp", sp["out"].t[ci, :, to:to + GS], ob.t, [ob.r], [sp["out"].r])
            for sp in tm:
                for j in range(GS // 128):
                    pbk = self.bank()
                    for kc in range(8):
                        self.mm(pbk.t[:, 0:512], h.t[:, kc, 16 + j * 128:16 + (j + 1) * 128],
                                w.t[:, kc, sp["col0"]:sp["col0"] + 512], kc == 0, kc == 7, [w.r, h.r], [pbk.r])
                    st = self.nxt(tst)
                    self.cp("act", st.t, pbk.t[:, 0:512], [pbk.r], [st.r])
                    ro = sp["roff"] + g0 + j * 128
                    self.dma("sp", sp["out"].t[ro:ro + 128, :], st.t, [st.r], [sp["out"].r])
            if post is not None:
                post(gg, g0, h)

    def phase_inproj(self):
        dbg = self.dbg
        self.x_full = self.inp("x_full", [L, D])
        self.x_ext = self.inp("x_ext", [EXT, D])
        ctx = self.inp("ctx", [256, D])
        w_in = self.inp("w_in", [D, 3072])
        cwin = self.inp("hy_cw", [128, 12, 4])
        gmix = self.inp("norm_mix_g", [2, D])
        cosk = self.inp("ropek_cos", [128, L])
        sink = self.inp("ropek_sin", [128, L])
        cosq = self.inp("ropeq_cos", [128, EXT])
        sinq = self.inp("ropeq_sin", [128, EXT])
        rp = self.inp("c_rperm", [128, 128], BF16)
        self.VX1 = self.scratch("VX1", [1024, L], BF16, debug=dbg)
        self.X2 = self.scratch("X2", [512, EXT], BF16, debug=dbg)
        self.KT = self.scratch("KT", [4, 128, L + 256], BF16, debug=dbg)
        self.QT = self.scratch("QT", [4, 128, EXT], BF16, debug=dbg)
        self.VT = self.scratch("VT", [L + 256, 512], BF16, debug=dbg)
        self.top = self.cmark
        w = self.alloc("w_in", [128, 8, 3072], BF16)
        self.load_w_bf16(w, w_in.t, 8)
        cw = self.alloc("cw", [128, 12, 4], F32)
        self.dma("sp", cw.t, cwin.t, [], [cw.r])
        self.rperm = self.alloc("rperm", [128, 128], BF16)
        self.dma("sp", self.rperm.t, rp.t, [], [self.rperm.r])
        nb = self.norm_bufs()
        mark = self.top
        A, SH = self.mod_tiles(0, 1, 0, 1, gmix.t[0])
        import os
        self.proj_pass(ctx, 256, A, SH, w,
                       [dict(kind="rope", col0=2048, n=4, cos=cosk, sin=sink, out=self.KT, toff=0)] if os.environ.get("CTXROPE") else
                       [dict(kind="plain", col0=2048, n=4, out=self.KT, toff=0)],
                       [dict(col0=2560, out=self.VT, roff=0)], nb)
        self.P.barrier()
        self.top = mark
        if self.stop_after == "ctx":
            return
        A, SH = self.mod_tiles(0, 0, 0, 1, gmix.t[0])
        mark2 = self.top
        self.proj_pass(self.x_full, L, A, SH, w,
                       [dict(kind="conv", col0=0, n=8, cw=cw, cwi0=0, out=self.VX1, row0=0),
                        dict(kind="rope", col0=2048, n=4, cos=cosk, sin=sink, out=self.KT, toff=256)],
                       [dict(col0=2560, out=self.VT, roff=256)], nb)
        self.P.barrier()
        self.top = mark2
        self.proj_pass(self.x_ext, EXT, A, SH, w,
                       [dict(kind="conv", col0=1024, n=4, cw=cw, cwi0=8, out=self.X2, row0=0),
                        dict(kind="rope", col0=1536, n=4, cos=cosq, sin=sinq, out=self.QT, toff=0)],
                       [], nb)


    def phase_attn(self):
        dal = self.inp("da_lambda", [4, 64])
        subg = self.inp("subln_col", [128, 1])
        self.YT = self.scratch("YT", [1024, EXT], BF16, debug=self.dbg)
        self.top = self.cmark
        LAM_INIT = 0.8 - 0.6 * math.exp(0.0)
        lt = self.alloc("lt", [128, 256], F32)
        pr = self.alloc("lpr", [128, 128], F32)
        ls = self.alloc("ls", [128, 4], F32)
        negl = self.alloc("negl", [128, 1], F32)
        gsub = self.alloc("gsub", [128, 1], F32)
        self.dma("sp", lt.t, dal.t.rearrange("a b -> (a b)").partition_broadcast(128), [], [lt.r])
        self.tt("dve", pr.t[:, 0:64], lt.t[:, 0:64], lt.t[:, 64:128], ALU.mult, [lt.r], [pr.r])
        self.tt("dve", pr.t[:, 64:128], lt.t[:, 128:192], lt.t[:, 192:256], ALU.mult, [lt.r, pr.r], [pr.r])
        self.act(lt.t[:, 0:64], pr.t[:, 0:64], AF.Identity, [pr.r, lt.r], [lt.r, ls.r], accum=ls.t[:, 0:1])
        self.act(lt.t[:, 64:128], pr.t[:, 64:128], AF.Identity, [pr.r, lt.r, ls.r], [lt.r, ls.r], accum=ls.t[:, 1:2])
        self.act(ls.t[:, 2:4], ls.t[:, 0:2], AF.Exp, [ls.r], [ls.r])
        self.tt("dve", negl.t, ls.t[:, 3:4], ls.t[:, 2:3], ALU.subtract, [ls.r], [negl.r])
        self.ts("dve", negl.t, negl.t, -LAM_INIT, None, ALU.add, None, [negl.r], [negl.r])
        self.dma("sp", gsub.t, subg.t, [], [gsub.r])
        self.ts("dve", gsub.t, gsub.t, 1.0 - LAM_INIT, None, ALU.mult, None, [gsub.r], [gsub.r])
        NK = (L + 256) // 128
        Kh = self.alloc("Kh", [128, L + 256], BF16)
        Vh = self.alloc("Vh", [128, NK, 128], BF16)
        Qh = self.alloc("Qh", [128, EXT], BF16)
        Eb = [self.alloc("Eb%d" % i, [128, 2, 512], BF16) for i in range(2)]
        f = {n_: self.alloc("at_" + n_, [128, 512], F32) for n_ in ("r0", "r1", "t0", "t1", "o", "rs", "y")}
        osq = self.alloc("at_osq", [128, 512], BF16)
        yb = [self.alloc("at_yb%d" % i, [128, 512], BF16) for i in range(2)]
        pb = self.pb
        it = 0
        for h in range(4):
            self.dma("sp", Kh.t, self.KT.t[h], [self.KT.r], [Kh.r])
            self.dma("sp", Vh.t, self.VT.t[:, h * 128:(h + 1) * 128].rearrange("(kt p) d -> p kt d", p=128), [self.VT.r], [Vh.r])
            self.dma("sp", Qh.t, self.QT.t[h], [self.QT.r], [Qh.r])
            groups = [(q0_, min(512, EXT - q0_)) for q0_ in range(0, EXT, 512)]

            def emit_qk(kt, q0, n):
                par = kt % 2
                for m in range(2):
                    sb_ = pb[2 * par + m]
                    self.mm(sb_.t[:, :n], Kh.t[64 * m:64 * m + 64, kt * 128:(kt + 1) * 128],
                            Qh.t[64 * m:64 * m + 64, q0:q0 + n], True, True, [Kh.r, Qh.r], [sb_.r])

            for gi_, (q0, n) in enumerate(groups):
                if gi_ == 0:
                    emit_qk(0, q0, n)
                for kt in range(NK):
                    par = kt % 2
                    if kt + 1 < NK:
                        emit_qk(kt + 1, q0, n)
                    E = Eb[par]
                    sv = self.ps[:, 1024 * par:1024 * par + 1024].rearrange("p (a b) -> p a b", a=2)[:, :, :n]
                    self.act(E.t[:, :, :n], sv, AF.Exp, [pb[2 * par].r, pb[2 * par + 1].r], [E.r], scale=0.125)
                    for m in range(2):
                        self.mm(pb[4 + m].t[:, :n], Vh.t[:, kt, :], E.t[:, m, :n], kt == 0, kt == NK - 1, [Vh.r, E.r], [pb[4 + m].r])
                    for m in range(2):
                        self.mm(pb[6 + m].t[:, :n], self.ones.t, E.t[:, m, :n], kt == 0, kt == NK - 1, [self.ones.r, E.r], [pb[6 + m].r])
                if gi_ + 1 < len(groups):
                    emit_qk(0, *groups[gi_ + 1])
                self.recip(f["r0"].t[:, :n], pb[6].t[:, :n], [pb[6].r], [f["r0"].r])
                self.recip(f["r1"].t[:, :n], pb[7].t[:, :n], [pb[7].r], [f["r1"].r])
                self.tt("dve", f["t0"].t[:, :n], pb[4].t[:, :n], f["r0"].t[:, :n], ALU.mult, [pb[4].r, f["r0"].r], [f["t0"].r])
                self.tt("dve", f["t1"].t[:, :n], pb[5].t[:, :n], f["r1"].t[:, :n], ALU.mult, [pb[5].r, f["r1"].r], [f["t1"].r])
                self.stt("dve", f["o"].t[:, :n], f["t1"].t[:, :n], negl.t[:, 0:1], f["t0"].t[:, :n], ALU.mult, ALU.add,
                         [f["t1"].r, f["t0"].r, negl.r], [f["o"].r])
                self.act(osq.t[:, :n], f["o"].t[:, :n], AF.Square, [f["o"].r], [osq.r])
                self.mm(pb[6].t[:, :n], self.ones.t, osq.t[:, :n], True, True, [self.ones.r, osq.r], [pb[6].r])
                self.act(f["rs"].t[:, :n], pb[6].t[:, :n], AF.Sqrt, [pb[6].r, self.epsb.r], [f["rs"].r],
                         bias=self.epsb.t[:, 0:1], scale=1.0 / 128)
                self.recip(f["rs"].t[:, :n], f["rs"].t[:, :n], [f["rs"].r], [f["rs"].r])
                self.tt("dve", f["y"].t[:, :n], f["o"].t[:, :n], f["rs"].t[:, :n], ALU.mult, [f["o"].r, f["rs"].r], [f["y"].r])
                y2 = yb[it % 2]
                it += 1
                self.ts("dve", y2.t[:, :n], f["y"].t[:, :n], gsub.t[:, 0:1], None, ALU.mult, None, [f["y"].r, gsub.r], [y2.r])
                r0 = 512 + h * 128
                self.dma("sp", self.YT.t[r0:r0 + 128, q0:q0 + n], y2.t[:, :n], [y2.r], [self.YT.r])


    def load_w_gated(self, dst, src_ap, kcn, gate_ap):
        mark = self.top
        G = self.alloc("gateG", [128, 1024], F32)
        self.dma("sp", G.t, gate_ap.partition_broadcast(128), [self.modv.r], [G.r])
        stg = self.pool_of("wstg", 2, [128, 1024], F32)
        for kc in range(kcn):
            st = self.nxt(stg)
            self.dma("sp", st.t, src_ap[kc * 128:(kc + 1) * 128, :], [], [st.r])
            self.tt("dve", dst.t[:, kc, :], st.t, G.t, ALU.mult, [st.r, G.r], [dst.r])
        self.P.barrier()
        self.top = mark

    def resid_store(self, xin, xout, actT, kcn, Wd, g0, final_g=None):
        for j in range(GS // 128):
            xr = self.nxt(self.rx)
            r0 = g0 + j * 128
            self.dma("sp", xr.t, xin.t[r0:r0 + 128, :], [xin.r], [xr.r])
            xo = self.nxt(self.ro)
            for half in range(2):
                pbk = self.bank()
                for kc in range(kcn):
                    self.mm(pbk.t[:, 0:512], actT.t[:, kc, j * 128:(j + 1) * 128], Wd.t[:, kc, half * 512:(half + 1) * 512],
                            kc == 0, kc == kcn - 1, [actT.r, Wd.r], [pbk.r])
                self.tt("dve", xo.t[:, half * 512:(half + 1) * 512], pbk.t[:, 0:512], xr.t[:, half * 512:(half + 1) * 512],
                        ALU.add, [pbk.r, xr.r], [xo.r])
            if final_g is not None:
                st = self.nxt(self.fst)
                self.act(self.fjunk.t, xo.t, AF.Square, [xo.r], [self.fjunk.r, st.r], accum=st.t[:, 0:1])
                self.act(st.t[:, 1:2], st.t[:, 0:1], AF.Sqrt, [st.r, self.epsb.r], [st.r], bias=self.epsb.t[:, 0:1], scale=1.0 / D)
                self.recip(st.t[:, 1:2], st.t[:, 1:2], [st.r], [st.r])
                self.stt("dve", xo.t, xo.t, st.t[:, 1:2], final_g.t, ALU.mult, ALU.mult, [xo.r, st.r, final_g.r], [xo.r])
            self.dma("sp", xout.t[r0:r0 + 128, :], xo.t, [xo.r], [xout.r])

    def resid_bufs(self):
        self.rx = self.pool_of("rx", 2, [128, 1024], F32)
        self.ro = self.pool_of("ro", 2, [128, 1024], F32)

    def phase_outproj(self, name, yT_dram, xin, w_ap, layer):
        xout = self.scratch(name, [EXT, D], F32, debug=self.dbg)
        self.top = self.cmark
        Wo = self.alloc("Wo", [128, 8, 1024], BF16)
        self.load_w_gated(Wo, w_ap, 8, self.modv.t[layer, 0, 2 * D:3 * D])
        self.resid_bufs()
        yb = self.pool_of("yTb", 2, [128, 8, GS], BF16)
        for g in range(EXT // GS):
            g0 = g * GS
            y = self.nxt(yb)
            self.dma("sp", y.t, yT_dram.t[:, g0:g0 + GS].rearrange("(kc p) t -> p kc t", p=128), [yT_dram.r], [y.r])
            self.resid_store(xin, xout, y, 8, Wo, g0)
        return xout

    def phase_ffn(self, name, xin, layer, final=False):
        wup = self.inp("ffn_w_up%d" % layer, [D, 2 * DFF])
        wdn = self.inp("ffn_w_down%d" % layer, [DFF, D])
        cwin = self.inp("ffn_cw%d" % layer, [128, 44, 4])
        gffn = self.inp("norm_ffn_g", [2, D]) if "norm_ffn_g" not in self.ins else self.ins["norm_ffn_g"]
        xout = self.outp("out", [EXT, D], F32) if final else self.scratch(name, [EXT, D], F32, debug=self.dbg)
        self.top = self.cmark
        Wu = self.alloc("Wu", [128, 8, 2 * DFF], BF16)
        self.load_w_bf16(Wu, wup.t, 8)
        Wd = self.alloc("Wd", [128, 22, 1024], BF16)
        self.load_w_gated(Wd, wdn.t, 22, self.modv.t[layer, 0, 5 * D:6 * D])
        cw = self.alloc("cwf", [128, 44, 4], F32)
        self.dma("sp", cw.t, cwin.t, [], [cw.r])
        fg = None
        if final:
            fgi = self.inp("final_norm_g", [D])
            fg = self.alloc("fg", [128, 1024], F32)
            self.dma("sp", fg.t, fgi.t.partition_broadcast(128), [], [fg.r])
            self.fst = self.pool_of("fst", 2, [128, 2], F32)
            self.fjunk = self.alloc("fjunk", [128, 1024], BF16)
        A, SH = self.mod_tiles(layer, 0, 3, 4, gffn.t[layer])
        nb = self.norm_bufs_small()
        self.resid_bufs_small()
        gT = self.alloc("gT", [128, 22, GS], BF16)
        sil = self.pool_of("sil", 2, [128, GS], F32)
        hold = {}

        def emit(ci, buf, g0):
            if ci < 22:
                sb_ = self.nxt(sil)
                self.act(sb_.t, buf.t, AF.Silu, [buf.r], [sb_.r])
                hold["s"] = sb_
            else:
                sb_ = hold["s"]
                self.tt("pool", gT.t[:, ci - 22, :], sb_.t, buf.t, ALU.mult, [sb_.r, buf.r], [gT.r])

        def post(gg, g0, h):
            self.resid_store(xin, xout, gT, 22, Wd, g0, final_g=fg)

        order = []
        for j in range(22):
            order += [j, 22 + j]
        self.proj_pass(xin, EXT, A, SH, Wu,
                       [dict(kind="conv", col0=0, n=44, cw=cw, cwi0=0, order=order, emit=emit)], [], nb, post=post)
        return xout

    def norm_bufs_small(self):
        nb = {}
        nb["x"] = self.pool_of("nx", 1, [128, 1024], F32)
        nb["tmp"] = self.alloc("ntmp", [128, 1024], F32)
        nb["xn"] = self.pool_of("nxn", 1, [128, 1024], BF16)
        nb["junk"] = self.alloc("njunk", [128, 1024], BF16)
        nb["stat"] = self.pool_of("nstat", 2, [128, 2], F32)
        return nb

    def resid_bufs_small(self):
        self.rx = self.pool_of("rx", 1, [128, 1024], F32)
        self.ro = self.pool_of("ro", 1, [128, 1024], F32)


    def phase_sgu(self, name, xin, layer=1):
        win = self.inp("sgu_w_in", [D, 2048])
        bcol_i = self.inp("sgu_bu_col", [128, 8])
        bv_i = self.inp("sgu_bv", [1024])
        lng_i = self.inp("sgu_ln_g", [1024])
        lnb_i = self.inp("sgu_ln_b", [1024])
        wsT_i = self.inp("sgu_wsT", [128, 8, 128])
        bs_i = self.inp("sgu_b_s", [1, 1024])
        wout = self.inp("sgu_w_out", [D, D])
        gmix = self.ins["norm_mix_g"]
        xout = self.scratch(name, [EXT, D], F32, debug=self.dbg)
        self.top = self.cmark
        Wi = self.alloc("Wi", [128, 8, 2048], BF16)
        self.load_w_bf16(Wi, win.t, 8)
        Wo = self.alloc("Wo", [128, 8, 1024], BF16)
        self.load_w_gated(Wo, wout.t, 8, self.modv.t[layer, 0, 2 * D:3 * D])
        wsT = self.alloc("wsT", [128, 8, 128], BF16)
        self.dma("pool", wsT.t, wsT_i.t, [], [wsT.r])
        bsr = self.alloc("bsr", [1, 1024], BF16)
        self.dma("pool", bsr.t, bs_i.t, [], [bsr.r])
        bcol = self.alloc("bcol", [128, 8], F32)
        self.dma("sp", bcol.t, bcol_i.t, [], [bcol.r])
        BV = self.alloc("BV", [128, 1024], F32)
        LNG = self.alloc("LNG", [128, 1024], F32)
        LNB = self.alloc("LNB", [128, 1024], F32)
        self.dma("sp", BV.t, bv_i.t.partition_broadcast(128), [], [BV.r])
        self.dma("sp", LNG.t, lng_i.t.partition_broadcast(128), [], [LNG.r])
        self.dma("sp", LNB.t, lnb_i.t.partition_broadcast(128), [], [LNB.r])
        A, SH = self.mod_tiles(layer, 0, 0, 1, gmix.t[layer])
        nb = self.norm_bufs()
        self.resid_bufs()
        hTb = self.pool_of("sg_hT", 2, [128, 8, GS], BF16)
        uT = self.alloc("sg_uT", [128, 8, GS], F32)
        guT = self.pool_of("sg_guT", 2, [128, 8, GS], BF16)
        vt = self.alloc("sg_vt", [128, 1024], F32)
        vg = self.alloc("sg_vg", [128, 1024], F32)
        vb = self.pool_of("sg_vb", 2, [128, 1024], BF16)
        st = self.pool_of("sg_st", 2, [128, 8], F32)
        for g in range(EXT // GS):
            g0 = g * GS
            hT = self.nxt(hTb)
            gu = self.nxt(guT)
            for j in range(GS // 128):
                xt = self.nxt(nb["x"])
                self.dma("sp", xt.t, xin.t[g0 + j * 128:g0 + (j + 1) * 128, :], [xin.r], [xt.r])
                self.norm_T(xt, A, SH, nb["tmp"], self.nxt(nb["xn"]), nb["junk"], self.nxt(nb["stat"]), hT, j * 128)
            for c in range(8):
                pbk = self.bank()
                for kc in range(8):
                    self.mm(pbk.t[:, 0:GS], Wi.t[:, kc, c * 128:(c + 1) * 128], hT.t[:, kc, :], kc == 0, kc == 7, [Wi.r, hT.r], [pbk.r])
                self.act(uT.t[:, c, :], pbk.t[:, 0:GS], AF.Gelu, [pbk.r, bcol.r], [uT.r], bias=bcol.t[:, c:c + 1])
            for j in range(GS // 128):
                for half in range(2):
                    pbk = self.bank()
                    for kc in range(8):
                        self.mm(pbk.t[:, 0:512], hT.t[:, kc, j * 128:(j + 1) * 128],
                                Wi.t[:, kc, 1024 + half * 512:1024 + (half + 1) * 512], kc == 0, kc == 7, [Wi.r, hT.r], [pbk.r])
                    self.tt("dve", vt.t[:, half * 512:(half + 1) * 512], pbk.t[:, 0:512], BV.t[:, half * 512:(half + 1) * 512],
                            ALU.add, [pbk.r, BV.r], [vt.r])
                s_ = self.nxt(st)
                self.act(vg.t, vt.t, AF.Gelu, [vt.r], [vg.r, s_.r], accum=s_.t[:, 0:1])
                self.act(vt.t, vg.t, AF.Square, [vg.r, s_.r], [vt.r, s_.r], accum=s_.t[:, 1:2])
                self.ts("dve", s_.t[:, 2:3], s_.t[:, 0:1], 1.0 / 1024, None, ALU.mult, None, [s_.r], [s_.r])
                self.tt("dve", s_.t[:, 3:4], s_.t[:, 2:3], s_.t[:, 2:3], ALU.mult, [s_.r], [s_.r])
                self.stt("dve", s_.t[:, 4:5], s_.t[:, 1:2], 1.0 / 1024, s_.t[:, 3:4], ALU.mult, ALU.subtract, [s_.r], [s_.r])
                self.act(s_.t[:, 5:6], s_.t[:, 4:5], AF.Sqrt, [s_.r, self.epsb.r], [s_.r], bias=self.epsb.t[:, 0:1])
                self.recip(s_.t[:, 5:6], s_.t[:, 5:6], [s_.r], [s_.r])
                self.ts("dve", vg.t, vg.t, s_.t[:, 2:3], s_.t[:, 5:6], ALU.subtract, ALU.mult, [vg.r, s_.r], [vg.r])
                self.tt("pool", vg.t, vg.t, LNG.t, ALU.mult, [vg.r, LNG.r], [vg.r])
                v2 = self.nxt(vb)
                self.tt("dve", v2.t, vg.t, LNB.t, ALU.add, [vg.r, LNB.r], [v2.r])
                for a in range(2):
                    pbk = self.bank()
                    for q in range(4):
                        gi = 4 * a + q
                        self.P.op("pe", (lambda o_, l_, r_: (lambda h_: h_.matmul(o_, l_, r_, start=True, stop=False)))(
                            pbk.t[:, q * 128:(q + 1) * 128], v2.t[:, gi * 128:(gi + 1) * 128], wsT.t[:, gi, :]),
                            reads=[v2.r, wsT.r], writes=[pbk.r], signal=False)
                        self.P.op("pe", (lambda o_, l_, r_: (lambda h_: h_.matmul(o_, l_, r_, start=False, stop=True)))(
                            pbk.t[:, q * 128:(q + 1) * 128], self.ones.t[0:1, :], bsr.t[0:1, gi * 128:(gi + 1) * 128]),
                            reads=[self.ones.r, bsr.r], writes=[pbk.r], signal=(q == 3))
                    self.tt("dve", gu.t[:, 4 * a:4 * a + 4, j * 128:(j + 1) * 128],
                            pbk.t[:, 0:512].rearrange("p (a b) -> p a b", a=4), uT.t[:, 4 * a:4 * a + 4, j * 128:(j + 1) * 128],
                            ALU.mult, [pbk.r, uT.r], [gu.r])
            self.resid_store(xin, xout, gu, 8, Wo, g0)
        return xout


    def fft_stage1(self, X, Ad, f1):
        stg = self.pool_of("s1stg", 2, [128, 4, 512], BF16)
        for s_ in range(64):
            st = self.nxt(stg)
            for q in range(4):
                pbk = self.pb[(4 * (s_ % 2)) + q]
                self.mm(pbk.t[:, 0:512], f1.t[:, q * 128:(q + 1) * 128], X.t[:, s_, :], True, True, [f1.r, X.r], [pbk.r])
                self.cp("act" if q < 2 else "dve", st.t[:, q, :], pbk.t[:, 0:512], [pbk.r], [st.r])
            self.dma("sp", Ad.t[:, :, s_, :].rearrange("q p c -> p q c"), st.t, [st.r], [Ad.r])

    def load_B(self, B, Ad, k1g, G):
        kc, p0 = k1g // 128, k1g % 128
        for ri in range(2):
            self.dma("sp", B.t[ri * 64:(ri + 1) * 64, :, :], Ad.t[2 * kc + ri, p0:p0 + G, :, :].rearrange("p s c -> s p c"),
                     [Ad.r], [B.r])

    def load_tab(self, Tb, tab, k1g, G):
        self.dma("sp", Tb.t, tab.t[k1g:k1g + G].rearrange("k p m -> p k m"), [], [Tb.r])

    def phase_hyena(self):
        G = 4
        zT = self.inp("hy_zT", [33, L])
        w1i = self.inp("hy_w1", [33, 64])
        w2i = self.inp("hy_w2", [64, 64])
        prmi = self.inp("hy_prm", [64, 4])
        w3i = self.inp("hy_w3", [64, 2048])
        dsk = self.inp("hy_bias", [2, 512])
        win = self.inp("hy_win", [128, 64, 512])
        f1i = self.inp("t_F1", [128, 512], BF16)
        Htab = self.inp("t_H", [256, 128, 128], BF16)
        Hctab = self.inp("t_Hc", [256, 128, 128], BF16)
        M1tab = self.inp("t_M1", [256, 128, 128], BF16)
        M2tab = self.inp("t_M2", [256, 128, 128], BF16)
        fvi = self.inp("t_Finv", [128, 512], BF16)
        fei = self.inp("t_FinvE", [128, 4, 68], BF16)
        A0 = self.scratch("fftA0", [4, 128, 64, 512], BF16)
        A1 = self.scratch("fftA1", [4, 128, 64, 512], BF16)
        Cd = self.scratch("fftC", [128, 256, 512], BF16)
        Hf = self.scratch("fftHf", [2, 2, 64, 256, 512], BF16)
        X1s = self.scratch("X1s", [128, 64, 512], BF16)
        self.top = self.cmark
        f1 = self.alloc("f1", [128, 512], BF16)
        fv = self.alloc("fv", [128, 512], BF16)
        fe = self.alloc("fe", [128, 4, 68], BF16)
        self.dma("sp", f1.t, f1i.t, [], [f1.r])
        self.dma("sp", fv.t, fvi.t, [], [fv.r])
        self.dma("sp", fe.t, fei.t, [], [fe.r])
        base = self.top
        h2T = self.alloc("h2T", [128, L], BF16)
        w3 = self.alloc("w3sb", [64, 2048], BF16)
        self.dma("pool", w3.t, w3i.t, [], [w3.r])
        mlp_mark = self.top
        w1 = self.alloc("w1sb", [33, 64], F32)
        w2 = self.alloc("w2sb", [64, 64], F32)
        prm = self.alloc("prm", [64, 4], F32)
        pr2 = self.alloc("pr2", [64, 4], F32)
        self.dma("sp", w1.t, w1i.t, [], [w1.r])
        self.dma("sp", w2.t, w2i.t, [], [w2.r])
        self.dma("sp", prm.t, prmi.t, [], [prm.r])
        for a in range(2):
            self.ts("dve", pr2.t[:, 2 * a:2 * a + 1], prm.t[:, 2 * a + 1:2 * a + 2], 1.0 / 3, None, ALU.mult, None, [prm.r, pr2.r], [pr2.r])
            self.tt("dve", pr2.t[:, 2 * a + 1:2 * a + 2], pr2.t[:, 2 * a:2 * a + 1], prm.t[:, 2 * a:2 * a + 1], ALU.mult, [prm.r, pr2.r], [pr2.r])
        ztb = self.pool_of("ztb", 2, [33, 512], F32)
        s3 = self.alloc("s3", [64, 512], F32)
        qq = self.alloc("qq", [64, 512], F32)
        h1 = self.alloc("h1", [64, 512], F32)

        def sin3(dst, src_ps, a):
            self.act(s3.t, src_ps.t[0:64, 0:512], AF.Sin, [src_ps.r, pr2.r], [s3.r], bias=pr2.t[:, 2 * a + 1:2 * a + 2], scale=pr2.t[:, 2 * a:2 * a + 1])
            self.tt("dve", qq.t, s3.t, s3.t, ALU.mult, [s3.r], [qq.r])
            self.ts("dve", qq.t, qq.t, -4.0, 3.0, ALU.mult, ALU.add, [qq.r], [qq.r])
            self.tt("dve", dst[0], qq.t, s3.t, ALU.mult, [qq.r, s3.r], [dst[1]])

        for ch in range(L // 512):
            zt = self.nxt(ztb)
            self.dma("sp", zt.t, zT.t[:, ch * 512:(ch + 1) * 512], [], [zt.r])
            pa, pb2 = self.pb[ch % 2], self.pb[2 + ch % 2]
            self.mm(pa.t[0:64, 0:512], w1.t, zt.t, True, True, [w1.r, zt.r], [pa.r])
            sin3((h1.t, h1.r), pa, 0)
            self.mm(pb2.t[0:64, 0:512], w2.t, h1.t, True, True, [w2.r, h1.r], [pb2.r])
            sin3((h2T.t[0:64, ch * 512:(ch + 1) * 512], h2T.r), pb2, 1)
        self.P.barrier()
        self.top = mlp_mark
        h2v = h2T.t[0:64, :].rearrange("p (j s) -> p s j", s=64)
        Dz = self.alloc("Dz", [128, 512], F32)
        omark = self.top
        for o in range(2):
            self.P.barrier()
            self.top = omark
            self.memset("pool", Dz.t, 0.0, [Dz.r])
            self.dma("sp", Dz.t[0:64, :], dsk.t[o].partition_broadcast(64), [], [Dz.r])
            Xf = self.alloc("Xf", [128, 64, 512], BF16)
            Xb = self.alloc("Xb", [128, 64, 512], BF16)
            wt = self.pool_of("wint", 2, [128, 512], F32)
            for s_ in range(64):
                wn = self.nxt(wt)
                self.dma("sp", wn.t, win.t[:, s_, :], [], [wn.r])
                for dr, Xd in ((0, Xf), (1, Xb)):
                    pbk = self.bank()
                    c0 = (o * 2 + dr) * 512
                    self.mm(pbk.t[:, 0:512], h2v[:, s_, :], w3.t[:, c0:c0 + 512], True, True, [h2T.r, w3.r], [pbk.r])
                    self.tt("dve", Xd.t[:, s_, :], pbk.t[:, 0:512], wn.t, ALU.mult, [pbk.r, wn.r], [Xd.r])
            self.memset("pool", Xb.t[0:1, 0, :], 0.0, [Xb.r])
            self.fft_stage1(Xf, A0, f1)
            self.fft_stage1(Xb, A1, f1)
            self.P.barrier()
            self.top = omark
            Bf = self.pool_of("Bf", 2, [128, G, 512], BF16)
            Bb = self.pool_of("Bb", 2, [128, G, 512], BF16)
            Ht = self.pool_of("Ht", 2, [128, G, 128], BF16)
            Hct = self.pool_of("Hct", 2, [128, G, 128], BF16)
            Hst = self.pool_of("Hst", 2, [128, G, 512], BF16)
            Hfo = Hf.t[o].rearrange("r k a c -> (r k) a c")
            def f_loads(k1g):
                bf, bb, ht, hct = self.nxt(Bf), self.nxt(Bb), self.nxt(Ht), self.nxt(Hct)
                self.load_B(bf, A0, k1g, G)
                self.load_B(bb, A1, k1g, G)
                self.load_tab(ht, Htab, k1g, G)
                self.load_tab(hct, Hctab, k1g, G)
                return bf, bb, ht, hct

            nxt_l = f_loads(0)
            for k1g in range(0, 256, G):
                bf, bb, ht, hct = nxt_l
                if k1g + G < 256:
                    nxt_l = f_loads(k1g + G)
                hs = self.nxt(Hst)
                for g in range(G):
                    pbk = self.bank()
                    self.mm(pbk.t[:, 0:512], ht.t[:, g, :], bf.t[:, g, :], True, False, [ht.r, bf.r], [pbk.r])
                    self.mm(pbk.t[:, 0:512], hct.t[:, g, :], bb.t[:, g, :], False, True, [hct.r, bb.r], [pbk.r])
                    self.tt("dve", hs.t[:, g, :], pbk.t[:, 0:512], Dz.t, ALU.add, [pbk.r, Dz.r], [hs.r])
                self.dma("sp", Hfo[:, k1g:k1g + G, :], hs.t, [hs.r], [Hf.r])
        self.P.barrier()
        self.top = base
        X = self.alloc("Xc", [128, 64, 512], BF16)
        cmark2 = self.top
        Ub = self.pool_of("Ub", 2, [128, L], BF16)
        xst = self.pool_of("xst", 2, [128, 8, 128], BF16)
        nt = 0
        for part in range(2):
            for cc in range(4):
                U = self.nxt(Ub)
                r0 = part * 512 + cc * 128
                self.dma("sp", U.t, self.VX1.t[r0:r0 + 128, :], [self.VX1.r], [U.r])
                Uv = U.t.rearrange("p (j s) -> p s j", s=64)
                for s0 in range(0, 64, 8):
                    pT = self.pb[6 + nt % 2]
                    nt += 1
                    pTv = pT.t.bitcast(BF16)
                    for i in range(8):
                        self.tr(pTv[:, i * 128:(i + 1) * 128], Uv[:, s0 + i, :], self.ident.t, [U.r, self.ident.r], [pT.r], signal=(i == 7))
                    src = pTv.rearrange("p (a b) -> p a b", a=8)
                    if part == 0:
                        self.cp("act", X.t[:, s0:s0 + 8, cc * 128:(cc + 1) * 128], src, [pT.r], [X.r])
                    else:
                        st = self.nxt(xst)
                        self.cp("act", st.t, src, [pT.r], [st.r])
                        self.dma("sp", X1s.t[:, s0:s0 + 8, cc * 128:(cc + 1) * 128], st.t, [st.r], [X1s.r])
        for conv in range(2):
            self.P.barrier()
            self.top = cmark2
            if conv == 1:
                x2sb = self.alloc("x2sb", [128, 4, EXT], BF16)
                yaT = self.alloc("yaT", [128, 4, EXT], BF16)
                self.dma("sp", x2sb.t, self.X2.t.rearrange("(cc p) t -> p cc t", p=128), [self.X2.r], [x2sb.r])
            m2 = self.top
            self.fft_stage1(X, A0, f1)
            self.P.barrier()
            self.top = m2
            Bt = self.pool_of("Bt", 2, [128, G, 512], BF16)
            Ht = self.pool_of("cHt", 2, [128, G, 128], BF16)
            M1t = self.pool_of("cM1", 2, [128, G, 128], BF16)
            M2t = self.pool_of("cM2", 2, [128, G, 128], BF16)
            HH1 = self.pool_of("HH1", 2, [128, G, 512], BF16)
            HH2 = self.pool_of("HH2", 2, [128, G, 512], BF16)
            T1 = self.pool_of("T1", 2, [128, 512], BF16)
            T2 = self.pool_of("T2", 2, [128, 512], BF16)
            Cst = self.pool_of("Cst", 2, [128, G, 512], BF16)
            def c_loads(k1g):
                bt, ht, m1, m2_, h1_, h2_ = (self.nxt(Bt), self.nxt(Ht), self.nxt(M1t), self.nxt(M2t), self.nxt(HH1), self.nxt(HH2))
                self.load_B(bt, A0, k1g, G)
                self.load_tab(ht, Htab, k1g, G)
                self.load_tab(m1, M1tab, k1g, G)
                self.load_tab(m2_, M2tab, k1g, G)
                for half in range(2):
                    self.dma("sp", h1_.t[half * 64:(half + 1) * 64, :, :], Hf.t[conv, 0, :, k1g:k1g + G, :], [Hf.r], [h1_.r])
                    self.dma("sp", h2_.t[half * 64:(half + 1) * 64, :, :], Hf.t[conv, 1, :, k1g:k1g + G, :], [Hf.r], [h2_.r])
                return bt, ht, m1, m2_, h1_, h2_

            nxt_l = c_loads(0)
            for k1g in range(0, 256, G):
                bt, ht, m1, m2_, h1_, h2_ = nxt_l
                if k1g + G < 256:
                    nxt_l = c_loads(k1g + G)
                cs_ = self.nxt(Cst)
                for g in range(G):
                    pu = self.bank()
                    self.mm(pu.t[:, 0:512], ht.t[:, g, :], bt.t[:, g, :], True, True, [ht.r, bt.r], [pu.r])
                    t1, t2 = self.nxt(T1), self.nxt(T2)
                    self.tt("dve", t1.t, pu.t[:, 0:512], h1_.t[:, g, :], ALU.mult, [pu.r, h1_.r], [t1.r])
                    self.tt("dve", t2.t, pu.t[:, 0:512], h2_.t[:, g, :], ALU.mult, [pu.r, h2_.r], [t2.r])
                    pc = self.bank()
                    self.mm(pc.t[:, 0:512], m1.t[:, g, :], t1.t, True, False, [m1.r, t1.r], [pc.r])
                    self.mm(pc.t[:, 0:512], m2_.t[:, g, :], t2.t, False, True, [m2_.r, t2.r], [pc.r])
                    self.cp("act", cs_.t[:, g, :], pc.t[:, 0:512], [pc.r], [cs_.r])
                self.dma("sp", Cd.t[:, k1g:k1g + G, :], cs_.t, [cs_.r], [Cd.r])
            self.P.barrier()
            self.top = m2
            Dt = self.pool_of("Dt", 2, [128, 4, 512], BF16)
            x1t = self.pool_of("x1t", 2, [128, 512], BF16)
            for s_ in range(64):
                dt_ = self.nxt(Dt)
                for kc in range(2):
                    for ri in range(2):
                        self.dma("sp", dt_.t[:, 2 * kc + ri, :], Cd.t[ri * 64 + s_, kc * 128:(kc + 1) * 128, :], [Cd.r], [dt_.r])
                if conv == 0:
                    xg = self.nxt(x1t)
                    self.dma("sp", xg.t, X1s.t[:, s_, :], [X1s.r], [xg.r])
                    pbk = self.bank()
                    for q in range(4):
                        self.mm(pbk.t[:, 0:512], fv.t[:, q * 128:(q + 1) * 128], dt_.t[:, q, :], q == 0, q == 3, [fv.r, dt_.r], [pbk.r])
                    self.tt("dve", X.t[:, s_, :], pbk.t[:, 0:512], xg.t, ALU.mult, [pbk.r, xg.r], [X.r])
                else:
                    pbk = self.bank()
                    for cc in range(4):
                        for q in range(4):
                            self.P.op("pe", (lambda o_, l_, r_, a_, b_: (lambda h_: h_.matmul(o_, l_, r_, start=a_, stop=b_)))(
                                pbk.t[:, cc * 68:(cc + 1) * 68], dt_.t[:, q, cc * 128:(cc + 1) * 128], fe.t[:, q, :], q == 0, q == 3),
                                reads=[dt_.r, fe.r], writes=[pbk.r], signal=(cc == 3 and q == 3))
                    x2v = x2sb.t.rearrange("p c (j s) -> p c s j", s=64)[:, :, s_, :]
                    yav = yaT.t.rearrange("p c (j s) -> p c s j", s=64)[:, :, s_, :]
                    self.tt("dve", yav, pbk.t[:, 0:272].rearrange("p (c j) -> p c j", c=4), x2v, ALU.mult, [pbk.r, x2sb.r], [yaT.r])
            if conv == 1:
                self.dma("sp", self.YT.t[0:512, :].rearrange("(cc p) t -> p cc t", p=128), yaT.t, [yaT.r], [self.YT.r])


def build_program(dbg=False, stop_after=None):
    kb = KB(stop_after)
    kb.dbg = dbg
    kb._bk = 0
    kb.consts()
    kb.phase_mod()
    kb.P.barrier()
    if stop_after == "mod":
        kb.P.build()
        return kb
    kb.phase_inproj()
    kb.P.barrier()
    if stop_after == "inproj":
        kb.P.build()
        return kb
    kb.phase_attn()
    kb.P.barrier()
    if stop_after == "attn":
        kb.P.build()
        return kb
    if os.environ.get("FAKE_YA"):
        ya = kb.inp("dbg_yaT", [512, EXT], BF16)
        st = kb.alloc("yast", [128, EXT], BF16)
        for c in range(4):
            kb.dma("sp", st.t, ya.t[c * 128:(c + 1) * 128, :], [ya.r], [st.r])
            kb.dma("sp", kb.YT.t[c * 128:(c + 1) * 128, :], st.t, [st.r], [kb.YT.r])
        kb.P.barrier()
    else:
        kb.phase_hyena()
        kb.P.barrier()
    if stop_after == "hyena":
        kb.P.build()
        return kb
    wo = kb.inp("ab_w_out", [D, D])
    x1 = kb.phase_outproj("X1o", kb.YT, kb.x_ext, wo.t, 0)
    kb.P.barrier()
    x2 = kb.phase_ffn("X2o", x1, 0)
    kb.P.barrier()
    x3 = kb.phase_sgu("X3o", x2, 1)
    kb.P.barrier()
    kb.phase_ffn("X4o", x3, 1, final=True)
    kb.P.barrier()
    kb.P.build()
    return kb


def rope_tables(pos_row, pos_col):
    T = pos_row.shape[0]
    cos = np.zeros((128, T), np.float32)
    sin = np.zeros((128, T), np.float32)
    inv = (10000.0 ** (-np.arange(16, dtype=np.float32) / 16)).astype(np.float32)
    for f in range(128):
        d = f % 64
        pos = pos_row if d < 32 else pos_col
        dd = d % 32
        i = dd % 16
        ang = pos.astype(np.float32) * inv[i]
        cos[f] = np.cos(ang)
        sin[f] = -np.sin(ang) if dd < 16 else np.sin(ang)
    return cos, sin


def rperm_matrix():
    m = np.zeros((128, 128), np.float32)
    for f in range(128):
        dd = (f % 64) % 32
        partner = f + 16 if dd < 16 else f - 16
        m[partner, f] = 1.0
    return m.astype(ml_dtypes.bfloat16)


def fft_tables(lo):
    bf = ml_dtypes.bfloat16
    N = 16384
    j = np.arange(128, dtype=np.float64)[:, None]
    k1 = np.arange(256, dtype=np.float64)[None, :]
    th = 2 * np.pi * ((j * k1) % 256) / 256
    F1 = np.zeros((128, 4, 128))
    Fi = np.zeros((128, 4, 128))
    for kc in range(2):
        F1[:, 2 * kc, :] = np.cos(th[:, kc * 128:(kc + 1) * 128])
        F1[:, 2 * kc + 1, :] = -np.sin(th[:, kc * 128:(kc + 1) * 128])
        Fi[:, 2 * kc, :] = np.cos(th[:, kc * 128:(kc + 1) * 128]).T / N
        Fi[:, 2 * kc + 1, :] = -np.sin(th[:, kc * 128:(kc + 1) * 128]).T / N
    j0 = lo // 64
    FiE = Fi[:, :, j0:j0 + 68]
    s = np.arange(64, dtype=np.float64)
    k2 = np.arange(64, dtype=np.float64)
    kk = np.arange(256, dtype=np.float64)
    ang = 2 * np.pi * ((s[None, :, None] * kk[:, None, None]) / N + ((s[None, :, None] * k2[None, None, :]) % 64) / 64)
    Zr, Zi = np.cos(ang), -np.sin(ang)
    H = np.zeros((256, 128, 128))
    H[:, 0:64, 0:64] = Zr
    H[:, 64:128, 0:64] = -Zi
    H[:, 0:64, 64:128] = Zi
    H[:, 64:128, 64:128] = Zr
    Hc = H.copy()
    Hc[:, :, 64:128] *= -1
    ZrT, ZiT = Zr.transpose(0, 2, 1), Zi.transpose(0, 2, 1)
    M1 = np.zeros((256, 128, 128))
    M1[:, 0:64, 0:64] = ZrT
    M1[:, 64:128, 0:64] = ZiT
    M1[:, 0:64, 64:128] = -ZiT
    M1[:, 64:128, 64:128] = ZrT
    M2 = np.zeros((256, 128, 128))
    M2[:, 0:64, 0:64] = ZiT
    M2[:, 64:128, 0:64] = -ZrT
    M2[:, 0:64, 64:128] = ZrT
    M2[:, 64:128, 64:128] = ZiT
    c = lambda a: np.ascontiguousarray(a.astype(np.float32)).astype(bf)
    return {"t_F1": c(F1.reshape(128, 512)), "t_Finv": c(Fi.reshape(128, 512)), "t_FinvE": c(FiE),
            "t_H": c(H), "t_Hc": c(Hc), "t_M1": c(M1), "t_M2": c(M2)}


def hyena_consts():
    f32 = np.float32
    bands = 16
    pos = np.arange(L, dtype=f32)
    t = np.linspace(0.0, 1.0, L, dtype=f32)[:, None]
    ang = (f32(2.0 * math.pi / L) * pos[:, None] * np.linspace(1e-4, bands - 1, bands, dtype=f32)[None, :]).astype(f32)
    z = np.concatenate([t, np.cos(ang), -np.sin(ang)], axis=-1).astype(f32)
    deltas = np.abs(np.linspace(math.log(1e-2) / 1.5, math.log(1e-2) / 0.3, 512, dtype=f32))
    window = (np.exp(-t * deltas[None, :]) + f32(0.05)).astype(f32)
    win = np.ascontiguousarray(window.reshape(128, 64, 512))
    return np.ascontiguousarray(z.T), win


_TAB = {}


def host_inputs(inputs, cid):
    b, hf = cid // 2, cid % 2
    lo = 0 if hf == 0 else L - EXT
    f32 = np.float32
    m = {}
    m["c_ident"] = np.eye(128, dtype=f32).astype(ml_dtypes.bfloat16)
    cc = np.stack([inputs["c"][b], inputs["c_ctx"]], -1).astype(f32)
    m["ccol"] = np.ascontiguousarray(cc.reshape(8, 128, 2).transpose(1, 0, 2))
    m["mod_w"] = inputs["mod_w"]
    m["mod_b"] = inputs["mod_b"]
    m["x_full"] = np.ascontiguousarray(inputs["x"][b])
    m["x_ext"] = np.ascontiguousarray(inputs["x"][b, lo:lo + EXT])
    m["ctx"] = np.ascontiguousarray(inputs["ctx"][b])
    m["w_in"] = np.ascontiguousarray(inputs["ab_w_in"][0])
    cw = np.concatenate([inputs["hy_conv_w"][0], inputs["hy_conv_b"][0][None]], 0)
    m["hy_cw"] = np.ascontiguousarray(cw.reshape(4, 12, 128).transpose(2, 1, 0)).astype(f32)
    m["norm_mix_g"] = inputs["norm_mix_g"]
    t = np.arange(L)
    ck, sk = rope_tables(t // 64, t % 64)
    m["ropek_cos"], m["ropek_sin"] = ck, sk
    m["ropeq_cos"] = np.ascontiguousarray(ck[:, lo:lo + EXT])
    m["ropeq_sin"] = np.ascontiguousarray(sk[:, lo:lo + EXT])
    m["c_rperm"] = rperm_matrix()
    m["da_lambda"] = np.ascontiguousarray(inputs["da_lambda"][0]).astype(f32)
    m["ab_w_out"] = np.ascontiguousarray(inputs["ab_w_out"][0])
    m["norm_ffn_g"] = inputs["norm_ffn_g"]
    m["final_norm_g"] = inputs["final_norm_g"]
    for i in range(2):
        m["ffn_w_up%d" % i] = np.ascontiguousarray(inputs["ffn_w_up"][i])
        m["ffn_w_down%d" % i] = np.ascontiguousarray(inputs["ffn_w_down"][i])
        fcw = np.concatenate([inputs["ffn_conv_w"][i], inputs["ffn_conv_b"][i][None]], 0)
        m["ffn_cw%d" % i] = np.ascontiguousarray(fcw.reshape(4, 44, 128).transpose(2, 1, 0)).astype(f32)
    m["sgu_w_in"] = np.ascontiguousarray(inputs["sgu_w_in"][0])
    m["sgu_bu_col"] = np.ascontiguousarray(inputs["sgu_b_in"][0][:1024].reshape(8, 128).T).astype(f32)
    m["sgu_bv"] = np.ascontiguousarray(inputs["sgu_b_in"][0][1024:])
    m["sgu_ln_g"] = inputs["sgu_ln_g"][0]
    m["sgu_ln_b"] = inputs["sgu_ln_b"][0]
    m["sgu_wsT"] = np.ascontiguousarray(inputs["sgu_w_s"][0].transpose(2, 0, 1)).astype(f32)
    m["sgu_b_s"] = np.ascontiguousarray(inputs["sgu_b_s"][0].reshape(1, 1024)).astype(f32)
    m["sgu_w_out"] = np.ascontiguousarray(inputs["sgu_w_out"][0])
    if lo not in _TAB:
        _TAB[lo] = fft_tables(lo)
    if "hc" not in _TAB:
        _TAB["hc"] = hyena_consts()
    m.update(_TAB[lo])
    m["hy_zT"], m["hy_win"] = _TAB["hc"]
    m["hy_w1"] = np.ascontiguousarray(inputs["hy_w1"][0])
    m["hy_w2"] = np.ascontiguousarray(inputs["hy_w2"][0])
    m["hy_prm"] = np.ascontiguousarray(np.stack([inputs["hy_b1"][0], inputs["hy_freq"][0][0], inputs["hy_b2"][0], inputs["hy_freq"][0][1]], -1)).astype(f32)
    m["hy_w3"] = np.ascontiguousarray(inputs["hy_w3"][0])
    m["hy_bias"] = np.ascontiguousarray(inputs["hy_bias"][0])
    m["subln_col"] = np.ascontiguousarray(inputs["da_subln_g"][0].reshape(128, 1)).astype(f32)
    return m


_CACHE = {}


def kernel(**inputs):
    inputs = {k: np.asarray(v) for k, v in inputs.items()}
    if "kb" not in _CACHE:
        _CACHE["kb"] = build_program()
    kb = _CACHE["kb"]
    in_maps = []
    for cid in range(8):
        m = host_inputs(inputs, cid)
        in_maps.append({k: m[k] for k in kb.ins})
    res = run_bass_kernel_spmd(kb.nc, in_maps, core_ids=list(range(8)))
    out = np.zeros((4, L, D), np.float32)
    for cid in range(8):
        b, hf = cid // 2, cid % 2
        o = res.results[cid]["out"]
        if hf == 0:
            out[b, :4096] = o[:4096]
        else:
            out[b, 4096:] = o[EXT - 4096:]
    return out
```

```python
import math
import os
import numpy as np
import ml_dtypes
import concourse.bass as bass
import concourse.mybir as mybir
from concourse.bass_utils import run_bass_kernel_spmd

F32 = mybir.dt.float32
BF16 = mybir.dt.bfloat16
U8 = mybir.dt.uint8
AF = mybir.ActivationFunctionType
ALU = mybir.AluOpType

ENGS = ("pe", "act", "dve", "pool", "sp")
SEM_ROT = 16000
L = 8192
EXT = 4352
NEXT_T = EXT // 128
D = 1024
DFF = 2816
EPS = 1e-6
GS = 256


class Res:
    __slots__ = ("name", "w", "r", "dsem", "dcnt", "excl", "lk")

    def __init__(self, name, excl=False):
        self.name = name
        self.lk = None
        self.excl = excl
        self.w = {}
        self.r = {}
        self.dsem = None
        self.dcnt = 0


class Prog:
    def __init__(self, nc):
        self.nc = nc
        self.ops = {e: [] for e in ENGS}
        self.cnt = {e: 0 for e in ENGS}
        self.epoch = {e: 0 for e in ENGS}
        self.sems = {}
        self.known = {e: {} for e in ENGS}
        self.dres = []
        self.meta = {e: [] for e in ENGS}
        self.free_sems = []
        self.nd = 0
        for e in ENGS:
            if e != "sp":
                self._engsem(e)

    def _engsem(self, e):
        k = ("E", e, self.epoch[e])
        if k not in self.sems:
            self.sems[k] = self.nc.alloc_semaphore(name="s_%s_%d" % (e, self.epoch[e]))
        return k

    def _semname(self, sem):
        for k, v in self.sems.items():
            if v is sem:
                return k
        return None

    def check_deadlock(self):
        val = {}
        pc = {e: 0 for e in ENGS}
        prog = True
        while prog:
            prog = False
            for e in ENGS:
                while pc[e] < len(self.meta[e]):
                    waits, inc, desc = self.meta[e][pc[e]]
                    if all(val.get(k, 0) >= v for k, v in waits):
                        if inc is not None:
                            val[inc[0]] = val.get(inc[0], 0) + inc[1]
                        pc[e] += 1
                        prog = True
                    else:
                        break
        bad = False
        for e in ENGS:
            if pc[e] < len(self.meta[e]):
                bad = True
                waits, inc, desc = self.meta[e][pc[e]]
                print("DEADLOCK", e, pc[e], len(self.meta[e]), desc, [(k, v, val.get(k, 0)) for k, v in waits if val.get(k, 0) < v])
        return not bad

    def _deps(self, reads, writes):
        deps = {}
        for r in reads:
            for k, v in r.w.items():
                if deps.get(k, 0) < v:
                    deps[k] = v
        for w in writes:
            for d in (w.w, w.r):
                for k, v in d.items():
                    if deps.get(k, 0) < v:
                        deps[k] = v
        return deps

    def _waits(self, eng, deps):
        kn = self.known[eng]
        out = []
        for k, v in deps.items():
            if kn.get(k, 0) < v:
                kn[k] = v
                out.append((self.sems[k], v))
        return out

    def op(self, eng, fn, reads=(), writes=(), signal=True):
        deps = self._deps(reads, writes)
        k = self._engsem(eng)
        for r in reads:
            if r.excl:
                for kk, vv in r.r.items():
                    if not (kk[0] == "E" and kk[1] == eng) and deps.get(kk, 0) < vv:
                        deps[kk] = vv
        if eng == "pe":
            deps = {kk: vv for kk, vv in deps.items() if kk[1] != "pe" or kk[0] != "E"}
        else:
            deps = {kk: vv for kk, vv in deps.items() if not (kk == k and vv > self.cnt[eng])}
        waits = self._waits(eng, deps)
        if signal:
            self.cnt[eng] += 1
            tok = (k, self.cnt[eng])
            sem = self.sems[k]
        else:
            tok = (k, self.cnt[eng] + 1)
            sem = None

        def emit(h, fn=fn, waits=waits, sem=sem):
            for s, v in waits:
                h.wait_ge(s, v)
            ins = fn(h)
            if sem is not None:
                ins.then_inc(sem, 1)

        self.ops[eng].append(emit)
        self.meta[eng].append(([(self._semname(s_), v_) for s_, v_ in waits], (k, 1) if signal else None,
                               "op r=%s w=%s" % ([r.name for r in reads], [w.name for w in writes])))
        kk, vv = tok
        for w in writes:
            w.w = {kk: vv}
            w.r = {}
        for r in reads:
            if r.r.get(kk, 0) < vv:
                r.r[kk] = vv
        if signal and self.cnt[eng] >= SEM_ROT:
            self.epoch[eng] += 1
            self.cnt[eng] = 0
            self._engsem(eng)

    def dma(self, eng, out, in_, reads=(), writes=(), store=False):
        assert len(writes) == 1
        wres = writes[0]
        owner = reads[0] if store else wres
        deps = {}
        for r in reads:
            for k, v in r.w.items():
                if deps.get(k, 0) < v:
                    deps[k] = v
        for d in ((wres.r,) if store else (wres.w, wres.r)):
            for k, v in d.items():
                if deps.get(k, 0) < v:
                    deps[k] = v
        if owner.dsem is None:
            if eng == "pool":
                self.nd += 1
                owner.dsem = ("S", self.nd)
                owner.dcnt = 0
                self.sems[owner.dsem] = self.nc.alloc_semaphore(name="sw_%d" % self.nd)
            elif self.free_sems:
                owner.dsem, owner.dcnt = self.free_sems.pop()
            else:
                self.nd += 1
                owner.dsem = ("D", self.nd)
                owner.dcnt = 0
                self.sems[owner.dsem] = self.nc.alloc_semaphore(name="d_%d" % self.nd)
            self.dres.append(owner)
        if (not store) and owner.lk == "load" and not wres.r:
            deps.pop(owner.dsem, None)
        owner.lk = "store" if store else "load"
        waits = self._waits(eng, deps)
        owner.dcnt += 1
        k, v = owner.dsem, 16 * owner.dcnt
        sem = self.sems[k]

        def emit(h, waits=waits, sem=sem, out=out, in_=in_):
            for s, vv in waits:
                h.wait_ge(s, vv)
            h.dma_start(out=out, in_=in_).then_inc(sem, 16)

        self.ops[eng].append(emit)
        self.meta[eng].append(([(self._semname(s_), v_) for s_, v_ in waits], (k, 16),
                               "dma r=%s w=%s" % ([r.name for r in reads], [w.name for w in writes])))
        if store:
            wres.w[k] = v
        else:
            wres.w = {k: v}
            wres.r = {}
        for r in reads:
            if r.r.get(k, 0) < v:
                r.r[k] = v

    def barrier(self):
        toks = {}
        for e in ENGS:
            if e == "sp":
                continue
            k = self._engsem(e)
            if self.cnt[e] > 0:
                toks[k] = self.cnt[e]
            if self.epoch[e] > 0:
                toks[("E", e, self.epoch[e] - 1)] = SEM_ROT
        for r in self.dres:
            toks[r.dsem] = 16 * r.dcnt
        for e in ENGS:
            waits = self._waits(e, dict(toks))

            def emit(h, waits=waits):
                for s, v in waits:
                    h.wait_ge(s, v)

            self.ops[e].append(emit)
            self.meta[e].append(([(self._semname(s_), v_) for s_, v_ in waits], None, "barrier"))
        keep = []
        for r in self.dres:
            if r.dsem[0] == "S":
                keep.append(r)
                continue
            self.free_sems.append((r.dsem, r.dcnt))
            r.dsem = None
        self.dres = keep

    def build(self):
        nc = self.nc
        with nc.Block() as block:
            @block.tensor
            def _(h):
                for f in self.ops["pe"]:
                    f(h)

            @block.scalar
            def _(h):
                for f in self.ops["act"]:
                    f(h)

            @block.vector
            def _(h):
                for f in self.ops["dve"]:
                    f(h)

            @block.gpsimd
            def _(h):
                for f in self.ops["pool"]:
                    f(h)

            @block.sync
            def _(h):
                for f in self.ops["sp"]:
                    f(h)


class Buf:
    __slots__ = ("t", "r")

    def __init__(self, t, r):
        self.t = t
        self.r = r


def _dtsize(dt):
    return 4 if dt == F32 else (2 if dt == BF16 else 1)


class KB:
    def __init__(self, stop_after=None):
        nc = bass.Bass("TRN2", target_bir_lowering=False)
        self.nc = nc
        self.P = Prog(nc)
        self.stop_after = stop_after
        self.ARENA = 207 * 1024
        self.arena = nc.alloc_sbuf_tensor("arena", [128, self.ARENA], U8)
        self.top = 0
        ps = nc.alloc_psum_tensor("psum", [128, 4096], F32)
        self.ps = ps
        self.pb = [Buf(ps[:, 512 * i:512 * (i + 1)], Res("pb%d" % i, excl=True)) for i in range(8)]
        self.dram = {}
        self.dram_names = set()
        self.ins = {}
        self.outs = {}

    def alloc(self, name, shape, dt):
        n = 1
        for s in shape[1:]:
            n *= s
        nb = n * _dtsize(dt)
        nb = (nb + 31) // 32 * 32
        off = self.top
        self.top += nb
        assert self.top <= self.ARENA, "SBUF arena overflow %s %d" % (name, self.top)
        v = self.arena[:shape[0], off:off + n * _dtsize(dt)].bitcast(dt)
        if len(shape) == 3:
            v = v.rearrange("p (a b) -> p a b", a=shape[1])
        elif len(shape) == 4:
            v = v.rearrange("p (a b c) -> p a b c", a=shape[1], b=shape[2])
        return Buf(v, Res(name))

    def inp(self, name, shape, dt=F32):
        t = self.nc.dram_tensor(name, list(shape), dt, kind="ExternalInput").ap()
        b = Buf(t, Res(name))
        self.ins[name] = b
        return b

    def outp(self, name, shape, dt=F32):
        t = self.nc.dram_tensor(name, list(shape), dt, kind="ExternalOutput").ap()
        self.dram_names.add(name)
        b = Buf(t, Res(name))
        self.outs[name] = b
        return b

    def scratch(self, name, shape, dt, debug=False):
        if debug:
            return self.outp(name, shape, dt)
        t = self.nc.dram_tensor(name, list(shape), dt).ap()
        self.dram_names.add(name)
        return Buf(t, Res(name))

    def mm(self, out, lhsT, rhs, start, stop, reads, writes):
        self.P.op("pe", lambda h: h.matmul(out, lhsT, rhs, start=start, stop=stop),
                  reads=reads, writes=writes, signal=bool(stop))

    def tr(self, out, in_, ident, reads, writes, signal=True):
        self.P.op("pe", lambda h: h.transpose(out, in_, ident), reads=reads, writes=writes, signal=signal)

    def act(self, out, in_, func, reads, writes, bias=None, scale=None, accum=None):
        kw = {}
        if bias is not None:
            kw["bias"] = bias
        if scale is not None:
            kw["scale"] = scale
        if accum is not None:
            kw["accum_out"] = accum
        self.P.op("act", lambda h: h.activation(out=out, in_=in_, func=func, **kw), reads=reads, writes=writes)

    def tt(self, eng, out, a, b, op, reads, writes):
        self.P.op(eng, lambda h: h.tensor_tensor(out=out, in0=a, in1=b, op=op), reads=reads, writes=writes)

    def ts(self, eng, out, a, s1, s2, op0, op1, reads, writes):
        if op1 is None:
            s2, op1 = 0.0, ALU.add
        self.P.op(eng, lambda h: h.tensor_scalar(out, a, s1, s2, op0, op1), reads=reads, writes=writes)

    def stt(self, eng, out, in0, scalar, in1, op0, op1, reads, writes):
        self.P.op(eng, lambda h: h.scalar_tensor_tensor(out=out, in0=in0, scalar=scalar, in1=in1, op0=op0, op1=op1),
                  reads=reads, writes=writes)

    def cp(self, eng, out, in_, reads, writes):
        if eng == "act":
            self.P.op("act", lambda h: h.activation(out=out, in_=in_, func=AF.Copy), reads=reads, writes=writes)
        else:
            self.P.op(eng, lambda h: h.tensor_copy(out, in_), reads=reads, writes=writes)

    def recip(self, out, in_, reads, writes):
        self.P.op("dve", lambda h: h.reciprocal(out, in_), reads=reads, writes=writes)

    def memset(self, eng, out, val, writes):
        self.P.op(eng, lambda h: h.memset(out, val), writes=writes)

    def dma(self, q, out, in_, reads, writes, store=None):
        if store is None:
            store = writes[0].name in self.dram_names
        self.P.dma(q, out, in_, reads=reads, writes=writes, store=store)

    def ld(self, q, dst, src_ap, src=None):
        self.P.dma(q, dst.t if isinstance(dst, Buf) else dst[0], src_ap,
                   reads=[src.r] if src is not None else [], writes=[dst.r if isinstance(dst, Buf) else dst[1]])

    def consts(self):
        self.ident = self.alloc("ident", [128, 128], BF16)
        self.ones = self.alloc("ones", [128, 128], BF16)
        self.epsb = self.alloc("epsb", [128, 1], F32)
        idin = self.inp("c_ident", [128, 128], BF16)
        self.dma("sp", self.ident.t, idin.t, [], [self.ident.r])
        self.memset("pool", self.ones.t, 1.0, [self.ones.r])
        self.memset("pool", self.epsb.t, EPS, [self.epsb.r])
        self.cmark = self.top

    def norm_T(self, xt, A, SH, tmp, xn, junk, stat, hT, col0, ntok=128, plain_g=None):
        ss = stat.t[:, 0:1]
        rs = stat.t[:, 1:2]
        self.act(junk.t, xt.t, AF.Square, [xt.r], [junk.r, stat.r], accum=ss)
        self.act(rs, ss, AF.Sqrt, [stat.r, self.epsb.r], [stat.r], bias=self.epsb.t[:, 0:1], scale=1.0 / D)
        self.recip(rs, rs, [stat.r], [stat.r])
        self.stt("dve", tmp.t, xt.t, rs, A.t, ALU.mult, ALU.mult, [xt.r, stat.r, A.r], [tmp.r])
        self.tt("pool", xn.t, tmp.t, SH.t, ALU.add, [tmp.r, SH.r], [xn.r])
        pT = self.pb[7]
        pTv = pT.t.bitcast(BF16)
        for kc in range(8):
            self.tr(pTv[:, kc * 128:kc * 128 + ntok], xn.t[:ntok, kc * 128:(kc + 1) * 128], self.ident.t[:ntok, :ntok],
                    [xn.r, self.ident.r], [pT.r], signal=(kc == 7))
        src = pTv.rearrange("p (a b) -> p a b", a=8)[:, :, :ntok]
        self.cp("act", hT.t[:, :, col0:col0 + ntok], src, [pT.r], [hT.r])

    def pool_of(self, name, n, shape, dt):
        return {"b": [self.alloc("%s%d" % (name, i), shape, dt) for i in range(n)], "i": 0}

    def nxt(self, pool):
        b = pool["b"][pool["i"] % len(pool["b"])]
        pool["i"] += 1
        return b

    def bank(self):
        b = self.pb[self._bk % 6]
        self._bk += 1
        return b

    def phase_mod(self):
        ccol = self.inp("ccol", [128, 8, 2])
        modw = self.inp("mod_w", [2, 1024, 6144])
        modb = self.inp("mod_b", [2, 6144])
        self.modv = self.scratch("modv", [2, 2, 6144], F32, debug=self.dbg)
        sc = self.alloc("scol", [128, 8, 2], F32)
        self.dma("sp", sc.t, ccol.t, [], [sc.r])
        self.act(sc.t, sc.t, AF.Silu, [sc.r], [sc.r])
        mrow = self.alloc("mrow", [2, 6144], F32)
        mb2 = self.alloc("mb2", [2, 6144], F32)
        wb = [self.alloc("mwb%d" % i, [128, 8, 512], F32) for i in range(2)]
        for i in range(2):
            self.dma("sp", mb2.t, modb.t[i].partition_broadcast(2), [], [mb2.r])
            for n in range(12):
                w = wb[n % 2]
                self.dma("sp", w.t, modw.t[i, :, n * 512:(n + 1) * 512].rearrange("(kc p) f -> p kc f", p=128), [], [w.r])
                pbk = self.pb[n % 2]
                for kc in range(8):
                    self.mm(pbk.t[0:2, :], sc.t[:, kc, :], w.t[:, kc, :], kc == 0, kc == 7, [sc.r, w.r], [pbk.r])
                self.tt("dve", mrow.t[:, n * 512:(n + 1) * 512], pbk.t[0:2, :], mb2.t[:, n * 512:(n + 1) * 512], ALU.add,
                        [pbk.r, mb2.r], [mrow.r])
            self.dma("sp", self.modv.t[i], mrow.t, [mrow.r], [self.modv.r])

    def mod_tiles(self, layer, which, i_sh, i_sc, g_ap):
        A = self.alloc("modA", [128, 1024], F32)
        SH = self.alloc("modSH", [128, 1024], F32)
        G = self.alloc("modG", [128, 1024], F32)
        mv = self.modv.t[layer, which]
        self.dma("sp", A.t, mv[i_sc * D:(i_sc + 1) * D].partition_broadcast(128), [self.modv.r], [A.r])
        self.dma("sp", SH.t, mv[i_sh * D:(i_sh + 1) * D].partition_broadcast(128), [self.modv.r], [SH.r])
        self.dma("sp", G.t, g_ap.partition_broadcast(128), [], [G.r])
        self.stt("dve", A.t, A.t, 1.0, G.t, ALU.add, ALU.mult, [A.r, G.r], [A.r])
        return A, SH

    def load_w_bf16(self, dst, src_ap, kcn):
        for kc in range(kcn):
            self.dma("pool", dst.t[:, kc, :], src_ap[kc * 128:(kc + 1) * 128, :], [], [dst.r])

    def norm_bufs(self):
        nb = {}
        nb["x"] = self.pool_of("nx", 2, [128, 1024], F32)
        nb["tmp"] = self.alloc("ntmp", [128, 1024], F32)
        nb["xn"] = self.pool_of("nxn", 2, [128, 1024], BF16)
        nb["junk"] = self.alloc("njunk", [128, 1024], BF16)
        nb["stat"] = self.pool_of("nstat", 2, [128, 2], F32)
        return nb

    def proj_pass(self, xin, T, A, SH, w, fm, tm, nb, post=None):
        skip = os.environ.get("KSKIP", "")
        fm = [sp for sp in fm if sp["kind"] not in skip.split(",")]
        ng = T // GS
        hT = [self.alloc("hT%d" % i, [128, 8, GS + 32], BF16) for i in range(2)]
        for hb in hT:
            self.memset("pool", hb.t, 0.0, [hb.r])
        tmpc = self.pool_of("tmpc", 2, [128, GS], F32)
        tmpd = self.pool_of("tmpd", 6, [128, GS], F32)
        obf = self.pool_of("obf", 4, [128, GS], BF16)
        tst = self.pool_of("tst", 2, [128, 512], BF16) if tm else None
        cs = self.pool_of("cs", 2, [128, 2, GS], F32) if any(sp_["kind"] == "rope" for sp_ in fm) else None
        for g in range(ng + 1):
            if g < ng:
                h = hT[g % 2]
                for j in range(GS // 128):
                    xt = self.nxt(nb["x"])
                    t0 = g * GS + j * 128
                    self.dma("sp", xt.t, xin.t[t0:t0 + 128, :], [xin.r], [xt.r])
                    self.norm_T(xt, A, SH, nb["tmp"], self.nxt(nb["xn"]), nb["junk"], self.nxt(nb["stat"]), h, 16 + j * 128)
                if g == 0:
                    self.memset("pool", h.t[:, :, 15:16], 0.0, [h.r])
                else:
                    self.cp("pool", h.t[:, :, 15:16], hT[(g - 1) % 2].t[:, :, GS + 15:GS + 16], [hT[(g - 1) % 2].r], [h.r])
            if g == 0:
                continue
            gg = g - 1
            h = hT[gg % 2]
            if g == ng:
                self.memset("pool", h.t[:, :, GS + 16:GS + 17], 0.0, [h.r])
            else:
                self.cp("pool", h.t[:, :, GS + 16:GS + 17], hT[g % 2].t[:, :, 16:17], [hT[g % 2].r], [h.r])
            g0 = gg * GS
            for sp in fm:
                kind = sp["kind"]
                if kind == "rope":
                    c = self.nxt(cs)
                    if "nodma" in os.environ.get("ROPEVAR", ""):
                        self.memset("pool", c.t, 1.0, [c.r])
                    else:
                        self.dma("sp", c.t[:, 0, :], sp["cos"].t[:, g0:g0 + GS], [], [c.r])
                        self.dma("sp", c.t[:, 1, :], sp["sin"].t[:, g0:g0 + GS], [], [c.r])
                for ci in sp.get("order", range(sp["n"])):
                    pbk = self.bank()
                    col = sp["col0"] + ci * 128
                    for kc in range(8):
                        self.mm(pbk.t[:, 0:GS + 4], w.t[:, kc, col:col + 128], h.t[:, kc, 14:GS + 18], kc == 0, kc == 7,
                                [w.r, h.r], [pbk.r])
                    if kind == "conv":
                        cw = sp["cw"]
                        k = sp["cwi0"] + ci
                        t1 = self.nxt(tmpc)
                        ob = self.nxt(obf)
                        self.act(t1.t, pbk.t[:, 2:GS + 2], AF.Identity, [pbk.r, cw.r], [t1.r],
                                 bias=cw.t[:, k, 3:4], scale=cw.t[:, k, 1:2])
                        self.stt("dve", t1.t, pbk.t[:, 1:GS + 1], cw.t[:, k, 0:1], t1.t, ALU.mult, ALU.add,
                                 [pbk.r, cw.r, t1.r], [t1.r])
                        if "emit" in sp:
                            t3 = self.nxt(tmpd)
                            self.stt("dve", t3.t, pbk.t[:, 3:GS + 3], cw.t[:, k, 2:3], t1.t, ALU.mult, ALU.add,
                                     [pbk.r, cw.r, t1.r], [t3.r])
                            sp["emit"](ci, t3, g0)
                        else:
                            self.stt("dve", ob.t, pbk.t[:, 3:GS + 3], cw.t[:, k, 2:3], t1.t, ALU.mult, ALU.add,
                                     [pbk.r, cw.r, t1.r], [ob.r])
                            r0 = sp["row0"] + ci * 128
                            self.dma("sp", sp["out"].t[r0:r0 + 128, g0:g0 + GS], ob.t, [ob.r], [sp["out"].r])
                    elif kind == "rope":
                        kb = self.nxt(obf)
                        ob = self.nxt(obf)
                        t1 = self.nxt(tmpc)
                        t2 = self.nxt(tmpd)
                        p2 = self.pb[6]
                        self.cp("act", kb.t, pbk.t[:, 2:GS + 2], [pbk.r], [kb.r])
                        if os.environ.get("ROPEMM", "1") == "1":
                            self.mm(p2.t[:, 0:GS], self.rperm.t, kb.t, True, True, [self.rperm.r, kb.r], [p2.r])
                        else:
                            p2 = pbk
                        if "nott" in os.environ.get("ROPEVAR", ""):
                            self.cp("dve", t1.t, pbk.t[:, 2:GS + 2], [pbk.r, c.r], [t1.r])
                            self.cp("dve", t2.t, p2.t[:, 0:GS], [p2.r, c.r], [t2.r])
                        else:
                            self.tt("dve", t1.t, pbk.t[:, 2:GS + 2], c.t[:, 0, :], ALU.mult, [pbk.r, c.r, kb.r], [t1.r])
                            self.tt("dve", t2.t, p2.t[:, 0:GS], c.t[:, 1, :], ALU.mult, [p2.r, c.r], [t2.r])
                        self.tt(os.environ.get("ROPEADD", "pool"), ob.t, t1.t, t2.t, ALU.add, [t1.r, t2.r], [ob.r])
                        to = sp["toff"] + g0
                        self.dma("sp", sp["out"].t[ci, :, to:to + GS], ob.t, [ob.r], [sp["out"].r])
                    else:
                        ob = self.nxt(obf)
                        self.cp("act", ob.t, pbk.t[:, 2:GS + 2], [pbk.r], [ob.r])
                        to = sp["toff"] + g0
                        self.dma("sp", sp["out"].t[ci, :, to:to + GS], ob.t, [ob.r], [sp["out"].r])
            for sp in tm:
                for j in range(GS // 128):
                    pbk = self.bank()
                    for kc in range(8):
                        self.mm(pbk.t[:, 0:512], h.t[:, kc, 16 + j * 128:16 + (j + 1) * 128],
                                w.t[:, kc, sp["col0"]:sp["col0"] + 512], kc == 0, kc == 7, [w.r, h.r], [pbk.r])
                    st = self.nxt(tst)
                    self.cp("act", st.t, pbk.t[:, 0:512], [pbk.r], [st.r])
                    ro = sp["roff"] + g0 + j * 128
                    self.dma("sp", sp["out"].t[ro:ro + 128, :], st.t, [st.r], [sp["out"].r])
            if post is not None:
                post(gg, g0, h)

    def phase_inproj(self):
        dbg = self.dbg
        self.x_full = self.inp("x_full", [L, D])
        self.x_ext = self.inp("x_ext", [EXT, D])
        ctx = self.inp("ctx", [256, D])
        w_in = self.inp("w_in", [D, 3072])
        cwin = self.inp("hy_cw", [128, 12, 4])
        gmix = self.inp("norm_mix_g", [2, D])
        cosk = self.inp("ropek_cos", [128, L])
        sink = self.inp("ropek_sin", [128, L])
        cosq = self.inp("ropeq_cos", [128, EXT])
        sinq = self.inp("ropeq_sin", [128, EXT])
        rp = self.inp("c_rperm", [128, 128], BF16)
        self.VX1 = self.scratch("VX1", [1024, L], BF16, debug=dbg)
        self.X2 = self.scratch("X2", [512, EXT], BF16, debug=dbg)
        self.KT = self.scratch("KT", [4, 128, L + 256], BF16, debug=dbg)
        self.QT = self.scratch("QT", [4, 128, EXT], BF16, debug=dbg)
        self.VT = self.scratch("VT", [L + 256, 512], BF16, debug=dbg)
        self.top = self.cmark
        w = self.alloc("w_in", [128, 8, 3072], BF16)
        self.load_w_bf16(w, w_in.t, 8)
        cw = self.alloc("cw", [128, 12, 4], F32)
        self.dma("sp", cw.t, cwin.t, [], [cw.r])
        self.rperm = self.alloc("rperm", [128, 128], BF16)
        self.dma("sp", self.rperm.t, rp.t, [], [self.rperm.r])
        nb = self.norm_bufs()
        mark = self.top
        A, SH = self.mod_tiles(0, 1, 0, 1, gmix.t[0])
        import os
        self.proj_pass(ctx, 256, A, SH, w,
                       [dict(kind="rope", col0=2048, n=4, cos=cosk, sin=sink, out=self.KT, toff=0)] if os.environ.get("CTXROPE") else
                       [dict(kind="plain", col0=2048, n=4, out=self.KT, toff=0)],
                       [dict(col0=2560, out=self.VT, roff=0)], nb)
        self.P.barrier()
        self.top = mark
        if self.stop_after == "ctx":
            return
        A, SH = self.mod_tiles(0, 0, 0, 1, gmix.t[0])
        mark2 = self.top
        self.proj_pass(self.x_full, L, A, SH, w,
                       [dict(kind="conv", col0=0, n=8, cw=cw, cwi0=0, out=self.VX1, row0=0),
                        dict(kind="rope", col0=2048, n=4, cos=cosk, sin=sink, out=self.KT, toff=256)],
                       [dict(col0=2560, out=self.VT, roff=256)], nb)
        self.P.barrier()
        self.top = mark2
        self.proj_pass(self.x_ext, EXT, A, SH, w,
                       [dict(kind="conv", col0=1024, n=4, cw=cw, cwi0=8, out=self.X2, row0=0),
                        dict(kind="rope", col0=1536, n=4, cos=cosq, sin=sinq, out=self.QT, toff=0)],
                       [], nb)


    def phase_attn(self):
        dal = self.inp("da_lambda", [4, 64])
        subg = self.inp("subln_col", [128, 1])
        self.YT = self.scratch("YT", [1024, EXT], BF16, debug=self.dbg)
        self.top = self.cmark
        LAM_INIT = 0.8 - 0.6 * math.exp(0.0)
        lt = self.alloc("lt", [128, 256], F32)
        pr = self.alloc("lpr", [128, 128], F32)
        ls = self.alloc("ls", [128, 4], F32)
        negl = self.alloc("negl", [128, 1], F32)
        gsub = self.alloc("gsub", [128, 1], F32)
        self.dma("sp", lt.t, dal.t.rearrange("a b -> (a b)").partition_broadcast(128), [], [lt.r])
        self.tt("dve", pr.t[:, 0:64], lt.t[:, 0:64], lt.t[:, 64:128], ALU.mult, [lt.r], [pr.r])
        self.tt("dve", pr.t[:, 64:128], lt.t[:, 128:192], lt.t[:, 192:256], ALU.mult, [lt.r, pr.r], [pr.r])
        self.act(lt.t[:, 0:64], pr.t[:, 0:64], AF.Identity, [pr.r, lt.r], [lt.r, ls.r], accum=ls.t[:, 0:1])
        self.act(lt.t[:, 64:128], pr.t[:, 64:128], AF.Identity, [pr.r, lt.r, ls.r], [lt.r, ls.r], accum=ls.t[:, 1:2])
        self.act(ls.t[:, 2:4], ls.t[:, 0:2], AF.Exp, [ls.r], [ls.r])
        self.tt("dve", negl.t, ls.t[:, 3:4], ls.t[:, 2:3], ALU.subtract, [ls.r], [negl.r])
        self.ts("dve", negl.t, negl.t, -LAM_INIT, None, ALU.add, None, [negl.r], [negl.r])
        self.dma("sp", gsub.t, subg.t, [], [gsub.r])
        self.ts("dve", gsub.t, gsub.t, 1.0 - LAM_INIT, None, ALU.mult, None, [gsub.r], [gsub.r])
        NK = (L + 256) // 128
        Kh = self.alloc("Kh", [128, L + 256], BF16)
        Vh = self.alloc("Vh", [128, NK, 128], BF16)
        Qh = self.alloc("Qh", [128, EXT], BF16)
        Eb = [self.alloc("Eb%d" % i, [128, 2, 512], BF16) for i in range(2)]
        f = {n_: self.alloc("at_" + n_, [128, 512], F32) for n_ in ("r0", "r1", "t0", "t1", "o", "rs", "y")}
        osq = self.alloc("at_osq", [128, 512], BF16)
        yb = [self.alloc("at_yb%d" % i, [128, 512], BF16) for i in range(2)]
        pb = self.pb
        it = 0
        for h in range(4):
            self.dma("sp", Kh.t, self.KT.t[h], [self.KT.r], [Kh.r])
            self.dma("sp", Vh.t, self.VT.t[:, h * 128:(h + 1) * 128].rearrange("(kt p) d -> p kt d", p=128), [self.VT.r], [Vh.r])
            self.dma("sp", Qh.t, self.QT.t[h], [self.QT.r], [Qh.r])
            groups = [(q0_, min(512, EXT - q0_)) for q0_ in range(0, EXT, 512)]

            def emit_qk(kt, q0, n):
                par = kt % 2
                for m in range(2):
                    sb_ = pb[2 * par + m]
                    self.mm(sb_.t[:, :n], Kh.t[64 * m:64 * m + 64, kt * 128:(kt + 1) * 128],
                            Qh.t[64 * m:64 * m + 64, q0:q0 + n], True, True, [Kh.r, Qh.r], [sb_.r])

            for gi_, (q0, n) in enumerate(groups):
                if gi_ == 0:
                    emit_qk(0, q0, n)
                for kt in range(NK):
                    par = kt % 2
                    if kt + 1 < NK:
                        emit_qk(kt + 1, q0, n)
                    E = Eb[par]
                    sv = self.ps[:, 1024 * par:1024 * par + 1024].rearrange("p (a b) -> p a b", a=2)[:, :, :n]
                    self.act(E.t[:, :, :n], sv, AF.Exp, [pb[2 * par].r, pb[2 * par + 1].r], [E.r], scale=0.125)
                    for m in range(2):
                        self.mm(pb[4 + m].t[:, :n], Vh.t[:, kt, :], E.t[:, m, :n], kt == 0, kt == NK - 1, [Vh.r, E.r], [pb[4 + m].r])
                    for m in range(2):
                        self.mm(pb[6 + m].t[:, :n], self.ones.t, E.t[:, m, :n], kt == 0, kt == NK - 1, [self.ones.r, E.r], [pb[6 + m].r])
                if gi_ + 1 < len(groups):
                    emit_qk(0, *groups[gi_ + 1])
                self.recip(f["r0"].t[:, :n], pb[6].t[:, :n], [pb[6].r], [f["r0"].r])
                self.recip(f["r1"].t[:, :n], pb[7].t[:, :n], [pb[7].r], [f["r1"].r])
                self.tt("dve", f["t0"].t[:, :n], pb[4].t[:, :n], f["r0"].t[:, :n], ALU.mult, [pb[4].r, f["r0"].r], [f["t0"].r])
                self.tt("dve", f["t1"].t[:, :n], pb[5].t[:, :n], f["r1"].t[:, :n], ALU.mult, [pb[5].r, f["r1"].r], [f["t1"].r])
                self.stt("dve", f["o"].t[:, :n], f["t1"].t[:, :n], negl.t[:, 0:1], f["t0"].t[:, :n], ALU.mult, ALU.add,
                         [f["t1"].r, f["t0"].r, negl.r], [f["o"].r])
                self.act(osq.t[:, :n], f["o"].t[:, :n], AF.Square, [f["o"].r], [osq.r])
                self.mm(pb[6].t[:, :n], self.ones.t, osq.t[:, :n], True, True, [self.ones.r, osq.r], [pb[6].r])
                self.act(f["rs"].t[:, :n], pb[6].t[:, :n], AF.Sqrt, [pb[6].r, self.epsb.r], [f["rs"].r],
                         bias=self.epsb.t[:, 0:1], scale=1.0 / 128)
                self.recip(f["rs"].t[:, :n], f["rs"].t[:, :n], [f["rs"].r], [f["rs"].r])
                self.tt("dve", f["y"].t[:, :n], f["o"].t[:, :n], f["rs"].t[:, :n], ALU.mult, [f["o"].r, f["rs"].r], [f["y"].r])
                y2 = yb[it % 2]
                it += 1
                self.ts("dve", y2.t[:, :n], f["y"].t[:, :n], gsub.t[:, 0:1], None, ALU.mult, None, [f["y"].r, gsub.r], [y2.r])
                r0 = 512 + h * 128
                self.dma("sp", self.YT.t[r0:r0 + 128, q0:q0 + n], y2.t[:, :n], [y2.r], [self.YT.r])


    def load_w_gated(self, dst, src_ap, kcn, gate_ap):
        mark = self.top
        G = self.alloc("gateG", [128, 1024], F32)
        self.dma("sp", G.t, gate_ap.partition_broadcast(128), [self.modv.r], [G.r])
        stg = self.pool_of("wstg", 2, [128, 1024], F32)
        for kc in range(kcn):
            st = self.nxt(stg)
            self.dma("sp", st.t, src_ap[kc * 128:(kc + 1) * 128, :], [], [st.r])
            self.tt("dve", dst.t[:, kc, :], st.t, G.t, ALU.mult, [st.r, G.r], [dst.r])
        self.P.barrier()
        self.top = mark

    def resid_store(self, xin, xout, actT, kcn, Wd, g0, final_g=None):
        for j in range(GS // 128):
            xr = self.nxt(self.rx)
            r0 = g0 + j * 128
            self.dma("sp", xr.t, xin.t[r0:r0 + 128, :], [xin.r], [xr.r])
            xo = self.nxt(self.ro)
            for half in range(2):
                pbk = self.bank()
                for kc in range(kcn):
                    self.mm(pbk.t[:, 0:512], actT.t[:, kc, j * 128:(j + 1) * 128], Wd.t[:, kc, half * 512:(half + 1) * 512],
                            kc == 0, kc == kcn - 1, [actT.r, Wd.r], [pbk.r])
                self.tt("dve", xo.t[:, half * 512:(half + 1) * 512], pbk.t[:, 0:512], xr.t[:, half * 512:(half + 1) * 512],
                        ALU.add, [pbk.r, xr.r], [xo.r])
            if final_g is not None:
                st = self.nxt(self.fst)
                self.act(self.fjunk.t, xo.t, AF.Square, [xo.r], [self.fjunk.r, st.r], accum=st.t[:, 0:1])
                self.act(st.t[:, 1:2], st.t[:, 0:1], AF.Sqrt, [st.r, self.epsb.r], [st.r], bias=self.epsb.t[:, 0:1], scale=1.0 / D)
                self.recip(st.t[:, 1:2], st.t[:, 1:2], [st.r], [st.r])
                self.stt("dve", xo.t, xo.t, st.t[:, 1:2], final_g.t, ALU.mult, ALU.mult, [xo.r, st.r, final_g.r], [xo.r])
            self.dma("sp", xout.t[r0:r0 + 128, :], xo.t, [xo.r], [xout.r])

    def resid_bufs(self):
        self.rx = self.pool_of("rx", 2, [128, 1024], F32)
        self.ro = self.pool_of("ro", 2, [128, 1024], F32)

    def phase_outproj(self, name, yT_dram, xin, w_ap, layer):
        xout = self.scratch(name, [EXT, D], F32, debug=self.dbg)
        self.top = self.cmark
        Wo = self.alloc("Wo", [128, 8, 1024], BF16)
        self.load_w_gated(Wo, w_ap, 8, self.modv.t[layer, 0, 2 * D:3 * D])
        self.resid_bufs()
        yb = self.pool_of("yTb", 2, [128, 8, GS], BF16)
        for g in range(EXT // GS):
            g0 = g * GS
            y = self.nxt(yb)
            self.dma("sp", y.t, yT_dram.t[:, g0:g0 + GS].rearrange("(kc p) t -> p kc t", p=128), [yT_dram.r], [y.r])
            self.resid_store(xin, xout, y, 8, Wo, g0)
        return xout

    def phase_ffn(self, name, xin, layer, final=False):
        wup = self.inp("ffn_w_up%d" % layer, [D, 2 * DFF])
        wdn = self.inp("ffn_w_down%d" % layer, [DFF, D])
        cwin = self.inp("ffn_cw%d" % layer, [128, 44, 4])
        gffn = self.inp("norm_ffn_g", [2, D]) if "norm_ffn_g" not in self.ins else self.ins["norm_ffn_g"]
        xout = self.outp("out", [EXT, D], F32) if final else self.scratch(name, [EXT, D], F32, debug=self.dbg)
        self.top = self.cmark
        Wu = self.alloc("Wu", [128, 8, 2 * DFF], BF16)
        self.load_w_bf16(Wu, wup.t, 8)
        Wd = self.alloc("Wd", [128, 22, 1024], BF16)
        self.load_w_gated(Wd, wdn.t, 22, self.modv.t[layer, 0, 5 * D:6 * D])
        cw = self.alloc("cwf", [128, 44, 4], F32)
        self.dma("sp", cw.t, cwin.t, [], [cw.r])
        fg = None
        if final:
            fgi = self.inp("final_norm_g", [D])
            fg = self.alloc("fg", [128, 1024], F32)
            self.dma("sp", fg.t, fgi.t.partition_broadcast(128), [], [fg.r])
            self.fst = self.pool_of("fst", 2, [128, 2], F32)
            self.fjunk = self.alloc("fjunk", [128, 1024], BF16)
        A, SH = self.mod_tiles(layer, 0, 3, 4, gffn.t[layer])
        nb = self.norm_bufs_small()
        self.resid_bufs_small()
        gT = self.alloc("gT", [128, 22, GS], BF16)
        sil = self.pool_of("sil", 2, [128, GS], F32)
        hold = {}

        pend = []

        def flush(keep):
            while len(pend) > keep:
                j_, gb, ub = pend.pop(0)
                sb_ = self.nxt(sil)
                self.act(sb_.t, gb.t, AF.Silu, [gb.r], [sb_.r])
                self.tt("pool", gT.t[:, j_, :], sb_.t, ub.t, ALU.mult, [sb_.r, ub.r], [gT.r])

        def emit(ci, buf, g0):
            if ci < 22:
                hold["g"] = buf
            else:
                pend.append((ci - 22, hold["g"], buf))
                flush(1)

        def post(gg, g0, h):
            flush(0)
            self.resid_store(xin, xout, gT, 22, Wd, g0, final_g=fg)

        order = []
        for j in range(22):
            order += [j, 22 + j]
        self.proj_pass(xin, EXT, A, SH, Wu,
                       [dict(kind="conv", col0=0, n=44, cw=cw, cwi0=0, order=order, emit=emit)], [], nb, post=post)
        return xout

    def norm_bufs_small(self):
        nb = {}
        nb["x"] = self.pool_of("nx", 1, [128, 1024], F32)
        nb["tmp"] = self.alloc("ntmp", [128, 1024], F32)
        nb["xn"] = self.pool_of("nxn", 1, [128, 1024], BF16)
        nb["junk"] = self.alloc("njunk", [128, 1024], BF16)
        nb["stat"] = self.pool_of("nstat", 2, [128, 2], F32)
        return nb

    def resid_bufs_small(self):
        self.rx = self.pool_of("rx", 1, [128, 1024], F32)
        self.ro = self.pool_of("ro", 1, [128, 1024], F32)


    def phase_sgu(self, name, xin, layer=1):
        win = self.inp("sgu_w_in", [D, 2048])
        bcol_i = self.inp("sgu_bu_col", [128, 8])
        bv_i = self.inp("sgu_bv", [1024])
        lng_i = self.inp("sgu_ln_g", [1024])
        lnb_i = self.inp("sgu_ln_b", [1024])
        wsT_i = self.inp("sgu_wsT", [128, 8, 128])
        bs_i = self.inp("sgu_b_s", [1, 1024])
        wout = self.inp("sgu_w_out", [D, D])
        gmix = self.ins["norm_mix_g"]
        xout = self.scratch(name, [EXT, D], F32, debug=self.dbg)
        self.top = self.cmark
        Wi = self.alloc("Wi", [128, 8, 2048], BF16)
        self.load_w_bf16(Wi, win.t, 8)
        Wo = self.alloc("Wo", [128, 8, 1024], BF16)
        self.load_w_gated(Wo, wout.t, 8, self.modv.t[layer, 0, 2 * D:3 * D])
        wsT = self.alloc("wsT", [128, 8, 128], BF16)
        self.dma("pool", wsT.t, wsT_i.t, [], [wsT.r])
        bsr = self.alloc("bsr", [1, 1024], BF16)
        self.dma("pool", bsr.t, bs_i.t, [], [bsr.r])
        bcol = self.alloc("bcol", [128, 8], F32)
        self.dma("sp", bcol.t, bcol_i.t, [], [bcol.r])
        BV = self.alloc("BV", [128, 1024], F32)
        LNG = self.alloc("LNG", [128, 1024], F32)
        LNB = self.alloc("LNB", [128, 1024], F32)
        self.dma("sp", BV.t, bv_i.t.partition_broadcast(128), [], [BV.r])
        self.dma("sp", LNG.t, lng_i.t.partition_broadcast(128), [], [LNG.r])
        self.dma("sp", LNB.t, lnb_i.t.partition_broadcast(128), [], [LNB.r])
        A, SH = self.mod_tiles(layer, 0, 0, 1, gmix.t[layer])
        nb = self.norm_bufs()
        self.resid_bufs()
        hTb = self.pool_of("sg_hT", 2, [128, 8, GS], BF16)
        uT = self.alloc("sg_uT", [128, 8, GS], F32)
        guT = self.pool_of("sg_guT", 2, [128, 8, GS], BF16)
        vt = self.alloc("sg_vt", [128, 1024], F32)
        vg = self.alloc("sg_vg", [128, 1024], F32)
        vb = self.pool_of("sg_vb", 2, [128, 1024], BF16)
        st = self.pool_of("sg_st", 2, [128, 8], F32)
        for g in range(EXT // GS):
            g0 = g * GS
            hT = self.nxt(hTb)
            gu = self.nxt(guT)
            for j in range(GS // 128):
                xt = self.nxt(nb["x"])
                self.dma("sp", xt.t, xin.t[g0 + j * 128:g0 + (j + 1) * 128, :], [xin.r], [xt.r])
                self.norm_T(xt, A, SH, nb["tmp"], self.nxt(nb["xn"]), nb["junk"], self.nxt(nb["stat"]), hT, j * 128)
            for c in range(8):
                pbk = self.bank()
                for kc in range(8):
                    self.mm(pbk.t[:, 0:GS], Wi.t[:, kc, c * 128:(c + 1) * 128], hT.t[:, kc, :], kc == 0, kc == 7, [Wi.r, hT.r], [pbk.r])
                self.act(uT.t[:, c, :], pbk.t[:, 0:GS], AF.Gelu, [pbk.r, bcol.r], [uT.r], bias=bcol.t[:, c:c + 1])
            for j in range(GS // 128):
                for half in range(2):
                    pbk = self.bank()
                    for kc in range(8):
                        self.mm(pbk.t[:, 0:512], hT.t[:, kc, j * 128:(j + 1) * 128],
                                Wi.t[:, kc, 1024 + half * 512:1024 + (half + 1) * 512], kc == 0, kc == 7, [Wi.r, hT.r], [pbk.r])
                    self.tt("dve", vt.t[:, half * 512:(half + 1) * 512], pbk.t[:, 0:512], BV.t[:, half * 512:(half + 1) * 512],
                            ALU.add, [pbk.r, BV.r], [vt.r])
                s_ = self.nxt(st)
                self.act(vg.t, vt.t, AF.Gelu, [vt.r], [vg.r, s_.r], accum=s_.t[:, 0:1])
                self.act(vt.t, vg.t, AF.Square, [vg.r, s_.r], [vt.r, s_.r], accum=s_.t[:, 1:2])
                self.ts("dve", s_.t[:, 2:3], s_.t[:, 0:1], 1.0 / 1024, None, ALU.mult, None, [s_.r], [s_.r])
                self.tt("dve", s_.t[:, 3:4], s_.t[:, 2:3], s_.t[:, 2:3], ALU.mult, [s_.r], [s_.r])
                self.stt("dve", s_.t[:, 4:5], s_.t[:, 1:2], 1.0 / 1024, s_.t[:, 3:4], ALU.mult, ALU.subtract, [s_.r], [s_.r])
                self.act(s_.t[:, 5:6], s_.t[:, 4:5], AF.Sqrt, [s_.r, self.epsb.r], [s_.r], bias=self.epsb.t[:, 0:1])
                self.recip(s_.t[:, 5:6], s_.t[:, 5:6], [s_.r], [s_.r])
                self.ts("dve", vg.t, vg.t, s_.t[:, 2:3], s_.t[:, 5:6], ALU.subtract, ALU.mult, [vg.r, s_.r], [vg.r])
                self.tt("pool", vg.t, vg.t, LNG.t, ALU.mult, [vg.r, LNG.r], [vg.r])
                v2 = self.nxt(vb)
                self.tt("dve", v2.t, vg.t, LNB.t, ALU.add, [vg.r, LNB.r], [v2.r])
                for a in range(2):
                    pbk = self.bank()
                    for q in range(4):
                        gi = 4 * a + q
                        self.P.op("pe", (lambda o_, l_, r_: (lambda h_: h_.matmul(o_, l_, r_, start=True, stop=False)))(
                            pbk.t[:, q * 128:(q + 1) * 128], v2.t[:, gi * 128:(gi + 1) * 128], wsT.t[:, gi, :]),
                            reads=[v2.r, wsT.r], writes=[pbk.r], signal=False)
                        self.P.op("pe", (lambda o_, l_, r_: (lambda h_: h_.matmul(o_, l_, r_, start=False, stop=True)))(
                            pbk.t[:, q * 128:(q + 1) * 128], self.ones.t[0:1, :], bsr.t[0:1, gi * 128:(gi + 1) * 128]),
                            reads=[self.ones.r, bsr.r], writes=[pbk.r], signal=(q == 3))
                    self.tt("dve", gu.t[:, 4 * a:4 * a + 4, j * 128:(j + 1) * 128],
                            pbk.t[:, 0:512].rearrange("p (a b) -> p a b", a=4), uT.t[:, 4 * a:4 * a + 4, j * 128:(j + 1) * 128],
                            ALU.mult, [pbk.r, uT.r], [gu.r])
            self.resid_store(xin, xout, gu, 8, Wo, g0)
        return xout


    def fft_stage1(self, X, Ad, f1):
        stg = self.pool_of("s1stg", 2, [128, 4, 512], BF16)
        for s_ in range(64):
            st = self.nxt(stg)
            for q in range(4):
                pbk = self.pb[(4 * (s_ % 2)) + q]
                self.mm(pbk.t[:, 0:512], f1.t[:, q * 128:(q + 1) * 128], X.t[:, s_, :], True, True, [f1.r, X.r], [pbk.r])
                self.cp("act" if q < 2 else "dve", st.t[:, q, :], pbk.t[:, 0:512], [pbk.r], [st.r])
            self.dma("sp", Ad.t[:, s_, :, :].rearrange("q p c -> p q c"), st.t, [st.r], [Ad.r])

    def load_B(self, B, Ad, k1g, G):
        kc, p0 = k1g // 128, k1g % 128
        for ri in range(2):
            self.dma("sp", B.t[ri * 64:(ri + 1) * 64, :, :], Ad.t[2 * kc + ri, :, p0:p0 + G, :], [Ad.r], [B.r])

    def load_tab(self, Tb, tab, k1g, G):
        self.dma("sp", Tb.t, tab.t[k1g // G].rearrange("p (k m) -> p k m", k=G), [], [Tb.r])

    def phase_hyena(self):
        G = 4
        zT = self.inp("hy_zT", [33, L])
        w1i = self.inp("hy_w1", [33, 64])
        w2i = self.inp("hy_w2", [64, 64])
        prmi = self.inp("hy_prm", [64, 4])
        w3i = self.inp("hy_w3", [64, 2048])
        dsk = self.inp("hy_bias", [2, 512])
        win = self.inp("hy_win", [128, 64, 512])
        f1i = self.inp("t_F1", [128, 512], BF16)
        Htab = self.inp("t_H", [64, 128, 512], BF16)
        Hctab = self.inp("t_Hc", [64, 128, 512], BF16)
        M1tab = self.inp("t_M1", [64, 128, 512], BF16)
        M2tab = self.inp("t_M2", [64, 128, 512], BF16)
        fvi = self.inp("t_Finv", [128, 512], BF16)
        fei = self.inp("t_FinvE", [128, 4, 68], BF16)
        A0 = self.scratch("fftA0", [4, 64, 128, 512], BF16)
        A1 = self.scratch("fftA1", [4, 64, 128, 512], BF16)
        Cd = self.scratch("fftC", [128, 256, 512], BF16)
        Hf = self.scratch("fftHf", [2, 2, 64, 256, 512], BF16)
        X1s = self.scratch("X1s", [128, 64, 512], BF16)
        self.top = self.cmark
        f1 = self.alloc("f1", [128, 512], BF16)
        fv = self.alloc("fv", [128, 512], BF16)
        fe = self.alloc("fe", [128, 4, 68], BF16)
        self.dma("sp", f1.t, f1i.t, [], [f1.r])
        self.dma("sp", fv.t, fvi.t, [], [fv.r])
        self.dma("sp", fe.t, fei.t, [], [fe.r])
        base = self.top
        h2T = self.alloc("h2T", [128, L], BF16)
        w3 = self.alloc("w3sb", [64, 2048], BF16)
        self.dma("pool", w3.t, w3i.t, [], [w3.r])
        mlp_mark = self.top
        w1 = self.alloc("w1sb", [33, 64], F32)
        w2 = self.alloc("w2sb", [64, 64], F32)
        prm = self.alloc("prm", [64, 4], F32)
        pr2 = self.alloc("pr2", [64, 4], F32)
        self.dma("sp", w1.t, w1i.t, [], [w1.r])
        self.dma("sp", w2.t, w2i.t, [], [w2.r])
        self.dma("sp", prm.t, prmi.t, [], [prm.r])
        for a in range(2):
            self.ts("dve", pr2.t[:, 2 * a:2 * a + 1], prm.t[:, 2 * a + 1:2 * a + 2], 1.0 / 3, None, ALU.mult, None, [prm.r, pr2.r], [pr2.r])
            self.tt("dve", pr2.t[:, 2 * a + 1:2 * a + 2], pr2.t[:, 2 * a:2 * a + 1], prm.t[:, 2 * a:2 * a + 1], ALU.mult, [prm.r, pr2.r], [pr2.r])
        ztb = self.pool_of("ztb", 2, [33, 512], F32)
        s3 = self.alloc("s3", [64, 512], F32)
        qq = self.alloc("qq", [64, 512], F32)
        h1 = self.alloc("h1", [64, 512], F32)

        def sin3(dst, src_ps, a):
            self.act(s3.t, src_ps.t[0:64, 0:512], AF.Sin, [src_ps.r, pr2.r], [s3.r], bias=pr2.t[:, 2 * a + 1:2 * a + 2], scale=pr2.t[:, 2 * a:2 * a + 1])
            self.tt("dve", qq.t, s3.t, s3.t, ALU.mult, [s3.r], [qq.r])
            self.ts("dve", qq.t, qq.t, -4.0, 3.0, ALU.mult, ALU.add, [qq.r], [qq.r])
            self.tt("dve", dst[0], qq.t, s3.t, ALU.mult, [qq.r, s3.r], [dst[1]])

        for ch in range(L // 512):
            zt = self.nxt(ztb)
            self.dma("sp", zt.t, zT.t[:, ch * 512:(ch + 1) * 512], [], [zt.r])
            pa, pb2 = self.pb[ch % 2], self.pb[2 + ch % 2]
            self.mm(pa.t[0:64, 0:512], w1.t, zt.t, True, True, [w1.r, zt.r], [pa.r])
            sin3((h1.t, h1.r), pa, 0)
            self.mm(pb2.t[0:64, 0:512], w2.t, h1.t, True, True, [w2.r, h1.r], [pb2.r])
            sin3((h2T.t[0:64, ch * 512:(ch + 1) * 512], h2T.r), pb2, 1)
        self.P.barrier()
        self.top = mlp_mark
        h2v = h2T.t[0:64, :].rearrange("p (j s) -> p s j", s=64)
        Dz = self.alloc("Dz", [128, 512], F32)
        omark = self.top
        for o in range(2):
            self.P.barrier()
            self.top = omark
            self.memset("pool", Dz.t, 0.0, [Dz.r])
            self.dma("sp", Dz.t[0:64, :], dsk.t[o].partition_broadcast(64), [], [Dz.r])
            Xf = self.alloc("Xf", [128, 64, 512], BF16)
            Xb = self.alloc("Xb", [128, 64, 512], BF16)
            wt = self.pool_of("wint", 2, [128, 512], F32)
            for s_ in range(64):
                wn = self.nxt(wt)
                self.dma("sp", wn.t, win.t[:, s_, :], [], [wn.r])
                for dr, Xd in ((0, Xf), (1, Xb)):
                    pbk = self.bank()
                    c0 = (o * 2 + dr) * 512
                    self.mm(pbk.t[:, 0:512], h2v[:, s_, :], w3.t[:, c0:c0 + 512], True, True, [h2T.r, w3.r], [pbk.r])
                    self.tt("dve", Xd.t[:, s_, :], pbk.t[:, 0:512], wn.t, ALU.mult, [pbk.r, wn.r], [Xd.r])
            self.memset("pool", Xb.t[0:1, 0, :], 0.0, [Xb.r])
            self.fft_stage1(Xf, A0, f1)
            self.fft_stage1(Xb, A1, f1)
            self.P.barrier()
            self.top = omark
            Bf = self.pool_of("Bf", 2, [128, G, 512], BF16)
            Bb = self.pool_of("Bb", 2, [128, G, 512], BF16)
            Ht = self.pool_of("Ht", 2, [128, G, 128], BF16)
            Hct = self.pool_of("Hct", 2, [128, G, 128], BF16)
            Hst = self.pool_of("Hst", 2, [128, G, 512], BF16)
            Hfo = Hf.t[o].rearrange("r k a c -> (r k) a c")
            def f_loads(k1g):
                bf, bb, ht, hct = self.nxt(Bf), self.nxt(Bb), self.nxt(Ht), self.nxt(Hct)
                self.load_B(bf, A0, k1g, G)
                self.load_B(bb, A1, k1g, G)
                self.load_tab(ht, Htab, k1g, G)
                self.load_tab(hct, Hctab, k1g, G)
                return bf, bb, ht, hct

            nxt_l = f_loads(0)
            for k1g in range(0, 256, G):
                bf, bb, ht, hct = nxt_l
                if k1g + G < 256:
                    nxt_l = f_loads(k1g + G)
                hs = self.nxt(Hst)
                for g in range(G):
                    pbk = self.bank()
                    self.mm(pbk.t[:, 0:512], ht.t[:, g, :], bf.t[:, g, :], True, False, [ht.r, bf.r], [pbk.r])
                    self.mm(pbk.t[:, 0:512], hct.t[:, g, :], bb.t[:, g, :], False, True, [hct.r, bb.r], [pbk.r])
                    self.tt("dve", hs.t[:, g, :], pbk.t[:, 0:512], Dz.t, ALU.add, [pbk.r, Dz.r], [hs.r])
                self.dma("sp", Hfo[:, k1g:k1g + G, :], hs.t, [hs.r], [Hf.r])
        self.P.barrier()
        self.top = base
        X = self.alloc("Xc", [128, 64, 512], BF16)
        cmark2 = self.top
        Ub = self.pool_of("Ub", 2, [128, L], BF16)
        xst = self.pool_of("xst", 2, [128, 8, 128], BF16)
        nt = 0
        for part in range(2):
            for cc in range(4):
                U = self.nxt(Ub)
                r0 = part * 512 + cc * 128
                self.dma("sp", U.t, self.VX1.t[r0:r0 + 128, :], [self.VX1.r], [U.r])
                Uv = U.t.rearrange("p (j s) -> p s j", s=64)
                for s0 in range(0, 64, 8):
                    pT = self.pb[6 + nt % 2]
                    nt += 1
                    pTv = pT.t.bitcast(BF16)
                    for i in range(8):
                        self.tr(pTv[:, i * 128:(i + 1) * 128], Uv[:, s0 + i, :], self.ident.t, [U.r, self.ident.r], [pT.r], signal=(i == 7))
                    src = pTv.rearrange("p (a b) -> p a b", a=8)
                    if part == 0:
                        self.cp("act", X.t[:, s0:s0 + 8, cc * 128:(cc + 1) * 128], src, [pT.r], [X.r])
                    else:
                        st = self.nxt(xst)
                        self.cp("act", st.t, src, [pT.r], [st.r])
                        self.dma("sp", X1s.t[:, s0:s0 + 8, cc * 128:(cc + 1) * 128], st.t, [st.r], [X1s.r])
        for conv in range(2):
            self.P.barrier()
            self.top = cmark2
            if conv == 1:
                x2sb = self.alloc("x2sb", [128, 4, EXT], BF16)
                yaT = self.alloc("yaT", [128, 4, EXT], BF16)
                self.dma("sp", x2sb.t, self.X2.t.rearrange("(cc p) t -> p cc t", p=128), [self.X2.r], [x2sb.r])
            m2 = self.top
            self.fft_stage1(X, A0, f1)
            self.P.barrier()
            self.top = m2
            Bt = self.pool_of("Bt", 2, [128, G, 512], BF16)
            Ht = self.pool_of("cHt", 2, [128, G, 128], BF16)
            M1t = self.pool_of("cM1", 2, [128, G, 128], BF16)
            M2t = self.pool_of("cM2", 2, [128, G, 128], BF16)
            HH1 = self.pool_of("HH1", 2, [128, G, 512], BF16)
            HH2 = self.pool_of("HH2", 2, [128, G, 512], BF16)
            T1 = self.pool_of("T1", 2, [128, 512], BF16)
            T2 = self.pool_of("T2", 2, [128, 512], BF16)
            Cst = self.pool_of("Cst", 2, [128, G, 512], BF16)
            def c_loads(k1g):
                bt, ht, m1, m2_, h1_, h2_ = (self.nxt(Bt), self.nxt(Ht), self.nxt(M1t), self.nxt(M2t), self.nxt(HH1), self.nxt(HH2))
                self.load_B(bt, A0, k1g, G)
                self.load_tab(ht, Htab, k1g, G)
                self.load_tab(m1, M1tab, k1g, G)
                self.load_tab(m2_, M2tab, k1g, G)
                for half in range(2):
                    self.dma("sp", h1_.t[half * 64:(half + 1) * 64, :, :], Hf.t[conv, 0, :, k1g:k1g + G, :], [Hf.r], [h1_.r])
                    self.dma("sp", h2_.t[half * 64:(half + 1) * 64, :, :], Hf.t[conv, 1, :, k1g:k1g + G, :], [Hf.r], [h2_.r])
                return bt, ht, m1, m2_, h1_, h2_

            nxt_l = c_loads(0)
            for k1g in range(0, 256, G):
                bt, ht, m1, m2_, h1_, h2_ = nxt_l
                if k1g + G < 256:
                    nxt_l = c_loads(k1g + G)
                cs_ = self.nxt(Cst)
                for g in range(G):
                    pu = self.bank()
                    self.mm(pu.t[:, 0:512], ht.t[:, g, :], bt.t[:, g, :], True, True, [ht.r, bt.r], [pu.r])
                    t1, t2 = self.nxt(T1), self.nxt(T2)
                    self.tt("dve", t1.t, pu.t[:, 0:512], h1_.t[:, g, :], ALU.mult, [pu.r, h1_.r], [t1.r])
                    self.tt("dve", t2.t, pu.t[:, 0:512], h2_.t[:, g, :], ALU.mult, [pu.r, h2_.r], [t2.r])
                    pc = self.bank()
                    self.mm(pc.t[:, 0:512], m1.t[:, g, :], t1.t, True, False, [m1.r, t1.r], [pc.r])
                    self.mm(pc.t[:, 0:512], m2_.t[:, g, :], t2.t, False, True, [m2_.r, t2.r], [pc.r])
                    self.cp("act", cs_.t[:, g, :], pc.t[:, 0:512], [pc.r], [cs_.r])
                self.dma("sp", Cd.t[:, k1g:k1g + G, :], cs_.t, [cs_.r], [Cd.r])
            self.P.barrier()
            self.top = m2
            Dt = self.pool_of("Dt", 2, [128, 4, 512], BF16)
            x1t = self.pool_of("x1t", 2, [128, 512], BF16)
            for s_ in range(64):
                dt_ = self.nxt(Dt)
                for kc in range(2):
                    for ri in range(2):
                        self.dma("sp", dt_.t[:, 2 * kc + ri, :], Cd.t[ri * 64 + s_, kc * 128:(kc + 1) * 128, :], [Cd.r], [dt_.r])
                if conv == 0:
                    xg = self.nxt(x1t)
                    self.dma("sp", xg.t, X1s.t[:, s_, :], [X1s.r], [xg.r])
                    pbk = self.bank()
                    for q in range(4):
                        self.mm(pbk.t[:, 0:512], fv.t[:, q * 128:(q + 1) * 128], dt_.t[:, q, :], q == 0, q == 3, [fv.r, dt_.r], [pbk.r])
                    self.tt("dve", X.t[:, s_, :], pbk.t[:, 0:512], xg.t, ALU.mult, [pbk.r, xg.r], [X.r])
                else:
                    pbk = self.bank()
                    for cc in range(4):
                        for q in range(4):
                            self.P.op("pe", (lambda o_, l_, r_, a_, b_: (lambda h_: h_.matmul(o_, l_, r_, start=a_, stop=b_)))(
                                pbk.t[:, cc * 68:(cc + 1) * 68], dt_.t[:, q, cc * 128:(cc + 1) * 128], fe.t[:, q, :], q == 0, q == 3),
                                reads=[dt_.r, fe.r], writes=[pbk.r], signal=(cc == 3 and q == 3))
                    x2v = x2sb.t.rearrange("p c (j s) -> p c s j", s=64)[:, :, s_, :]
                    yav = yaT.t.rearrange("p c (j s) -> p c s j", s=64)[:, :, s_, :]
                    self.tt("dve", yav, pbk.t[:, 0:272].rearrange("p (c j) -> p c j", c=4), x2v, ALU.mult, [pbk.r, x2sb.r], [yaT.r])
            if conv == 1:
                self.dma("sp", self.YT.t[0:512, :].rearrange("(cc p) t -> p cc t", p=128), yaT.t, [yaT.r], [self.YT.r])


def build_program(dbg=False, stop_after=None):
    kb = KB(stop_after)
    kb.dbg = dbg
    kb._bk = 0
    kb.consts()
    kb.phase_mod()
    kb.P.barrier()
    if stop_after == "mod":
        kb.P.build()
        return kb
    kb.phase_inproj()
    kb.P.barrier()
    if stop_after == "inproj":
        kb.P.build()
        return kb
    kb.phase_attn()
    kb.P.barrier()
    if stop_after == "attn":
        kb.P.build()
        return kb
    if os.environ.get("FAKE_YA"):
        ya = kb.inp("dbg_yaT", [512, EXT], BF16)
        st = kb.alloc("yast", [128, EXT], BF16)
        for c in range(4):
            kb.dma("sp", st.t, ya.t[c * 128:(c + 1) * 128, :], [ya.r], [st.r])
            kb.dma("sp", kb.YT.t[c * 128:(c + 1) * 128, :], st.t, [st.r], [kb.YT.r])
        kb.P.barrier()
    else:
        kb.phase_hyena()
        kb.P.barrier()
    if stop_after == "hyena":
        kb.P.build()
        return kb
    wo = kb.inp("ab_w_out", [D, D])
    x1 = kb.phase_outproj("X1o", kb.YT, kb.x_ext, wo.t, 0)
    kb.P.barrier()
    x2 = kb.phase_ffn("X2o", x1, 0)
    kb.P.barrier()
    x3 = kb.phase_sgu("X3o", x2, 1)
    kb.P.barrier()
    kb.phase_ffn("X4o", x3, 1, final=True)
    kb.P.barrier()
    kb.P.build()
    return kb


def rope_tables(pos_row, pos_col):
    T = pos_row.shape[0]
    cos = np.zeros((128, T), np.float32)
    sin = np.zeros((128, T), np.float32)
    inv = (10000.0 ** (-np.arange(16, dtype=np.float32) / 16)).astype(np.float32)
    for f in range(128):
        d = f % 64
        pos = pos_row if d < 32 else pos_col
        dd = d % 32
        i = dd % 16
        ang = pos.astype(np.float32) * inv[i]
        cos[f] = np.cos(ang)
        sin[f] = -np.sin(ang) if dd < 16 else np.sin(ang)
    return cos, sin


def rperm_matrix():
    m = np.zeros((128, 128), np.float32)
    for f in range(128):
        dd = (f % 64) % 32
        partner = f + 16 if dd < 16 else f - 16
        m[partner, f] = 1.0
    return m.astype(ml_dtypes.bfloat16)


def fft_tables(lo):
    bf = ml_dtypes.bfloat16
    N = 16384
    j = np.arange(128, dtype=np.float64)[:, None]
    k1 = np.arange(256, dtype=np.float64)[None, :]
    th = 2 * np.pi * ((j * k1) % 256) / 256
    F1 = np.zeros((128, 4, 128))
    Fi = np.zeros((128, 4, 128))
    for kc in range(2):
        F1[:, 2 * kc, :] = np.cos(th[:, kc * 128:(kc + 1) * 128])
        F1[:, 2 * kc + 1, :] = -np.sin(th[:, kc * 128:(kc + 1) * 128])
        Fi[:, 2 * kc, :] = np.cos(th[:, kc * 128:(kc + 1) * 128]).T / N
        Fi[:, 2 * kc + 1, :] = -np.sin(th[:, kc * 128:(kc + 1) * 128]).T / N
    j0 = lo // 64
    FiE = Fi[:, :, j0:j0 + 68]
    s = np.arange(64, dtype=np.float64)
    k2 = np.arange(64, dtype=np.float64)
    kk = np.arange(256, dtype=np.float64)
    ang = 2 * np.pi * ((s[None, :, None] * kk[:, None, None]) / N + ((s[None, :, None] * k2[None, None, :]) % 64) / 64)
    Zr, Zi = np.cos(ang), -np.sin(ang)
    H = np.zeros((256, 128, 128))
    H[:, 0:64, 0:64] = Zr
    H[:, 64:128, 0:64] = -Zi
    H[:, 0:64, 64:128] = Zi
    H[:, 64:128, 64:128] = Zr
    Hc = H.copy()
    Hc[:, :, 64:128] *= -1
    ZrT, ZiT = Zr.transpose(0, 2, 1), Zi.transpose(0, 2, 1)
    M1 = np.zeros((256, 128, 128))
    M1[:, 0:64, 0:64] = ZrT
    M1[:, 64:128, 0:64] = ZiT
    M1[:, 0:64, 64:128] = -ZiT
    M1[:, 64:128, 64:128] = ZrT
    M2 = np.zeros((256, 128, 128))
    M2[:, 0:64, 0:64] = ZiT
    M2[:, 64:128, 0:64] = -ZrT
    M2[:, 0:64, 64:128] = ZrT
    M2[:, 64:128, 64:128] = ZiT
    c = lambda a: np.ascontiguousarray(a.astype(np.float32)).astype(bf)
    grp = lambda a: c(a.reshape(64, 4, 128, 128).transpose(0, 2, 1, 3).reshape(64, 128, 512))
    return {"t_F1": c(F1.reshape(128, 512)), "t_Finv": c(Fi.reshape(128, 512)), "t_FinvE": c(FiE),
            "t_H": grp(H), "t_Hc": grp(Hc), "t_M1": grp(M1), "t_M2": grp(M2)}


def hyena_consts():
    f32 = np.float32
    bands = 16
    pos = np.arange(L, dtype=f32)
    t = np.linspace(0.0, 1.0, L, dtype=f32)[:, None]
    ang = (f32(2.0 * math.pi / L) * pos[:, None] * np.linspace(1e-4, bands - 1, bands, dtype=f32)[None, :]).astype(f32)
    z = np.concatenate([t, np.cos(ang), -np.sin(ang)], axis=-1).astype(f32)
    deltas = np.abs(np.linspace(math.log(1e-2) / 1.5, math.log(1e-2) / 0.3, 512, dtype=f32))
    window = (np.exp(-t * deltas[None, :]) + f32(0.05)).astype(f32)
    win = np.ascontiguousarray(window.reshape(128, 64, 512))
    return np.ascontiguousarray(z.T), win


_TAB = {}


def host_inputs(inputs, cid):
    b, hf = cid // 2, cid % 2
    lo = 0 if hf == 0 else L - EXT
    f32 = np.float32
    m = {}
    m["c_ident"] = np.eye(128, dtype=f32).astype(ml_dtypes.bfloat16)
    cc = np.stack([inputs["c"][b], inputs["c_ctx"]], -1).astype(f32)
    m["ccol"] = np.ascontiguousarray(cc.reshape(8, 128, 2).transpose(1, 0, 2))
    m["mod_w"] = inputs["mod_w"]
    m["mod_b"] = inputs["mod_b"]
    m["x_full"] = np.ascontiguousarray(inputs["x"][b])
    m["x_ext"] = np.ascontiguousarray(inputs["x"][b, lo:lo + EXT])
    m["ctx"] = np.ascontiguousarray(inputs["ctx"][b])
    m["w_in"] = np.ascontiguousarray(inputs["ab_w_in"][0])
    cw = np.concatenate([inputs["hy_conv_w"][0], inputs["hy_conv_b"][0][None]], 0)
    m["hy_cw"] = np.ascontiguousarray(cw.reshape(4, 12, 128).transpose(2, 1, 0)).astype(f32)
    m["norm_mix_g"] = inputs["norm_mix_g"]
    t = np.arange(L)
    ck, sk = rope_tables(t // 64, t % 64)
    m["ropek_cos"], m["ropek_sin"] = ck, sk
    m["ropeq_cos"] = np.ascontiguousarray(ck[:, lo:lo + EXT])
    m["ropeq_sin"] = np.ascontiguousarray(sk[:, lo:lo + EXT])
    m["c_rperm"] = rperm_matrix()
    m["da_lambda"] = np.ascontiguousarray(inputs["da_lambda"][0]).astype(f32)
    m["ab_w_out"] = np.ascontiguousarray(inputs["ab_w_out"][0])
    m["norm_ffn_g"] = inputs["norm_ffn_g"]
    m["final_norm_g"] = inputs["final_norm_g"]
    for i in range(2):
        m["ffn_w_up%d" % i] = np.ascontiguousarray(inputs["ffn_w_up"][i])
        m["ffn_w_down%d" % i] = np.ascontiguousarray(inputs["ffn_w_down"][i])
        fcw = np.concatenate([inputs["ffn_conv_w"][i], inputs["ffn_conv_b"][i][None]], 0)
        m["ffn_cw%d" % i] = np.ascontiguousarray(fcw.reshape(4, 44, 128).transpose(2, 1, 0)).astype(f32)
    m["sgu_w_in"] = np.ascontiguousarray(inputs["sgu_w_in"][0])
    m["sgu_bu_col"] = np.ascontiguousarray(inputs["sgu_b_in"][0][:1024].reshape(8, 128).T).astype(f32)
    m["sgu_bv"] = np.ascontiguousarray(inputs["sgu_b_in"][0][1024:])
    m["sgu_ln_g"] = inputs["sgu_ln_g"][0]
    m["sgu_ln_b"] = inputs["sgu_ln_b"][0]
    m["sgu_wsT"] = np.ascontiguousarray(inputs["sgu_w_s"][0].transpose(2, 0, 1)).astype(f32)
    m["sgu_b_s"] = np.ascontiguousarray(inputs["sgu_b_s"][0].reshape(1, 1024)).astype(f32)
    m["sgu_w_out"] = np.ascontiguousarray(inputs["sgu_w_out"][0])
    if lo not in _TAB:
        _TAB[lo] = fft_tables(lo)
    if "hc" not in _TAB:
        _TAB["hc"] = hyena_consts()
    m.update(_TAB[lo])
    m["hy_zT"], m["hy_win"] = _TAB["hc"]
    m["hy_w1"] = np.ascontiguousarray(inputs["hy_w1"][0])
    m["hy_w2"] = np.ascontiguousarray(inputs["hy_w2"][0])
    m["hy_prm"] = np.ascontiguousarray(np.stack([inputs["hy_b1"][0], inputs["hy_freq"][0][0], inputs["hy_b2"][0], inputs["hy_freq"][0][1]], -1)).astype(f32)
    m["hy_w3"] = np.ascontiguousarray(inputs["hy_w3"][0])
    m["hy_bias"] = np.ascontiguousarray(inputs["hy_bias"][0])
    m["subln_col"] = np.ascontiguousarray(inputs["da_subln_g"][0].reshape(128, 1)).astype(f32)
    return m


_CACHE = {}


def kernel(**inputs):
    inputs = {k: np.asarray(v) for k, v in inputs.items()}
    if "kb" not in _CACHE:
        _CACHE["kb"] = build_program()
    kb = _CACHE["kb"]
    in_maps = []
    for cid in range(8):
        m = host_inputs(inputs, cid)
        in_maps.append({k: m[k] for k in kb.ins})
    res = run_bass_kernel_spmd(kb.nc, in_maps, core_ids=list(range(8)))
    out = np.zeros((4, L, D), np.float32)
    for cid in range(8):
        b, hf = cid // 2, cid % 2
        o = res.results[cid]["out"]
        if hf == 0:
            out[b, :4096] = o[:4096]
        else:
            out[b, 4096:] = o[EXT - 4096:]
    return out
```

```python
import math
import os
import numpy as np
import ml_dtypes
import concourse.bass as bass
import concourse.mybir as mybir
from concourse.bass_utils import run_bass_kernel_spmd

F32 = mybir.dt.float32
BF16 = mybir.dt.bfloat16
U8 = mybir.dt.uint8
AF = mybir.ActivationFunctionType
ALU = mybir.AluOpType

ENGS = ("pe", "act", "dve", "pool", "sp")
SEM_ROT = 16000
L = 8192
EXT = 4352
NEXT_T = EXT // 128
D = 1024
DFF = 2816
EPS = 1e-6
GS = 256


class Res:
    __slots__ = ("name", "w", "r", "dsem", "dcnt", "excl", "lk")

    def __init__(self, name, excl=False):
        self.name = name
        self.lk = None
        self.excl = excl
        self.w = {}
        self.r = {}
        self.dsem = None
        self.dcnt = 0


class Prog:
    def __init__(self, nc):
        self.nc = nc
        self.ops = {e: [] for e in ENGS}
        self.cnt = {e: 0 for e in ENGS}
        self.epoch = {e: 0 for e in ENGS}
        self.sems = {}
        self.known = {e: {} for e in ENGS}
        self.dres = []
        self.meta = {e: [] for e in ENGS}
        self.free_sems = []
        self.nd = 0
        for e in ENGS:
            if e != "sp":
                self._engsem(e)

    def _engsem(self, e):
        k = ("E", e, self.epoch[e])
        if k not in self.sems:
            self.sems[k] = self.nc.alloc_semaphore(name="s_%s_%d" % (e, self.epoch[e]))
        return k

    def _semname(self, sem):
        for k, v in self.sems.items():
            if v is sem:
                return k
        return None

    def check_deadlock(self):
        val = {}
        pc = {e: 0 for e in ENGS}
        prog = True
        while prog:
            prog = False
            for e in ENGS:
                while pc[e] < len(self.meta[e]):
                    waits, inc, desc = self.meta[e][pc[e]]
                    if all(val.get(k, 0) >= v for k, v in waits):
                        if inc is not None:
                            val[inc[0]] = val.get(inc[0], 0) + inc[1]
                        pc[e] += 1
                        prog = True
                    else:
                        break
        bad = False
        for e in ENGS:
            if pc[e] < len(self.meta[e]):
                bad = True
                waits, inc, desc = self.meta[e][pc[e]]
                print("DEADLOCK", e, pc[e], len(self.meta[e]), desc, [(k, v, val.get(k, 0)) for k, v in waits if val.get(k, 0) < v])
        return not bad

    def _deps(self, reads, writes):
        deps = {}
        for r in reads:
            for k, v in r.w.items():
                if deps.get(k, 0) < v:
                    deps[k] = v
        for w in writes:
            for d in (w.w, w.r):
                for k, v in d.items():
                    if deps.get(k, 0) < v:
                        deps[k] = v
        return deps

    def _waits(self, eng, deps):
        kn = self.known[eng]
        out = []
        for k, v in deps.items():
            if kn.get(k, 0) < v:
                kn[k] = v
                out.append((self.sems[k], v))
        return out

    def op(self, eng, fn, reads=(), writes=(), signal=True):
        deps = self._deps(reads, writes)
        k = self._engsem(eng)
        for r in reads:
            if r.excl:
                for kk, vv in r.r.items():
                    if not (kk[0] == "E" and kk[1] == eng) and deps.get(kk, 0) < vv:
                        deps[kk] = vv
        if eng == "pe":
            deps = {kk: vv for kk, vv in deps.items() if kk[1] != "pe" or kk[0] != "E"}
        else:
            deps = {kk: vv for kk, vv in deps.items() if not (kk == k and vv > self.cnt[eng])}
        waits = self._waits(eng, deps)
        if signal:
            self.cnt[eng] += 1
            tok = (k, self.cnt[eng])
            sem = self.sems[k]
        else:
            tok = (k, self.cnt[eng] + 1)
            sem = None

        def emit(h, fn=fn, waits=waits, sem=sem):
            for s, v in waits:
                h.wait_ge(s, v)
            ins = fn(h)
            if sem is not None:
                ins.then_inc(sem, 1)

        self.ops[eng].append(emit)
        self.meta[eng].append(([(self._semname(s_), v_) for s_, v_ in waits], (k, 1) if signal else None,
                               "op r=%s w=%s" % ([r.name for r in reads], [w.name for w in writes])))
        kk, vv = tok
        for w in writes:
            w.w = {kk: vv}
            w.r = {}
        for r in reads:
            if r.r.get(kk, 0) < vv:
                r.r[kk] = vv
        if signal and self.cnt[eng] >= SEM_ROT:
            self.epoch[eng] += 1
            self.cnt[eng] = 0
            self._engsem(eng)

    def dma(self, eng, out, in_, reads=(), writes=(), store=False):
        assert len(writes) == 1
        wres = writes[0]
        owner = reads[0] if store else wres
        deps = {}
        for r in reads:
            for k, v in r.w.items():
                if deps.get(k, 0) < v:
                    deps[k] = v
        for d in ((wres.r,) if store else (wres.w, wres.r)):
            for k, v in d.items():
                if deps.get(k, 0) < v:
                    deps[k] = v
        if owner.dsem is None:
            if eng == "pool":
                self.nd += 1
                owner.dsem = ("S", self.nd)
                owner.dcnt = 0
                self.sems[owner.dsem] = self.nc.alloc_semaphore(name="sw_%d" % self.nd)
            elif self.free_sems:
                owner.dsem, owner.dcnt = self.free_sems.pop()
            else:
                self.nd += 1
                owner.dsem = ("D", self.nd)
                owner.dcnt = 0
                self.sems[owner.dsem] = self.nc.alloc_semaphore(name="d_%d" % self.nd)
            self.dres.append(owner)
        if (not store) and owner.lk == "load" and not wres.r:
            deps.pop(owner.dsem, None)
        owner.lk = "store" if store else "load"
        waits = self._waits(eng, deps)
        owner.dcnt += 1
        k, v = owner.dsem, 16 * owner.dcnt
        sem = self.sems[k]

        def emit(h, waits=waits, sem=sem, out=out, in_=in_):
            for s, vv in waits:
                h.wait_ge(s, vv)
            h.dma_start(out=out, in_=in_).then_inc(sem, 16)

        self.ops[eng].append(emit)
        self.meta[eng].append(([(self._semname(s_), v_) for s_, v_ in waits], (k, 16),
                               "dma r=%s w=%s" % ([r.name for r in reads], [w.name for w in writes])))
        if store:
            wres.w[k] = v
        else:
            wres.w = {k: v}
            wres.r = {}
        for r in reads:
            if r.r.get(k, 0) < v:
                r.r[k] = v

    def barrier(self):
        toks = {}
        for e in ENGS:
            if e == "sp":
                continue
            k = self._engsem(e)
            if self.cnt[e] > 0:
                toks[k] = self.cnt[e]
            if self.epoch[e] > 0:
                toks[("E", e, self.epoch[e] - 1)] = SEM_ROT
        for r in self.dres:
            toks[r.dsem] = 16 * r.dcnt
        for e in ENGS:
            waits = self._waits(e, dict(toks))

            def emit(h, waits=waits):
                for s, v in waits:
                    h.wait_ge(s, v)

            self.ops[e].append(emit)
            self.meta[e].append(([(self._semname(s_), v_) for s_, v_ in waits], None, "barrier"))
        keep = []
        for r in self.dres:
            if r.dsem[0] == "S":
                keep.append(r)
                continue
            self.free_sems.append((r.dsem, r.dcnt))
            r.dsem = None
        self.dres = keep

    def build(self):
        nc = self.nc
        with nc.Block() as block:
            @block.tensor
            def _(h):
                for f in self.ops["pe"]:
                    f(h)

            @block.scalar
            def _(h):
                for f in self.ops["act"]:
                    f(h)

            @block.vector
            def _(h):
                for f in self.ops["dve"]:
                    f(h)

            @block.gpsimd
            def _(h):
                for f in self.ops["pool"]:
                    f(h)

            @block.sync
            def _(h):
                for f in self.ops["sp"]:
                    f(h)


class Buf:
    __slots__ = ("t", "r")

    def __init__(self, t, r):
        self.t = t
        self.r = r


def _dtsize(dt):
    return 4 if dt == F32 else (2 if dt == BF16 else 1)


class KB:
    def __init__(self, stop_after=None):
        nc = bass.Bass("TRN2", target_bir_lowering=False)
        self.nc = nc
        self.P = Prog(nc)
        self.stop_after = stop_after
        self.ARENA = 207 * 1024
        self.arena = nc.alloc_sbuf_tensor("arena", [128, self.ARENA], U8)
        self.top = 0
        ps = nc.alloc_psum_tensor("psum", [128, 4096], F32)
        self.ps = ps
        self.pb = [Buf(ps[:, 512 * i:512 * (i + 1)], Res("pb%d" % i, excl=True)) for i in range(8)]
        self.dram = {}
        self.dram_names = set()
        self.ins = {}
        self.outs = {}

    def alloc(self, name, shape, dt):
        n = 1
        for s in shape[1:]:
            n *= s
        nb = n * _dtsize(dt)
        nb = (nb + 31) // 32 * 32
        off = self.top
        self.top += nb
        assert self.top <= self.ARENA, "SBUF arena overflow %s %d" % (name, self.top)
        v = self.arena[:shape[0], off:off + n * _dtsize(dt)].bitcast(dt)
        if len(shape) == 3:
            v = v.rearrange("p (a b) -> p a b", a=shape[1])
        elif len(shape) == 4:
            v = v.rearrange("p (a b c) -> p a b c", a=shape[1], b=shape[2])
        return Buf(v, Res(name))

    def inp(self, name, shape, dt=F32):
        t = self.nc.dram_tensor(name, list(shape), dt, kind="ExternalInput").ap()
        b = Buf(t, Res(name))
        self.ins[name] = b
        return b

    def outp(self, name, shape, dt=F32):
        t = self.nc.dram_tensor(name, list(shape), dt, kind="ExternalOutput").ap()
        self.dram_names.add(name)
        b = Buf(t, Res(name))
        self.outs[name] = b
        return b

    def scratch(self, name, shape, dt, debug=False):
        if debug:
            return self.outp(name, shape, dt)
        t = self.nc.dram_tensor(name, list(shape), dt).ap()
        self.dram_names.add(name)
        return Buf(t, Res(name))

    def mm(self, out, lhsT, rhs, start, stop, reads, writes):
        self.P.op("pe", lambda h: h.matmul(out, lhsT, rhs, start=start, stop=stop),
                  reads=reads, writes=writes, signal=bool(stop))

    def tr(self, out, in_, ident, reads, writes, signal=True):
        self.P.op("pe", lambda h: h.transpose(out, in_, ident), reads=reads, writes=writes, signal=signal)

    def act(self, out, in_, func, reads, writes, bias=None, scale=None, accum=None):
        kw = {}
        if bias is not None:
            kw["bias"] = bias
        if scale is not None:
            kw["scale"] = scale
        if accum is not None:
            kw["accum_out"] = accum
        self.P.op("act", lambda h: h.activation(out=out, in_=in_, func=func, **kw), reads=reads, writes=writes)

    def tt(self, eng, out, a, b, op, reads, writes):
        self.P.op(eng, lambda h: h.tensor_tensor(out=out, in0=a, in1=b, op=op), reads=reads, writes=writes)

    def ts(self, eng, out, a, s1, s2, op0, op1, reads, writes):
        if op1 is None:
            s2, op1 = 0.0, ALU.add
        self.P.op(eng, lambda h: h.tensor_scalar(out, a, s1, s2, op0, op1), reads=reads, writes=writes)

    def stt(self, eng, out, in0, scalar, in1, op0, op1, reads, writes):
        self.P.op(eng, lambda h: h.scalar_tensor_tensor(out=out, in0=in0, scalar=scalar, in1=in1, op0=op0, op1=op1),
                  reads=reads, writes=writes)

    def cp(self, eng, out, in_, reads, writes):
        if eng == "act":
            self.P.op("act", lambda h: h.activation(out=out, in_=in_, func=AF.Copy), reads=reads, writes=writes)
        else:
            self.P.op(eng, lambda h: h.tensor_copy(out, in_), reads=reads, writes=writes)

    def recip(self, out, in_, reads, writes):
        self.P.op("dve", lambda h: h.reciprocal(out, in_), reads=reads, writes=writes)

    def memset(self, eng, out, val, writes):
        self.P.op(eng, lambda h: h.memset(out, val), writes=writes)

    def dma(self, q, out, in_, reads, writes, store=None):
        if store is None:
            store = writes[0].name in self.dram_names
        self.P.dma(q, out, in_, reads=reads, writes=writes, store=store)

    def ld(self, q, dst, src_ap, src=None):
        self.P.dma(q, dst.t if isinstance(dst, Buf) else dst[0], src_ap,
                   reads=[src.r] if src is not None else [], writes=[dst.r if isinstance(dst, Buf) else dst[1]])

    def consts(self):
        self.ident = self.alloc("ident", [128, 128], BF16)
        self.ones = self.alloc("ones", [128, 128], BF16)
        self.epsb = self.alloc("epsb", [128, 1], F32)
        idin = self.inp("c_ident", [128, 128], BF16)
        self.dma("sp", self.ident.t, idin.t, [], [self.ident.r])
        self.memset("pool", self.ones.t, 1.0, [self.ones.r])
        self.memset("pool", self.epsb.t, EPS, [self.epsb.r])
        self.cmark = self.top

    def norm_pre(self, xt, A, SH, tmp, xn, junk, stat):
        ss = stat.t[:, 0:1]
        rs = stat.t[:, 1:2]
        self.act(junk.t, xt.t, AF.Square, [xt.r], [junk.r, stat.r], accum=ss)
        self.act(rs, ss, AF.Sqrt, [stat.r, self.epsb.r], [stat.r], bias=self.epsb.t[:, 0:1], scale=1.0 / D)
        self.recip(rs, rs, [stat.r], [stat.r])
        self.stt("dve", tmp.t, xt.t, rs, A.t, ALU.mult, ALU.mult, [xt.r, stat.r, A.r], [tmp.r])
        self.tt("pool", xn.t, tmp.t, SH.t, ALU.add, [tmp.r, SH.r], [xn.r])

    def norm_tr(self, xn, hT, col0, ntok=128):
        pT = self.pb[7]
        pTv = pT.t.bitcast(BF16)
        for kc in range(8):
            self.tr(pTv[:, kc * 128:kc * 128 + ntok], xn.t[:ntok, kc * 128:(kc + 1) * 128], self.ident.t[:ntok, :ntok],
                    [xn.r, self.ident.r], [pT.r], signal=(kc == 7))
        src = pTv.rearrange("p (a b) -> p a b", a=8)[:, :, :ntok]
        self.cp("act", hT.t[:, :, col0:col0 + ntok], src, [pT.r], [hT.r])

    def norm_T(self, xt, A, SH, tmp, xn, junk, stat, hT, col0, ntok=128, plain_g=None):
        self.norm_pre(xt, A, SH, tmp, xn, junk, stat)
        self.norm_tr(xn, hT, col0, ntok)

    def pool_of(self, name, n, shape, dt):
        return {"b": [self.alloc("%s%d" % (name, i), shape, dt) for i in range(n)], "i": 0}

    def nxt(self, pool):
        b = pool["b"][pool["i"] % len(pool["b"])]
        pool["i"] += 1
        return b

    def bank(self):
        b = self.pb[self._bk % 6]
        self._bk += 1
        return b

    def phase_mod(self):
        ccol = self.inp("ccol", [128, 8, 2])
        modw = self.inp("mod_w", [2, 1024, 6144])
        modb = self.inp("mod_b", [2, 6144])
        self.modv = self.scratch("modv", [2, 2, 6144], F32, debug=self.dbg)
        sc = self.alloc("scol", [128, 8, 2], F32)
        self.dma("sp", sc.t, ccol.t, [], [sc.r])
        self.act(sc.t, sc.t, AF.Silu, [sc.r], [sc.r])
        mrow = self.alloc("mrow", [2, 6144], F32)
        mb2 = self.alloc("mb2", [2, 6144], F32)
        wb = [self.alloc("mwb%d" % i, [128, 8, 512], F32) for i in range(2)]
        for i in range(2):
            self.dma("sp", mb2.t, modb.t[i].partition_broadcast(2), [], [mb2.r])
            for n in range(12):
                w = wb[n % 2]
                self.dma("sp", w.t, modw.t[i, :, n * 512:(n + 1) * 512].rearrange("(kc p) f -> p kc f", p=128), [], [w.r])
                pbk = self.pb[n % 2]
                for kc in range(8):
                    self.mm(pbk.t[0:2, :], sc.t[:, kc, :], w.t[:, kc, :], kc == 0, kc == 7, [sc.r, w.r], [pbk.r])
                self.tt("dve", mrow.t[:, n * 512:(n + 1) * 512], pbk.t[0:2, :], mb2.t[:, n * 512:(n + 1) * 512], ALU.add,
                        [pbk.r, mb2.r], [mrow.r])
            self.dma("sp", self.modv.t[i], mrow.t, [mrow.r], [self.modv.r])

    def mod_tiles(self, layer, which, i_sh, i_sc, g_ap):
        A = self.alloc("modA", [128, 1024], F32)
        SH = self.alloc("modSH", [128, 1024], F32)
        mv = self.modv.t[layer, which]
        self.dma("sp", A.t, mv[i_sc * D:(i_sc + 1) * D].partition_broadcast(128), [self.modv.r], [A.r])
        self.dma("sp", SH.t, g_ap.partition_broadcast(128), [], [SH.r])
        self.stt("dve", A.t, A.t, 1.0, SH.t, ALU.add, ALU.mult, [A.r, SH.r], [A.r])
        self.dma("sp", SH.t, mv[i_sh * D:(i_sh + 1) * D].partition_broadcast(128), [self.modv.r], [SH.r])
        return A, SH

    def load_w_bf16(self, dst, src_ap, kcn):
        for kc in range(kcn):
            self.dma("pool", dst.t[:, kc, :], src_ap[kc * 128:(kc + 1) * 128, :], [], [dst.r])

    def norm_bufs(self):
        nb = {}
        nb["x"] = self.pool_of("nx", 2, [128, 1024], F32)
        nb["tmp"] = self.alloc("ntmp", [128, 1024], F32)
        nb["xn"] = self.pool_of("nxn", 2, [128, 1024], BF16)
        nb["junk"] = self.alloc("njunk", [128, 1024], BF16)
        nb["stat"] = self.pool_of("nstat", 2, [128, 2], F32)
        return nb

    def proj_pass(self, xin, T, A, SH, w, fm, tm, nb, post=None):
        skip = os.environ.get("KSKIP", "")
        fm = [sp for sp in fm if sp["kind"] not in skip.split(",")]
        ng = T // GS
        hT = [self.alloc("hT%d" % i, [128, 8, GS + 32], BF16) for i in range(3)]
        for hb in hT:
            self.memset("pool", hb.t, 0.0, [hb.r])
        tmpc = self.pool_of("tmpc", 2, [128, GS], F32)
        tmpd = self.pool_of("tmpd", 6, [128, GS], F32)
        obf = self.pool_of("obf", 4 if not any("emit" in sp_ for sp_ in fm) else 1, [128, GS], BF16)
        tst = self.pool_of("tst", 2, [128, 512], BF16) if tm else None
        cs = self.pool_of("cs", 2, [128, 2, GS], F32) if any(sp_["kind"] == "rope" for sp_ in fm) else None
        pend_xn = {}

        def pre(k):
            xs = []
            for j in range(GS // 128):
                xt = self.nxt(nb["x"])
                t0 = k * GS + j * 128
                self.dma("sp", xt.t, xin.t[t0:t0 + 128, :], [xin.r], [xt.r])
                xn = self.nxt(nb["xn"])
                self.norm_pre(xt, A, SH, nb["tmp"], xn, nb["junk"], self.nxt(nb["stat"]))
                xs.append(xn)
            pend_xn[k] = xs

        def trn(k):
            h_ = hT[k % 3]
            for j, xn in enumerate(pend_xn.pop(k)):
                self.norm_tr(xn, h_, 16 + j * 128)
            if k == 0:
                self.memset("pool", h_.t[:, :, 15:16], 0.0, [h_.r])
            else:
                hp = hT[(k - 1) % 3]
                self.cp("pool", h_.t[:, :, 15:16], hp.t[:, :, GS + 15:GS + 16], [hp.r], [h_.r])

        for k0 in range(min(2, ng)):
            pre(k0)
            trn(k0)
        for gg in range(ng):
            if gg + 2 < ng:
                pre(gg + 2)
            h = hT[gg % 3]
            if gg == ng - 1:
                self.memset("pool", h.t[:, :, GS + 16:GS + 17], 0.0, [h.r])
            else:
                hn = hT[(gg + 1) % 3]
                self.cp("pool", h.t[:, :, GS + 16:GS + 17], hn.t[:, :, 16:17], [hn.r], [h.r])
            g0 = gg * GS
            for sp in fm:
                kind = sp["kind"]
                if kind == "rope":
                    c = self.nxt(cs)
                    if "nodma" in os.environ.get("ROPEVAR", ""):
                        self.memset("pool", c.t, 1.0, [c.r])
                    else:
                        self.dma("sp", c.t[:, 0, :], sp["cos"].t[:, g0:g0 + GS], [], [c.r])
                        self.dma("sp", c.t[:, 1, :], sp["sin"].t[:, g0:g0 + GS], [], [c.r])
                for ci in sp.get("order", range(sp["n"])):
                    pbk = self.bank()
                    col = sp["col0"] + ci * 128
                    for kc in range(8):
                        self.mm(pbk.t[:, 0:GS + 4], w.t[:, kc, col:col + 128], h.t[:, kc, 14:GS + 18], kc == 0, kc == 7,
                                [w.r, h.r], [pbk.r])
                    if kind == "conv":
                        cw = sp["cw"]
                        k = sp["cwi0"] + ci
                        t1 = self.nxt(tmpc)
                        ob = self.nxt(obf)
                        self.act(t1.t, pbk.t[:, 2:GS + 2], AF.Identity, [pbk.r, cw.r], [t1.r],
                                 bias=cw.t[:, k, 3:4], scale=cw.t[:, k, 1:2])
                        self.stt("dve", t1.t, pbk.t[:, 1:GS + 1], cw.t[:, k, 0:1], t1.t, ALU.mult, ALU.add,
                                 [pbk.r, cw.r, t1.r], [t1.r])
                        if "emit" in sp:
                            t3 = self.nxt(tmpd)
                            self.stt("dve", t3.t, pbk.t[:, 3:GS + 3], cw.t[:, k, 2:3], t1.t, ALU.mult, ALU.add,
                                     [pbk.r, cw.r, t1.r], [t3.r])
                            sp["emit"](ci, t3, g0)
                        else:
                            self.stt("dve", ob.t, pbk.t[:, 3:GS + 3], cw.t[:, k, 2:3], t1.t, ALU.mult, ALU.add,
                                     [pbk.r, cw.r, t1.r], [ob.r])
                            r0 = sp["row0"] + ci * 128
                            self.dma("sp", sp["out"].t[r0:r0 + 128, g0:g0 + GS], ob.t, [ob.r], [sp["out"].r])
                    elif kind == "rope":
                        kb = self.nxt(obf)
                        ob = self.nxt(obf)
                        t1 = self.nxt(tmpc)
                        t2 = self.nxt(tmpd)
                        p2 = self.pb[6]
                        self.cp("act", kb.t, pbk.t[:, 2:GS + 2], [pbk.r], [kb.r])
                        if os.environ.get("ROPEMM", "1") == "1":
                            self.mm(p2.t[:, 0:GS], self.rperm.t, kb.t, True, True, [self.rperm.r, kb.r], [p2.r])
                        else:
                            p2 = pbk
                        if "nott" in os.environ.get("ROPEVAR", ""):
                            self.cp("dve", t1.t, pbk.t[:, 2:GS + 2], [pbk.r, c.r], [t1.r])
                            self.cp("dve", t2.t, p2.t[:, 0:GS], [p2.r, c.r], [t2.r])
                        else:
                            self.tt("dve", t1.t, pbk.t[:, 2:GS + 2], c.t[:, 0, :], ALU.mult, [pbk.r, c.r, kb.r], [t1.r])
                            self.tt("dve", t2.t, p2.t[:, 0:GS], c.t[:, 1, :], ALU.mult, [p2.r, c.r], [t2.r])
                        self.tt(os.environ.get("ROPEADD", "pool"), ob.t, t1.t, t2.t, ALU.add, [t1.r, t2.r], [ob.r])
                        to = sp["toff"] + g0
                        self.dma("sp", sp["out"].t[ci, :, to:to + GS], ob.t, [ob.r], [sp["out"].r])
                    else:
                        ob = self.nxt(obf)
                        self.cp("act", ob.t, pbk.t[:, 2:GS + 2], [pbk.r], [ob.r])
                        to = sp["toff"] + g0
                        self.dma("sp", sp["out"].t[ci, :, to:to + GS], ob.t, [ob.r], [sp["out"].r])
            for sp in tm:
                for j in range(GS // 128):
                    pbk = self.bank()
                    for kc in range(8):
                        self.mm(pbk.t[:, 0:512], h.t[:, kc, 16 + j * 128:16 + (j + 1) * 128],
                                w.t[:, kc, sp["col0"]:sp["col0"] + 512], kc == 0, kc == 7, [w.r, h.r], [pbk.r])
                    st = self.nxt(tst)
                    self.cp("act", st.t, pbk.t[:, 0:512], [pbk.r], [st.r])
                    ro = sp["roff"] + g0 + j * 128
                    self.dma("sp", sp["out"].t[ro:ro + 128, :], st.t, [st.r], [sp["out"].r])
            if post is not None:
                post(gg, g0, h)
            if gg + 2 < ng:
                trn(gg + 2)

    def phase_inproj(self):
        dbg = self.dbg
        self.x_full = self.inp("x_full", [L, D])
        self.x_ext = self.inp("x_ext", [EXT, D])
        ctx = self.inp("ctx", [256, D])
        w_in = self.inp("w_in", [D, 3072])
        cwin = self.inp("hy_cw", [128, 12, 4])
        gmix = self.inp("norm_mix_g", [2, D])
        cosk = self.inp("ropek_cos", [128, L])
        sink = self.inp("ropek_sin", [128, L])
        cosq = self.inp("ropeq_cos", [128, EXT])
        sinq = self.inp("ropeq_sin", [128, EXT])
        rp = self.inp("c_rperm", [128, 128], BF16)
        self.VX1 = self.scratch("VX1", [1024, L], BF16, debug=dbg)
        self.X2 = self.scratch("X2", [512, EXT], BF16, debug=dbg)
        self.KT = self.scratch("KT", [4, 128, L + 256], BF16, debug=dbg)
        self.QT = self.scratch("QT", [4, 128, EXT], BF16, debug=dbg)
        self.VT = self.scratch("VT", [L + 256, 512], BF16, debug=dbg)
        self.top = self.cmark
        w = self.alloc("w_in", [128, 8, 3072], BF16)
        self.load_w_bf16(w, w_in.t, 8)
        cw = self.alloc("cw", [128, 12, 4], F32)
        self.dma("sp", cw.t, cwin.t, [], [cw.r])
        self.rperm = self.alloc("rperm", [128, 128], BF16)
        self.dma("sp", self.rperm.t, rp.t, [], [self.rperm.r])
        nb = self.norm_bufs()
        mark = self.top
        A, SH = self.mod_tiles(0, 1, 0, 1, gmix.t[0])
        import os
        self.proj_pass(ctx, 256, A, SH, w,
                       [dict(kind="rope", col0=2048, n=4, cos=cosk, sin=sink, out=self.KT, toff=0)] if os.environ.get("CTXROPE") else
                       [dict(kind="plain", col0=2048, n=4, out=self.KT, toff=0)],
                       [dict(col0=2560, out=self.VT, roff=0)], nb)
        self.P.barrier()
        self.top = mark
        if self.stop_after == "ctx":
            return
        A, SH = self.mod_tiles(0, 0, 0, 1, gmix.t[0])
        mark2 = self.top
        self.proj_pass(self.x_full, L, A, SH, w,
                       [dict(kind="conv", col0=0, n=8, cw=cw, cwi0=0, out=self.VX1, row0=0),
                        dict(kind="rope", col0=2048, n=4, cos=cosk, sin=sink, out=self.KT, toff=256)],
                       [dict(col0=2560, out=self.VT, roff=256)], nb)
        self.P.barrier()
        self.top = mark2
        self.proj_pass(self.x_ext, EXT, A, SH, w,
                       [dict(kind="conv", col0=1024, n=4, cw=cw, cwi0=8, out=self.X2, row0=0),
                        dict(kind="rope", col0=1536, n=4, cos=cosq, sin=sinq, out=self.QT, toff=0)],
                       [], nb)


    def phase_attn(self):
        dal = self.inp("da_lambda", [4, 64])
        subg = self.inp("subln_col", [128, 1])
        self.YT = self.scratch("YT", [1024, EXT], BF16, debug=self.dbg)
        self.top = self.cmark
        LAM_INIT = 0.8 - 0.6 * math.exp(0.0)
        lt = self.alloc("lt", [128, 256], F32)
        pr = self.alloc("lpr", [128, 128], F32)
        ls = self.alloc("ls", [128, 4], F32)
        negl = self.alloc("negl", [128, 1], F32)
        gsub = self.alloc("gsub", [128, 1], F32)
        self.dma("sp", lt.t, dal.t.rearrange("a b -> (a b)").partition_broadcast(128), [], [lt.r])
        self.tt("dve", pr.t[:, 0:64], lt.t[:, 0:64], lt.t[:, 64:128], ALU.mult, [lt.r], [pr.r])
        self.tt("dve", pr.t[:, 64:128], lt.t[:, 128:192], lt.t[:, 192:256], ALU.mult, [lt.r, pr.r], [pr.r])
        self.act(lt.t[:, 0:64], pr.t[:, 0:64], AF.Identity, [pr.r, lt.r], [lt.r, ls.r], accum=ls.t[:, 0:1])
        self.act(lt.t[:, 64:128], pr.t[:, 64:128], AF.Identity, [pr.r, lt.r, ls.r], [lt.r, ls.r], accum=ls.t[:, 1:2])
        self.act(ls.t[:, 2:4], ls.t[:, 0:2], AF.Exp, [ls.r], [ls.r])
        self.tt("dve", negl.t, ls.t[:, 3:4], ls.t[:, 2:3], ALU.subtract, [ls.r], [negl.r])
        self.ts("dve", negl.t, negl.t, -LAM_INIT, None, ALU.add, None, [negl.r], [negl.r])
        self.dma("sp", gsub.t, subg.t, [], [gsub.r])
        self.ts("dve", gsub.t, gsub.t, 1.0 - LAM_INIT, None, ALU.mult, None, [gsub.r], [gsub.r])
        NK = (L + 256) // 128
        Kh = self.alloc("Kh", [128, L + 256], BF16)
        Vh = self.alloc("Vh", [128, NK, 128], BF16)
        Qh = self.alloc("Qh", [128, EXT], BF16)
        Eb = [self.alloc("Eb%d" % i, [128, 2, 512], BF16) for i in range(2)]
        f = {n_: self.alloc("at_" + n_, [128, 512], F32) for n_ in ("r0", "r1", "t0", "t1", "o", "rs", "y")}
        osq = self.alloc("at_osq", [128, 512], BF16)
        acc0 = self.alloc("at_acc0", [128, 512], F32)
        ones32 = self.alloc("at_ones32", [128, 128], F32)
        self.memset("pool", ones32.t, 1.0, [ones32.r])
        yb = [self.alloc("at_yb%d" % i, [128, 512], BF16) for i in range(2)]
        pb = self.pb
        it = 0
        for h in range(4):
            self.dma("sp", Kh.t, self.KT.t[h], [self.KT.r], [Kh.r])
            self.dma("sp", Vh.t, self.VT.t[:, h * 128:(h + 1) * 128].rearrange("(kt p) d -> p kt d", p=128), [self.VT.r], [Vh.r])
            self.dma("sp", Qh.t, self.QT.t[h], [self.QT.r], [Qh.r])
            groups = [(q0_, min(512, EXT - q0_)) for q0_ in range(0, EXT, 512)]

            def emit_qk(kt, q0, n):
                par = kt % 2
                for m in range(2):
                    sb_ = pb[2 * par + m]
                    self.mm(sb_.t[:, :n], Kh.t[64 * m:64 * m + 64, kt * 128:(kt + 1) * 128],
                            Qh.t[64 * m:64 * m + 64, q0:q0 + n], True, True, [Kh.r, Qh.r], [sb_.r])

            for gi_, (q0, n) in enumerate(groups):
                if gi_ == 0:
                    emit_qk(0, q0, n)
                for kt in range(NK):
                    par = kt % 2
                    if kt + 1 < NK:
                        emit_qk(kt + 1, q0, n)
                    E = Eb[par]
                    sv = self.ps[:, 1024 * par:1024 * par + 1024].rearrange("p (a b) -> p a b", a=2)[:, :, :n]
                    self.act(E.t[:, :, :n], sv, AF.Exp, [pb[2 * par].r, pb[2 * par + 1].r], [E.r], scale=0.125)
                    for m in range(2):
                        self.mm(pb[4 + m].t[:, :n], Vh.t[:, kt, :], E.t[:, m, :n], kt == 0, kt == NK - 1, [Vh.r, E.r], [pb[4 + m].r])
                    self.mm(pb[7].t[:, :n], self.ones.t, E.t[:, 1, :n], kt == 0, kt == NK - 1, [self.ones.r, E.r], [pb[7].r])
                    if kt == 0:
                        self.cp("dve", acc0.t[:, :n], E.t[:, 0, :n], [E.r], [acc0.r])
                    else:
                        self.tt("dve", acc0.t[:, :n], acc0.t[:, :n], E.t[:, 0, :n], ALU.add, [acc0.r, E.r], [acc0.r])
                self.mm(pb[6].t[:, :n], ones32.t, acc0.t[:, :n], True, True, [ones32.r, acc0.r], [pb[6].r])
                if gi_ + 1 < len(groups):
                    emit_qk(0, *groups[gi_ + 1])
                self.recip(f["r0"].t[:, :n], pb[6].t[:, :n], [pb[6].r], [f["r0"].r])
                self.recip(f["r1"].t[:, :n], pb[7].t[:, :n], [pb[7].r], [f["r1"].r])
                self.tt("dve", f["t0"].t[:, :n], pb[4].t[:, :n], f["r0"].t[:, :n], ALU.mult, [pb[4].r, f["r0"].r], [f["t0"].r])
                self.tt("dve", f["t1"].t[:, :n], pb[5].t[:, :n], f["r1"].t[:, :n], ALU.mult, [pb[5].r, f["r1"].r], [f["t1"].r])
                self.stt("dve", f["o"].t[:, :n], f["t1"].t[:, :n], negl.t[:, 0:1], f["t0"].t[:, :n], ALU.mult, ALU.add,
                         [f["t1"].r, f["t0"].r, negl.r], [f["o"].r])
                self.act(osq.t[:, :n], f["o"].t[:, :n], AF.Square, [f["o"].r], [osq.r])
                self.mm(pb[6].t[:, :n], self.ones.t, osq.t[:, :n], True, True, [self.ones.r, osq.r], [pb[6].r])
                self.act(f["rs"].t[:, :n], pb[6].t[:, :n], AF.Sqrt, [pb[6].r, self.epsb.r], [f["rs"].r],
                         bias=self.epsb.t[:, 0:1], scale=1.0 / 128)
                self.recip(f["rs"].t[:, :n], f["rs"].t[:, :n], [f["rs"].r], [f["rs"].r])
                self.tt("dve", f["y"].t[:, :n], f["o"].t[:, :n], f["rs"].t[:, :n], ALU.mult, [f["o"].r, f["rs"].r], [f["y"].r])
                y2 = yb[it % 2]
                it += 1
                self.ts("dve", y2.t[:, :n], f["y"].t[:, :n], gsub.t[:, 0:1], None, ALU.mult, None, [f["y"].r, gsub.r], [y2.r])
                r0 = 512 + h * 128
                self.dma("sp", self.YT.t[r0:r0 + 128, q0:q0 + n], y2.t[:, :n], [y2.r], [self.YT.r])


    def load_w_gated(self, dst, src_ap, kcn, gate_ap):
        mark = self.top
        G = self.alloc("gateG", [128, 1024], F32)
        self.dma("sp", G.t, gate_ap.partition_broadcast(128), [self.modv.r], [G.r])
        stg = self.pool_of("wstg", 2, [128, 1024], F32)
        for kc in range(kcn):
            st = self.nxt(stg)
            self.dma("sp", st.t, src_ap[kc * 128:(kc + 1) * 128, :], [], [st.r])
            self.tt("dve", dst.t[:, kc, :], st.t, G.t, ALU.mult, [st.r, G.r], [dst.r])
        self.P.barrier()
        self.top = mark

    def resid_store(self, xin, xout, actT, kcn, Wd, g0, final_g=None):
        for j in range(GS // 128):
            xr = self.nxt(self.rx)
            r0 = g0 + j * 128
            self.dma("sp", xr.t, xin.t[r0:r0 + 128, :], [xin.r], [xr.r])
            xo = self.nxt(self.ro)
            for half in range(2):
                pbk = self.bank()
                for kc in range(kcn):
                    self.mm(pbk.t[:, 0:512], actT.t[:, kc, j * 128:(j + 1) * 128], Wd.t[:, kc, half * 512:(half + 1) * 512],
                            kc == 0, kc == kcn - 1, [actT.r, Wd.r], [pbk.r])
                self.tt("dve", xo.t[:, half * 512:(half + 1) * 512], pbk.t[:, 0:512], xr.t[:, half * 512:(half + 1) * 512],
                        ALU.add, [pbk.r, xr.r], [xo.r])
            if final_g is not None:
                st = self.nxt(self.fst)
                self.act(self.fjunk.t, xo.t, AF.Square, [xo.r], [self.fjunk.r, st.r], accum=st.t[:, 0:1])
                self.act(st.t[:, 1:2], st.t[:, 0:1], AF.Sqrt, [st.r, self.epsb.r], [st.r], bias=self.epsb.t[:, 0:1], scale=1.0 / D)
                self.recip(st.t[:, 1:2], st.t[:, 1:2], [st.r], [st.r])
                self.stt("dve", xo.t, xo.t, st.t[:, 1:2], final_g.t, ALU.mult, ALU.mult, [xo.r, st.r, final_g.r], [xo.r])
            self.dma("sp", xout.t[r0:r0 + 128, :], xo.t, [xo.r], [xout.r])

    def resid_bufs(self):
        self.rx = self.pool_of("rx", 2, [128, 1024], F32)
        self.ro = self.pool_of("ro", 2, [128, 1024], F32)

    def phase_outproj(self, name, yT_dram, xin, w_ap, layer):
        xout = self.scratch(name, [EXT, D], F32, debug=self.dbg)
        self.top = self.cmark
        Wo = self.alloc("Wo", [128, 8, 1024], BF16)
        self.load_w_gated(Wo, w_ap, 8, self.modv.t[layer, 0, 2 * D:3 * D])
        self.resid_bufs()
        yb = self.pool_of("yTb", 2, [128, 8, GS], BF16)
        for g in range(EXT // GS):
            g0 = g * GS
            y = self.nxt(yb)
            self.dma("sp", y.t, yT_dram.t[:, g0:g0 + GS].rearrange("(kc p) t -> p kc t", p=128), [yT_dram.r], [y.r])
            self.resid_store(xin, xout, y, 8, Wo, g0)
        return xout

    def phase_ffn(self, name, xin, layer, final=False):
        wup = self.inp("ffn_w_up%d" % layer, [D, 2 * DFF])
        wdn = self.inp("ffn_w_down%d" % layer, [DFF, D])
        cwin = self.inp("ffn_cw%d" % layer, [128, 44, 4])
        gffn = self.inp("norm_ffn_g", [2, D]) if "norm_ffn_g" not in self.ins else self.ins["norm_ffn_g"]
        xout = self.outp("out", [EXT, D], F32) if final else self.scratch(name, [EXT, D], F32, debug=self.dbg)
        self.top = self.cmark
        Wu = self.alloc("Wu", [128, 8, 2 * DFF], BF16)
        self.load_w_bf16(Wu, wup.t, 8)
        Wd = self.alloc("Wd", [128, 22, 1024], BF16)
        self.load_w_gated(Wd, wdn.t, 22, self.modv.t[layer, 0, 5 * D:6 * D])
        cw = self.alloc("cwf", [128, 44, 4], F32)
        self.dma("sp", cw.t, cwin.t, [], [cw.r])
        fg = None
        if final:
            fgi = self.inp("final_norm_g", [D])
            fg = self.alloc("fg", [128, 1024], F32)
            self.dma("sp", fg.t, fgi.t.partition_broadcast(128), [], [fg.r])
            self.fst = self.pool_of("fst", 2, [128, 2], F32)
            self.fjunk = self.alloc("fjunk", [128, 1024], BF16)
        A, SH = self.mod_tiles(layer, 0, 3, 4, gffn.t[layer])
        nb = self.norm_bufs_small()
        self.resid_bufs_small()
        gT = self.alloc("gT", [128, 22, GS], BF16)
        sil = self.pool_of("sil", 2, [128, GS], F32)
        hold = {}

        pend = []

        def flush(keep):
            while len(pend) > keep:
                j_, gb, ub = pend.pop(0)
                sb_ = self.nxt(sil)
                self.act(sb_.t, gb.t, AF.Silu, [gb.r], [sb_.r])
                self.tt("pool", gT.t[:, j_, :], sb_.t, ub.t, ALU.mult, [sb_.r, ub.r], [gT.r])

        def emit(ci, buf, g0):
            if ci < 22:
                hold["g"] = buf
            else:
                pend.append((ci - 22, hold["g"], buf))
                flush(1)

        def post(gg, g0, h):
            flush(0)
            self.resid_store(xin, xout, gT, 22, Wd, g0, final_g=fg)

        order = []
        for j in range(22):
            order += [j, 22 + j]
        self.proj_pass(xin, EXT, A, SH, Wu,
                       [dict(kind="conv", col0=0, n=44, cw=cw, cwi0=0, order=order, emit=emit)], [], nb, post=post)
        return xout

    def norm_bufs_small(self):
        nb = {}
        nb["x"] = self.pool_of("nx", 1, [128, 1024], F32)
        nb["tmp"] = self.alloc("ntmp", [128, 1024], F32)
        nb["xn"] = self.pool_of("nxn", 2, [128, 1024], BF16)
        nb["junk"] = self.alloc("njunk", [128, 1024], BF16)
        nb["stat"] = self.pool_of("nstat", 2, [128, 2], F32)
        return nb

    def resid_bufs_small(self):
        self.rx = self.pool_of("rx", 1, [128, 1024], F32)
        self.ro = self.pool_of("ro", 1, [128, 1024], F32)


    def phase_sgu(self, name, xin, layer=1):
        win = self.inp("sgu_w_in", [D, 2048])
        bcol_i = self.inp("sgu_bu_col", [128, 8])
        bv_i = self.inp("sgu_bv", [1024])
        lng_i = self.inp("sgu_ln_g", [1024])
        lnb_i = self.inp("sgu_ln_b", [1024])
        wsT_i = self.inp("sgu_wsT", [128, 8, 128])
        bs_i = self.inp("sgu_b_s", [1, 1024])
        wout = self.inp("sgu_w_out", [D, D])
        gmix = self.ins["norm_mix_g"]
        xout = self.scratch(name, [EXT, D], F32, debug=self.dbg)
        self.top = self.cmark
        Wi = self.alloc("Wi", [128, 8, 2048], BF16)
        self.load_w_bf16(Wi, win.t, 8)
        Wo = self.alloc("Wo", [128, 8, 1024], BF16)
        self.load_w_gated(Wo, wout.t, 8, self.modv.t[layer, 0, 2 * D:3 * D])
        wsT = self.alloc("wsT", [128, 8, 128], BF16)
        self.dma("pool", wsT.t, wsT_i.t, [], [wsT.r])
        bsr = self.alloc("bsr", [1, 1024], BF16)
        self.dma("pool", bsr.t, bs_i.t, [], [bsr.r])
        bcol = self.alloc("bcol", [128, 8], F32)
        self.dma("sp", bcol.t, bcol_i.t, [], [bcol.r])
        BV = self.alloc("BV", [128, 1024], F32)
        LNG = self.alloc("LNG", [128, 1024], F32)
        LNB = self.alloc("LNB", [128, 1024], F32)
        self.dma("sp", BV.t, bv_i.t.partition_broadcast(128), [], [BV.r])
        self.dma("sp", LNG.t, lng_i.t.partition_broadcast(128), [], [LNG.r])
        self.dma("sp", LNB.t, lnb_i.t.partition_broadcast(128), [], [LNB.r])
        A, SH = self.mod_tiles(layer, 0, 0, 1, gmix.t[layer])
        nb = self.norm_bufs()
        self.resid_bufs()
        hTb = self.pool_of("sg_hT", 2, [128, 8, GS], BF16)
        uT = self.alloc("sg_uT", [128, 8, GS], F32)
        guT = self.pool_of("sg_guT", 2, [128, 8, GS], BF16)
        vtp = self.pool_of("sg_vt", 2, [128, 1024], F32)
        vgp = self.pool_of("sg_vg", 2, [128, 1024], F32)
        vb = self.pool_of("sg_vb", 2, [128, 1024], BF16)
        st = self.pool_of("sg_st", 2, [128, 8], F32)
        for g in range(EXT // GS):
            g0 = g * GS
            hT = self.nxt(hTb)
            gu = self.nxt(guT)
            for j in range(GS // 128):
                xt = self.nxt(nb["x"])
                self.dma("sp", xt.t, xin.t[g0 + j * 128:g0 + (j + 1) * 128, :], [xin.r], [xt.r])
                self.norm_T(xt, A, SH, nb["tmp"], self.nxt(nb["xn"]), nb["junk"], self.nxt(nb["stat"]), hT, j * 128)
            for c in range(8):
                pbk = self.bank()
                for kc in range(8):
                    self.mm(pbk.t[:, 0:GS], Wi.t[:, kc, c * 128:(c + 1) * 128], hT.t[:, kc, :], kc == 0, kc == 7, [Wi.r, hT.r], [pbk.r])
                self.act(uT.t[:, c, :], pbk.t[:, 0:GS], AF.Gelu, [pbk.r, bcol.r], [uT.r], bias=bcol.t[:, c:c + 1])
            for j in range(GS // 128):
                vt, vg = self.nxt(vtp), self.nxt(vgp)
                for half in range(2):
                    pbk = self.bank()
                    for kc in range(8):
                        self.mm(pbk.t[:, 0:512], hT.t[:, kc, j * 128:(j + 1) * 128],
                                Wi.t[:, kc, 1024 + half * 512:1024 + (half + 1) * 512], kc == 0, kc == 7, [Wi.r, hT.r], [pbk.r])
                    self.tt("dve", vt.t[:, half * 512:(half + 1) * 512], pbk.t[:, 0:512], BV.t[:, half * 512:(half + 1) * 512],
                            ALU.add, [pbk.r, BV.r], [vt.r])
                s_ = self.nxt(st)
                self.act(vg.t, vt.t, AF.Gelu, [vt.r], [vg.r, s_.r], accum=s_.t[:, 0:1])
                self.act(vt.t, vg.t, AF.Square, [vg.r, s_.r], [vt.r, s_.r], accum=s_.t[:, 1:2])
                self.ts("dve", s_.t[:, 2:3], s_.t[:, 0:1], 1.0 / 1024, None, ALU.mult, None, [s_.r], [s_.r])
                self.tt("dve", s_.t[:, 3:4], s_.t[:, 2:3], s_.t[:, 2:3], ALU.mult, [s_.r], [s_.r])
                self.stt("dve", s_.t[:, 4:5], s_.t[:, 1:2], 1.0 / 1024, s_.t[:, 3:4], ALU.mult, ALU.subtract, [s_.r], [s_.r])
                self.act(s_.t[:, 5:6], s_.t[:, 4:5], AF.Sqrt, [s_.r, self.epsb.r], [s_.r], bias=self.epsb.t[:, 0:1])
                self.recip(s_.t[:, 5:6], s_.t[:, 5:6], [s_.r], [s_.r])
                self.ts("dve", vg.t, vg.t, s_.t[:, 2:3], s_.t[:, 5:6], ALU.subtract, ALU.mult, [vg.r, s_.r], [vg.r])
                self.tt("pool", vg.t, vg.t, LNG.t, ALU.mult, [vg.r, LNG.r], [vg.r])
                v2 = self.nxt(vb)
                self.tt("dve", v2.t, vg.t, LNB.t, ALU.add, [vg.r, LNB.r], [v2.r])
                for a in range(2):
                    pbk = self.bank()
                    for q in range(4):
                        gi = 4 * a + q
                        self.P.op("pe", (lambda o_, l_, r_: (lambda h_: h_.matmul(o_, l_, r_, start=True, stop=False)))(
                            pbk.t[:, q * 128:(q + 1) * 128], v2.t[:, gi * 128:(gi + 1) * 128], wsT.t[:, gi, :]),
                            reads=[v2.r, wsT.r], writes=[pbk.r], signal=False)
                        self.P.op("pe", (lambda o_, l_, r_: (lambda h_: h_.matmul(o_, l_, r_, start=False, stop=True)))(
                            pbk.t[:, q * 128:(q + 1) * 128], self.ones.t[0:1, :], bsr.t[0:1, gi * 128:(gi + 1) * 128]),
                            reads=[self.ones.r, bsr.r], writes=[pbk.r], signal=(q == 3))
                    self.tt("dve", gu.t[:, 4 * a:4 * a + 4, j * 128:(j + 1) * 128],
                            pbk.t[:, 0:512].rearrange("p (a b) -> p a b", a=4), uT.t[:, 4 * a:4 * a + 4, j * 128:(j + 1) * 128],
                            ALU.mult, [pbk.r, uT.r], [gu.r])
            self.resid_store(xin, xout, gu, 8, Wo, g0)
        return xout


    def fft_stage1(self, X, Ad, f1):
        stg = self.pool_of("s1stg", 2, [128, 4, 512], BF16)
        for s_ in range(64):
            st = self.nxt(stg)
            for q in range(4):
                pbk = self.pb[(4 * (s_ % 2)) + q]
                self.mm(pbk.t[:, 0:512], f1.t[:, q * 128:(q + 1) * 128], X.t[:, s_, :], True, True, [f1.r, X.r], [pbk.r])
                self.cp("act" if q < 2 else "dve", st.t[:, q, :], pbk.t[:, 0:512], [pbk.r], [st.r])
            self.dma("sp", Ad.t[:, s_, :, :].rearrange("q p c -> p q c"), st.t, [st.r], [Ad.r])

    def load_B(self, B, Ad, k1g, G):
        kc, p0 = k1g // 128, k1g % 128
        for ri in range(2):
            self.dma("sp", B.t[ri * 64:(ri + 1) * 64, :, :], Ad.t[2 * kc + ri, :, p0:p0 + G, :], [Ad.r], [B.r])

    def load_tab(self, Tb, tab, k1g, G):
        self.dma("sp", Tb.t, tab.t[k1g // G].rearrange("p (k m) -> p k m", k=G), [], [Tb.r])

    def phase_hyena(self):
        G = 4
        zT = self.inp("hy_zT", [33, L])
        w1i = self.inp("hy_w1", [33, 64])
        w2i = self.inp("hy_w2", [64, 64])
        prmi = self.inp("hy_prm", [64, 4])
        w3i = self.inp("hy_w3", [64, 2048])
        dsk = self.inp("hy_bias", [2, 512])
        win = self.inp("hy_win", [128, 64, 512])
        f1i = self.inp("t_F1", [128, 512], BF16)
        Htab = self.inp("t_H", [64, 128, 512], BF16)
        Hctab = self.inp("t_Hc", [64, 128, 512], BF16)
        M1tab = self.inp("t_M1", [64, 128, 512], BF16)
        M2tab = self.inp("t_M2", [64, 128, 512], BF16)
        fvi = self.inp("t_Finv", [128, 512], BF16)
        fei = self.inp("t_FinvE", [128, 4, 68], BF16)
        A0 = self.scratch("fftA0", [4, 64, 128, 512], BF16)
        A1 = self.scratch("fftA1", [4, 64, 128, 512], BF16)
        Cd = self.scratch("fftC", [128, 256, 512], BF16)
        Hf = self.scratch("fftHf", [2, 2, 64, 256, 512], BF16)
        X1s = self.scratch("X1s", [128, 64, 512], BF16)
        self.top = self.cmark
        f1 = self.alloc("f1", [128, 512], BF16)
        fv = self.alloc("fv", [128, 512], BF16)
        fe = self.alloc("fe", [128, 4, 68], BF16)
        self.dma("sp", f1.t, f1i.t, [], [f1.r])
        self.dma("sp", fv.t, fvi.t, [], [fv.r])
        self.dma("sp", fe.t, fei.t, [], [fe.r])
        base = self.top
        h2T = self.alloc("h2T", [128, L], BF16)
        w3 = self.alloc("w3sb", [64, 2048], BF16)
        self.dma("pool", w3.t, w3i.t, [], [w3.r])
        mlp_mark = self.top
        w1 = self.alloc("w1sb", [33, 64], F32)
        w2 = self.alloc("w2sb", [64, 64], F32)
        prm = self.alloc("prm", [64, 4], F32)
        pr2 = self.alloc("pr2", [64, 4], F32)
        self.dma("sp", w1.t, w1i.t, [], [w1.r])
        self.dma("sp", w2.t, w2i.t, [], [w2.r])
        self.dma("sp", prm.t, prmi.t, [], [prm.r])
        for a in range(2):
            self.ts("dve", pr2.t[:, 2 * a:2 * a + 1], prm.t[:, 2 * a + 1:2 * a + 2], 1.0 / 3, None, ALU.mult, None, [prm.r, pr2.r], [pr2.r])
            self.tt("dve", pr2.t[:, 2 * a + 1:2 * a + 2], pr2.t[:, 2 * a:2 * a + 1], prm.t[:, 2 * a:2 * a + 1], ALU.mult, [prm.r, pr2.r], [pr2.r])
        ztb = self.pool_of("ztb", 2, [33, 512], F32)
        s3 = self.alloc("s3", [64, 512], F32)
        qq = self.alloc("qq", [64, 512], F32)
        h1 = self.alloc("h1", [64, 512], F32)

        def sin3(dst, src_ps, a):
            self.act(s3.t, src_ps.t[0:64, 0:512], AF.Sin, [src_ps.r, pr2.r], [s3.r], bias=pr2.t[:, 2 * a + 1:2 * a + 2], scale=pr2.t[:, 2 * a:2 * a + 1])
            self.tt("dve", qq.t, s3.t, s3.t, ALU.mult, [s3.r], [qq.r])
            self.ts("dve", qq.t, qq.t, -4.0, 3.0, ALU.mult, ALU.add, [qq.r], [qq.r])
            self.tt("dve", dst[0], qq.t, s3.t, ALU.mult, [qq.r, s3.r], [dst[1]])

        for ch in range(L // 512):
            zt = self.nxt(ztb)
            self.dma("sp", zt.t, zT.t[:, ch * 512:(ch + 1) * 512], [], [zt.r])
            pa, pb2 = self.pb[ch % 2], self.pb[2 + ch % 2]
            self.mm(pa.t[0:64, 0:512], w1.t, zt.t, True, True, [w1.r, zt.r], [pa.r])
            sin3((h1.t, h1.r), pa, 0)
            self.mm(pb2.t[0:64, 0:512], w2.t, h1.t, True, True, [w2.r, h1.r], [pb2.r])
            sin3((h2T.t[0:64, ch * 512:(ch + 1) * 512], h2T.r), pb2, 1)
        self.P.barrier()
        self.top = mlp_mark
        h2v = h2T.t[0:64, :].rearrange("p (j s) -> p s j", s=64)
        Dz = self.alloc("Dz", [128, 512], F32)
        omark = self.top
        for o in range(2):
            self.P.barrier()
            self.top = omark
            self.memset("pool", Dz.t, 0.0, [Dz.r])
            self.dma("sp", Dz.t[0:64, :], dsk.t[o].partition_broadcast(64), [], [Dz.r])
            Xf = self.alloc("Xf", [128, 64, 512], BF16)
            Xb = self.alloc("Xb", [128, 64, 512], BF16)
            wt = self.pool_of("wint", 2, [128, 512], F32)
            for s_ in range(64):
                wn = self.nxt(wt)
                self.dma("sp", wn.t, win.t[:, s_, :], [], [wn.r])
                for dr, Xd in ((0, Xf), (1, Xb)):
                    pbk = self.bank()
                    c0 = (o * 2 + dr) * 512
                    self.mm(pbk.t[:, 0:512], h2v[:, s_, :], w3.t[:, c0:c0 + 512], True, True, [h2T.r, w3.r], [pbk.r])
                    self.tt("dve", Xd.t[:, s_, :], pbk.t[:, 0:512], wn.t, ALU.mult, [pbk.r, wn.r], [Xd.r])
            self.memset("pool", Xb.t[0:1, 0, :], 0.0, [Xb.r])
            self.fft_stage1(Xf, A0, f1)
            self.fft_stage1(Xb, A1, f1)
            self.P.barrier()
            self.top = omark
            Bf = self.pool_of("Bf", 2, [128, G, 512], BF16)
            Bb = self.pool_of("Bb", 2, [128, G, 512], BF16)
            Ht = self.pool_of("Ht", 2, [128, G, 128], BF16)
            Hct = self.pool_of("Hct", 2, [128, G, 128], BF16)
            Hst = self.pool_of("Hst", 2, [128, G, 512], BF16)
            Hfo = Hf.t[o].rearrange("r k a c -> (r k) a c")
            def f_loads(k1g):
                bf, bb, ht, hct = self.nxt(Bf), self.nxt(Bb), self.nxt(Ht), self.nxt(Hct)
                self.load_B(bf, A0, k1g, G)
                self.load_B(bb, A1, k1g, G)
                self.load_tab(ht, Htab, k1g, G)
                self.load_tab(hct, Hctab, k1g, G)
                return bf, bb, ht, hct

            nxt_l = f_loads(0)
            for k1g in range(0, 256, G):
                bf, bb, ht, hct = nxt_l
                if k1g + G < 256:
                    nxt_l = f_loads(k1g + G)
                hs = self.nxt(Hst)
                for g in range(G):
                    pbk = self.bank()
                    self.mm(pbk.t[:, 0:512], ht.t[:, g, :], bf.t[:, g, :], True, False, [ht.r, bf.r], [pbk.r])
                    self.mm(pbk.t[:, 0:512], hct.t[:, g, :], bb.t[:, g, :], False, True, [hct.r, bb.r], [pbk.r])
                    self.tt("dve", hs.t[:, g, :], pbk.t[:, 0:512], Dz.t, ALU.add, [pbk.r, Dz.r], [hs.r])
                self.dma("sp", Hfo[:, k1g:k1g + G, :], hs.t, [hs.r], [Hf.r])
        self.P.barrier()
        self.top = base
        X = self.alloc("Xc", [128, 64, 512], BF16)
        cmark2 = self.top
        Ub = self.pool_of("Ub", 2, [128, L], BF16)
        xst = self.pool_of("xst", 2, [128, 8, 128], BF16)
        nt = 0
        for part in range(2):
            for cc in range(4):
                U = self.nxt(Ub)
                r0 = part * 512 + cc * 128
                self.dma("sp", U.t, self.VX1.t[r0:r0 + 128, :], [self.VX1.r], [U.r])
                Uv = U.t.rearrange("p (j s) -> p s j", s=64)
                for s0 in range(0, 64, 8):
                    pT = self.pb[6 + nt % 2]
                    nt += 1
                    pTv = pT.t.bitcast(BF16)
                    for i in range(8):
                        self.tr(pTv[:, i * 128:(i + 1) * 128], Uv[:, s0 + i, :], self.ident.t, [U.r, self.ident.r], [pT.r], signal=(i == 7))
                    src = pTv.rearrange("p (a b) -> p a b", a=8)
                    if part == 0:
                        self.cp("act", X.t[:, s0:s0 + 8, cc * 128:(cc + 1) * 128], src, [pT.r], [X.r])
                    else:
                        st = self.nxt(xst)
                        self.cp("act", st.t, src, [pT.r], [st.r])
                        self.dma("sp", X1s.t[:, s0:s0 + 8, cc * 128:(cc + 1) * 128], st.t, [st.r], [X1s.r])
        for conv in range(2):
            self.P.barrier()
            self.top = cmark2
            if conv == 1:
                x2sb = self.alloc("x2sb", [128, 4, EXT], BF16)
                yaT = self.alloc("yaT", [128, 4, EXT], BF16)
                self.dma("sp", x2sb.t, self.X2.t.rearrange("(cc p) t -> p cc t", p=128), [self.X2.r], [x2sb.r])
            m2 = self.top
            self.fft_stage1(X, A0, f1)
            self.P.barrier()
            self.top = m2
            Bt = self.pool_of("Bt", 2, [128, G, 512], BF16)
            Ht = self.pool_of("cHt", 2, [128, G, 128], BF16)
            M1t = self.pool_of("cM1", 2, [128, G, 128], BF16)
            M2t = self.pool_of("cM2", 2, [128, G, 128], BF16)
            HH1 = self.pool_of("HH1", 2, [128, G, 512], BF16)
            HH2 = self.pool_of("HH2", 2, [128, G, 512], BF16)
            T1 = self.pool_of("T1", 2, [128, 512], BF16)
            T2 = self.pool_of("T2", 2, [128, 512], BF16)
            Cst = self.pool_of("Cst", 2, [128, G, 512], BF16)
            def c_loads(k1g):
                bt, ht, m1, m2_, h1_, h2_ = (self.nxt(Bt), self.nxt(Ht), self.nxt(M1t), self.nxt(M2t), self.nxt(HH1), self.nxt(HH2))
                self.load_B(bt, A0, k1g, G)
                self.load_tab(ht, Htab, k1g, G)
                self.load_tab(m1, M1tab, k1g, G)
                self.load_tab(m2_, M2tab, k1g, G)
                for half in range(2):
                    self.dma("sp", h1_.t[half * 64:(half + 1) * 64, :, :], Hf.t[conv, 0, :, k1g:k1g + G, :], [Hf.r], [h1_.r])
                    self.dma("sp", h2_.t[half * 64:(half + 1) * 64, :, :], Hf.t[conv, 1, :, k1g:k1g + G, :], [Hf.r], [h2_.r])
                return bt, ht, m1, m2_, h1_, h2_

            nxt_l = c_loads(0)
            for k1g in range(0, 256, G):
                bt, ht, m1, m2_, h1_, h2_ = nxt_l
                if k1g + G < 256:
                    nxt_l = c_loads(k1g + G)
                cs_ = self.nxt(Cst)
                for g in range(G):
                    pu = self.bank()
                    self.mm(pu.t[:, 0:512], ht.t[:, g, :], bt.t[:, g, :], True, True, [ht.r, bt.r], [pu.r])
                    t1, t2 = self.nxt(T1), self.nxt(T2)
                    self.tt("dve", t1.t, pu.t[:, 0:512], h1_.t[:, g, :], ALU.mult, [pu.r, h1_.r], [t1.r])
                    self.tt("dve", t2.t, pu.t[:, 0:512], h2_.t[:, g, :], ALU.mult, [pu.r, h2_.r], [t2.r])
                    pc = self.bank()
                    self.mm(pc.t[:, 0:512], m1.t[:, g, :], t1.t, True, False, [m1.r, t1.r], [pc.r])
                    self.mm(pc.t[:, 0:512], m2_.t[:, g, :], t2.t, False, True, [m2_.r, t2.r], [pc.r])
                    self.cp("act", cs_.t[:, g, :], pc.t[:, 0:512], [pc.r], [cs_.r])
                self.dma("sp", Cd.t[:, k1g:k1g + G, :], cs_.t, [cs_.r], [Cd.r])
            self.P.barrier()
            self.top = m2
            Dt = self.pool_of("Dt", 2, [128, 4, 512], BF16)
            x1t = self.pool_of("x1t", 2, [128, 512], BF16)
            for s_ in range(64):
                dt_ = self.nxt(Dt)
                for kc in range(2):
                    for ri in range(2):
                        self.dma("sp", dt_.t[:, 2 * kc + ri, :], Cd.t[ri * 64 + s_, kc * 128:(kc + 1) * 128, :], [Cd.r], [dt_.r])
                if conv == 0:
                    xg = self.nxt(x1t)
                    self.dma("sp", xg.t, X1s.t[:, s_, :], [X1s.r], [xg.r])
                    pbk = self.bank()
                    for q in range(4):
                        self.mm(pbk.t[:, 0:512], fv.t[:, q * 128:(q + 1) * 128], dt_.t[:, q, :], q == 0, q == 3, [fv.r, dt_.r], [pbk.r])
                    self.tt("dve", X.t[:, s_, :], pbk.t[:, 0:512], xg.t, ALU.mult, [pbk.r, xg.r], [X.r])
                else:
                    pbk = self.bank()
                    for cc in range(4):
                        for q in range(4):
                            self.P.op("pe", (lambda o_, l_, r_, a_, b_: (lambda h_: h_.matmul(o_, l_, r_, start=a_, stop=b_)))(
                                pbk.t[:, cc * 68:(cc + 1) * 68], dt_.t[:, q, cc * 128:(cc + 1) * 128], fe.t[:, q, :], q == 0, q == 3),
                                reads=[dt_.r, fe.r], writes=[pbk.r], signal=(cc == 3 and q == 3))
                    x2v = x2sb.t.rearrange("p c (j s) -> p c s j", s=64)[:, :, s_, :]
                    yav = yaT.t.rearrange("p c (j s) -> p c s j", s=64)[:, :, s_, :]
                    self.tt("dve", yav, pbk.t[:, 0:272].rearrange("p (c j) -> p c j", c=4), x2v, ALU.mult, [pbk.r, x2sb.r], [yaT.r])
            if conv == 1:
                self.dma("sp", self.YT.t[0:512, :].rearrange("(cc p) t -> p cc t", p=128), yaT.t, [yaT.r], [self.YT.r])


def build_program(dbg=False, stop_after=None):
    kb = KB(stop_after)
    kb.dbg = dbg
    kb._bk = 0
    kb.consts()
    kb.phase_mod()
    kb.P.barrier()
    if stop_after == "mod":
        kb.P.build()
        return kb
    kb.phase_inproj()
    kb.P.barrier()
    if stop_after == "inproj":
        kb.P.build()
        return kb
    kb.phase_attn()
    kb.P.barrier()
    if stop_after == "attn":
        kb.P.build()
        return kb
    if os.environ.get("FAKE_YA"):
        ya = kb.inp("dbg_yaT", [512, EXT], BF16)
        st = kb.alloc("yast", [128, EXT], BF16)
        for c in range(4):
            kb.dma("sp", st.t, ya.t[c * 128:(c + 1) * 128, :], [ya.r], [st.r])
            kb.dma("sp", kb.YT.t[c * 128:(c + 1) * 128, :], st.t, [st.r], [kb.YT.r])
        kb.P.barrier()
    else:
        kb.phase_hyena()
        kb.P.barrier()
    if stop_after == "hyena":
        kb.P.build()
        return kb
    wo = kb.inp("ab_w_out", [D, D])
    x1 = kb.phase_outproj("X1o", kb.YT, kb.x_ext, wo.t, 0)
    kb.P.barrier()
    x2 = kb.phase_ffn("X2o", x1, 0)
    kb.P.barrier()
    x3 = kb.phase_sgu("X3o", x2, 1)
    kb.P.barrier()
    kb.phase_ffn("X4o", x3, 1, final=True)
    kb.P.barrier()
    kb.P.build()
    return kb


def rope_tables(pos_row, pos_col):
    T = pos_row.shape[0]
    cos = np.zeros((128, T), np.float32)
    sin = np.zeros((128, T), np.float32)
    inv = (10000.0 ** (-np.arange(16, dtype=np.float32) / 16)).astype(np.float32)
    for f in range(128):
        d = f % 64
        pos = pos_row if d < 32 else pos_col
        dd = d % 32
        i = dd % 16
        ang = pos.astype(np.float32) * inv[i]
        cos[f] = np.cos(ang)
        sin[f] = -np.sin(ang) if dd < 16 else np.sin(ang)
    return cos, sin


def rperm_matrix():
    m = np.zeros((128, 128), np.float32)
    for f in range(128):
        dd = (f % 64) % 32
        partner = f + 16 if dd < 16 else f - 16
        m[partner, f] = 1.0
    return m.astype(ml_dtypes.bfloat16)


def fft_tables(lo):
    bf = ml_dtypes.bfloat16
    N = 16384
    j = np.arange(128, dtype=np.float64)[:, None]
    k1 = np.arange(256, dtype=np.float64)[None, :]
    th = 2 * np.pi * ((j * k1) % 256) / 256
    F1 = np.zeros((128, 4, 128))
    Fi = np.zeros((128, 4, 128))
    for kc in range(2):
        F1[:, 2 * kc, :] = np.cos(th[:, kc * 128:(kc + 1) * 128])
        F1[:, 2 * kc + 1, :] = -np.sin(th[:, kc * 128:(kc + 1) * 128])
        Fi[:, 2 * kc, :] = np.cos(th[:, kc * 128:(kc + 1) * 128]).T / N
        Fi[:, 2 * kc + 1, :] = -np.sin(th[:, kc * 128:(kc + 1) * 128]).T / N
    j0 = lo // 64
    FiE = Fi[:, :, j0:j0 + 68]
    s = np.arange(64, dtype=np.float64)
    k2 = np.arange(64, dtype=np.float64)
    kk = np.arange(256, dtype=np.float64)
    ang = 2 * np.pi * ((s[None, :, None] * kk[:, None, None]) / N + ((s[None, :, None] * k2[None, None, :]) % 64) / 64)
    Zr, Zi = np.cos(ang), -np.sin(ang)
    H = np.zeros((256, 128, 128))
    H[:, 0:64, 0:64] = Zr
    H[:, 64:128, 0:64] = -Zi
    H[:, 0:64, 64:128] = Zi
    H[:, 64:128, 64:128] = Zr
    Hc = H.copy()
    Hc[:, :, 64:128] *= -1
    ZrT, ZiT = Zr.transpose(0, 2, 1), Zi.transpose(0, 2, 1)
    M1 = np.zeros((256, 128, 128))
    M1[:, 0:64, 0:64] = ZrT
    M1[:, 64:128, 0:64] = ZiT
    M1[:, 0:64, 64:128] = -ZiT
    M1[:, 64:128, 64:128] = ZrT
    M2 = np.zeros((256, 128, 128))
    M2[:, 0:64, 0:64] = ZiT
    M2[:, 64:128, 0:64] = -ZrT
    M2[:, 0:64, 64:128] = ZrT
    M2[:, 64:128, 64:128] = ZiT
    c = lambda a: np.ascontiguousarray(a.astype(np.float32)).astype(bf)
    grp = lambda a: c(a.reshape(64, 4, 128, 128).transpose(0, 2, 1, 3).reshape(64, 128, 512))
    return {"t_F1": c(F1.reshape(128, 512)), "t_Finv": c(Fi.reshape(128, 512)), "t_FinvE": c(FiE),
            "t_H": grp(H), "t_Hc": grp(Hc), "t_M1": grp(M1), "t_M2": grp(M2)}


def hyena_consts():
    f32 = np.float32
    bands = 16
    pos = np.arange(L, dtype=f32)
    t = np.linspace(0.0, 1.0, L, dtype=f32)[:, None]
    ang = (f32(2.0 * math.pi / L) * pos[:, None] * np.linspace(1e-4, bands - 1, bands, dtype=f32)[None, :]).astype(f32)
    z = np.concatenate([t, np.cos(ang), -np.sin(ang)], axis=-1).astype(f32)
    deltas = np.abs(np.linspace(math.log(1e-2) / 1.5, math.log(1e-2) / 0.3, 512, dtype=f32))
    window = (np.exp(-t * deltas[None, :]) + f32(0.05)).astype(f32)
    win = np.ascontiguousarray(window.reshape(128, 64, 512))
    return np.ascontiguousarray(z.T), win


_TAB = {}


def host_inputs(inputs, cid):
    b, hf = cid // 2, cid % 2
    lo = 0 if hf == 0 else L - EXT
    f32 = np.float32
    m = {}
    m["c_ident"] = np.eye(128, dtype=f32).astype(ml_dtypes.bfloat16)
    cc = np.stack([inputs["c"][b], inputs["c_ctx"]], -1).astype(f32)
    m["ccol"] = np.ascontiguousarray(cc.reshape(8, 128, 2).transpose(1, 0, 2))
    m["mod_w"] = inputs["mod_w"]
    m["mod_b"] = inputs["mod_b"]
    m["x_full"] = np.ascontiguousarray(inputs["x"][b])
    m["x_ext"] = np.ascontiguousarray(inputs["x"][b, lo:lo + EXT])
    m["ctx"] = np.ascontiguousarray(inputs["ctx"][b])
    m["w_in"] = np.ascontiguousarray(inputs["ab_w_in"][0])
    cw = np.concatenate([inputs["hy_conv_w"][0], inputs["hy_conv_b"][0][None]], 0)
    m["hy_cw"] = np.ascontiguousarray(cw.reshape(4, 12, 128).transpose(2, 1, 0)).astype(f32)
    m["norm_mix_g"] = inputs["norm_mix_g"]
    t = np.arange(L)
    ck, sk = rope_tables(t // 64, t % 64)
    m["ropek_cos"], m["ropek_sin"] = ck, sk
    m["ropeq_cos"] = np.ascontiguousarray(ck[:, lo:lo + EXT])
    m["ropeq_sin"] = np.ascontiguousarray(sk[:, lo:lo + EXT])
    m["c_rperm"] = rperm_matrix()
    m["da_lambda"] = np.ascontiguousarray(inputs["da_lambda"][0]).astype(f32)
    m["ab_w_out"] = np.ascontiguousarray(inputs["ab_w_out"][0])
    m["norm_ffn_g"] = inputs["norm_ffn_g"]
    m["final_norm_g"] = inputs["final_norm_g"]
    for i in range(2):
        m["ffn_w_up%d" % i] = np.ascontiguousarray(inputs["ffn_w_up"][i])
        m["ffn_w_down%d" % i] = np.ascontiguousarray(inputs["ffn_w_down"][i])
        fcw = np.concatenate([inputs["ffn_conv_w"][i], inputs["ffn_conv_b"][i][None]], 0)
        m["ffn_cw%d" % i] = np.ascontiguousarray(fcw.reshape(4, 44, 128).transpose(2, 1, 0)).astype(f32)
    m["sgu_w_in"] = np.ascontiguousarray(inputs["sgu_w_in"][0])
    m["sgu_bu_col"] = np.ascontiguousarray(inputs["sgu_b_in"][0][:1024].reshape(8, 128).T).astype(f32)
    m["sgu_bv"] = np.ascontiguousarray(inputs["sgu_b_in"][0][1024:])
    m["sgu_ln_g"] = inputs["sgu_ln_g"][0]
    m["sgu_ln_b"] = inputs["sgu_ln_b"][0]
    m["sgu_wsT"] = np.ascontiguousarray(inputs["sgu_w_s"][0].transpose(2, 0, 1)).astype(f32)
    m["sgu_b_s"] = np.ascontiguousarray(inputs["sgu_b_s"][0].reshape(1, 1024)).astype(f32)
    m["sgu_w_out"] = np.ascontiguousarray(inputs["sgu_w_out"][0])
    if lo not in _TAB:
        _TAB[lo] = fft_tables(lo)
    if "hc" not in _TAB:
        _TAB["hc"] = hyena_consts()
    m.update(_TAB[lo])
    m["hy_zT"], m["hy_win"] = _TAB["hc"]
    m["hy_w1"] = np.ascontiguousarray(inputs["hy_w1"][0])
    m["hy_w2"] = np.ascontiguousarray(inputs["hy_w2"][0])
    m["hy_prm"] = np.ascontiguousarray(np.stack([inputs["hy_b1"][0], inputs["hy_freq"][0][0], inputs["hy_b2"][0], inputs["hy_freq"][0][1]], -1)).astype(f32)
    m["hy_w3"] = np.ascontiguousarray(inputs["hy_w3"][0])
    m["hy_bias"] = np.ascontiguousarray(inputs["hy_bias"][0])
    m["subln_col"] = np.ascontiguousarray(inputs["da_subln_g"][0].reshape(128, 1)).astype(f32)
    return m


_CACHE = {}


def kernel(**inputs):
    inputs = {k: np.asarray(v) for k, v in inputs.items()}
    if "kb" not in _CACHE:
        _CACHE["kb"] = build_program()
    kb = _CACHE["kb"]
    in_maps = []
    for cid in range(8):
        m = host_inputs(inputs, cid)
        in_maps.append({k: m[k] for k in kb.ins})
    res = run_bass_kernel_spmd(kb.nc, in_maps, core_ids=list(range(8)))
    out = np.zeros((4, L, D), np.float32)
    for cid in range(8):
        b, hf = cid // 2, cid % 2
        o = res.results[cid]["out"]
        if hf == 0:
            out[b, :4096] = o[:4096]
        else:
            out[b, 4096:] = o[EXT - 4096:]
    return out
```

```python
import math
import os
import numpy as np
import ml_dtypes
import concourse.bass as bass
import concourse.mybir as mybir
from concourse.bass_utils import run_bass_kernel_spmd

F32 = mybir.dt.float32
BF16 = mybir.dt.bfloat16
U8 = mybir.dt.uint8
AF = mybir.ActivationFunctionType
ALU = mybir.AluOpType

ENGS = ("pe", "act", "dve", "pool", "sp")
SEM_ROT = 16000
L = 8192
EXT = 4352
NEXT_T = EXT // 128
D = 1024
DFF = 2816
EPS = 1e-6
GS = 256


class Res:
    __slots__ = ("name", "w", "r", "dsem", "dcnt", "excl", "lk")

    def __init__(self, name, excl=False):
        self.name = name
        self.lk = None
        self.excl = excl
        self.w = {}
        self.r = {}
        self.dsem = None
        self.dcnt = 0


class Prog:
    def __init__(self, nc):
        self.nc = nc
        self.ops = {e: [] for e in ENGS}
        self.cnt = {e: 0 for e in ENGS}
        self.epoch = {e: 0 for e in ENGS}
        self.sems = {}
        self.known = {e: {} for e in ENGS}
        self.dres = []
        self.meta = {e: [] for e in ENGS}
        self.free_sems = []
        self.nd = 0
        for e in ENGS:
            if e != "sp":
                self._engsem(e)

    def _engsem(self, e):
        k = ("E", e, self.epoch[e])
        if k not in self.sems:
            self.sems[k] = self.nc.alloc_semaphore(name="s_%s_%d" % (e, self.epoch[e]))
        return k

    def _semname(self, sem):
        for k, v in self.sems.items():
            if v is sem:
                return k
        return None

    def check_deadlock(self):
        val = {}
        pc = {e: 0 for e in ENGS}
        prog = True
        while prog:
            prog = False
            for e in ENGS:
                while pc[e] < len(self.meta[e]):
                    waits, inc, desc = self.meta[e][pc[e]]
                    if all(val.get(k, 0) >= v for k, v in waits):
                        if inc is not None:
                            val[inc[0]] = val.get(inc[0], 0) + inc[1]
                        pc[e] += 1
                        prog = True
                    else:
                        break
        bad = False
        for e in ENGS:
            if pc[e] < len(self.meta[e]):
                bad = True
                waits, inc, desc = self.meta[e][pc[e]]
                print("DEADLOCK", e, pc[e], len(self.meta[e]), desc, [(k, v, val.get(k, 0)) for k, v in waits if val.get(k, 0) < v])
        return not bad

    def _deps(self, reads, writes):
        deps = {}
        for r in reads:
            for k, v in r.w.items():
                if deps.get(k, 0) < v:
                    deps[k] = v
        for w in writes:
            for d in (w.w, w.r):
                for k, v in d.items():
                    if deps.get(k, 0) < v:
                        deps[k] = v
        return deps

    def _waits(self, eng, deps):
        kn = self.known[eng]
        out = []
        for k, v in deps.items():
            if kn.get(k, 0) < v:
                kn[k] = v
                out.append((self.sems[k], v))
        return out

    def op(self, eng, fn, reads=(), writes=(), signal=True):
        deps = self._deps(reads, writes)
        k = self._engsem(eng)
        for r in reads:
            if r.excl:
                for kk, vv in r.r.items():
                    if not (kk[0] == "E" and kk[1] == eng) and deps.get(kk, 0) < vv:
                        deps[kk] = vv
        if eng == "pe":
            deps = {kk: vv for kk, vv in deps.items() if kk[1] != "pe" or kk[0] != "E"}
        else:
            deps = {kk: vv for kk, vv in deps.items() if not (kk == k and vv > self.cnt[eng])}
        waits = self._waits(eng, deps)
        if signal:
            self.cnt[eng] += 1
            tok = (k, self.cnt[eng])
            sem = self.sems[k]
        else:
            tok = (k, self.cnt[eng] + 1)
            sem = None

        def emit(h, fn=fn, waits=waits, sem=sem):
            for s, v in waits:
                h.wait_ge(s, v)
            ins = fn(h)
            if sem is not None:
                ins.then_inc(sem, 1)

        self.ops[eng].append(emit)
        self.meta[eng].append(([(self._semname(s_), v_) for s_, v_ in waits], (k, 1) if signal else None,
                               "op r=%s w=%s" % ([r.name for r in reads], [w.name for w in writes])))
        kk, vv = tok
        for w in writes:
            w.w = {kk: vv}
            w.r = {}
        for r in reads:
            if r.r.get(kk, 0) < vv:
                r.r[kk] = vv
        if signal and self.cnt[eng] >= SEM_ROT:
            self.epoch[eng] += 1
            self.cnt[eng] = 0
            self._engsem(eng)

    def dma(self, eng, out, in_, reads=(), writes=(), store=False):
        assert len(writes) == 1
        wres = writes[0]
        owner = reads[0] if store else wres
        deps = {}
        for r in reads:
            for k, v in r.w.items():
                if deps.get(k, 0) < v:
                    deps[k] = v
        for d in ((wres.r,) if store else (wres.w, wres.r)):
            for k, v in d.items():
                if deps.get(k, 0) < v:
                    deps[k] = v
        if owner.dsem is None:
            if eng == "pool":
                self.nd += 1
                owner.dsem = ("S", self.nd)
                owner.dcnt = 0
                self.sems[owner.dsem] = self.nc.alloc_semaphore(name="sw_%d" % self.nd)
            elif self.free_sems:
                owner.dsem, owner.dcnt = self.free_sems.pop()
            else:
                self.nd += 1
                owner.dsem = ("D", self.nd)
                owner.dcnt = 0
                self.sems[owner.dsem] = self.nc.alloc_semaphore(name="d_%d" % self.nd)
            self.dres.append(owner)
        if (not store) and owner.lk == "load" and not wres.r:
            deps.pop(owner.dsem, None)
        owner.lk = "store" if store else "load"
        waits = self._waits(eng, deps)
        owner.dcnt += 1
        k, v = owner.dsem, 16 * owner.dcnt
        sem = self.sems[k]

        def emit(h, waits=waits, sem=sem, out=out, in_=in_):
            for s, vv in waits:
                h.wait_ge(s, vv)
            h.dma_start(out=out, in_=in_).then_inc(sem, 16)

        self.ops[eng].append(emit)
        self.meta[eng].append(([(self._semname(s_), v_) for s_, v_ in waits], (k, 16),
                               "dma r=%s w=%s" % ([r.name for r in reads], [w.name for w in writes])))
        if store:
            wres.w[k] = v
        else:
            wres.w = {k: v}
            wres.r = {}
        for r in reads:
            if r.r.get(k, 0) < v:
                r.r[k] = v

    def barrier(self):
        toks = {}
        for e in ENGS:
            if e == "sp":
                continue
            k = self._engsem(e)
            if self.cnt[e] > 0:
                toks[k] = self.cnt[e]
            if self.epoch[e] > 0:
                toks[("E", e, self.epoch[e] - 1)] = SEM_ROT
        for r in self.dres:
            toks[r.dsem] = 16 * r.dcnt
        for e in ENGS:
            waits = self._waits(e, dict(toks))

            def emit(h, waits=waits):
                for s, v in waits:
                    h.wait_ge(s, v)

            self.ops[e].append(emit)
            self.meta[e].append(([(self._semname(s_), v_) for s_, v_ in waits], None, "barrier"))
        keep = []
        for r in self.dres:
            if r.dsem[0] == "S":
                keep.append(r)
                continue
            self.free_sems.append((r.dsem, r.dcnt))
            r.dsem = None
        self.dres = keep

    def build(self):
        nc = self.nc
        with nc.Block() as block:
            @block.tensor
            def _(h):
                for f in self.ops["pe"]:
                    f(h)

            @block.scalar
            def _(h):
                for f in self.ops["act"]:
                    f(h)

            @block.vector
            def _(h):
                for f in self.ops["dve"]:
                    f(h)

            @block.gpsimd
            def _(h):
                for f in self.ops["pool"]:
                    f(h)

            @block.sync
            def _(h):
                for f in self.ops["sp"]:
                    f(h)


class Buf:
    __slots__ = ("t", "r")

    def __init__(self, t, r):
        self.t = t
        self.r = r


def _dtsize(dt):
    return 4 if dt == F32 else (2 if dt == BF16 else 1)


class KB:
    def __init__(self, stop_after=None):
        nc = bass.Bass("TRN2", target_bir_lowering=False)
        self.nc = nc
        self.P = Prog(nc)
        self.stop_after = stop_after
        self.ARENA = 207 * 1024
        self.arena = nc.alloc_sbuf_tensor("arena", [128, self.ARENA], U8)
        self.top = 0
        ps = nc.alloc_psum_tensor("psum", [128, 4096], F32)
        self.ps = ps
        self.pb = [Buf(ps[:, 512 * i:512 * (i + 1)], Res("pb%d" % i, excl=True)) for i in range(8)]
        self.dram = {}
        self.dram_names = set()
        self.ins = {}
        self.outs = {}

    def alloc(self, name, shape, dt):
        n = 1
        for s in shape[1:]:
            n *= s
        nb = n * _dtsize(dt)
        nb = (nb + 31) // 32 * 32
        off = self.top
        self.top += nb
        assert self.top <= self.ARENA, "SBUF arena overflow %s %d" % (name, self.top)
        v = self.arena[:shape[0], off:off + n * _dtsize(dt)].bitcast(dt)
        if len(shape) == 3:
            v = v.rearrange("p (a b) -> p a b", a=shape[1])
        elif len(shape) == 4:
            v = v.rearrange("p (a b c) -> p a b c", a=shape[1], b=shape[2])
        return Buf(v, Res(name))

    def inp(self, name, shape, dt=F32):
        t = self.nc.dram_tensor(name, list(shape), dt, kind="ExternalInput").ap()
        b = Buf(t, Res(name))
        self.ins[name] = b
        return b

    def outp(self, name, shape, dt=F32):
        t = self.nc.dram_tensor(name, list(shape), dt, kind="ExternalOutput").ap()
        self.dram_names.add(name)
        b = Buf(t, Res(name))
        self.outs[name] = b
        return b

    def scratch(self, name, shape, dt, debug=False):
        if debug:
            return self.outp(name, shape, dt)
        t = self.nc.dram_tensor(name, list(shape), dt).ap()
        self.dram_names.add(name)
        return Buf(t, Res(name))

    def mm(self, out, lhsT, rhs, start, stop, reads, writes):
        self.P.op("pe", lambda h: h.matmul(out, lhsT, rhs, start=start, stop=stop),
                  reads=reads, writes=writes, signal=bool(stop))

    def tr(self, out, in_, ident, reads, writes, signal=True):
        self.P.op("pe", lambda h: h.transpose(out, in_, ident), reads=reads, writes=writes, signal=signal)

    def act(self, out, in_, func, reads, writes, bias=None, scale=None, accum=None):
        kw = {}
        if bias is not None:
            kw["bias"] = bias
        if scale is not None:
            kw["scale"] = scale
        if accum is not None:
            kw["accum_out"] = accum
        self.P.op("act", lambda h: h.activation(out=out, in_=in_, func=func, **kw), reads=reads, writes=writes)

    def tt(self, eng, out, a, b, op, reads, writes):
        self.P.op(eng, lambda h: h.tensor_tensor(out=out, in0=a, in1=b, op=op), reads=reads, writes=writes)

    def ts(self, eng, out, a, s1, s2, op0, op1, reads, writes):
        if op1 is None:
            s2, op1 = 0.0, ALU.add
        self.P.op(eng, lambda h: h.tensor_scalar(out, a, s1, s2, op0, op1), reads=reads, writes=writes)

    def stt(self, eng, out, in0, scalar, in1, op0, op1, reads, writes):
        self.P.op(eng, lambda h: h.scalar_tensor_tensor(out=out, in0=in0, scalar=scalar, in1=in1, op0=op0, op1=op1),
                  reads=reads, writes=writes)

    def cp(self, eng, out, in_, reads, writes):
        if eng == "act":
            self.P.op("act", lambda h: h.activation(out=out, in_=in_, func=AF.Copy), reads=reads, writes=writes)
        else:
            self.P.op(eng, lambda h: h.tensor_copy(out, in_), reads=reads, writes=writes)

    def recip(self, out, in_, reads, writes):
        self.P.op("dve", lambda h: h.reciprocal(out, in_), reads=reads, writes=writes)

    def memset(self, eng, out, val, writes):
        self.P.op(eng, lambda h: h.memset(out, val), writes=writes)

    def dma(self, q, out, in_, reads, writes, store=None):
        if store is None:
            store = writes[0].name in self.dram_names
        self.P.dma(q, out, in_, reads=reads, writes=writes, store=store)

    def ld(self, q, dst, src_ap, src=None):
        self.P.dma(q, dst.t if isinstance(dst, Buf) else dst[0], src_ap,
                   reads=[src.r] if src is not None else [], writes=[dst.r if isinstance(dst, Buf) else dst[1]])

    def consts(self):
        self.ident = self.alloc("ident", [128, 128], BF16)
        self.ones = self.alloc("ones", [128, 128], BF16)
        self.epsb = self.alloc("epsb", [128, 1], F32)
        idin = self.inp("c_ident", [128, 128], BF16)
        self.dma("sp", self.ident.t, idin.t, [], [self.ident.r])
        self.memset("pool", self.ones.t, 1.0, [self.ones.r])
        self.memset("pool", self.epsb.t, EPS, [self.epsb.r])
        self.cmark = self.top

    def norm_pre(self, xt, A, SH, tmp, xn, junk, stat):
        ss = stat.t[:, 0:1]
        rs = stat.t[:, 1:2]
        self.act(junk.t, xt.t, AF.Square, [xt.r], [junk.r, stat.r], accum=ss)
        self.act(rs, ss, AF.Sqrt, [stat.r, self.epsb.r], [stat.r], bias=self.epsb.t[:, 0:1], scale=1.0 / D)
        self.recip(rs, rs, [stat.r], [stat.r])
        self.stt("dve", tmp.t, xt.t, rs, A.t, ALU.mult, ALU.mult, [xt.r, stat.r, A.r], [tmp.r])
        self.tt("pool", xn.t, tmp.t, SH.t, ALU.add, [tmp.r, SH.r], [xn.r])

    def norm_tr(self, xn, hT, col0, ntok=128):
        pT = self.pb[7]
        pTv = pT.t.bitcast(BF16)
        for kc in range(8):
            self.tr(pTv[:, kc * 128:kc * 128 + ntok], xn.t[:ntok, kc * 128:(kc + 1) * 128], self.ident.t[:ntok, :ntok],
                    [xn.r, self.ident.r], [pT.r], signal=(kc == 7))
        src = pTv.rearrange("p (a b) -> p a b", a=8)[:, :, :ntok]
        self.cp("act", hT.t[:, :, col0:col0 + ntok], src, [pT.r], [hT.r])

    def norm_T(self, xt, A, SH, tmp, xn, junk, stat, hT, col0, ntok=128, plain_g=None):
        self.norm_pre(xt, A, SH, tmp, xn, junk, stat)
        self.norm_tr(xn, hT, col0, ntok)

    def pool_of(self, name, n, shape, dt):
        return {"b": [self.alloc("%s%d" % (name, i), shape, dt) for i in range(n)], "i": 0}

    def nxt(self, pool):
        b = pool["b"][pool["i"] % len(pool["b"])]
        pool["i"] += 1
        return b

    def bank(self):
        b = self.pb[self._bk % 6]
        self._bk += 1
        return b

    def phase_mod(self):
        ccol = self.inp("ccol", [128, 8, 2])
        modw = self.inp("mod_w", [2, 1024, 6144])
        modb = self.inp("mod_b", [2, 6144])
        self.modv = self.scratch("modv", [2, 2, 6144], F32, debug=self.dbg)
        sc = self.alloc("scol", [128, 8, 2], F32)
        self.dma("sp", sc.t, ccol.t, [], [sc.r])
        self.act(sc.t, sc.t, AF.Silu, [sc.r], [sc.r])
        mrow = self.alloc("mrow", [2, 6144], F32)
        mb2 = self.alloc("mb2", [2, 6144], F32)
        wb = [self.alloc("mwb%d" % i, [128, 8, 512], F32) for i in range(2)]
        for i in range(2):
            self.dma("sp", mb2.t, modb.t[i].partition_broadcast(2), [], [mb2.r])
            for n in range(12):
                w = wb[n % 2]
                self.dma("sp", w.t, modw.t[i, :, n * 512:(n + 1) * 512].rearrange("(kc p) f -> p kc f", p=128), [], [w.r])
                pbk = self.pb[n % 2]
                for kc in range(8):
                    self.mm(pbk.t[0:2, :], sc.t[:, kc, :], w.t[:, kc, :], kc == 0, kc == 7, [sc.r, w.r], [pbk.r])
                self.tt("dve", mrow.t[:, n * 512:(n + 1) * 512], pbk.t[0:2, :], mb2.t[:, n * 512:(n + 1) * 512], ALU.add,
                        [pbk.r, mb2.r], [mrow.r])
            self.dma("sp", self.modv.t[i], mrow.t, [mrow.r], [self.modv.r])

    def mod_tiles(self, layer, which, i_sh, i_sc, g_ap):
        A = self.alloc("modA", [128, 1024], F32)
        SH = self.alloc("modSH", [128, 1024], F32)
        mv = self.modv.t[layer, which]
        self.dma("sp", A.t, mv[i_sc * D:(i_sc + 1) * D].partition_broadcast(128), [self.modv.r], [A.r])
        self.dma("sp", SH.t, g_ap.partition_broadcast(128), [], [SH.r])
        self.stt("dve", A.t, A.t, 1.0, SH.t, ALU.add, ALU.mult, [A.r, SH.r], [A.r])
        self.dma("sp", SH.t, mv[i_sh * D:(i_sh + 1) * D].partition_broadcast(128), [self.modv.r], [SH.r])
        return A, SH

    def load_w_bf16(self, dst, src_ap, kcn):
        for kc in range(kcn):
            self.dma("pool", dst.t[:, kc, :], src_ap[kc * 128:(kc + 1) * 128, :], [], [dst.r])

    def norm_bufs(self):
        nb = {}
        nb["x"] = self.pool_of("nx", 2, [128, 1024], F32)
        nb["tmp"] = self.alloc("ntmp", [128, 1024], F32)
        nb["xn"] = self.pool_of("nxn", 2, [128, 1024], BF16)
        nb["junk"] = self.alloc("njunk", [128, 1024], BF16)
        nb["stat"] = self.pool_of("nstat", 2, [128, 2], F32)
        return nb

    def proj_pass(self, xin, T, A, SH, w, fm, tm, nb, post=None):
        skip = os.environ.get("KSKIP", "")
        fm = [sp for sp in fm if sp["kind"] not in skip.split(",")]
        ng = T // GS
        hT = [self.alloc("hT%d" % i, [128, 8, GS + 32], BF16) for i in range(3)]
        for hb in hT:
            self.memset("pool", hb.t, 0.0, [hb.r])
        tmpc = self.pool_of("tmpc", 2, [128, GS], F32)
        tmpd = self.pool_of("tmpd", 6, [128, GS], F32)
        obf = self.pool_of("obf", 4 if not any("emit" in sp_ for sp_ in fm) else 1, [128, GS], BF16)
        tst = self.pool_of("tst", 2, [128, 512], BF16) if tm else None
        cs = self.pool_of("cs", 2, [128, 2, GS], F32) if any(sp_["kind"] == "rope" for sp_ in fm) else None
        pend_xn = {}

        def pre(k):
            xs = []
            for j in range(GS // 128):
                xt = self.nxt(nb["x"])
                t0 = k * GS + j * 128
                self.dma("sp", xt.t, xin.t[t0:t0 + 128, :], [xin.r], [xt.r])
                xn = self.nxt(nb["xn"])
                self.norm_pre(xt, A, SH, nb["tmp"], xn, nb["junk"], self.nxt(nb["stat"]))
                xs.append(xn)
            pend_xn[k] = xs

        def trn(k):
            h_ = hT[k % 3]
            for j, xn in enumerate(pend_xn.pop(k)):
                self.norm_tr(xn, h_, 16 + j * 128)
            if k == 0:
                self.memset("pool", h_.t[:, :, 15:16], 0.0, [h_.r])
            else:
                hp = hT[(k - 1) % 3]
                self.cp("pool", h_.t[:, :, 15:16], hp.t[:, :, GS + 15:GS + 16], [hp.r], [h_.r])

        for k0 in range(min(2, ng)):
            pre(k0)
            trn(k0)
        for gg in range(ng):
            if gg + 2 < ng:
                pre(gg + 2)
            h = hT[gg % 3]
            if gg == ng - 1:
                self.memset("pool", h.t[:, :, GS + 16:GS + 17], 0.0, [h.r])
            else:
                hn = hT[(gg + 1) % 3]
                self.cp("pool", h.t[:, :, GS + 16:GS + 17], hn.t[:, :, 16:17], [hn.r], [h.r])
            g0 = gg * GS
            for sp in fm:
                kind = sp["kind"]
                if kind == "rope":
                    c = self.nxt(cs)
                    if "nodma" in os.environ.get("ROPEVAR", ""):
                        self.memset("pool", c.t, 1.0, [c.r])
                    else:
                        self.dma("sp", c.t[:, 0, :], sp["cos"].t[:, g0:g0 + GS], [], [c.r])
                        self.dma("sp", c.t[:, 1, :], sp["sin"].t[:, g0:g0 + GS], [], [c.r])
                rope_pend = []
                for ci in sp.get("order", range(sp["n"])):
                    pbk = self.bank()
                    col = sp["col0"] + ci * 128
                    for kc in range(8):
                        self.mm(pbk.t[:, 0:GS + 4], w.t[:, kc, col:col + 128], h.t[:, kc, 14:GS + 18], kc == 0, kc == 7,
                                [w.r, h.r], [pbk.r])
                    if kind == "conv":
                        cw = sp["cw"]
                        k = sp["cwi0"] + ci
                        t1 = self.nxt(tmpc)
                        ob = self.nxt(obf)
                        self.act(t1.t, pbk.t[:, 2:GS + 2], AF.Identity, [pbk.r, cw.r], [t1.r],
                                 bias=cw.t[:, k, 3:4], scale=cw.t[:, k, 1:2])
                        self.stt("dve", t1.t, pbk.t[:, 1:GS + 1], cw.t[:, k, 0:1], t1.t, ALU.mult, ALU.add,
                                 [pbk.r, cw.r, t1.r], [t1.r])
                        if "emit" in sp:
                            t3 = self.nxt(tmpd)
                            self.stt("dve", t3.t, pbk.t[:, 3:GS + 3], cw.t[:, k, 2:3], t1.t, ALU.mult, ALU.add,
                                     [pbk.r, cw.r, t1.r], [t3.r])
                            sp["emit"](ci, t3, g0)
                        else:
                            self.stt("dve", ob.t, pbk.t[:, 3:GS + 3], cw.t[:, k, 2:3], t1.t, ALU.mult, ALU.add,
                                     [pbk.r, cw.r, t1.r], [ob.r])
                            r0 = sp["row0"] + ci * 128
                            self.dma("sp", sp["out"].t[r0:r0 + 128, g0:g0 + GS], ob.t, [ob.r], [sp["out"].r])
                    elif kind == "rope":
                        kb = self.nxt(obf)
                        self.cp("act", kb.t, pbk.t[:, 2:GS + 2], [pbk.r], [kb.r])

                        def rope_tail(pbk=pbk, kb=kb, ci=ci, c=c, sp=sp, to=sp["toff"] + g0):
                            ob = self.nxt(obf)
                            t1 = self.nxt(tmpc)
                            t2 = self.nxt(tmpd)
                            p2 = self.pb[6]
                            self.mm(p2.t[:, 0:GS], self.rperm.t, kb.t, True, True, [self.rperm.r, kb.r], [p2.r])
                            self.tt("dve", t1.t, pbk.t[:, 2:GS + 2], c.t[:, 0, :], ALU.mult, [pbk.r, c.r, kb.r], [t1.r])
                            self.tt("dve", t2.t, p2.t[:, 0:GS], c.t[:, 1, :], ALU.mult, [p2.r, c.r], [t2.r])
                            self.tt("pool", ob.t, t1.t, t2.t, ALU.add, [t1.r, t2.r], [ob.r])
                            self.dma("sp", sp["out"].t[ci, :, to:to + GS], ob.t, [ob.r], [sp["out"].r])

                        rope_pend.append(rope_tail)
                        if len(rope_pend) > 1:
                            rope_pend.pop(0)()
                    else:
                        ob = self.nxt(obf)
                        self.cp("act", ob.t, pbk.t[:, 2:GS + 2], [pbk.r], [ob.r])
                        to = sp["toff"] + g0
                        self.dma("sp", sp["out"].t[ci, :, to:to + GS], ob.t, [ob.r], [sp["out"].r])
                while rope_pend:
                    rope_pend.pop(0)()
            for sp in tm:
                for j in range(GS // 128):
                    pbk = self.bank()
                    for kc in range(8):
                        self.mm(pbk.t[:, 0:512], h.t[:, kc, 16 + j * 128:16 + (j + 1) * 128],
                                w.t[:, kc, sp["col0"]:sp["col0"] + 512], kc == 0, kc == 7, [w.r, h.r], [pbk.r])
                    st = self.nxt(tst)
                    self.cp("act", st.t, pbk.t[:, 0:512], [pbk.r], [st.r])
                    ro = sp["roff"] + g0 + j * 128
                    self.dma("sp", sp["out"].t[ro:ro + 128, :], st.t, [st.r], [sp["out"].r])
            if post is not None:
                post(gg, g0, h)
            if gg + 2 < ng:
                trn(gg + 2)

    def phase_inproj(self):
        dbg = self.dbg
        self.x_full = self.inp("x_full", [L, D])
        self.x_ext = self.inp("x_ext", [EXT, D])
        ctx = self.inp("ctx", [256, D])
        w_in = self.inp("w_in", [D, 3072])
        cwin = self.inp("hy_cw", [128, 12, 4])
        gmix = self.inp("norm_mix_g", [2, D])
        cosk = self.inp("ropek_cos", [128, L])
        sink = self.inp("ropek_sin", [128, L])
        cosq = self.inp("ropeq_cos", [128, EXT])
        sinq = self.inp("ropeq_sin", [128, EXT])
        rp = self.inp("c_rperm", [128, 128], BF16)
        self.VX1 = self.scratch("VX1", [1024, L], BF16, debug=dbg)
        self.X2 = self.scratch("X2", [512, EXT], BF16, debug=dbg)
        self.KT = self.scratch("KT", [4, 128, L + 256], BF16, debug=dbg)
        self.QT = self.scratch("QT", [4, 128, EXT], BF16, debug=dbg)
        self.VT = self.scratch("VT", [L + 256, 512], BF16, debug=dbg)
        self.top = self.cmark
        w = self.alloc("w_in", [128, 8, 3072], BF16)
        self.load_w_bf16(w, w_in.t, 8)
        cw = self.alloc("cw", [128, 12, 4], F32)
        self.dma("sp", cw.t, cwin.t, [], [cw.r])
        self.rperm = self.alloc("rperm", [128, 128], BF16)
        self.dma("sp", self.rperm.t, rp.t, [], [self.rperm.r])
        nb = self.norm_bufs()
        mark = self.top
        A, SH = self.mod_tiles(0, 1, 0, 1, gmix.t[0])
        import os
        self.proj_pass(ctx, 256, A, SH, w,
                       [dict(kind="rope", col0=2048, n=4, cos=cosk, sin=sink, out=self.KT, toff=0)] if os.environ.get("CTXROPE") else
                       [dict(kind="plain", col0=2048, n=4, out=self.KT, toff=0)],
                       [dict(col0=2560, out=self.VT, roff=0)], nb)
        self.P.barrier()
        self.top = mark
        if self.stop_after == "ctx":
            return
        A, SH = self.mod_tiles(0, 0, 0, 1, gmix.t[0])
        mark2 = self.top
        self.proj_pass(self.x_full, L, A, SH, w,
                       [dict(kind="conv", col0=0, n=8, cw=cw, cwi0=0, out=self.VX1, row0=0),
                        dict(kind="rope", col0=2048, n=4, cos=cosk, sin=sink, out=self.KT, toff=256)],
                       [dict(col0=2560, out=self.VT, roff=256)], nb)
        self.P.barrier()
        self.top = mark2
        self.proj_pass(self.x_ext, EXT, A, SH, w,
                       [dict(kind="conv", col0=1024, n=4, cw=cw, cwi0=8, out=self.X2, row0=0),
                        dict(kind="rope", col0=1536, n=4, cos=cosq, sin=sinq, out=self.QT, toff=0)],
                       [], nb)


    def phase_attn(self):
        dal = self.inp("da_lambda", [4, 64])
        subg = self.inp("subln_col", [128, 1])
        self.YT = self.scratch("YT", [1024, EXT], BF16, debug=self.dbg)
        self.top = self.cmark
        LAM_INIT = 0.8 - 0.6 * math.exp(0.0)
        lt = self.alloc("lt", [128, 256], F32)
        pr = self.alloc("lpr", [128, 128], F32)
        ls = self.alloc("ls", [128, 4], F32)
        negl = self.alloc("negl", [128, 1], F32)
        gsub = self.alloc("gsub", [128, 1], F32)
        self.dma("sp", lt.t, dal.t.rearrange("a b -> (a b)").partition_broadcast(128), [], [lt.r])
        self.tt("dve", pr.t[:, 0:64], lt.t[:, 0:64], lt.t[:, 64:128], ALU.mult, [lt.r], [pr.r])
        self.tt("dve", pr.t[:, 64:128], lt.t[:, 128:192], lt.t[:, 192:256], ALU.mult, [lt.r, pr.r], [pr.r])
        self.act(lt.t[:, 0:64], pr.t[:, 0:64], AF.Identity, [pr.r, lt.r], [lt.r, ls.r], accum=ls.t[:, 0:1])
        self.act(lt.t[:, 64:128], pr.t[:, 64:128], AF.Identity, [pr.r, lt.r, ls.r], [lt.r, ls.r], accum=ls.t[:, 1:2])
        self.act(ls.t[:, 2:4], ls.t[:, 0:2], AF.Exp, [ls.r], [ls.r])
        self.tt("dve", negl.t, ls.t[:, 3:4], ls.t[:, 2:3], ALU.subtract, [ls.r], [negl.r])
        self.ts("dve", negl.t, negl.t, -LAM_INIT, None, ALU.add, None, [negl.r], [negl.r])
        self.dma("sp", gsub.t, subg.t, [], [gsub.r])
        self.ts("dve", gsub.t, gsub.t, 1.0 - LAM_INIT, None, ALU.mult, None, [gsub.r], [gsub.r])
        NK = (L + 256) // 128
        Kh = self.alloc("Kh", [128, L + 256], BF16)
        Vh = self.alloc("Vh", [128, NK, 128], BF16)
        Qh = self.alloc("Qh", [128, EXT], BF16)
        Eb = [self.alloc("Eb%d" % i, [128, 2, 512], BF16) for i in range(2)]
        f = {n_: self.alloc("at_" + n_, [128, 512], F32) for n_ in ("r0", "r1", "t0", "t1", "o", "rs", "y")}
        osq = self.alloc("at_osq", [128, 512], BF16)
        acc0 = self.alloc("at_acc0", [128, 512], F32)
        ones32 = self.alloc("at_ones32", [128, 128], F32)
        self.memset("pool", ones32.t, 1.0, [ones32.r])
        yb = [self.alloc("at_yb%d" % i, [128, 512], BF16) for i in range(2)]
        pb = self.pb
        itc = [0]
        epi_pend = []
        for h in range(4):
            self.dma("sp", Kh.t, self.KT.t[h], [self.KT.r], [Kh.r])
            self.dma("sp", Vh.t, self.VT.t[:, h * 128:(h + 1) * 128].rearrange("(kt p) d -> p kt d", p=128), [self.VT.r], [Vh.r])
            self.dma("sp", Qh.t, self.QT.t[h], [self.QT.r], [Qh.r])
            groups = [(q0_, min(512, EXT - q0_)) for q0_ in range(0, EXT, 512)]

            def emit_qk(kt, q0, n):
                par = kt % 2
                for m in range(2):
                    sb_ = pb[2 * par + m]
                    self.mm(sb_.t[:, :n], Kh.t[64 * m:64 * m + 64, kt * 128:(kt + 1) * 128],
                            Qh.t[64 * m:64 * m + 64, q0:q0 + n], True, True, [Kh.r, Qh.r], [sb_.r])

            for gi_, (q0, n) in enumerate(groups):
                if gi_ == 0:
                    emit_qk(0, q0, n)
                for kt in range(NK):
                    par = kt % 2
                    if kt + 1 < NK:
                        emit_qk(kt + 1, q0, n)
                    E = Eb[par]
                    sv = self.ps[:, 1024 * par:1024 * par + 1024].rearrange("p (a b) -> p a b", a=2)[:, :, :n]
                    self.act(E.t[:, :, :n], sv, AF.Exp, [pb[2 * par].r, pb[2 * par + 1].r], [E.r], scale=0.125)
                    for m in range(2):
                        self.mm(pb[4 + m].t[:, :n], Vh.t[:, kt, :], E.t[:, m, :n], kt == 0, kt == NK - 1, [Vh.r, E.r], [pb[4 + m].r])
                    self.mm(pb[7].t[:, :n], self.ones.t, E.t[:, 1, :n], kt == 0, kt == NK - 1, [self.ones.r, E.r], [pb[7].r])
                    if kt == 2 and epi_pend:
                        epi_pend.pop(0)()
                    if kt == 0:
                        self.cp("dve", acc0.t[:, :n], E.t[:, 0, :n], [E.r], [acc0.r])
                    else:
                        self.tt("dve", acc0.t[:, :n], acc0.t[:, :n], E.t[:, 0, :n], ALU.add, [acc0.r, E.r], [acc0.r])
                self.mm(pb[6].t[:, :n], ones32.t, acc0.t[:, :n], True, True, [ones32.r, acc0.r], [pb[6].r])
                if gi_ + 1 < len(groups):
                    emit_qk(0, *groups[gi_ + 1])
                self.recip(f["r0"].t[:, :n], pb[6].t[:, :n], [pb[6].r], [f["r0"].r])
                self.recip(f["r1"].t[:, :n], pb[7].t[:, :n], [pb[7].r], [f["r1"].r])
                self.tt("dve", f["t0"].t[:, :n], pb[4].t[:, :n], f["r0"].t[:, :n], ALU.mult, [pb[4].r, f["r0"].r], [f["t0"].r])
                self.tt("dve", f["t1"].t[:, :n], pb[5].t[:, :n], f["r1"].t[:, :n], ALU.mult, [pb[5].r, f["r1"].r], [f["t1"].r])
                def epi_b(n=n, q0=q0, h=h):
                    self.stt("dve", f["o"].t[:, :n], f["t1"].t[:, :n], negl.t[:, 0:1], f["t0"].t[:, :n], ALU.mult, ALU.add,
                             [f["t1"].r, f["t0"].r, negl.r], [f["o"].r])
                    self.act(osq.t[:, :n], f["o"].t[:, :n], AF.Square, [f["o"].r], [osq.r])
                    self.mm(pb[6].t[:, :n], self.ones.t, osq.t[:, :n], True, True, [self.ones.r, osq.r], [pb[6].r])
                    self.act(f["rs"].t[:, :n], pb[6].t[:, :n], AF.Sqrt, [pb[6].r, self.epsb.r], [f["rs"].r],
                             bias=self.epsb.t[:, 0:1], scale=1.0 / 128)
                    self.recip(f["rs"].t[:, :n], f["rs"].t[:, :n], [f["rs"].r], [f["rs"].r])
                    self.tt("dve", f["y"].t[:, :n], f["o"].t[:, :n], f["rs"].t[:, :n], ALU.mult, [f["o"].r, f["rs"].r], [f["y"].r])
                    y2 = yb[itc[0] % 2]
                    itc[0] += 1
                    self.ts("dve", y2.t[:, :n], f["y"].t[:, :n], gsub.t[:, 0:1], None, ALU.mult, None, [f["y"].r, gsub.r], [y2.r])
                    r0 = 512 + h * 128
                    self.dma("sp", self.YT.t[r0:r0 + 128, q0:q0 + n], y2.t[:, :n], [y2.r], [self.YT.r])
                epi_pend.append(epi_b)
        while epi_pend:
            epi_pend.pop(0)()

    def load_w_gated(self, dst, src_ap, kcn, gate_ap):
        mark = self.top
        G = self.alloc("gateG", [128, 1024], F32)
        self.dma("sp", G.t, gate_ap.partition_broadcast(128), [self.modv.r], [G.r])
        stg = self.pool_of("wstg", 2, [128, 1024], F32)
        for kc in range(kcn):
            st = self.nxt(stg)
            self.dma("sp", st.t, src_ap[kc * 128:(kc + 1) * 128, :], [], [st.r])
            self.tt("dve", dst.t[:, kc, :], st.t, G.t, ALU.mult, [st.r, G.r], [dst.r])
        self.P.barrier()
        self.top = mark

    def resid_store(self, xin, xout, actT, kcn, Wd, g0, final_g=None):
        for j in range(GS // 128):
            xr = self.nxt(self.rx)
            r0 = g0 + j * 128
            self.dma("sp", xr.t, xin.t[r0:r0 + 128, :], [xin.r], [xr.r])
            xo = self.nxt(self.ro)
            for half in range(2):
                pbk = self.bank()
                for kc in range(kcn):
                    self.mm(pbk.t[:, 0:512], actT.t[:, kc, j * 128:(j + 1) * 128], Wd.t[:, kc, half * 512:(half + 1) * 512],
                            kc == 0, kc == kcn - 1, [actT.r, Wd.r], [pbk.r])
                self.tt("dve", xo.t[:, half * 512:(half + 1) * 512], pbk.t[:, 0:512], xr.t[:, half * 512:(half + 1) * 512],
                        ALU.add, [pbk.r, xr.r], [xo.r])
            if final_g is not None:
                st = self.nxt(self.fst)
                self.act(self.fjunk.t, xo.t, AF.Square, [xo.r], [self.fjunk.r, st.r], accum=st.t[:, 0:1])
                self.act(st.t[:, 1:2], st.t[:, 0:1], AF.Sqrt, [st.r, self.epsb.r], [st.r], bias=self.epsb.t[:, 0:1], scale=1.0 / D)
                self.recip(st.t[:, 1:2], st.t[:, 1:2], [st.r], [st.r])
                self.stt("dve", xo.t, xo.t, st.t[:, 1:2], final_g.t, ALU.mult, ALU.mult, [xo.r, st.r, final_g.r], [xo.r])
            self.dma("sp", xout.t[r0:r0 + 128, :], xo.t, [xo.r], [xout.r])

    def resid_bufs(self):
        self.rx = self.pool_of("rx", 2, [128, 1024], F32)
        self.ro = self.pool_of("ro", 2, [128, 1024], F32)

    def phase_outproj(self, name, yT_dram, xin, w_ap, layer):
        xout = self.scratch(name, [EXT, D], F32, debug=self.dbg)
        self.top = self.cmark
        Wo = self.alloc("Wo", [128, 8, 1024], BF16)
        self.load_w_gated(Wo, w_ap, 8, self.modv.t[layer, 0, 2 * D:3 * D])
        self.resid_bufs()
        yb = self.pool_of("yTb", 2, [128, 8, GS], BF16)
        for g in range(EXT // GS):
            g0 = g * GS
            y = self.nxt(yb)
            self.dma("sp", y.t, yT_dram.t[:, g0:g0 + GS].rearrange("(kc p) t -> p kc t", p=128), [yT_dram.r], [y.r])
            self.resid_store(xin, xout, y, 8, Wo, g0)
        return xout

    def phase_ffn(self, name, xin, layer, final=False):
        wup = self.inp("ffn_w_up%d" % layer, [D, 2 * DFF])
        wdn = self.inp("ffn_w_down%d" % layer, [DFF, D])
        cwin = self.inp("ffn_cw%d" % layer, [128, 44, 4])
        gffn = self.inp("norm_ffn_g", [2, D]) if "norm_ffn_g" not in self.ins else self.ins["norm_ffn_g"]
        xout = self.outp("out", [EXT, D], F32) if final else self.scratch(name, [EXT, D], F32, debug=self.dbg)
        self.top = self.cmark
        Wu = self.alloc("Wu", [128, 8, 2 * DFF], BF16)
        self.load_w_bf16(Wu, wup.t, 8)
        Wd = self.alloc("Wd", [128, 22, 1024], BF16)
        self.load_w_gated(Wd, wdn.t, 22, self.modv.t[layer, 0, 5 * D:6 * D])
        cw = self.alloc("cwf", [128, 44, 4], F32)
        self.dma("sp", cw.t, cwin.t, [], [cw.r])
        fg = None
        if final:
            fgi = self.inp("final_norm_g", [D])
            fg = self.alloc("fg", [128, 1024], F32)
            self.dma("sp", fg.t, fgi.t.partition_broadcast(128), [], [fg.r])
            self.fst = self.pool_of("fst", 2, [128, 2], F32)
            self.fjunk = self.alloc("fjunk", [128, 1024], BF16)
        A, SH = self.mod_tiles(layer, 0, 3, 4, gffn.t[layer])
        nb = self.norm_bufs_small()
        self.resid_bufs_small()
        gT = self.alloc("gT", [128, 22, GS], BF16)
        sil = self.pool_of("sil", 2, [128, GS], F32)
        hold = {}

        pend = []

        def flush(keep):
            while len(pend) > keep:
                j_, gb, ub = pend.pop(0)
                sb_ = self.nxt(sil)
                self.act(sb_.t, gb.t, AF.Silu, [gb.r], [sb_.r])
                self.tt("pool", gT.t[:, j_, :], sb_.t, ub.t, ALU.mult, [sb_.r, ub.r], [gT.r])

        def emit(ci, buf, g0):
            if ci < 22:
                hold["g"] = buf
            else:
                pend.append((ci - 22, hold["g"], buf))
                flush(1)

        def post(gg, g0, h):
            flush(0)
            self.resid_store(xin, xout, gT, 22, Wd, g0, final_g=fg)

        order = []
        for j in range(22):
            order += [j, 22 + j]
        self.proj_pass(xin, EXT, A, SH, Wu,
                       [dict(kind="conv", col0=0, n=44, cw=cw, cwi0=0, order=order, emit=emit)], [], nb, post=post)
        return xout

    def norm_bufs_small(self):
        nb = {}
        nb["x"] = self.pool_of("nx", 1, [128, 1024], F32)
        nb["tmp"] = self.alloc("ntmp", [128, 1024], F32)
        nb["xn"] = self.pool_of("nxn", 2, [128, 1024], BF16)
        nb["junk"] = self.alloc("njunk", [128, 1024], BF16)
        nb["stat"] = self.pool_of("nstat", 2, [128, 2], F32)
        return nb

    def resid_bufs_small(self):
        self.rx = self.pool_of("rx", 1, [128, 1024], F32)
        self.ro = self.pool_of("ro", 1, [128, 1024], F32)


    def phase_sgu(self, name, xin, layer=1):
        win = self.inp("sgu_w_in", [D, 2048])
        bcol_i = self.inp("sgu_bu_col", [128, 8])
        bv_i = self.inp("sgu_bv", [1024])
        lng_i = self.inp("sgu_ln_g", [1024])
        lnb_i = self.inp("sgu_ln_b", [1024])
        wsT_i = self.inp("sgu_wsT", [128, 8, 128])
        bs_i = self.inp("sgu_b_s", [1, 1024])
        wout = self.inp("sgu_w_out", [D, D])
        gmix = self.ins["norm_mix_g"]
        xout = self.scratch(name, [EXT, D], F32, debug=self.dbg)
        self.top = self.cmark
        Wi = self.alloc("Wi", [128, 8, 2048], BF16)
        self.load_w_bf16(Wi, win.t, 8)
        Wo = self.alloc("Wo", [128, 8, 1024], BF16)
        self.load_w_gated(Wo, wout.t, 8, self.modv.t[layer, 0, 2 * D:3 * D])
        wsT = self.alloc("wsT", [128, 8, 128], BF16)
        self.dma("pool", wsT.t, wsT_i.t, [], [wsT.r])
        bsr = self.alloc("bsr", [1, 1024], BF16)
        self.dma("pool", bsr.t, bs_i.t, [], [bsr.r])
        bcol = self.alloc("bcol", [128, 8], F32)
        self.dma("sp", bcol.t, bcol_i.t, [], [bcol.r])
        BV = self.alloc("BV", [128, 1024], F32)
        LNG = self.alloc("LNG", [128, 1024], F32)
        LNB = self.alloc("LNB", [128, 1024], F32)
        self.dma("sp", BV.t, bv_i.t.partition_broadcast(128), [], [BV.r])
        self.dma("sp", LNG.t, lng_i.t.partition_broadcast(128), [], [LNG.r])
        self.dma("sp", LNB.t, lnb_i.t.partition_broadcast(128), [], [LNB.r])
        A, SH = self.mod_tiles(layer, 0, 0, 1, gmix.t[layer])
        nb = self.norm_bufs()
        self.resid_bufs()
        hTb = self.pool_of("sg_hT", 2, [128, 8, GS], BF16)
        uT = self.alloc("sg_uT", [128, 8, GS], F32)
        guT = self.pool_of("sg_guT", 2, [128, 8, GS], BF16)
        vtp = self.pool_of("sg_vt", 2, [128, 1024], F32)
        vgp = self.pool_of("sg_vg", 2, [128, 1024], F32)
        vb = self.pool_of("sg_vb", 2, [128, 1024], BF16)
        st = self.pool_of("sg_st", 2, [128, 8], F32)
        NT = GS // 128
        ngr = EXT // GS
        pend = {}

        def s_pre(g):
            xs = []
            for j in range(NT):
                xt = self.nxt(nb["x"])
                self.dma("sp", xt.t, xin.t[g * GS + j * 128:g * GS + (j + 1) * 128, :], [xin.r], [xt.r])
                xn = self.nxt(nb["xn"])
                self.norm_pre(xt, A, SH, nb["tmp"], xn, nb["junk"], self.nxt(nb["stat"]))
                xs.append(xn)
            pend[g] = (self.nxt(hTb), xs)

        def s_tr(g):
            h_, xs = pend[g]
            for j, xn in enumerate(xs):
                self.norm_tr(xn, h_, j * 128)

        s_pre(0)
        s_tr(0)
        for g in range(ngr):
            g0 = g * GS
            hT = pend.pop(g)[0]
            gu = self.nxt(guT)
            ubanks = []
            for c in range(8):
                pbk = self.bank()
                for kc in range(8):
                    self.mm(pbk.t[:, 0:GS], Wi.t[:, kc, c * 128:(c + 1) * 128], hT.t[:, kc, :], kc == 0, kc == 7, [Wi.r, hT.r], [pbk.r])
                self.act(uT.t[:, c, :], pbk.t[:, 0:GS], AF.Gelu, [pbk.r, bcol.r], [uT.r], bias=bcol.t[:, c:c + 1])
            tl = []
            for j in range(NT):
                vt, vg, s_, v2 = self.nxt(vtp), self.nxt(vgp), self.nxt(st), self.nxt(vb)
                for half in range(2):
                    pbk = self.bank()
                    for kc in range(8):
                        self.mm(pbk.t[:, 0:512], hT.t[:, kc, j * 128:(j + 1) * 128],
                                Wi.t[:, kc, 1024 + half * 512:1024 + (half + 1) * 512], kc == 0, kc == 7, [Wi.r, hT.r], [pbk.r])
                    self.tt("dve", vt.t[:, half * 512:(half + 1) * 512], pbk.t[:, 0:512], BV.t[:, half * 512:(half + 1) * 512],
                            ALU.add, [pbk.r, BV.r], [vt.r])
                tl.append((vt, vg, s_, v2))
            if g + 1 < ngr:
                s_pre(g + 1)
            for vt, vg, s_, v2 in tl:
                self.act(vg.t, vt.t, AF.Gelu, [vt.r], [vg.r, s_.r], accum=s_.t[:, 0:1])
            for vt, vg, s_, v2 in tl:
                self.act(vt.t, vg.t, AF.Square, [vg.r, s_.r], [vt.r, s_.r], accum=s_.t[:, 1:2])
            for vt, vg, s_, v2 in tl:
                self.ts("dve", s_.t[:, 2:3], s_.t[:, 0:1], 1.0 / 1024, None, ALU.mult, None, [s_.r], [s_.r])
                self.tt("dve", s_.t[:, 3:4], s_.t[:, 2:3], s_.t[:, 2:3], ALU.mult, [s_.r], [s_.r])
                self.stt("dve", s_.t[:, 4:5], s_.t[:, 1:2], 1.0 / 1024, s_.t[:, 3:4], ALU.mult, ALU.subtract, [s_.r], [s_.r])
            for vt, vg, s_, v2 in tl:
                self.act(s_.t[:, 5:6], s_.t[:, 4:5], AF.Sqrt, [s_.r, self.epsb.r], [s_.r], bias=self.epsb.t[:, 0:1])
            for vt, vg, s_, v2 in tl:
                self.recip(s_.t[:, 5:6], s_.t[:, 5:6], [s_.r], [s_.r])
                self.ts("dve", vg.t, vg.t, s_.t[:, 2:3], s_.t[:, 5:6], ALU.subtract, ALU.mult, [vg.r, s_.r], [vg.r])
            for vt, vg, s_, v2 in tl:
                self.tt("pool", vg.t, vg.t, LNG.t, ALU.mult, [vg.r, LNG.r], [vg.r])
            for vt, vg, s_, v2 in tl:
                self.tt("dve", v2.t, vg.t, LNB.t, ALU.add, [vg.r, LNB.r], [v2.r])
            if g + 1 < ngr:
                s_tr(g + 1)
            for j, (vt, vg, s_, v2) in enumerate(tl):
                for a in range(2):
                    pbk = self.bank()
                    for q in range(4):
                        gi = 4 * a + q
                        self.P.op("pe", (lambda o_, l_, r_: (lambda h_: h_.matmul(o_, l_, r_, start=True, stop=False)))(
                            pbk.t[:, q * 128:(q + 1) * 128], v2.t[:, gi * 128:(gi + 1) * 128], wsT.t[:, gi, :]),
                            reads=[v2.r, wsT.r], writes=[pbk.r], signal=False)
                        self.P.op("pe", (lambda o_, l_, r_: (lambda h_: h_.matmul(o_, l_, r_, start=False, stop=True)))(
                            pbk.t[:, q * 128:(q + 1) * 128], self.ones.t[0:1, :], bsr.t[0:1, gi * 128:(gi + 1) * 128]),
                            reads=[self.ones.r, bsr.r], writes=[pbk.r], signal=(q == 3))
                    self.tt("dve", gu.t[:, 4 * a:4 * a + 4, j * 128:(j + 1) * 128],
                            pbk.t[:, 0:512].rearrange("p (a b) -> p a b", a=4), uT.t[:, 4 * a:4 * a + 4, j * 128:(j + 1) * 128],
                            ALU.mult, [pbk.r, uT.r], [gu.r])
            self.resid_store(xin, xout, gu, 8, Wo, g0)
        return xout


    def fft_stage1(self, X, Ad, f1):
        stg = self.pool_of("s1stg", 2, [128, 4, 512], BF16)
        for s_ in range(64):
            st = self.nxt(stg)
            for q in range(4):
                pbk = self.pb[(4 * (s_ % 2)) + q]
                self.mm(pbk.t[:, 0:512], f1.t[:, q * 128:(q + 1) * 128], X.t[:, s_, :], True, True, [f1.r, X.r], [pbk.r])
                self.cp("act" if q < 2 else "dve", st.t[:, q, :], pbk.t[:, 0:512], [pbk.r], [st.r])
            self.dma("sp", Ad.t[:, s_, :, :].rearrange("q p c -> p q c"), st.t, [st.r], [Ad.r])

    def load_B(self, B, Ad, k1g, G):
        kc, p0 = k1g // 128, k1g % 128
        for ri in range(2):
            self.dma("sp", B.t[ri * 64:(ri + 1) * 64, :, :], Ad.t[2 * kc + ri, :, p0:p0 + G, :], [Ad.r], [B.r])

    def load_tab(self, Tb, tab, k1g, G):
        self.dma("sp", Tb.t, tab.t[k1g // G].rearrange("p (k m) -> p k m", k=G), [], [Tb.r])

    def phase_hyena(self):
        G = 4
        zT = self.inp("hy_zT", [33, L])
        w1i = self.inp("hy_w1", [33, 64])
        w2i = self.inp("hy_w2", [64, 64])
        prmi = self.inp("hy_prm", [64, 4])
        w3i = self.inp("hy_w3", [64, 2048])
        dsk = self.inp("hy_bias", [2, 512])
        win = self.inp("hy_win", [128, 64, 512])
        f1i = self.inp("t_F1", [128, 512], BF16)
        Htab = self.inp("t_H", [64, 128, 512], BF16)
        Hctab = self.inp("t_Hc", [64, 128, 512], BF16)
        M1tab = self.inp("t_M1", [64, 128, 512], BF16)
        M2tab = self.inp("t_M2", [64, 128, 512], BF16)
        fvi = self.inp("t_Finv", [128, 512], BF16)
        fei = self.inp("t_FinvE", [128, 4, 68], BF16)
        A0 = self.scratch("fftA0", [4, 64, 128, 512], BF16)
        A1 = self.scratch("fftA1", [4, 64, 128, 512], BF16)
        Cd = self.scratch("fftC", [128, 256, 512], BF16)
        Hf = self.scratch("fftHf", [2, 2, 64, 256, 512], BF16)
        X1s = self.scratch("X1s", [128, 64, 512], BF16)
        self.top = self.cmark
        f1 = self.alloc("f1", [128, 512], BF16)
        fv = self.alloc("fv", [128, 512], BF16)
        fe = self.alloc("fe", [128, 4, 68], BF16)
        self.dma("sp", f1.t, f1i.t, [], [f1.r])
        self.dma("sp", fv.t, fvi.t, [], [fv.r])
        self.dma("sp", fe.t, fei.t, [], [fe.r])
        base = self.top
        h2T = self.alloc("h2T", [128, L], BF16)
        w3 = self.alloc("w3sb", [64, 2048], BF16)
        self.dma("pool", w3.t, w3i.t, [], [w3.r])
        mlp_mark = self.top
        w1 = self.alloc("w1sb", [33, 64], F32)
        w2 = self.alloc("w2sb", [64, 64], F32)
        prm = self.alloc("prm", [64, 4], F32)
        pr2 = self.alloc("pr2", [64, 4], F32)
        self.dma("sp", w1.t, w1i.t, [], [w1.r])
        self.dma("sp", w2.t, w2i.t, [], [w2.r])
        self.dma("sp", prm.t, prmi.t, [], [prm.r])
        for a in range(2):
            self.ts("dve", pr2.t[:, 2 * a:2 * a + 1], prm.t[:, 2 * a + 1:2 * a + 2], 1.0 / 3, None, ALU.mult, None, [prm.r, pr2.r], [pr2.r])
            self.tt("dve", pr2.t[:, 2 * a + 1:2 * a + 2], pr2.t[:, 2 * a:2 * a + 1], prm.t[:, 2 * a:2 * a + 1], ALU.mult, [prm.r, pr2.r], [pr2.r])
        ztb = self.pool_of("ztb", 2, [33, 512], F32)
        s3 = self.alloc("s3", [64, 512], F32)
        qq = self.alloc("qq", [64, 512], F32)
        h1 = self.alloc("h1", [64, 512], F32)

        def sin3(dst, src_ps, a):
            self.act(s3.t, src_ps.t[0:64, 0:512], AF.Sin, [src_ps.r, pr2.r], [s3.r], bias=pr2.t[:, 2 * a + 1:2 * a + 2], scale=pr2.t[:, 2 * a:2 * a + 1])
            self.tt("dve", qq.t, s3.t, s3.t, ALU.mult, [s3.r], [qq.r])
            self.ts("dve", qq.t, qq.t, -4.0, 3.0, ALU.mult, ALU.add, [qq.r], [qq.r])
            self.tt("dve", dst[0], qq.t, s3.t, ALU.mult, [qq.r, s3.r], [dst[1]])

        for ch in range(L // 512):
            zt = self.nxt(ztb)
            self.dma("sp", zt.t, zT.t[:, ch * 512:(ch + 1) * 512], [], [zt.r])
            pa, pb2 = self.pb[ch % 2], self.pb[2 + ch % 2]
            self.mm(pa.t[0:64, 0:512], w1.t, zt.t, True, True, [w1.r, zt.r], [pa.r])
            sin3((h1.t, h1.r), pa, 0)
            self.mm(pb2.t[0:64, 0:512], w2.t, h1.t, True, True, [w2.r, h1.r], [pb2.r])
            sin3((h2T.t[0:64, ch * 512:(ch + 1) * 512], h2T.r), pb2, 1)
        self.P.barrier()
        self.top = mlp_mark
        h2v = h2T.t[0:64, :].rearrange("p (j s) -> p s j", s=64)
        Dz = self.alloc("Dz", [128, 512], F32)
        omark = self.top
        for o in range(2):
            self.P.barrier()
            self.top = omark
            self.memset("pool", Dz.t, 0.0, [Dz.r])
            self.dma("sp", Dz.t[0:64, :], dsk.t[o].partition_broadcast(64), [], [Dz.r])
            Xf = self.alloc("Xf", [128, 64, 512], BF16)
            Xb = self.alloc("Xb", [128, 64, 512], BF16)
            wt = self.pool_of("wint", 2, [128, 512], F32)
            for s_ in range(64):
                wn = self.nxt(wt)
                self.dma("sp", wn.t, win.t[:, s_, :], [], [wn.r])
                for dr, Xd in ((0, Xf), (1, Xb)):
                    pbk = self.bank()
                    c0 = (o * 2 + dr) * 512
                    self.mm(pbk.t[:, 0:512], h2v[:, s_, :], w3.t[:, c0:c0 + 512], True, True, [h2T.r, w3.r], [pbk.r])
                    self.tt("dve", Xd.t[:, s_, :], pbk.t[:, 0:512], wn.t, ALU.mult, [pbk.r, wn.r], [Xd.r])
            self.memset("pool", Xb.t[0:1, 0, :], 0.0, [Xb.r])
            self.fft_stage1(Xf, A0, f1)
            self.fft_stage1(Xb, A1, f1)
            self.P.barrier()
            self.top = omark
            Bf = self.pool_of("Bf", 2, [128, G, 512], BF16)
            Bb = self.pool_of("Bb", 2, [128, G, 512], BF16)
            Ht = self.pool_of("Ht", 2, [128, G, 128], BF16)
            Hct = self.pool_of("Hct", 2, [128, G, 128], BF16)
            Hst = self.pool_of("Hst", 2, [128, G, 512], BF16)
            Hfo = Hf.t[o].rearrange("r k a c -> (r k) a c")
            def f_loads(k1g):
                bf, bb, ht, hct = self.nxt(Bf), self.nxt(Bb), self.nxt(Ht), self.nxt(Hct)
                self.load_B(bf, A0, k1g, G)
                self.load_B(bb, A1, k1g, G)
                self.load_tab(ht, Htab, k1g, G)
                self.load_tab(hct, Hctab, k1g, G)
                return bf, bb, ht, hct

            nxt_l = f_loads(0)
            for k1g in range(0, 256, G):
                bf, bb, ht, hct = nxt_l
                if k1g + G < 256:
                    nxt_l = f_loads(k1g + G)
                hs = self.nxt(Hst)
                for g in range(G):
                    pbk = self.bank()
                    self.mm(pbk.t[:, 0:512], ht.t[:, g, :], bf.t[:, g, :], True, False, [ht.r, bf.r], [pbk.r])
                    self.mm(pbk.t[:, 0:512], hct.t[:, g, :], bb.t[:, g, :], False, True, [hct.r, bb.r], [pbk.r])
                    self.tt("dve", hs.t[:, g, :], pbk.t[:, 0:512], Dz.t, ALU.add, [pbk.r, Dz.r], [hs.r])
                self.dma("sp", Hfo[:, k1g:k1g + G, :], hs.t, [hs.r], [Hf.r])
        self.P.barrier()
        self.top = base
        X = self.alloc("Xc", [128, 64, 512], BF16)
        cmark2 = self.top
        Ub = self.pool_of("Ub", 2, [128, L], BF16)
        xst = self.pool_of("xst", 2, [128, 8, 128], BF16)
        nt = 0
        for part in range(2):
            for cc in range(4):
                U = self.nxt(Ub)
                r0 = part * 512 + cc * 128
                self.dma("sp", U.t, self.VX1.t[r0:r0 + 128, :], [self.VX1.r], [U.r])
                Uv = U.t.rearrange("p (j s) -> p s j", s=64)
                for s0 in range(0, 64, 8):
                    pT = self.pb[6 + nt % 2]
                    nt += 1
                    pTv = pT.t.bitcast(BF16)
                    for i in range(8):
                        self.tr(pTv[:, i * 128:(i + 1) * 128], Uv[:, s0 + i, :], self.ident.t, [U.r, self.ident.r], [pT.r], signal=(i == 7))
                    src = pTv.rearrange("p (a b) -> p a b", a=8)
                    if part == 0:
                        self.cp("act", X.t[:, s0:s0 + 8, cc * 128:(cc + 1) * 128], src, [pT.r], [X.r])
                    else:
                        st = self.nxt(xst)
                        self.cp("act", st.t, src, [pT.r], [st.r])
                        self.dma("sp", X1s.t[:, s0:s0 + 8, cc * 128:(cc + 1) * 128], st.t, [st.r], [X1s.r])
        for conv in range(2):
            self.P.barrier()
            self.top = cmark2
            if conv == 1:
                x2sb = self.alloc("x2sb", [128, 4, EXT], BF16)
                yaT = self.alloc("yaT", [128, 4, EXT], BF16)
                self.dma("sp", x2sb.t, self.X2.t.rearrange("(cc p) t -> p cc t", p=128), [self.X2.r], [x2sb.r])
            m2 = self.top
            self.fft_stage1(X, A0, f1)
            self.P.barrier()
            self.top = m2
            Bt = self.pool_of("Bt", 2, [128, G, 512], BF16)
            Ht = self.pool_of("cHt", 2, [128, G, 128], BF16)
            M1t = self.pool_of("cM1", 2, [128, G, 128], BF16)
            M2t = self.pool_of("cM2", 2, [128, G, 128], BF16)
            HH1 = self.pool_of("HH1", 2, [128, G, 512], BF16)
            HH2 = self.pool_of("HH2", 2, [128, G, 512], BF16)
            T1 = self.pool_of("T1", 2, [128, 512], BF16)
            T2 = self.pool_of("T2", 2, [128, 512], BF16)
            Cst = self.pool_of("Cst", 2, [128, G, 512], BF16)
            def c_loads(k1g):
                bt, ht, m1, m2_, h1_, h2_ = (self.nxt(Bt), self.nxt(Ht), self.nxt(M1t), self.nxt(M2t), self.nxt(HH1), self.nxt(HH2))
                self.load_B(bt, A0, k1g, G)
                self.load_tab(ht, Htab, k1g, G)
                self.load_tab(m1, M1tab, k1g, G)
                self.load_tab(m2_, M2tab, k1g, G)
                for half in range(2):
                    self.dma("sp", h1_.t[half * 64:(half + 1) * 64, :, :], Hf.t[conv, 0, :, k1g:k1g + G, :], [Hf.r], [h1_.r])
                    self.dma("sp", h2_.t[half * 64:(half + 1) * 64, :, :], Hf.t[conv, 1, :, k1g:k1g + G, :], [Hf.r], [h2_.r])
                return bt, ht, m1, m2_, h1_, h2_

            nxt_l = c_loads(0)
            for k1g in range(0, 256, G):
                bt, ht, m1, m2_, h1_, h2_ = nxt_l
                if k1g + G < 256:
                    nxt_l = c_loads(k1g + G)
                cs_ = self.nxt(Cst)
                for g in range(G):
                    pu = self.bank()
                    self.mm(pu.t[:, 0:512], ht.t[:, g, :], bt.t[:, g, :], True, True, [ht.r, bt.r], [pu.r])
                    t1, t2 = self.nxt(T1), self.nxt(T2)
                    self.tt("dve", t1.t, pu.t[:, 0:512], h1_.t[:, g, :], ALU.mult, [pu.r, h1_.r], [t1.r])
                    self.tt("dve", t2.t, pu.t[:, 0:512], h2_.t[:, g, :], ALU.mult, [pu.r, h2_.r], [t2.r])
                    pc = self.bank()
                    self.mm(pc.t[:, 0:512], m1.t[:, g, :], t1.t, True, False, [m1.r, t1.r], [pc.r])
                    self.mm(pc.t[:, 0:512], m2_.t[:, g, :], t2.t, False, True, [m2_.r, t2.r], [pc.r])
                    self.cp("act", cs_.t[:, g, :], pc.t[:, 0:512], [pc.r], [cs_.r])
                self.dma("sp", Cd.t[:, k1g:k1g + G, :], cs_.t, [cs_.r], [Cd.r])
            self.P.barrier()
            self.top = m2
            Dt = self.pool_of("Dt", 2, [128, 4, 512], BF16)
            x1t = self.pool_of("x1t", 2, [128, 512], BF16)
            for s_ in range(64):
                dt_ = self.nxt(Dt)
                for kc in range(2):
                    for ri in range(2):
                        self.dma("sp", dt_.t[:, 2 * kc + ri, :], Cd.t[ri * 64 + s_, kc * 128:(kc + 1) * 128, :], [Cd.r], [dt_.r])
                if conv == 0:
                    xg = self.nxt(x1t)
                    self.dma("sp", xg.t, X1s.t[:, s_, :], [X1s.r], [xg.r])
                    pbk = self.bank()
                    for q in range(4):
                        self.mm(pbk.t[:, 0:512], fv.t[:, q * 128:(q + 1) * 128], dt_.t[:, q, :], q == 0, q == 3, [fv.r, dt_.r], [pbk.r])
                    self.tt("dve", X.t[:, s_, :], pbk.t[:, 0:512], xg.t, ALU.mult, [pbk.r, xg.r], [X.r])
                else:
                    pbk = self.bank()
                    for cc in range(4):
                        for q in range(4):
                            self.P.op("pe", (lambda o_, l_, r_, a_, b_: (lambda h_: h_.matmul(o_, l_, r_, start=a_, stop=b_)))(
                                pbk.t[:, cc * 68:(cc + 1) * 68], dt_.t[:, q, cc * 128:(cc + 1) * 128], fe.t[:, q, :], q == 0, q == 3),
                                reads=[dt_.r, fe.r], writes=[pbk.r], signal=(cc == 3 and q == 3))
                    x2v = x2sb.t.rearrange("p c (j s) -> p c s j", s=64)[:, :, s_, :]
                    yav = yaT.t.rearrange("p c (j s) -> p c s j", s=64)[:, :, s_, :]
                    self.tt("dve", yav, pbk.t[:, 0:272].rearrange("p (c j) -> p c j", c=4), x2v, ALU.mult, [pbk.r, x2sb.r], [yaT.r])
            if conv == 1:
                self.dma("sp", self.YT.t[0:512, :].rearrange("(cc p) t -> p cc t", p=128), yaT.t, [yaT.r], [self.YT.r])


def build_program(dbg=False, stop_after=None):
    kb = KB(stop_after)
    kb.dbg = dbg
    kb._bk = 0
    kb.consts()
    kb.phase_mod()
    kb.P.barrier()
    if stop_after == "mod":
        kb.P.build()
        return kb
    kb.phase_inproj()
    kb.P.barrier()
    if stop_after == "inproj":
        kb.P.build()
        return kb
    kb.phase_attn()
    kb.P.barrier()
    if stop_after == "attn":
        kb.P.build()
        return kb
    if os.environ.get("FAKE_YA"):
        ya = kb.inp("dbg_yaT", [512, EXT], BF16)
        st = kb.alloc("yast", [128, EXT], BF16)
        for c in range(4):
            kb.dma("sp", st.t, ya.t[c * 128:(c + 1) * 128, :], [ya.r], [st.r])
            kb.dma("sp", kb.YT.t[c * 128:(c + 1) * 128, :], st.t, [st.r], [kb.YT.r])
        kb.P.barrier()
    else:
        kb.phase_hyena()
        kb.P.barrier()
    if stop_after == "hyena":
        kb.P.build()
        return kb
    wo = kb.inp("ab_w_out", [D, D])
    x1 = kb.phase_outproj("X1o", kb.YT, kb.x_ext, wo.t, 0)
    kb.P.barrier()
    x2 = kb.phase_ffn("X2o", x1, 0)
    kb.P.barrier()
    x3 = kb.phase_sgu("X3o", x2, 1)
    kb.P.barrier()
    kb.phase_ffn("X4o", x3, 1, final=True)
    kb.P.barrier()
    kb.P.build()
    return kb


def rope_tables(pos_row, pos_col):
    T = pos_row.shape[0]
    cos = np.zeros((128, T), np.float32)
    sin = np.zeros((128, T), np.float32)
    inv = (10000.0 ** (-np.arange(16, dtype=np.float32) / 16)).astype(np.float32)
    for f in range(128):
        d = f % 64
        pos = pos_row if d < 32 else pos_col
        dd = d % 32
        i = dd % 16
        ang = pos.astype(np.float32) * inv[i]
        cos[f] = np.cos(ang)
        sin[f] = -np.sin(ang) if dd < 16 else np.sin(ang)
    return cos, sin


def rperm_matrix():
    m = np.zeros((128, 128), np.float32)
    for f in range(128):
        dd = (f % 64) % 32
        partner = f + 16 if dd < 16 else f - 16
        m[partner, f] = 1.0
    return m.astype(ml_dtypes.bfloat16)


def fft_tables(lo):
    bf = ml_dtypes.bfloat16
    N = 16384
    j = np.arange(128, dtype=np.float64)[:, None]
    k1 = np.arange(256, dtype=np.float64)[None, :]
    th = 2 * np.pi * ((j * k1) % 256) / 256
    F1 = np.zeros((128, 4, 128))
    Fi = np.zeros((128, 4, 128))
    for kc in range(2):
        F1[:, 2 * kc, :] = np.cos(th[:, kc * 128:(kc + 1) * 128])
        F1[:, 2 * kc + 1, :] = -np.sin(th[:, kc * 128:(kc + 1) * 128])
        Fi[:, 2 * kc, :] = np.cos(th[:, kc * 128:(kc + 1) * 128]).T / N
        Fi[:, 2 * kc + 1, :] = -np.sin(th[:, kc * 128:(kc + 1) * 128]).T / N
    j0 = lo // 64
    FiE = Fi[:, :, j0:j0 + 68]
    s = np.arange(64, dtype=np.float64)
    k2 = np.arange(64, dtype=np.float64)
    kk = np.arange(256, dtype=np.float64)
    ang = 2 * np.pi * ((s[None, :, None] * kk[:, None, None]) / N + ((s[None, :, None] * k2[None, None, :]) % 64) / 64)
    Zr, Zi = np.cos(ang), -np.sin(ang)
    H = np.zeros((256, 128, 128))
    H[:, 0:64, 0:64] = Zr
    H[:, 64:128, 0:64] = -Zi
    H[:, 0:64, 64:128] = Zi
    H[:, 64:128, 64:128] = Zr
    Hc = H.copy()
    Hc[:, :, 64:128] *= -1
    ZrT, ZiT = Zr.transpose(0, 2, 1), Zi.transpose(0, 2, 1)
    M1 = np.zeros((256, 128, 128))
    M1[:, 0:64, 0:64] = ZrT
    M1[:, 64:128, 0:64] = ZiT
    M1[:, 0:64, 64:128] = -ZiT
    M1[:, 64:128, 64:128] = ZrT
    M2 = np.zeros((256, 128, 128))
    M2[:, 0:64, 0:64] = ZiT
    M2[:, 64:128, 0:64] = -ZrT
    M2[:, 0:64, 64:128] = ZrT
    M2[:, 64:128, 64:128] = ZiT
    c = lambda a: np.ascontiguousarray(a.astype(np.float32)).astype(bf)
    grp = lambda a: c(a.reshape(64, 4, 128, 128).transpose(0, 2, 1, 3).reshape(64, 128, 512))
    return {"t_F1": c(F1.reshape(128, 512)), "t_Finv": c(Fi.reshape(128, 512)), "t_FinvE": c(FiE),
            "t_H": grp(H), "t_Hc": grp(Hc), "t_M1": grp(M1), "t_M2": grp(M2)}


def hyena_consts():
    f32 = np.float32
    bands = 16
    pos = np.arange(L, dtype=f32)
    t = np.linspace(0.0, 1.0, L, dtype=f32)[:, None]
    ang = (f32(2.0 * math.pi / L) * pos[:, None] * np.linspace(1e-4, bands - 1, bands, dtype=f32)[None, :]).astype(f32)
    z = np.concatenate([t, np.cos(ang), -np.sin(ang)], axis=-1).astype(f32)
    deltas = np.abs(np.linspace(math.log(1e-2) / 1.5, math.log(1e-2) / 0.3, 512, dtype=f32))
    window = (np.exp(-t * deltas[None, :]) + f32(0.05)).astype(f32)
    win = np.ascontiguousarray(window.reshape(128, 64, 512))
    return np.ascontiguousarray(z.T), win


_TAB = {}


def host_inputs(inputs, cid):
    b, hf = cid // 2, cid % 2
    lo = 0 if hf == 0 else L - EXT
    f32 = np.float32
    m = {}
    m["c_ident"] = np.eye(128, dtype=f32).astype(ml_dtypes.bfloat16)
    cc = np.stack([inputs["c"][b], inputs["c_ctx"]], -1).astype(f32)
    m["ccol"] = np.ascontiguousarray(cc.reshape(8, 128, 2).transpose(1, 0, 2))
    m["mod_w"] = inputs["mod_w"]
    m["mod_b"] = inputs["mod_b"]
    m["x_full"] = np.ascontiguousarray(inputs["x"][b])
    m["x_ext"] = np.ascontiguousarray(inputs["x"][b, lo:lo + EXT])
    m["ctx"] = np.ascontiguousarray(inputs["ctx"][b])
    m["w_in"] = np.ascontiguousarray(inputs["ab_w_in"][0])
    cw = np.concatenate([inputs["hy_conv_w"][0], inputs["hy_conv_b"][0][None]], 0)
    m["hy_cw"] = np.ascontiguousarray(cw.reshape(4, 12, 128).transpose(2, 1, 0)).astype(f32)
    m["norm_mix_g"] = inputs["norm_mix_g"]
    t = np.arange(L)
    ck, sk = rope_tables(t // 64, t % 64)
    m["ropek_cos"], m["ropek_sin"] = ck, sk
    m["ropeq_cos"] = np.ascontiguousarray(ck[:, lo:lo + EXT])
    m["ropeq_sin"] = np.ascontiguousarray(sk[:, lo:lo + EXT])
    m["c_rperm"] = rperm_matrix()
    m["da_lambda"] = np.ascontiguousarray(inputs["da_lambda"][0]).astype(f32)
    m["ab_w_out"] = np.ascontiguousarray(inputs["ab_w_out"][0])
    m["norm_ffn_g"] = inputs["norm_ffn_g"]
    m["final_norm_g"] = inputs["final_norm_g"]
    for i in range(2):
        m["ffn_w_up%d" % i] = np.ascontiguousarray(inputs["ffn_w_up"][i])
        m["ffn_w_down%d" % i] = np.ascontiguousarray(inputs["ffn_w_down"][i])
        fcw = np.concatenate([inputs["ffn_conv_w"][i], inputs["ffn_conv_b"][i][None]], 0)
        m["ffn_cw%d" % i] = np.ascontiguousarray(fcw.reshape(4, 44, 128).transpose(2, 1, 0)).astype(f32)
    m["sgu_w_in"] = np.ascontiguousarray(inputs["sgu_w_in"][0])
    m["sgu_bu_col"] = np.ascontiguousarray(inputs["sgu_b_in"][0][:1024].reshape(8, 128).T).astype(f32)
    m["sgu_bv"] = np.ascontiguousarray(inputs["sgu_b_in"][0][1024:])
    m["sgu_ln_g"] = inputs["sgu_ln_g"][0]
    m["sgu_ln_b"] = inputs["sgu_ln_b"][0]
    m["sgu_wsT"] = np.ascontiguousarray(inputs["sgu_w_s"][0].transpose(2, 0, 1)).astype(f32)
    m["sgu_b_s"] = np.ascontiguousarray(inputs["sgu_b_s"][0].reshape(1, 1024)).astype(f32)
    m["sgu_w_out"] = np.ascontiguousarray(inputs["sgu_w_out"][0])
    if lo not in _TAB:
        _TAB[lo] = fft_tables(lo)
    if "hc" not in _TAB:
        _TAB["hc"] = hyena_consts()
    m.update(_TAB[lo])
    m["hy_zT"], m["hy_win"] = _TAB["hc"]
    m["hy_w1"] = np.ascontiguousarray(inputs["hy_w1"][0])
    m["hy_w2"] = np.ascontiguousarray(inputs["hy_w2"][0])
    m["hy_prm"] = np.ascontiguousarray(np.stack([inputs["hy_b1"][0], inputs["hy_freq"][0][0], inputs["hy_b2"][0], inputs["hy_freq"][0][1]], -1)).astype(f32)
    m["hy_w3"] = np.ascontiguousarray(inputs["hy_w3"][0])
    m["hy_bias"] = np.ascontiguousarray(inputs["hy_bias"][0])
    m["subln_col"] = np.ascontiguousarray(inputs["da_subln_g"][0].reshape(128, 1)).astype(f32)
    return m


_CACHE = {}


def kernel(**inputs):
    inputs = {k: np.asarray(v) for k, v in inputs.items()}
    if "kb" not in _CACHE:
        _CACHE["kb"] = build_program()
    kb = _CACHE["kb"]
    in_maps = []
    for cid in range(8):
        m = host_inputs(inputs, cid)
        in_maps.append({k: m[k] for k in kb.ins})
    res = run_bass_kernel_spmd(kb.nc, in_maps, core_ids=list(range(8)))
    out = np.zeros((4, L, D), np.float32)
    for cid in range(8):
        b, hf = cid // 2, cid % 2
        o = res.results[cid]["out"]
        if hf == 0:
            out[b, :4096] = o[:4096]
        else:
            out[b, 4096:] = o[EXT - 4096:]
    return out
```

```python
import math
import os
import numpy as np
import ml_dtypes
import concourse.bass as bass
import concourse.mybir as mybir
from concourse.bass_utils import run_bass_kernel_spmd

F32 = mybir.dt.float32
BF16 = mybir.dt.bfloat16
U8 = mybir.dt.uint8
AF = mybir.ActivationFunctionType
ALU = mybir.AluOpType

ENGS = ("pe", "act", "dve", "pool", "sp")
SEM_ROT = 16000
L = 8192
EXT = 4352
NEXT_T = EXT // 128
D = 1024
DFF = 2816
EPS = 1e-6
GS = 256


class Res:
    __slots__ = ("name", "w", "r", "dsem", "dcnt", "excl", "lk")

    def __init__(self, name, excl=False):
        self.name = name
        self.lk = None
        self.excl = excl
        self.w = {}
        self.r = {}
        self.dsem = None
        self.dcnt = 0


class Prog:
    def __init__(self, nc):
        self.nc = nc
        self.ops = {e: [] for e in ENGS}
        self.cnt = {e: 0 for e in ENGS}
        self.epoch = {e: 0 for e in ENGS}
        self.sems = {}
        self.known = {e: {} for e in ENGS}
        self.dres = []
        self.meta = {e: [] for e in ENGS}
        self.free_sems = []
        self.nd = 0
        for e in ENGS:
            if e != "sp":
                self._engsem(e)

    def _engsem(self, e):
        k = ("E", e, self.epoch[e])
        if k not in self.sems:
            self.sems[k] = self.nc.alloc_semaphore(name="s_%s_%d" % (e, self.epoch[e]))
        return k

    def _semname(self, sem):
        for k, v in self.sems.items():
            if v is sem:
                return k
        return None

    def check_deadlock(self):
        val = {}
        pc = {e: 0 for e in ENGS}
        prog = True
        while prog:
            prog = False
            for e in ENGS:
                while pc[e] < len(self.meta[e]):
                    waits, inc, desc = self.meta[e][pc[e]]
                    if all(val.get(k, 0) >= v for k, v in waits):
                        if inc is not None:
                            val[inc[0]] = val.get(inc[0], 0) + inc[1]
                        pc[e] += 1
                        prog = True
                    else:
                        break
        bad = False
        for e in ENGS:
            if pc[e] < len(self.meta[e]):
                bad = True
                waits, inc, desc = self.meta[e][pc[e]]
                print("DEADLOCK", e, pc[e], len(self.meta[e]), desc, [(k, v, val.get(k, 0)) for k, v in waits if val.get(k, 0) < v])
        return not bad

    def _deps(self, reads, writes):
        deps = {}
        for r in reads:
            for k, v in r.w.items():
                if deps.get(k, 0) < v:
                    deps[k] = v
        for w in writes:
            for d in (w.w, w.r):
                for k, v in d.items():
                    if deps.get(k, 0) < v:
                        deps[k] = v
        return deps

    def _waits(self, eng, deps):
        kn = self.known[eng]
        out = []
        for k, v in deps.items():
            if kn.get(k, 0) < v:
                kn[k] = v
                out.append((self.sems[k], v))
        return out

    def op(self, eng, fn, reads=(), writes=(), signal=True):
        deps = self._deps(reads, writes)
        k = self._engsem(eng)
        for r in reads:
            if r.excl:
                for kk, vv in r.r.items():
                    if not (kk[0] == "E" and kk[1] == eng) and deps.get(kk, 0) < vv:
                        deps[kk] = vv
        if eng == "pe":
            deps = {kk: vv for kk, vv in deps.items() if kk[1] != "pe" or kk[0] != "E"}
        else:
            deps = {kk: vv for kk, vv in deps.items() if not (kk == k and vv > self.cnt[eng])}
        waits = self._waits(eng, deps)
        if signal:
            self.cnt[eng] += 1
            tok = (k, self.cnt[eng])
            sem = self.sems[k]
        else:
            tok = (k, self.cnt[eng] + 1)
            sem = None

        def emit(h, fn=fn, waits=waits, sem=sem):
            for s, v in waits:
                h.wait_ge(s, v)
            ins = fn(h)
            if sem is not None:
                ins.then_inc(sem, 1)

        self.ops[eng].append(emit)
        self.meta[eng].append(([(self._semname(s_), v_) for s_, v_ in waits], (k, 1) if signal else None,
                               "op r=%s w=%s" % ([r.name for r in reads], [w.name for w in writes])))
        kk, vv = tok
        for w in writes:
            w.w = {kk: vv}
            w.r = {}
        for r in reads:
            if r.r.get(kk, 0) < vv:
                r.r[kk] = vv
        if signal and self.cnt[eng] >= SEM_ROT:
            self.epoch[eng] += 1
            self.cnt[eng] = 0
            self._engsem(eng)

    def dma(self, eng, out, in_, reads=(), writes=(), store=False):
        assert len(writes) == 1
        wres = writes[0]
        owner = reads[0] if store else wres
        deps = {}
        for r in reads:
            for k, v in r.w.items():
                if deps.get(k, 0) < v:
                    deps[k] = v
        for d in ((wres.r,) if store else (wres.w, wres.r)):
            for k, v in d.items():
                if deps.get(k, 0) < v:
                    deps[k] = v
        if owner.dsem is None:
            if eng == "pool":
                self.nd += 1
                owner.dsem = ("S", self.nd)
                owner.dcnt = 0
                self.sems[owner.dsem] = self.nc.alloc_semaphore(name="sw_%d" % self.nd)
            elif self.free_sems:
                owner.dsem, owner.dcnt = self.free_sems.pop()
            else:
                self.nd += 1
                owner.dsem = ("D", self.nd)
                owner.dcnt = 0
                self.sems[owner.dsem] = self.nc.alloc_semaphore(name="d_%d" % self.nd)
            self.dres.append(owner)
        if (not store) and owner.lk == "load" and not wres.r:
            deps.pop(owner.dsem, None)
        owner.lk = "store" if store else "load"
        waits = self._waits(eng, deps)
        owner.dcnt += 1
        k, v = owner.dsem, 16 * owner.dcnt
        sem = self.sems[k]

        def emit(h, waits=waits, sem=sem, out=out, in_=in_):
            for s, vv in waits:
                h.wait_ge(s, vv)
            h.dma_start(out=out, in_=in_).then_inc(sem, 16)

        self.ops[eng].append(emit)
        self.meta[eng].append(([(self._semname(s_), v_) for s_, v_ in waits], (k, 16),
                               "dma r=%s w=%s" % ([r.name for r in reads], [w.name for w in writes])))
        if store:
            wres.w[k] = v
        else:
            wres.w = {k: v}
            wres.r = {}
        for r in reads:
            if r.r.get(k, 0) < v:
                r.r[k] = v

    def barrier(self):
        toks = {}
        for e in ENGS:
            if e == "sp":
                continue
            k = self._engsem(e)
            if self.cnt[e] > 0:
                toks[k] = self.cnt[e]
            if self.epoch[e] > 0:
                toks[("E", e, self.epoch[e] - 1)] = SEM_ROT
        for r in self.dres:
            toks[r.dsem] = 16 * r.dcnt
        for e in ENGS:
            waits = self._waits(e, dict(toks))

            def emit(h, waits=waits):
                for s, v in waits:
                    h.wait_ge(s, v)

            self.ops[e].append(emit)
            self.meta[e].append(([(self._semname(s_), v_) for s_, v_ in waits], None, "barrier"))
        keep = []
        for r in self.dres:
            if r.dsem[0] == "S":
                keep.append(r)
                continue
            self.free_sems.append((r.dsem, r.dcnt))
            r.dsem = None
        self.dres = keep

    def build(self):
        nc = self.nc
        with nc.Block() as block:
            @block.tensor
            def _(h):
                for f in self.ops["pe"]:
                    f(h)

            @block.scalar
            def _(h):
                for f in self.ops["act"]:
                    f(h)

            @block.vector
            def _(h):
                for f in self.ops["dve"]:
                    f(h)

            @block.gpsimd
            def _(h):
                for f in self.ops["pool"]:
                    f(h)

            @block.sync
            def _(h):
                for f in self.ops["sp"]:
                    f(h)


class Buf:
    __slots__ = ("t", "r")

    def __init__(self, t, r):
        self.t = t
        self.r = r


def _dtsize(dt):
    return 4 if dt == F32 else (2 if dt == BF16 else 1)


class KB:
    def __init__(self, stop_after=None):
        nc = bass.Bass("TRN2", target_bir_lowering=False)
        self.nc = nc
        self.P = Prog(nc)
        self.stop_after = stop_after
        self.ARENA = 207 * 1024
        self.arena = nc.alloc_sbuf_tensor("arena", [128, self.ARENA], U8)
        self.top = 0
        ps = nc.alloc_psum_tensor("psum", [128, 4096], F32)
        self.ps = ps
        self.pb = [Buf(ps[:, 512 * i:512 * (i + 1)], Res("pb%d" % i, excl=True)) for i in range(8)]
        self.dram = {}
        self.dram_names = set()
        self.ins = {}
        self.outs = {}

    def alloc(self, name, shape, dt):
        n = 1
        for s in shape[1:]:
            n *= s
        nb = n * _dtsize(dt)
        nb = (nb + 31) // 32 * 32
        off = self.top
        self.top += nb
        assert self.top <= self.ARENA, "SBUF arena overflow %s %d" % (name, self.top)
        v = self.arena[:shape[0], off:off + n * _dtsize(dt)].bitcast(dt)
        if len(shape) == 3:
            v = v.rearrange("p (a b) -> p a b", a=shape[1])
        elif len(shape) == 4:
            v = v.rearrange("p (a b c) -> p a b c", a=shape[1], b=shape[2])
        return Buf(v, Res(name))

    def inp(self, name, shape, dt=F32):
        t = self.nc.dram_tensor(name, list(shape), dt, kind="ExternalInput").ap()
        b = Buf(t, Res(name))
        self.ins[name] = b
        return b

    def outp(self, name, shape, dt=F32):
        t = self.nc.dram_tensor(name, list(shape), dt, kind="ExternalOutput").ap()
        self.dram_names.add(name)
        b = Buf(t, Res(name))
        self.outs[name] = b
        return b

    def scratch(self, name, shape, dt, debug=False):
        if debug:
            return self.outp(name, shape, dt)
        t = self.nc.dram_tensor(name, list(shape), dt).ap()
        self.dram_names.add(name)
        return Buf(t, Res(name))

    def mm(self, out, lhsT, rhs, start, stop, reads, writes):
        self.P.op("pe", lambda h: h.matmul(out, lhsT, rhs, start=start, stop=stop),
                  reads=reads, writes=writes, signal=bool(stop))

    def tr(self, out, in_, ident, reads, writes, signal=True):
        self.P.op("pe", lambda h: h.transpose(out, in_, ident), reads=reads, writes=writes, signal=signal)

    def act(self, out, in_, func, reads, writes, bias=None, scale=None, accum=None):
        kw = {}
        if bias is not None:
            kw["bias"] = bias
        if scale is not None:
            kw["scale"] = scale
        if accum is not None:
            kw["accum_out"] = accum
        self.P.op("act", lambda h: h.activation(out=out, in_=in_, func=func, **kw), reads=reads, writes=writes)

    def tt(self, eng, out, a, b, op, reads, writes):
        self.P.op(eng, lambda h: h.tensor_tensor(out=out, in0=a, in1=b, op=op), reads=reads, writes=writes)

    def ts(self, eng, out, a, s1, s2, op0, op1, reads, writes):
        if op1 is None:
            s2, op1 = 0.0, ALU.add
        self.P.op(eng, lambda h: h.tensor_scalar(out, a, s1, s2, op0, op1), reads=reads, writes=writes)

    def stt(self, eng, out, in0, scalar, in1, op0, op1, reads, writes):
        self.P.op(eng, lambda h: h.scalar_tensor_tensor(out=out, in0=in0, scalar=scalar, in1=in1, op0=op0, op1=op1),
                  reads=reads, writes=writes)

    def cp(self, eng, out, in_, reads, writes):
        if eng == "act":
            self.P.op("act", lambda h: h.activation(out=out, in_=in_, func=AF.Copy), reads=reads, writes=writes)
        else:
            self.P.op(eng, lambda h: h.tensor_copy(out, in_), reads=reads, writes=writes)

    def recip(self, out, in_, reads, writes):
        self.P.op("dve", lambda h: h.reciprocal(out, in_), reads=reads, writes=writes)

    def memset(self, eng, out, val, writes):
        self.P.op(eng, lambda h: h.memset(out, val), writes=writes)

    def dma(self, q, out, in_, reads, writes, store=None):
        if store is None:
            store = writes[0].name in self.dram_names
        self.P.dma(q, out, in_, reads=reads, writes=writes, store=store)

    def ld(self, q, dst, src_ap, src=None):
        self.P.dma(q, dst.t if isinstance(dst, Buf) else dst[0], src_ap,
                   reads=[src.r] if src is not None else [], writes=[dst.r if isinstance(dst, Buf) else dst[1]])

    def consts(self):
        self.ident = self.alloc("ident", [128, 128], BF16)
        self.ones = self.alloc("ones", [128, 128], BF16)
        self.epsb = self.alloc("epsb", [128, 1], F32)
        idin = self.inp("c_ident", [128, 128], BF16)
        self.dma("sp", self.ident.t, idin.t, [], [self.ident.r])
        self.memset("pool", self.ones.t, 1.0, [self.ones.r])
        self.memset("pool", self.epsb.t, EPS, [self.epsb.r])
        self.cmark = self.top

    def norm_pre(self, xt, A, SH, tmp, xn, junk, stat):
        ss = stat.t[:, 0:1]
        rs = stat.t[:, 1:2]
        self.act(junk.t, xt.t, AF.Square, [xt.r], [junk.r, stat.r], accum=ss)
        self.act(rs, ss, AF.Sqrt, [stat.r, self.epsb.r], [stat.r], bias=self.epsb.t[:, 0:1], scale=1.0 / D)
        self.recip(rs, rs, [stat.r], [stat.r])
        self.stt("dve", tmp.t, xt.t, rs, A.t, ALU.mult, ALU.mult, [xt.r, stat.r, A.r], [tmp.r])
        self.tt("pool", xn.t, tmp.t, SH.t, ALU.add, [tmp.r, SH.r], [xn.r])

    def norm_tr(self, xn, hT, col0, ntok=128):
        pT = self.pb[7]
        pTv = pT.t.bitcast(BF16)
        for kc in range(8):
            self.tr(pTv[:, kc * 128:kc * 128 + ntok], xn.t[:ntok, kc * 128:(kc + 1) * 128], self.ident.t[:ntok, :ntok],
                    [xn.r, self.ident.r], [pT.r], signal=(kc == 7))
        src = pTv.rearrange("p (a b) -> p a b", a=8)[:, :, :ntok]
        self.cp("act", hT.t[:, :, col0:col0 + ntok], src, [pT.r], [hT.r])

    def norm_T(self, xt, A, SH, tmp, xn, junk, stat, hT, col0, ntok=128, plain_g=None):
        self.norm_pre(xt, A, SH, tmp, xn, junk, stat)
        self.norm_tr(xn, hT, col0, ntok)

    def pool_of(self, name, n, shape, dt):
        return {"b": [self.alloc("%s%d" % (name, i), shape, dt) for i in range(n)], "i": 0}

    def nxt(self, pool):
        b = pool["b"][pool["i"] % len(pool["b"])]
        pool["i"] += 1
        return b

    def bank(self):
        b = self.pb[self._bk % 6]
        self._bk += 1
        return b

    def phase_mod(self):
        ccol = self.inp("ccol", [128, 8, 2])
        modw = self.inp("mod_w", [2, 1024, 6144])
        modb = self.inp("mod_b", [2, 6144])
        self.modv = self.scratch("modv", [2, 2, 6144], F32, debug=self.dbg)
        sc = self.alloc("scol", [128, 8, 2], F32)
        self.dma("sp", sc.t, ccol.t, [], [sc.r])
        self.act(sc.t, sc.t, AF.Silu, [sc.r], [sc.r])
        mrow = self.alloc("mrow", [2, 6144], F32)
        mb2 = self.alloc("mb2", [2, 6144], F32)
        wb = [self.alloc("mwb%d" % i, [128, 8, 512], F32) for i in range(2)]
        for i in range(2):
            self.dma("sp", mb2.t, modb.t[i].partition_broadcast(2), [], [mb2.r])
            for n in range(12):
                w = wb[n % 2]
                self.dma("sp", w.t, modw.t[i, :, n * 512:(n + 1) * 512].rearrange("(kc p) f -> p kc f", p=128), [], [w.r])
                pbk = self.pb[n % 2]
                for kc in range(8):
                    self.mm(pbk.t[0:2, :], sc.t[:, kc, :], w.t[:, kc, :], kc == 0, kc == 7, [sc.r, w.r], [pbk.r])
                self.tt("dve", mrow.t[:, n * 512:(n + 1) * 512], pbk.t[0:2, :], mb2.t[:, n * 512:(n + 1) * 512], ALU.add,
                        [pbk.r, mb2.r], [mrow.r])
            self.dma("sp", self.modv.t[i], mrow.t, [mrow.r], [self.modv.r])

    def mod_tiles(self, layer, which, i_sh, i_sc, g_ap):
        A = self.alloc("modA", [128, 1024], F32)
        SH = self.alloc("modSH", [128, 1024], F32)
        mv = self.modv.t[layer, which]
        self.dma("sp", A.t, mv[i_sc * D:(i_sc + 1) * D].partition_broadcast(128), [self.modv.r], [A.r])
        self.dma("sp", SH.t, g_ap.partition_broadcast(128), [], [SH.r])
        self.stt("dve", A.t, A.t, 1.0, SH.t, ALU.add, ALU.mult, [A.r, SH.r], [A.r])
        self.dma("sp", SH.t, mv[i_sh * D:(i_sh + 1) * D].partition_broadcast(128), [self.modv.r], [SH.r])
        return A, SH

    def load_w_bf16(self, dst, src_ap, kcn):
        for kc in range(kcn):
            self.dma("pool", dst.t[:, kc, :], src_ap[kc * 128:(kc + 1) * 128, :], [], [dst.r])

    def norm_bufs(self):
        nb = {}
        nb["x"] = self.pool_of("nx", 2, [128, 1024], F32)
        nb["tmp"] = self.alloc("ntmp", [128, 1024], F32)
        nb["xn"] = self.pool_of("nxn", 2, [128, 1024], BF16)
        nb["junk"] = self.alloc("njunk", [128, 1024], BF16)
        nb["stat"] = self.pool_of("nstat", 2, [128, 2], F32)
        return nb

    def proj_pass(self, xin, T, A, SH, w, fm, tm, nb, post=None):
        skip = os.environ.get("KSKIP", "")
        fm = [sp for sp in fm if sp["kind"] not in skip.split(",")]
        ng = T // GS
        hT = [self.alloc("hT%d" % i, [128, 8, GS + 32], BF16) for i in range(3)]
        for hb in hT:
            self.memset("pool", hb.t, 0.0, [hb.r])
        tmpc = self.pool_of("tmpc", 2, [128, GS], F32)
        tmpd = self.pool_of("tmpd", 6, [128, GS], F32)
        obf = self.pool_of("obf", 4 if not any("emit" in sp_ for sp_ in fm) else 1, [128, GS], BF16)
        tst = self.pool_of("tst", 2, [128, 512], BF16) if tm else None
        cs = self.pool_of("cs", 2, [128, 2, GS], F32) if any(sp_["kind"] == "rope" for sp_ in fm) else None
        pend_xn = {}

        def pre(k):
            xs = []
            for j in range(GS // 128):
                xt = self.nxt(nb["x"])
                t0 = k * GS + j * 128
                self.dma("sp", xt.t, xin.t[t0:t0 + 128, :], [xin.r], [xt.r])
                xn = self.nxt(nb["xn"])
                self.norm_pre(xt, A, SH, nb["tmp"], xn, nb["junk"], self.nxt(nb["stat"]))
                xs.append(xn)
            pend_xn[k] = xs

        def trn(k):
            h_ = hT[k % 3]
            for j, xn in enumerate(pend_xn.pop(k)):
                self.norm_tr(xn, h_, 16 + j * 128)
            if k == 0:
                self.memset("pool", h_.t[:, :, 15:16], 0.0, [h_.r])
            else:
                hp = hT[(k - 1) % 3]
                self.cp("pool", h_.t[:, :, 15:16], hp.t[:, :, GS + 15:GS + 16], [hp.r], [h_.r])

        for k0 in range(min(2, ng)):
            pre(k0)
            trn(k0)
        for gg in range(ng):
            if gg + 2 < ng:
                pre(gg + 2)
            h = hT[gg % 3]
            if gg == ng - 1:
                self.memset("pool", h.t[:, :, GS + 16:GS + 17], 0.0, [h.r])
            else:
                hn = hT[(gg + 1) % 3]
                self.cp("pool", h.t[:, :, GS + 16:GS + 17], hn.t[:, :, 16:17], [hn.r], [h.r])
            g0 = gg * GS
            for sp in fm:
                kind = sp["kind"]
                if kind == "rope":
                    c = self.nxt(cs)
                    if "nodma" in os.environ.get("ROPEVAR", ""):
                        self.memset("pool", c.t, 1.0, [c.r])
                    else:
                        self.dma("sp", c.t[:, 0, :], sp["cos"].t[:, g0:g0 + GS], [], [c.r])
                        self.dma("sp", c.t[:, 1, :], sp["sin"].t[:, g0:g0 + GS], [], [c.r])
                rope_pend = []
                for ci in sp.get("order", range(sp["n"])):
                    pbk = self.bank()
                    col = sp["col0"] + ci * 128
                    for kc in range(8):
                        self.mm(pbk.t[:, 0:GS + 4], w.t[:, kc, col:col + 128], h.t[:, kc, 14:GS + 18], kc == 0, kc == 7,
                                [w.r, h.r], [pbk.r])
                    if kind == "conv":
                        cw = sp["cw"]
                        k = sp["cwi0"] + ci
                        t1 = self.nxt(tmpc)
                        ob = self.nxt(obf)
                        self.act(t1.t, pbk.t[:, 2:GS + 2], AF.Identity, [pbk.r, cw.r], [t1.r],
                                 bias=cw.t[:, k, 3:4], scale=cw.t[:, k, 1:2])
                        self.stt("dve", t1.t, pbk.t[:, 1:GS + 1], cw.t[:, k, 0:1], t1.t, ALU.mult, ALU.add,
                                 [pbk.r, cw.r, t1.r], [t1.r])
                        if "emit" in sp:
                            t3 = self.nxt(tmpd)
                            self.stt("dve", t3.t, pbk.t[:, 3:GS + 3], cw.t[:, k, 2:3], t1.t, ALU.mult, ALU.add,
                                     [pbk.r, cw.r, t1.r], [t3.r])
                            sp["emit"](ci, t3, g0)
                        else:
                            self.stt("dve", ob.t, pbk.t[:, 3:GS + 3], cw.t[:, k, 2:3], t1.t, ALU.mult, ALU.add,
                                     [pbk.r, cw.r, t1.r], [ob.r])
                            r0 = sp["row0"] + ci * 128
                            self.dma("sp", sp["out"].t[r0:r0 + 128, g0:g0 + GS], ob.t, [ob.r], [sp["out"].r])
                    elif kind == "rope":
                        kb = self.nxt(obf)
                        self.cp("act", kb.t, pbk.t[:, 2:GS + 2], [pbk.r], [kb.r])

                        def rope_tail(pbk=pbk, kb=kb, ci=ci, c=c, sp=sp, to=sp["toff"] + g0):
                            ob = self.nxt(obf)
                            t1 = self.nxt(tmpc)
                            t2 = self.nxt(tmpd)
                            p2 = self.pb[6]
                            self.mm(p2.t[:, 0:GS], self.rperm.t, kb.t, True, True, [self.rperm.r, kb.r], [p2.r])
                            self.tt("dve", t1.t, pbk.t[:, 2:GS + 2], c.t[:, 0, :], ALU.mult, [pbk.r, c.r, kb.r], [t1.r])
                            self.tt("dve", t2.t, p2.t[:, 0:GS], c.t[:, 1, :], ALU.mult, [p2.r, c.r], [t2.r])
                            self.tt("pool", ob.t, t1.t, t2.t, ALU.add, [t1.r, t2.r], [ob.r])
                            self.dma("sp", sp["out"].t[ci, :, to:to + GS], ob.t, [ob.r], [sp["out"].r])

                        rope_pend.append(rope_tail)
                        if len(rope_pend) > 1:
                            rope_pend.pop(0)()
                    else:
                        ob = self.nxt(obf)
                        self.cp("act", ob.t, pbk.t[:, 2:GS + 2], [pbk.r], [ob.r])
                        to = sp["toff"] + g0
                        self.dma("sp", sp["out"].t[ci, :, to:to + GS], ob.t, [ob.r], [sp["out"].r])
                while rope_pend:
                    rope_pend.pop(0)()
            for sp in tm:
                for j in range(GS // 128):
                    pbk = self.bank()
                    for kc in range(8):
                        self.mm(pbk.t[:, 0:512], h.t[:, kc, 16 + j * 128:16 + (j + 1) * 128],
                                w.t[:, kc, sp["col0"]:sp["col0"] + 512], kc == 0, kc == 7, [w.r, h.r], [pbk.r])
                    st = self.nxt(tst)
                    self.cp("act", st.t, pbk.t[:, 0:512], [pbk.r], [st.r])
                    ro = sp["roff"] + g0 + j * 128
                    self.dma("sp", sp["out"].t[ro:ro + 128, :], st.t, [st.r], [sp["out"].r])
            if post is not None:
                post(gg, g0, h)
            if gg + 2 < ng:
                trn(gg + 2)

    def phase_inproj(self):
        dbg = self.dbg
        self.x_full = self.inp("x_full", [L, D])
        self.x_ext = self.inp("x_ext", [EXT, D])
        ctx = self.inp("ctx", [256, D])
        w_in = self.inp("w_in", [D, 3072])
        cwin = self.inp("hy_cw", [128, 12, 4])
        gmix = self.inp("norm_mix_g", [2, D])
        cosk = self.inp("ropek_cos", [128, L])
        sink = self.inp("ropek_sin", [128, L])
        cosq = self.inp("ropeq_cos", [128, EXT])
        sinq = self.inp("ropeq_sin", [128, EXT])
        rp = self.inp("c_rperm", [128, 128], BF16)
        self.VX1 = self.scratch("VX1", [1024, L], BF16, debug=dbg)
        self.X2 = self.scratch("X2", [512, EXT], BF16, debug=dbg)
        self.KT = self.scratch("KT", [4, 128, L + 256], BF16, debug=dbg)
        self.QT = self.scratch("QT", [4, 128, EXT], BF16, debug=dbg)
        self.VT = self.scratch("VT", [L + 256, 512], BF16, debug=dbg)
        self.top = self.cmark
        w = self.alloc("w_in", [128, 8, 3072], BF16)
        self.load_w_bf16(w, w_in.t, 8)
        cw = self.alloc("cw", [128, 12, 4], F32)
        self.dma("sp", cw.t, cwin.t, [], [cw.r])
        self.rperm = self.alloc("rperm", [128, 128], BF16)
        self.dma("sp", self.rperm.t, rp.t, [], [self.rperm.r])
        nb = self.norm_bufs()
        mark = self.top
        A, SH = self.mod_tiles(0, 1, 0, 1, gmix.t[0])
        import os
        self.proj_pass(ctx, 256, A, SH, w,
                       [dict(kind="rope", col0=2048, n=4, cos=cosk, sin=sink, out=self.KT, toff=0)] if os.environ.get("CTXROPE") else
                       [dict(kind="plain", col0=2048, n=4, out=self.KT, toff=0)],
                       [dict(col0=2560, out=self.VT, roff=0)], nb)
        self.P.barrier()
        self.top = mark
        if self.stop_after == "ctx":
            return
        A, SH = self.mod_tiles(0, 0, 0, 1, gmix.t[0])
        mark2 = self.top
        self.proj_pass(self.x_full, L, A, SH, w,
                       [dict(kind="conv", col0=0, n=8, cw=cw, cwi0=0, out=self.VX1, row0=0),
                        dict(kind="rope", col0=2048, n=4, cos=cosk, sin=sink, out=self.KT, toff=256)],
                       [dict(col0=2560, out=self.VT, roff=256)], nb)
        self.P.barrier()
        self.top = mark2
        self.proj_pass(self.x_ext, EXT, A, SH, w,
                       [dict(kind="conv", col0=1024, n=4, cw=cw, cwi0=8, out=self.X2, row0=0),
                        dict(kind="rope", col0=1536, n=4, cos=cosq, sin=sinq, out=self.QT, toff=0)],
                       [], nb)


    def phase_attn(self):
        dal = self.inp("da_lambda", [4, 64])
        subg = self.inp("subln_col", [128, 1])
        self.YT = self.scratch("YT", [1024, EXT], BF16, debug=self.dbg)
        self.top = self.cmark
        LAM_INIT = 0.8 - 0.6 * math.exp(0.0)
        lt = self.alloc("lt", [128, 256], F32)
        pr = self.alloc("lpr", [128, 128], F32)
        ls = self.alloc("ls", [128, 4], F32)
        negl = self.alloc("negl", [128, 1], F32)
        gsub = self.alloc("gsub", [128, 1], F32)
        self.dma("sp", lt.t, dal.t.rearrange("a b -> (a b)").partition_broadcast(128), [], [lt.r])
        self.tt("dve", pr.t[:, 0:64], lt.t[:, 0:64], lt.t[:, 64:128], ALU.mult, [lt.r], [pr.r])
        self.tt("dve", pr.t[:, 64:128], lt.t[:, 128:192], lt.t[:, 192:256], ALU.mult, [lt.r, pr.r], [pr.r])
        self.act(lt.t[:, 0:64], pr.t[:, 0:64], AF.Identity, [pr.r, lt.r], [lt.r, ls.r], accum=ls.t[:, 0:1])
        self.act(lt.t[:, 64:128], pr.t[:, 64:128], AF.Identity, [pr.r, lt.r, ls.r], [lt.r, ls.r], accum=ls.t[:, 1:2])
        self.act(ls.t[:, 2:4], ls.t[:, 0:2], AF.Exp, [ls.r], [ls.r])
        self.tt("dve", negl.t, ls.t[:, 3:4], ls.t[:, 2:3], ALU.subtract, [ls.r], [negl.r])
        self.ts("dve", negl.t, negl.t, -LAM_INIT, None, ALU.add, None, [negl.r], [negl.r])
        self.dma("sp", gsub.t, subg.t, [], [gsub.r])
        self.ts("dve", gsub.t, gsub.t, 1.0 - LAM_INIT, None, ALU.mult, None, [gsub.r], [gsub.r])
        NK = (L + 256) // 128
        Kh = self.alloc("Kh", [128, L + 256], BF16)
        Vh = self.alloc("Vh", [128, NK, 128], BF16)
        Qh = self.alloc("Qh", [128, EXT], BF16)
        Eb = [self.alloc("Eb%d" % i, [128, 2, 512], BF16) for i in range(2)]
        f = {n_: self.alloc("at_" + n_, [128, 512], F32) for n_ in ("r0", "r1", "t0", "t1", "o", "rs", "y")}
        osq = self.alloc("at_osq", [128, 512], BF16)
        acc0 = self.alloc("at_acc0", [128, 512], F32)
        ones32 = self.alloc("at_ones32", [128, 128], F32)
        self.memset("pool", ones32.t, 1.0, [ones32.r])
        yb = [self.alloc("at_yb%d" % i, [128, 512], BF16) for i in range(2)]
        pb = self.pb
        itc = [0]
        epi_pend = []
        for h in range(4):
            self.dma("sp", Kh.t, self.KT.t[h], [self.KT.r], [Kh.r])
            self.dma("sp", Vh.t, self.VT.t[:, h * 128:(h + 1) * 128].rearrange("(kt p) d -> p kt d", p=128), [self.VT.r], [Vh.r])
            self.dma("sp", Qh.t, self.QT.t[h], [self.QT.r], [Qh.r])
            groups = [(q0_, min(512, EXT - q0_)) for q0_ in range(0, EXT, 512)]

            def emit_qk(kt, q0, n):
                par = kt % 2
                for m in range(2):
                    sb_ = pb[2 * par + m]
                    self.mm(sb_.t[:, :n], Kh.t[64 * m:64 * m + 64, kt * 128:(kt + 1) * 128],
                            Qh.t[64 * m:64 * m + 64, q0:q0 + n], True, True, [Kh.r, Qh.r], [sb_.r])

            for gi_, (q0, n) in enumerate(groups):
                if gi_ == 0:
                    emit_qk(0, q0, n)
                for kt in range(NK):
                    par = kt % 2
                    if kt + 1 < NK:
                        emit_qk(kt + 1, q0, n)
                    E = Eb[par]
                    sv = self.ps[:, 1024 * par:1024 * par + 1024].rearrange("p (a b) -> p a b", a=2)[:, :, :n]
                    self.act(E.t[:, :, :n], sv, AF.Exp, [pb[2 * par].r, pb[2 * par + 1].r], [E.r], scale=0.125)
                    for m in range(2):
                        self.mm(pb[4 + m].t[:, :n], Vh.t[:, kt, :], E.t[:, m, :n], kt == 0, kt == NK - 1, [Vh.r, E.r], [pb[4 + m].r])
                    self.mm(pb[7].t[:, :n], self.ones.t, E.t[:, 1, :n], kt == 0, kt == NK - 1, [self.ones.r, E.r], [pb[7].r])
                    if kt == 2 and epi_pend:
                        epi_pend.pop(0)()
                    if kt == 0:
                        self.cp("dve", acc0.t[:, :n], E.t[:, 0, :n], [E.r], [acc0.r])
                    else:
                        self.tt("dve", acc0.t[:, :n], acc0.t[:, :n], E.t[:, 0, :n], ALU.add, [acc0.r, E.r], [acc0.r])
                self.mm(pb[6].t[:, :n], ones32.t, acc0.t[:, :n], True, True, [ones32.r, acc0.r], [pb[6].r])
                if gi_ + 1 < len(groups):
                    emit_qk(0, *groups[gi_ + 1])
                self.recip(f["r0"].t[:, :n], pb[6].t[:, :n], [pb[6].r], [f["r0"].r])
                self.recip(f["r1"].t[:, :n], pb[7].t[:, :n], [pb[7].r], [f["r1"].r])
                self.tt("dve", f["t0"].t[:, :n], pb[4].t[:, :n], f["r0"].t[:, :n], ALU.mult, [pb[4].r, f["r0"].r], [f["t0"].r])
                self.tt("dve", f["t1"].t[:, :n], pb[5].t[:, :n], f["r1"].t[:, :n], ALU.mult, [pb[5].r, f["r1"].r], [f["t1"].r])
                def epi_b(n=n, q0=q0, h=h):
                    self.stt("dve", f["o"].t[:, :n], f["t1"].t[:, :n], negl.t[:, 0:1], f["t0"].t[:, :n], ALU.mult, ALU.add,
                             [f["t1"].r, f["t0"].r, negl.r], [f["o"].r])
                    self.act(osq.t[:, :n], f["o"].t[:, :n], AF.Square, [f["o"].r], [osq.r])
                    self.mm(pb[6].t[:, :n], self.ones.t, osq.t[:, :n], True, True, [self.ones.r, osq.r], [pb[6].r])
                    self.act(f["rs"].t[:, :n], pb[6].t[:, :n], AF.Sqrt, [pb[6].r, self.epsb.r], [f["rs"].r],
                             bias=self.epsb.t[:, 0:1], scale=1.0 / 128)
                    self.recip(f["rs"].t[:, :n], f["rs"].t[:, :n], [f["rs"].r], [f["rs"].r])
                    self.tt("dve", f["y"].t[:, :n], f["o"].t[:, :n], f["rs"].t[:, :n], ALU.mult, [f["o"].r, f["rs"].r], [f["y"].r])
                    y2 = yb[itc[0] % 2]
                    itc[0] += 1
                    self.ts("dve", y2.t[:, :n], f["y"].t[:, :n], gsub.t[:, 0:1], None, ALU.mult, None, [f["y"].r, gsub.r], [y2.r])
                    r0 = 512 + h * 128
                    self.dma("sp", self.YT.t[r0:r0 + 128, q0:q0 + n], y2.t[:, :n], [y2.r], [self.YT.r])
                epi_pend.append(epi_b)
        while epi_pend:
            epi_pend.pop(0)()

    def load_w_gated(self, dst, src_ap, kcn, gate_ap):
        mark = self.top
        G = self.alloc("gateG", [128, 1024], F32)
        self.dma("sp", G.t, gate_ap.partition_broadcast(128), [self.modv.r], [G.r])
        stg = self.pool_of("wstg", 2, [128, 1024], F32)
        for kc in range(kcn):
            st = self.nxt(stg)
            self.dma("sp", st.t, src_ap[kc * 128:(kc + 1) * 128, :], [], [st.r])
            self.tt("dve", dst.t[:, kc, :], st.t, G.t, ALU.mult, [st.r, G.r], [dst.r])
        self.P.barrier()
        self.top = mark

    def resid_store(self, xin, xout, actT, kcn, Wd, g0, final_g=None):
        for j in range(GS // 128):
            xr = self.nxt(self.rx)
            r0 = g0 + j * 128
            self.dma("sp", xr.t, xin.t[r0:r0 + 128, :], [xin.r], [xr.r])
            xo = self.nxt(self.ro)
            for half in range(2):
                pbk = self.bank()
                for kc in range(kcn):
                    self.mm(pbk.t[:, 0:512], actT.t[:, kc, j * 128:(j + 1) * 128], Wd.t[:, kc, half * 512:(half + 1) * 512],
                            kc == 0, kc == kcn - 1, [actT.r, Wd.r], [pbk.r])
                self.tt("dve", xo.t[:, half * 512:(half + 1) * 512], pbk.t[:, 0:512], xr.t[:, half * 512:(half + 1) * 512],
                        ALU.add, [pbk.r, xr.r], [xo.r])
            if final_g is not None:
                st = self.nxt(self.fst)
                self.act(self.fjunk.t, xo.t, AF.Square, [xo.r], [self.fjunk.r, st.r], accum=st.t[:, 0:1])
                self.act(st.t[:, 1:2], st.t[:, 0:1], AF.Sqrt, [st.r, self.epsb.r], [st.r], bias=self.epsb.t[:, 0:1], scale=1.0 / D)
                self.recip(st.t[:, 1:2], st.t[:, 1:2], [st.r], [st.r])
                self.stt("dve", xo.t, xo.t, st.t[:, 1:2], final_g.t, ALU.mult, ALU.mult, [xo.r, st.r, final_g.r], [xo.r])
            self.dma("sp", xout.t[r0:r0 + 128, :], xo.t, [xo.r], [xout.r])

    def resid_bufs(self):
        self.rx = self.pool_of("rx", 2, [128, 1024], F32)
        self.ro = self.pool_of("ro", 2, [128, 1024], F32)

    def phase_outproj(self, name, yT_dram, xin, w_ap, layer):
        xout = self.scratch(name, [EXT, D], F32, debug=self.dbg)
        self.top = self.cmark
        Wo = self.alloc("Wo", [128, 8, 1024], BF16)
        self.load_w_gated(Wo, w_ap, 8, self.modv.t[layer, 0, 2 * D:3 * D])
        self.resid_bufs()
        yb = self.pool_of("yTb", 2, [128, 8, GS], BF16)
        for g in range(EXT // GS):
            g0 = g * GS
            y = self.nxt(yb)
            self.dma("sp", y.t, yT_dram.t[:, g0:g0 + GS].rearrange("(kc p) t -> p kc t", p=128), [yT_dram.r], [y.r])
            self.resid_store(xin, xout, y, 8, Wo, g0)
        return xout

    def phase_ffn(self, name, xin, layer, final=False):
        wup = self.inp("ffn_w_up%d" % layer, [D, 2 * DFF])
        wdn = self.inp("ffn_w_down%d" % layer, [DFF, D])
        cwin = self.inp("ffn_cw%d" % layer, [128, 44, 4])
        gffn = self.inp("norm_ffn_g", [2, D]) if "norm_ffn_g" not in self.ins else self.ins["norm_ffn_g"]
        xout = self.outp("out", [EXT, D], F32) if final else self.scratch(name, [EXT, D], F32, debug=self.dbg)
        self.top = self.cmark
        Wu = self.alloc("Wu", [128, 8, 2 * DFF], BF16)
        self.load_w_bf16(Wu, wup.t, 8)
        Wd = self.alloc("Wd", [128, 22, 1024], BF16)
        self.load_w_gated(Wd, wdn.t, 22, self.modv.t[layer, 0, 5 * D:6 * D])
        cw = self.alloc("cwf", [128, 44, 4], F32)
        self.dma("sp", cw.t, cwin.t, [], [cw.r])
        fg = None
        if final:
            fgi = self.inp("final_norm_g", [D])
            fg = self.alloc("fg", [128, 1024], F32)
            self.dma("sp", fg.t, fgi.t.partition_broadcast(128), [], [fg.r])
            self.fst = self.pool_of("fst", 2, [128, 2], F32)
            self.fjunk = self.alloc("fjunk", [128, 1024], BF16)
        A, SH = self.mod_tiles(layer, 0, 3, 4, gffn.t[layer])
        nb = self.norm_bufs_small()
        self.resid_bufs_small()
        gT = self.alloc("gT", [128, 22, GS], BF16)
        sil = self.pool_of("sil", 2, [128, GS], F32)
        hold = {}

        pend = []

        def flush(keep):
            while len(pend) > keep:
                j_, gb, ub = pend.pop(0)
                sb_ = self.nxt(sil)
                self.act(sb_.t, gb.t, AF.Silu, [gb.r], [sb_.r])
                self.tt("pool", gT.t[:, j_, :], sb_.t, ub.t, ALU.mult, [sb_.r, ub.r], [gT.r])

        def emit(ci, buf, g0):
            if ci < 22:
                hold["g"] = buf
            else:
                pend.append((ci - 22, hold["g"], buf))
                flush(1)

        def post(gg, g0, h):
            flush(0)
            self.resid_store(xin, xout, gT, 22, Wd, g0, final_g=fg)

        order = []
        for j in range(22):
            order += [j, 22 + j]
        self.proj_pass(xin, EXT, A, SH, Wu,
                       [dict(kind="conv", col0=0, n=44, cw=cw, cwi0=0, order=order, emit=emit)], [], nb, post=post)
        return xout

    def norm_bufs_small(self):
        nb = {}
        nb["x"] = self.pool_of("nx", 1, [128, 1024], F32)
        nb["tmp"] = self.alloc("ntmp", [128, 1024], F32)
        nb["xn"] = self.pool_of("nxn", 2, [128, 1024], BF16)
        nb["junk"] = self.alloc("njunk", [128, 1024], BF16)
        nb["stat"] = self.pool_of("nstat", 2, [128, 2], F32)
        return nb

    def resid_bufs_small(self):
        self.rx = self.pool_of("rx", 1, [128, 1024], F32)
        self.ro = self.pool_of("ro", 1, [128, 1024], F32)


    def phase_sgu(self, name, xin, layer=1):
        win = self.inp("sgu_w_in", [D, 2048])
        bcol_i = self.inp("sgu_bu_col", [128, 8])
        bv_i = self.inp("sgu_bv", [1024])
        lng_i = self.inp("sgu_ln_g", [1024])
        lnb_i = self.inp("sgu_ln_b", [1024])
        wsT_i = self.inp("sgu_wsT", [128, 8, 128])
        bs_i = self.inp("sgu_b_s", [1, 1024])
        wout = self.inp("sgu_w_out", [D, D])
        gmix = self.ins["norm_mix_g"]
        xout = self.scratch(name, [EXT, D], F32, debug=self.dbg)
        self.top = self.cmark
        Wi = self.alloc("Wi", [128, 8, 2048], BF16)
        self.load_w_bf16(Wi, win.t, 8)
        Wo = self.alloc("Wo", [128, 8, 1024], BF16)
        self.load_w_gated(Wo, wout.t, 8, self.modv.t[layer, 0, 2 * D:3 * D])
        wsT = self.alloc("wsT", [128, 8, 128], BF16)
        self.dma("pool", wsT.t, wsT_i.t, [], [wsT.r])
        bsr = self.alloc("bsr", [1, 1024], BF16)
        self.dma("pool", bsr.t, bs_i.t, [], [bsr.r])
        bcol = self.alloc("bcol", [128, 8], F32)
        self.dma("sp", bcol.t, bcol_i.t, [], [bcol.r])
        BV = self.alloc("BV", [128, 1024], F32)
        LNG = self.alloc("LNG", [128, 1024], F32)
        LNB = self.alloc("LNB", [128, 1024], F32)
        self.dma("sp", BV.t, bv_i.t.partition_broadcast(128), [], [BV.r])
        self.dma("sp", LNG.t, lng_i.t.partition_broadcast(128), [], [LNG.r])
        self.dma("sp", LNB.t, lnb_i.t.partition_broadcast(128), [], [LNB.r])
        A, SH = self.mod_tiles(layer, 0, 0, 1, gmix.t[layer])
        nb = self.norm_bufs()
        self.resid_bufs()
        hTb = self.pool_of("sg_hT", 2, [128, 8, GS], BF16)
        uT = self.alloc("sg_uT", [128, 8, GS], F32)
        guT = self.pool_of("sg_guT", 2, [128, 8, GS], BF16)
        vtp = self.pool_of("sg_vt", 2, [128, 1024], F32)
        vgp = self.pool_of("sg_vg", 2, [128, 1024], F32)
        vb = self.pool_of("sg_vb", 2, [128, 1024], BF16)
        st = self.pool_of("sg_st", 2, [128, 8], F32)
        NT = GS // 128
        ngr = EXT // GS
        pend = {}

        def s_pre(g):
            xs = []
            for j in range(NT):
                xt = self.nxt(nb["x"])
                self.dma("sp", xt.t, xin.t[g * GS + j * 128:g * GS + (j + 1) * 128, :], [xin.r], [xt.r])
                xn = self.nxt(nb["xn"])
                self.norm_pre(xt, A, SH, nb["tmp"], xn, nb["junk"], self.nxt(nb["stat"]))
                xs.append(xn)
            pend[g] = (self.nxt(hTb), xs)

        def s_tr(g):
            h_, xs = pend[g]
            for j, xn in enumerate(xs):
                self.norm_tr(xn, h_, j * 128)

        s_pre(0)
        s_tr(0)
        for g in range(ngr):
            g0 = g * GS
            hT = pend.pop(g)[0]
            gu = self.nxt(guT)
            ubanks = []
            for c in range(8):
                pbk = self.bank()
                for kc in range(8):
                    self.mm(pbk.t[:, 0:GS], Wi.t[:, kc, c * 128:(c + 1) * 128], hT.t[:, kc, :], kc == 0, kc == 7, [Wi.r, hT.r], [pbk.r])
                self.act(uT.t[:, c, :], pbk.t[:, 0:GS], AF.Gelu, [pbk.r, bcol.r], [uT.r], bias=bcol.t[:, c:c + 1])
            tl = []
            for j in range(NT):
                vt, vg, s_, v2 = self.nxt(vtp), self.nxt(vgp), self.nxt(st), self.nxt(vb)
                for half in range(2):
                    pbk = self.bank()
                    for kc in range(8):
                        self.mm(pbk.t[:, 0:512], hT.t[:, kc, j * 128:(j + 1) * 128],
                                Wi.t[:, kc, 1024 + half * 512:1024 + (half + 1) * 512], kc == 0, kc == 7, [Wi.r, hT.r], [pbk.r])
                    self.tt("dve", vt.t[:, half * 512:(half + 1) * 512], pbk.t[:, 0:512], BV.t[:, half * 512:(half + 1) * 512],
                            ALU.add, [pbk.r, BV.r], [vt.r])
                tl.append((vt, vg, s_, v2))
            if g + 1 < ngr:
                s_pre(g + 1)
            for vt, vg, s_, v2 in tl:
                self.act(vg.t, vt.t, AF.Gelu, [vt.r], [vg.r, s_.r], accum=s_.t[:, 0:1])
            for vt, vg, s_, v2 in tl:
                self.act(vt.t, vg.t, AF.Square, [vg.r, s_.r], [vt.r, s_.r], accum=s_.t[:, 1:2])
            for vt, vg, s_, v2 in tl:
                self.ts("dve", s_.t[:, 2:3], s_.t[:, 0:1], 1.0 / 1024, None, ALU.mult, None, [s_.r], [s_.r])
                self.tt("dve", s_.t[:, 3:4], s_.t[:, 2:3], s_.t[:, 2:3], ALU.mult, [s_.r], [s_.r])
                self.stt("dve", s_.t[:, 4:5], s_.t[:, 1:2], 1.0 / 1024, s_.t[:, 3:4], ALU.mult, ALU.subtract, [s_.r], [s_.r])
            for vt, vg, s_, v2 in tl:
                self.act(s_.t[:, 5:6], s_.t[:, 4:5], AF.Sqrt, [s_.r, self.epsb.r], [s_.r], bias=self.epsb.t[:, 0:1])
            for vt, vg, s_, v2 in tl:
                self.recip(s_.t[:, 5:6], s_.t[:, 5:6], [s_.r], [s_.r])
                self.ts("dve", vg.t, vg.t, s_.t[:, 2:3], s_.t[:, 5:6], ALU.subtract, ALU.mult, [vg.r, s_.r], [vg.r])
            for vt, vg, s_, v2 in tl:
                self.tt("pool", vg.t, vg.t, LNG.t, ALU.mult, [vg.r, LNG.r], [vg.r])
            for vt, vg, s_, v2 in tl:
                self.tt("dve", v2.t, vg.t, LNB.t, ALU.add, [vg.r, LNB.r], [v2.r])
            if g + 1 < ngr:
                s_tr(g + 1)
            for j, (vt, vg, s_, v2) in enumerate(tl):
                for a in range(2):
                    pbk = self.bank()
                    for q in range(4):
                        gi = 4 * a + q
                        self.P.op("pe", (lambda o_, l_, r_: (lambda h_: h_.matmul(o_, l_, r_, start=True, stop=False)))(
                            pbk.t[:, q * 128:(q + 1) * 128], v2.t[:, gi * 128:(gi + 1) * 128], wsT.t[:, gi, :]),
                            reads=[v2.r, wsT.r], writes=[pbk.r], signal=False)
                        self.P.op("pe", (lambda o_, l_, r_: (lambda h_: h_.matmul(o_, l_, r_, start=False, stop=True)))(
                            pbk.t[:, q * 128:(q + 1) * 128], self.ones.t[0:1, :], bsr.t[0:1, gi * 128:(gi + 1) * 128]),
                            reads=[self.ones.r, bsr.r], writes=[pbk.r], signal=(q == 3))
                    self.tt("dve", gu.t[:, 4 * a:4 * a + 4, j * 128:(j + 1) * 128],
                            pbk.t[:, 0:512].rearrange("p (a b) -> p a b", a=4), uT.t[:, 4 * a:4 * a + 4, j * 128:(j + 1) * 128],
                            ALU.mult, [pbk.r, uT.r], [gu.r])
            self.resid_store(xin, xout, gu, 8, Wo, g0)
        return xout


    def fft_stage1(self, X, Ad, f1):
        stg = self.pool_of("s1stg", 2, [128, 4, 512], BF16)
        for s_ in range(64):
            st = self.nxt(stg)
            for q in range(4):
                pbk = self.pb[(4 * (s_ % 2)) + q]
                mrows = 128 if q < 2 else 4
                self.mm(pbk.t[0:mrows, 0:512], f1.t[:, q * 128:q * 128 + mrows], X.t[:, s_, :], True, True, [f1.r, X.r], [pbk.r])
                self.cp("act" if q % 2 == 0 else "dve", st.t[0:mrows, q, :], pbk.t[0:mrows, 0:512], [pbk.r], [st.r])
            self.dma("sp", Ad.t[0:2, s_, :, :].rearrange("q p c -> p q c"), st.t[:, 0:2, :], [st.r], [Ad.r])
            self.dma("sp", Ad.t[2:4, s_, 0:4, :].rearrange("q p c -> p q c"), st.t[0:4, 2:4, :], [st.r], [Ad.r])

    def load_B(self, B, Ad, k1g, G):
        kc, p0 = k1g // 128, k1g % 128
        for ri in range(2):
            self.dma("sp", B.t[ri * 64:(ri + 1) * 64, :, :], Ad.t[2 * kc + ri, :, p0:p0 + G, :], [Ad.r], [B.r])

    def load_tab(self, Tb, tab, k1g, G):
        self.dma("sp", Tb.t, tab.t[k1g // G].rearrange("p (k m) -> p k m", k=G), [], [Tb.r])

    def phase_hyena(self):
        G = 4
        zT = self.inp("hy_zT", [33, L])
        w1i = self.inp("hy_w1", [33, 64])
        w2i = self.inp("hy_w2", [64, 64])
        prmi = self.inp("hy_prm", [64, 4])
        w3i = self.inp("hy_w3", [64, 2048])
        dsk = self.inp("hy_bias", [2, 512])
        win = self.inp("hy_win", [128, 64, 512])
        f1i = self.inp("t_F1", [128, 512], BF16)
        Htab = self.inp("t_H", [64, 128, 512], BF16)
        Hctab = self.inp("t_Hc", [64, 128, 512], BF16)
        M1tab = self.inp("t_M1", [64, 128, 512], BF16)
        M2tab = self.inp("t_M2", [64, 128, 512], BF16)
        fvi = self.inp("t_Finv", [128, 512], BF16)
        fei = self.inp("t_FinvE", [128, 4, 68], BF16)
        A0 = self.scratch("fftA0", [4, 64, 128, 512], BF16)
        A1 = self.scratch("fftA1", [4, 64, 128, 512], BF16)
        Cd = self.scratch("fftC", [128, 256, 512], BF16)
        Hf = self.scratch("fftHf", [2, 2, 64, 256, 512], BF16)
        X1s = self.scratch("X1s", [128, 64, 512], BF16)
        self.top = self.cmark
        f1 = self.alloc("f1", [128, 512], BF16)
        fv = self.alloc("fv", [128, 512], BF16)
        fe = self.alloc("fe", [128, 4, 68], BF16)
        self.dma("sp", f1.t, f1i.t, [], [f1.r])
        self.dma("sp", fv.t, fvi.t, [], [fv.r])
        self.dma("sp", fe.t, fei.t, [], [fe.r])
        base = self.top
        h2T = self.alloc("h2T", [128, L], BF16)
        w3 = self.alloc("w3sb", [64, 2048], BF16)
        self.dma("pool", w3.t, w3i.t, [], [w3.r])
        mlp_mark = self.top
        w1 = self.alloc("w1sb", [33, 64], F32)
        w2 = self.alloc("w2sb", [64, 64], F32)
        prm = self.alloc("prm", [64, 4], F32)
        pr2 = self.alloc("pr2", [64, 4], F32)
        self.dma("sp", w1.t, w1i.t, [], [w1.r])
        self.dma("sp", w2.t, w2i.t, [], [w2.r])
        self.dma("sp", prm.t, prmi.t, [], [prm.r])
        for a in range(2):
            self.ts("dve", pr2.t[:, 2 * a:2 * a + 1], prm.t[:, 2 * a + 1:2 * a + 2], 1.0 / 3, None, ALU.mult, None, [prm.r, pr2.r], [pr2.r])
            self.tt("dve", pr2.t[:, 2 * a + 1:2 * a + 2], pr2.t[:, 2 * a:2 * a + 1], prm.t[:, 2 * a:2 * a + 1], ALU.mult, [prm.r, pr2.r], [pr2.r])
        ztb = self.pool_of("ztb", 2, [33, 512], F32)
        s3 = self.alloc("s3", [64, 512], F32)
        qq = self.alloc("qq", [64, 512], F32)
        h1 = self.alloc("h1", [64, 512], F32)

        def sin3(dst, src_ps, a):
            self.act(s3.t, src_ps.t[0:64, 0:512], AF.Sin, [src_ps.r, pr2.r], [s3.r], bias=pr2.t[:, 2 * a + 1:2 * a + 2], scale=pr2.t[:, 2 * a:2 * a + 1])
            self.tt("dve", qq.t, s3.t, s3.t, ALU.mult, [s3.r], [qq.r])
            self.ts("dve", qq.t, qq.t, -4.0, 3.0, ALU.mult, ALU.add, [qq.r], [qq.r])
            self.tt("dve", dst[0], qq.t, s3.t, ALU.mult, [qq.r, s3.r], [dst[1]])

        for ch in range(L // 512):
            zt = self.nxt(ztb)
            self.dma("sp", zt.t, zT.t[:, ch * 512:(ch + 1) * 512], [], [zt.r])
            pa, pb2 = self.pb[ch % 2], self.pb[2 + ch % 2]
            self.mm(pa.t[0:64, 0:512], w1.t, zt.t, True, True, [w1.r, zt.r], [pa.r])
            sin3((h1.t, h1.r), pa, 0)
            self.mm(pb2.t[0:64, 0:512], w2.t, h1.t, True, True, [w2.r, h1.r], [pb2.r])
            sin3((h2T.t[0:64, ch * 512:(ch + 1) * 512], h2T.r), pb2, 1)
        self.P.barrier()
        self.top = mlp_mark
        h2v = h2T.t[0:64, :].rearrange("p (j s) -> p s j", s=64)
        Dz = self.alloc("Dz", [128, 512], F32)
        omark = self.top
        for o in range(2):
            self.P.barrier()
            self.top = omark
            self.memset("pool", Dz.t, 0.0, [Dz.r])
            self.dma("sp", Dz.t[0:64, :], dsk.t[o].partition_broadcast(64), [], [Dz.r])
            Xf = self.alloc("Xf", [128, 64, 512], BF16)
            Xb = self.alloc("Xb", [128, 64, 512], BF16)
            wt = self.pool_of("wint", 2, [128, 512], F32)
            for s_ in range(64):
                wn = self.nxt(wt)
                self.dma("sp", wn.t, win.t[:, s_, :], [], [wn.r])
                for dr, Xd in ((0, Xf), (1, Xb)):
                    pbk = self.bank()
                    c0 = (o * 2 + dr) * 512
                    self.mm(pbk.t[:, 0:512], h2v[:, s_, :], w3.t[:, c0:c0 + 512], True, True, [h2T.r, w3.r], [pbk.r])
                    self.tt("dve", Xd.t[:, s_, :], pbk.t[:, 0:512], wn.t, ALU.mult, [pbk.r, wn.r], [Xd.r])
            self.memset("pool", Xb.t[0:1, 0, :], 0.0, [Xb.r])
            self.fft_stage1(Xf, A0, f1)
            self.fft_stage1(Xb, A1, f1)
            self.P.barrier()
            self.top = omark
            Bf = self.pool_of("Bf", 2, [128, G, 512], BF16)
            Bb = self.pool_of("Bb", 2, [128, G, 512], BF16)
            Ht = self.pool_of("Ht", 2, [128, G, 128], BF16)
            Hct = self.pool_of("Hct", 2, [128, G, 128], BF16)
            Hst = self.pool_of("Hst", 2, [128, G, 512], BF16)
            Hfo = Hf.t[o].rearrange("r k a c -> (r k) a c")
            def f_loads(k1g):
                bf, bb, ht, hct = self.nxt(Bf), self.nxt(Bb), self.nxt(Ht), self.nxt(Hct)
                self.load_B(bf, A0, k1g, G)
                self.load_B(bb, A1, k1g, G)
                self.load_tab(ht, Htab, k1g, G)
                self.load_tab(hct, Hctab, k1g, G)
                return bf, bb, ht, hct

            nxt_l = f_loads(0)
            for k1g in range(0, 128 + G, G):
                bf, bb, ht, hct = nxt_l
                if k1g + G < 128 + G:
                    nxt_l = f_loads(k1g + G)
                hs = self.nxt(Hst)
                for g in range(G):
                    pbk = self.bank()
                    self.mm(pbk.t[:, 0:512], ht.t[:, g, :], bf.t[:, g, :], True, False, [ht.r, bf.r], [pbk.r])
                    self.mm(pbk.t[:, 0:512], hct.t[:, g, :], bb.t[:, g, :], False, True, [hct.r, bb.r], [pbk.r])
                    self.tt("dve", hs.t[:, g, :], pbk.t[:, 0:512], Dz.t, ALU.add, [pbk.r, Dz.r], [hs.r])
                self.dma("sp", Hfo[:, k1g:k1g + G, :], hs.t, [hs.r], [Hf.r])
        self.P.barrier()
        self.top = base
        X = self.alloc("Xc", [128, 64, 512], BF16)
        cmark2 = self.top
        Ub = self.pool_of("Ub", 2, [128, L], BF16)
        xst = self.pool_of("xst", 2, [128, 8, 128], BF16)
        nt = 0
        for part in range(2):
            for cc in range(4):
                U = self.nxt(Ub)
                r0 = part * 512 + cc * 128
                self.dma("sp", U.t, self.VX1.t[r0:r0 + 128, :], [self.VX1.r], [U.r])
                Uv = U.t.rearrange("p (j s) -> p s j", s=64)
                for s0 in range(0, 64, 8):
                    pT = self.pb[6 + nt % 2]
                    nt += 1
                    pTv = pT.t.bitcast(BF16)
                    for i in range(8):
                        self.tr(pTv[:, i * 128:(i + 1) * 128], Uv[:, s0 + i, :], self.ident.t, [U.r, self.ident.r], [pT.r], signal=(i == 7))
                    src = pTv.rearrange("p (a b) -> p a b", a=8)
                    if part == 0:
                        self.cp("act", X.t[:, s0:s0 + 8, cc * 128:(cc + 1) * 128], src, [pT.r], [X.r])
                    else:
                        st = self.nxt(xst)
                        self.cp("act", st.t, src, [pT.r], [st.r])
                        self.dma("sp", X1s.t[:, s0:s0 + 8, cc * 128:(cc + 1) * 128], st.t, [st.r], [X1s.r])
        for conv in range(2):
            self.P.barrier()
            self.top = cmark2
            if conv == 1:
                x2sb = self.alloc("x2sb", [128, 4, EXT], BF16)
                yaT = self.alloc("yaT", [128, 4, EXT], BF16)
                self.dma("sp", x2sb.t, self.X2.t.rearrange("(cc p) t -> p cc t", p=128), [self.X2.r], [x2sb.r])
            m2 = self.top
            self.fft_stage1(X, A0, f1)
            self.P.barrier()
            self.top = m2
            Bt = self.pool_of("Bt", 2, [128, G, 512], BF16)
            Ht = self.pool_of("cHt", 2, [128, G, 128], BF16)
            M1t = self.pool_of("cM1", 2, [128, G, 128], BF16)
            M2t = self.pool_of("cM2", 2, [128, G, 128], BF16)
            HH1 = self.pool_of("HH1", 2, [128, G, 512], BF16)
            HH2 = self.pool_of("HH2", 2, [128, G, 512], BF16)
            T1 = self.pool_of("T1", 2, [128, 512], BF16)
            T2 = self.pool_of("T2", 2, [128, 512], BF16)
            Cst = self.pool_of("Cst", 2, [128, G, 512], BF16)
            def c_loads(k1g):
                bt, ht, m1, m2_, h1_, h2_ = (self.nxt(Bt), self.nxt(Ht), self.nxt(M1t), self.nxt(M2t), self.nxt(HH1), self.nxt(HH2))
                self.load_B(bt, A0, k1g, G)
                self.load_tab(ht, Htab, k1g, G)
                self.load_tab(m1, M1tab, k1g, G)
                self.load_tab(m2_, M2tab, k1g, G)
                for half in range(2):
                    self.dma("sp", h1_.t[half * 64:(half + 1) * 64, :, :], Hf.t[conv, 0, :, k1g:k1g + G, :], [Hf.r], [h1_.r])
                    self.dma("sp", h2_.t[half * 64:(half + 1) * 64, :, :], Hf.t[conv, 1, :, k1g:k1g + G, :], [Hf.r], [h2_.r])
                return bt, ht, m1, m2_, h1_, h2_

            nxt_l = c_loads(0)
            for k1g in range(0, 128 + G, G):
                bt, ht, m1, m2_, h1_, h2_ = nxt_l
                if k1g + G < 128 + G:
                    nxt_l = c_loads(k1g + G)
                cs_ = self.nxt(Cst)
                for g in range(G):
                    pu = self.bank()
                    self.mm(pu.t[:, 0:512], ht.t[:, g, :], bt.t[:, g, :], True, True, [ht.r, bt.r], [pu.r])
                    t1, t2 = self.nxt(T1), self.nxt(T2)
                    self.tt("dve", t1.t, pu.t[:, 0:512], h1_.t[:, g, :], ALU.mult, [pu.r, h1_.r], [t1.r])
                    self.tt("dve", t2.t, pu.t[:, 0:512], h2_.t[:, g, :], ALU.mult, [pu.r, h2_.r], [t2.r])
                    pc = self.bank()
                    self.mm(pc.t[:, 0:512], m1.t[:, g, :], t1.t, True, False, [m1.r, t1.r], [pc.r])
                    self.mm(pc.t[:, 0:512], m2_.t[:, g, :], t2.t, False, True, [m2_.r, t2.r], [pc.r])
                    self.cp("act", cs_.t[:, g, :], pc.t[:, 0:512], [pc.r], [cs_.r])
                self.dma("sp", Cd.t[:, k1g:k1g + G, :], cs_.t, [cs_.r], [Cd.r])
            self.P.barrier()
            self.top = m2
            Dt = self.pool_of("Dt", 2, [128, 4, 512], BF16)
            x1t = self.pool_of("x1t", 2, [128, 512], BF16)
            for s_ in range(64):
                dt_ = self.nxt(Dt)
                for ri in range(2):
                    self.dma("sp", dt_.t[:, ri, :], Cd.t[ri * 64 + s_, 0:128, :], [Cd.r], [dt_.r])
                    self.dma("sp", dt_.t[0:4, 2 + ri, :], Cd.t[ri * 64 + s_, 128:132, :], [Cd.r], [dt_.r])
                if conv == 0:
                    xg = self.nxt(x1t)
                    self.dma("sp", xg.t, X1s.t[:, s_, :], [X1s.r], [xg.r])
                    pbk = self.bank()
                    for q in range(4):
                        kr = 128 if q < 2 else 4
                        self.mm(pbk.t[:, 0:512], fv.t[0:kr, q * 128:(q + 1) * 128], dt_.t[0:kr, q, :], q == 0, q == 3, [fv.r, dt_.r], [pbk.r])
                    self.tt("dve", X.t[:, s_, :], pbk.t[:, 0:512], xg.t, ALU.mult, [pbk.r, xg.r], [X.r])
                else:
                    pbk = self.bank()
                    for cc in range(4):
                        for q in range(4):
                            self.P.op("pe", (lambda o_, l_, r_, a_, b_: (lambda h_: h_.matmul(o_, l_, r_, start=a_, stop=b_)))(
                                pbk.t[:, cc * 68:(cc + 1) * 68], dt_.t[0:(128 if q < 2 else 4), q, cc * 128:(cc + 1) * 128],
                                fe.t[0:(128 if q < 2 else 4), q, :], q == 0, q == 3),
                                reads=[dt_.r, fe.r], writes=[pbk.r], signal=(cc == 3 and q == 3))
                    x2v = x2sb.t.rearrange("p c (j s) -> p c s j", s=64)[:, :, s_, :]
                    yav = yaT.t.rearrange("p c (j s) -> p c s j", s=64)[:, :, s_, :]
                    self.tt("dve", yav, pbk.t[:, 0:272].rearrange("p (c j) -> p c j", c=4), x2v, ALU.mult, [pbk.r, x2sb.r], [yaT.r])
            if conv == 1:
                self.dma("sp", self.YT.t[0:512, :].rearrange("(cc p) t -> p cc t", p=128), yaT.t, [yaT.r], [self.YT.r])


def build_program(dbg=False, stop_after=None):
    kb = KB(stop_after)
    kb.dbg = dbg
    kb._bk = 0
    kb.consts()
    kb.phase_mod()
    kb.P.barrier()
    if stop_after == "mod":
        kb.P.build()
        return kb
    kb.phase_inproj()
    kb.P.barrier()
    if stop_after == "inproj":
        kb.P.build()
        return kb
    kb.phase_attn()
    kb.P.barrier()
    if stop_after == "attn":
        kb.P.build()
        return kb
    if os.environ.get("FAKE_YA"):
        ya = kb.inp("dbg_yaT", [512, EXT], BF16)
        st = kb.alloc("yast", [128, EXT], BF16)
        for c in range(4):
            kb.dma("sp", st.t, ya.t[c * 128:(c + 1) * 128, :], [ya.r], [st.r])
            kb.dma("sp", kb.YT.t[c * 128:(c + 1) * 128, :], st.t, [st.r], [kb.YT.r])
        kb.P.barrier()
    else:
        kb.phase_hyena()
        kb.P.barrier()
    if stop_after == "hyena":
        kb.P.build()
        return kb
    wo = kb.inp("ab_w_out", [D, D])
    x1 = kb.phase_outproj("X1o", kb.YT, kb.x_ext, wo.t, 0)
    kb.P.barrier()
    x2 = kb.phase_ffn("X2o", x1, 0)
    kb.P.barrier()
    x3 = kb.phase_sgu("X3o", x2, 1)
    kb.P.barrier()
    kb.phase_ffn("X4o", x3, 1, final=True)
    kb.P.barrier()
    kb.P.build()
    return kb


def rope_tables(pos_row, pos_col):
    T = pos_row.shape[0]
    cos = np.zeros((128, T), np.float32)
    sin = np.zeros((128, T), np.float32)
    inv = (10000.0 ** (-np.arange(16, dtype=np.float32) / 16)).astype(np.float32)
    for f in range(128):
        d = f % 64
        pos = pos_row if d < 32 else pos_col
        dd = d % 32
        i = dd % 16
        ang = pos.astype(np.float32) * inv[i]
        cos[f] = np.cos(ang)
        sin[f] = -np.sin(ang) if dd < 16 else np.sin(ang)
    return cos, sin


def rperm_matrix():
    m = np.zeros((128, 128), np.float32)
    for f in range(128):
        dd = (f % 64) % 32
        partner = f + 16 if dd < 16 else f - 16
        m[partner, f] = 1.0
    return m.astype(ml_dtypes.bfloat16)


def fft_tables(lo):
    bf = ml_dtypes.bfloat16
    N = 16384
    j = np.arange(128, dtype=np.float64)[:, None]
    k1 = np.arange(256, dtype=np.float64)[None, :]
    th = 2 * np.pi * ((j * k1) % 256) / 256
    F1 = np.zeros((128, 4, 128))
    Fi = np.zeros((128, 4, 128))
    for kc in range(2):
        F1[:, 2 * kc, :] = np.cos(th[:, kc * 128:(kc + 1) * 128])
        F1[:, 2 * kc + 1, :] = -np.sin(th[:, kc * 128:(kc + 1) * 128])
        Fi[:, 2 * kc, :] = np.cos(th[:, kc * 128:(kc + 1) * 128]).T / N
        Fi[:, 2 * kc + 1, :] = -np.sin(th[:, kc * 128:(kc + 1) * 128]).T / N
    wgt = np.zeros((128, 4, 1))
    wgt[:, 0:2, 0] = 2.0
    wgt[0, 0:2, 0] = 1.0
    wgt[0, 2:4, 0] = 1.0
    Fi = Fi * wgt
    j0 = lo // 64
    FiE = Fi[:, :, j0:j0 + 68]
    s = np.arange(64, dtype=np.float64)
    k2 = np.arange(64, dtype=np.float64)
    kk = np.arange(256, dtype=np.float64)
    ang = 2 * np.pi * ((s[None, :, None] * kk[:, None, None]) / N + ((s[None, :, None] * k2[None, None, :]) % 64) / 64)
    Zr, Zi = np.cos(ang), -np.sin(ang)
    H = np.zeros((256, 128, 128))
    H[:, 0:64, 0:64] = Zr
    H[:, 64:128, 0:64] = -Zi
    H[:, 0:64, 64:128] = Zi
    H[:, 64:128, 64:128] = Zr
    Hc = H.copy()
    Hc[:, :, 64:128] *= -1
    ZrT, ZiT = Zr.transpose(0, 2, 1), Zi.transpose(0, 2, 1)
    M1 = np.zeros((256, 128, 128))
    M1[:, 0:64, 0:64] = ZrT
    M1[:, 64:128, 0:64] = ZiT
    M1[:, 0:64, 64:128] = -ZiT
    M1[:, 64:128, 64:128] = ZrT
    M2 = np.zeros((256, 128, 128))
    M2[:, 0:64, 0:64] = ZiT
    M2[:, 64:128, 0:64] = -ZrT
    M2[:, 0:64, 64:128] = ZrT
    M2[:, 64:128, 64:128] = ZiT
    c = lambda a: np.ascontiguousarray(a.astype(np.float32)).astype(bf)
    grp = lambda a: c(a.reshape(64, 4, 128, 128).transpose(0, 2, 1, 3).reshape(64, 128, 512))
    return {"t_F1": c(F1.reshape(128, 512)), "t_Finv": c(Fi.reshape(128, 512)), "t_FinvE": c(FiE),
            "t_H": grp(H), "t_Hc": grp(Hc), "t_M1": grp(M1), "t_M2": grp(M2)}


def hyena_consts():
    f32 = np.float32
    bands = 16
    pos = np.arange(L, dtype=f32)
    t = np.linspace(0.0, 1.0, L, dtype=f32)[:, None]
    ang = (f32(2.0 * math.pi / L) * pos[:, None] * np.linspace(1e-4, bands - 1, bands, dtype=f32)[None, :]).astype(f32)
    z = np.concatenate([t, np.cos(ang), -np.sin(ang)], axis=-1).astype(f32)
    deltas = np.abs(np.linspace(math.log(1e-2) / 1.5, math.log(1e-2) / 0.3, 512, dtype=f32))
    window = (np.exp(-t * deltas[None, :]) + f32(0.05)).astype(f32)
    win = np.ascontiguousarray(window.reshape(128, 64, 512))
    return np.ascontiguousarray(z.T), win


_TAB = {}


def host_inputs(inputs, cid):
    b, hf = cid // 2, cid % 2
    lo = 0 if hf == 0 else L - EXT
    f32 = np.float32
    m = {}
    m["c_ident"] = np.eye(128, dtype=f32).astype(ml_dtypes.bfloat16)
    cc = np.stack([inputs["c"][b], inputs["c_ctx"]], -1).astype(f32)
    m["ccol"] = np.ascontiguousarray(cc.reshape(8, 128, 2).transpose(1, 0, 2))
    m["mod_w"] = inputs["mod_w"]
    m["mod_b"] = inputs["mod_b"]
    m["x_full"] = np.ascontiguousarray(inputs["x"][b])
    m["x_ext"] = np.ascontiguousarray(inputs["x"][b, lo:lo + EXT])
    m["ctx"] = np.ascontiguousarray(inputs["ctx"][b])
    m["w_in"] = np.ascontiguousarray(inputs["ab_w_in"][0])
    cw = np.concatenate([inputs["hy_conv_w"][0], inputs["hy_conv_b"][0][None]], 0)
    m["hy_cw"] = np.ascontiguousarray(cw.reshape(4, 12, 128).transpose(2, 1, 0)).astype(f32)
    m["norm_mix_g"] = inputs["norm_mix_g"]
    t = np.arange(L)
    ck, sk = rope_tables(t // 64, t % 64)
    m["ropek_cos"], m["ropek_sin"] = ck, sk
    m["ropeq_cos"] = np.ascontiguousarray(ck[:, lo:lo + EXT])
    m["ropeq_sin"] = np.ascontiguousarray(sk[:, lo:lo + EXT])
    m["c_rperm"] = rperm_matrix()
    m["da_lambda"] = np.ascontiguousarray(inputs["da_lambda"][0]).astype(f32)
    m["ab_w_out"] = np.ascontiguousarray(inputs["ab_w_out"][0])
    m["norm_ffn_g"] = inputs["norm_ffn_g"]
    m["final_norm_g"] = inputs["final_norm_g"]
    for i in range(2):
        m["ffn_w_up%d" % i] = np.ascontiguousarray(inputs["ffn_w_up"][i])
        m["ffn_w_down%d" % i] = np.ascontiguousarray(inputs["ffn_w_down"][i])
        fcw = np.concatenate([inputs["ffn_conv_w"][i], inputs["ffn_conv_b"][i][None]], 0)
        m["ffn_cw%d" % i] = np.ascontiguousarray(fcw.reshape(4, 44, 128).transpose(2, 1, 0)).astype(f32)
    m["sgu_w_in"] = np.ascontiguousarray(inputs["sgu_w_in"][0])
    m["sgu_bu_col"] = np.ascontiguousarray(inputs["sgu_b_in"][0][:1024].reshape(8, 128).T).astype(f32)
    m["sgu_bv"] = np.ascontiguousarray(inputs["sgu_b_in"][0][1024:])
    m["sgu_ln_g"] = inputs["sgu_ln_g"][0]
    m["sgu_ln_b"] = inputs["sgu_ln_b"][0]
    m["sgu_wsT"] = np.ascontiguousarray(inputs["sgu_w_s"][0].transpose(2, 0, 1)).astype(f32)
    m["sgu_b_s"] = np.ascontiguousarray(inputs["sgu_b_s"][0].reshape(1, 1024)).astype(f32)
    m["sgu_w_out"] = np.ascontiguousarray(inputs["sgu_w_out"][0])
    if lo not in _TAB:
        _TAB[lo] = fft_tables(lo)
    if "hc" not in _TAB:
        _TAB["hc"] = hyena_consts()
    m.update(_TAB[lo])
    m["hy_zT"], m["hy_win"] = _TAB["hc"]
    m["hy_w1"] = np.ascontiguousarray(inputs["hy_w1"][0])
    m["hy_w2"] = np.ascontiguousarray(inputs["hy_w2"][0])
    m["hy_prm"] = np.ascontiguousarray(np.stack([inputs["hy_b1"][0], inputs["hy_freq"][0][0], inputs["hy_b2"][0], inputs["hy_freq"][0][1]], -1)).astype(f32)
    m["hy_w3"] = np.ascontiguousarray(inputs["hy_w3"][0])
    m["hy_bias"] = np.ascontiguousarray(inputs["hy_bias"][0])
    m["subln_col"] = np.ascontiguousarray(inputs["da_subln_g"][0].reshape(128, 1)).astype(f32)
    return m


_CACHE = {}


def kernel(**inputs):
    inputs = {k: np.asarray(v) for k, v in inputs.items()}
    if "kb" not in _CACHE:
        _CACHE["kb"] = build_program()
    kb = _CACHE["kb"]
    in_maps = []
    for cid in range(8):
        m = host_inputs(inputs, cid)
        in_maps.append({k: m[k] for k in kb.ins})
    res = run_bass_kernel_spmd(kb.nc, in_maps, core_ids=list(range(8)))
    out = np.zeros((4, L, D), np.float32)
    for cid in range(8):
        b, hf = cid // 2, cid % 2
        o = res.results[cid]["out"]
        if hf == 0:
            out[b, :4096] = o[:4096]
        else:
            out[b, 4096:] = o[EXT - 4096:]
    return out
```

```python
import math
import os
import numpy as np
import ml_dtypes
import concourse.bass as bass
import concourse.mybir as mybir
from concourse.bass_utils import run_bass_kernel_spmd

F32 = mybir.dt.float32
BF16 = mybir.dt.bfloat16
U8 = mybir.dt.uint8
AF = mybir.ActivationFunctionType
ALU = mybir.AluOpType

ENGS = ("pe", "act", "dve", "pool", "sp")
SEM_ROT = 16000
L = 8192
EXT = 4352
NEXT_T = EXT // 128
D = 1024
DFF = 2816
EPS = 1e-6
GS = 256


class Res:
    __slots__ = ("name", "w", "r", "dsem", "dcnt", "excl", "lk")

    def __init__(self, name, excl=False):
        self.name = name
        self.lk = None
        self.excl = excl
        self.w = {}
        self.r = {}
        self.dsem = None
        self.dcnt = 0


class Prog:
    def __init__(self, nc):
        self.nc = nc
        self.ops = {e: [] for e in ENGS}
        self.cnt = {e: 0 for e in ENGS}
        self.epoch = {e: 0 for e in ENGS}
        self.sems = {}
        self.known = {e: {} for e in ENGS}
        self.dres = []
        self.meta = {e: [] for e in ENGS}
        self.free_sems = []
        self.nd = 0
        for e in ENGS:
            if e != "sp":
                self._engsem(e)

    def _engsem(self, e):
        k = ("E", e, self.epoch[e])
        if k not in self.sems:
            self.sems[k] = self.nc.alloc_semaphore(name="s_%s_%d" % (e, self.epoch[e]))
        return k

    def _semname(self, sem):
        for k, v in self.sems.items():
            if v is sem:
                return k
        return None

    def check_deadlock(self):
        val = {}
        pc = {e: 0 for e in ENGS}
        prog = True
        while prog:
            prog = False
            for e in ENGS:
                while pc[e] < len(self.meta[e]):
                    waits, inc, desc = self.meta[e][pc[e]]
                    if all(val.get(k, 0) >= v for k, v in waits):
                        if inc is not None:
                            val[inc[0]] = val.get(inc[0], 0) + inc[1]
                        pc[e] += 1
                        prog = True
                    else:
                        break
        bad = False
        for e in ENGS:
            if pc[e] < len(self.meta[e]):
                bad = True
                waits, inc, desc = self.meta[e][pc[e]]
                print("DEADLOCK", e, pc[e], len(self.meta[e]), desc, [(k, v, val.get(k, 0)) for k, v in waits if val.get(k, 0) < v])
        return not bad

    def _deps(self, reads, writes):
        deps = {}
        for r in reads:
            for k, v in r.w.items():
                if deps.get(k, 0) < v:
                    deps[k] = v
        for w in writes:
            for d in (w.w, w.r):
                for k, v in d.items():
                    if deps.get(k, 0) < v:
                        deps[k] = v
        return deps

    def _waits(self, eng, deps):
        kn = self.known[eng]
        out = []
        for k, v in deps.items():
            if kn.get(k, 0) < v:
                kn[k] = v
                out.append((self.sems[k], v))
        return out

    def op(self, eng, fn, reads=(), writes=(), signal=True):
        deps = self._deps(reads, writes)
        k = self._engsem(eng)
        for r in reads:
            if r.excl:
                for kk, vv in r.r.items():
                    if not (kk[0] == "E" and kk[1] == eng) and deps.get(kk, 0) < vv:
                        deps[kk] = vv
        if eng == "pe":
            deps = {kk: vv for kk, vv in deps.items() if kk[1] != "pe" or kk[0] != "E"}
        else:
            deps = {kk: vv for kk, vv in deps.items() if not (kk == k and vv > self.cnt[eng])}
        waits = self._waits(eng, deps)
        if signal:
            self.cnt[eng] += 1
            tok = (k, self.cnt[eng])
            sem = self.sems[k]
        else:
            tok = (k, self.cnt[eng] + 1)
            sem = None

        def emit(h, fn=fn, waits=waits, sem=sem):
            for s, v in waits:
                h.wait_ge(s, v)
            ins = fn(h)
            if sem is not None:
                ins.then_inc(sem, 1)

        self.ops[eng].append(emit)
        self.meta[eng].append(([(self._semname(s_), v_) for s_, v_ in waits], (k, 1) if signal else None,
                               "op r=%s w=%s" % ([r.name for r in reads], [w.name for w in writes])))
        kk, vv = tok
        for w in writes:
            w.w = {kk: vv}
            w.r = {}
        for r in reads:
            if r.r.get(kk, 0) < vv:
                r.r[kk] = vv
        if signal and self.cnt[eng] >= SEM_ROT:
            self.epoch[eng] += 1
            self.cnt[eng] = 0
            self._engsem(eng)

    def dma(self, eng, out, in_, reads=(), writes=(), store=False):
        assert len(writes) == 1
        wres = writes[0]
        owner = reads[0] if store else wres
        deps = {}
        for r in reads:
            for k, v in r.w.items():
                if deps.get(k, 0) < v:
                    deps[k] = v
        for d in ((wres.r,) if store else (wres.w, wres.r)):
            for k, v in d.items():
                if deps.get(k, 0) < v:
                    deps[k] = v
        if owner.dsem is None:
            if eng == "pool":
                self.nd += 1
                owner.dsem = ("S", self.nd)
                owner.dcnt = 0
                self.sems[owner.dsem] = self.nc.alloc_semaphore(name="sw_%d" % self.nd)
            elif self.free_sems:
                owner.dsem, owner.dcnt = self.free_sems.pop()
            else:
                self.nd += 1
                owner.dsem = ("D", self.nd)
                owner.dcnt = 0
                self.sems[owner.dsem] = self.nc.alloc_semaphore(name="d_%d" % self.nd)
            self.dres.append(owner)
        if (not store) and owner.lk == "load" and not wres.r:
            deps.pop(owner.dsem, None)
        owner.lk = "store" if store else "load"
        waits = self._waits(eng, deps)
        owner.dcnt += 1
        k, v = owner.dsem, 16 * owner.dcnt
        sem = self.sems[k]

        def emit(h, waits=waits, sem=sem, out=out, in_=in_):
            for s, vv in waits:
                h.wait_ge(s, vv)
            h.dma_start(out=out, in_=in_).then_inc(sem, 16)

        self.ops[eng].append(emit)
        self.meta[eng].append(([(self._semname(s_), v_) for s_, v_ in waits], (k, 16),
                               "dma r=%s w=%s" % ([r.name for r in reads], [w.name for w in writes])))
        if store:
            wres.w[k] = v
        else:
            wres.w = {k: v}
            wres.r = {}
        for r in reads:
            if r.r.get(k, 0) < v:
                r.r[k] = v

    def barrier(self):
        toks = {}
        for e in ENGS:
            if e == "sp":
                continue
            k = self._engsem(e)
            if self.cnt[e] > 0:
                toks[k] = self.cnt[e]
            if self.epoch[e] > 0:
                toks[("E", e, self.epoch[e] - 1)] = SEM_ROT
        for r in self.dres:
            toks[r.dsem] = 16 * r.dcnt
        for e in ENGS:
            waits = self._waits(e, dict(toks))

            def emit(h, waits=waits):
                for s, v in waits:
                    h.wait_ge(s, v)

            self.ops[e].append(emit)
            self.meta[e].append(([(self._semname(s_), v_) for s_, v_ in waits], None, "barrier"))
        keep = []
        for r in self.dres:
            if r.dsem[0] == "S":
                keep.append(r)
                continue
            self.free_sems.append((r.dsem, r.dcnt))
            r.dsem = None
        self.dres = keep

    def build(self):
        nc = self.nc
        with nc.Block() as block:
            @block.tensor
            def _(h):
                for f in self.ops["pe"]:
                    f(h)

            @block.scalar
            def _(h):
                for f in self.ops["act"]:
                    f(h)

            @block.vector
            def _(h):
                for f in self.ops["dve"]:
                    f(h)

            @block.gpsimd
            def _(h):
                for f in self.ops["pool"]:
                    f(h)

            @block.sync
            def _(h):
                for f in self.ops["sp"]:
                    f(h)


class Buf:
    __slots__ = ("t", "r")

    def __init__(self, t, r):
        self.t = t
        self.r = r


def _dtsize(dt):
    return 4 if dt == F32 else (2 if dt == BF16 else 1)


class KB:
    def __init__(self, stop_after=None):
        nc = bass.Bass("TRN2", target_bir_lowering=False)
        self.nc = nc
        self.P = Prog(nc)
        self.stop_after = stop_after
        self.ARENA = 207 * 1024
        self.arena = nc.alloc_sbuf_tensor("arena", [128, self.ARENA], U8)
        self.top = 0
        ps = nc.alloc_psum_tensor("psum", [128, 4096], F32)
        self.ps = ps
        self.pb = [Buf(ps[:, 512 * i:512 * (i + 1)], Res("pb%d" % i, excl=True)) for i in range(8)]
        self.dram = {}
        self.dram_names = set()
        self.ins = {}
        self.outs = {}

    def alloc(self, name, shape, dt):
        n = 1
        for s in shape[1:]:
            n *= s
        nb = n * _dtsize(dt)
        nb = (nb + 31) // 32 * 32
        off = self.top
        self.top += nb
        assert self.top <= self.ARENA, "SBUF arena overflow %s %d" % (name, self.top)
        v = self.arena[:shape[0], off:off + n * _dtsize(dt)].bitcast(dt)
        if len(shape) == 3:
            v = v.rearrange("p (a b) -> p a b", a=shape[1])
        elif len(shape) == 4:
            v = v.rearrange("p (a b c) -> p a b c", a=shape[1], b=shape[2])
        return Buf(v, Res(name))

    def inp(self, name, shape, dt=F32):
        t = self.nc.dram_tensor(name, list(shape), dt, kind="ExternalInput").ap()
        b = Buf(t, Res(name))
        self.ins[name] = b
        return b

    def outp(self, name, shape, dt=F32):
        t = self.nc.dram_tensor(name, list(shape), dt, kind="ExternalOutput").ap()
        self.dram_names.add(name)
        b = Buf(t, Res(name))
        self.outs[name] = b
        return b

    def scratch(self, name, shape, dt, debug=False):
        if debug:
            return self.outp(name, shape, dt)
        t = self.nc.dram_tensor(name, list(shape), dt).ap()
        self.dram_names.add(name)
        return Buf(t, Res(name))

    def mm(self, out, lhsT, rhs, start, stop, reads, writes):
        self.P.op("pe", lambda h: h.matmul(out, lhsT, rhs, start=start, stop=stop),
                  reads=reads, writes=writes, signal=bool(stop))

    def tr(self, out, in_, ident, reads, writes, signal=True):
        self.P.op("pe", lambda h: h.transpose(out, in_, ident), reads=reads, writes=writes, signal=signal)

    def act(self, out, in_, func, reads, writes, bias=None, scale=None, accum=None):
        kw = {}
        if bias is not None:
            kw["bias"] = bias
        if scale is not None:
            kw["scale"] = scale
        if accum is not None:
            kw["accum_out"] = accum
        self.P.op("act", lambda h: h.activation(out=out, in_=in_, func=func, **kw), reads=reads, writes=writes)

    def tt(self, eng, out, a, b, op, reads, writes):
        self.P.op(eng, lambda h: h.tensor_tensor(out=out, in0=a, in1=b, op=op), reads=reads, writes=writes)

    def ts(self, eng, out, a, s1, s2, op0, op1, reads, writes):
        if op1 is None:
            s2, op1 = 0.0, ALU.add
        self.P.op(eng, lambda h: h.tensor_scalar(out, a, s1, s2, op0, op1), reads=reads, writes=writes)

    def stt(self, eng, out, in0, scalar, in1, op0, op1, reads, writes):
        self.P.op(eng, lambda h: h.scalar_tensor_tensor(out=out, in0=in0, scalar=scalar, in1=in1, op0=op0, op1=op1),
                  reads=reads, writes=writes)

    def cp(self, eng, out, in_, reads, writes):
        if eng == "act":
            self.P.op("act", lambda h: h.activation(out=out, in_=in_, func=AF.Copy), reads=reads, writes=writes)
        else:
            self.P.op(eng, lambda h: h.tensor_copy(out, in_), reads=reads, writes=writes)

    def recip(self, out, in_, reads, writes):
        self.P.op("dve", lambda h: h.reciprocal(out, in_), reads=reads, writes=writes)

    def memset(self, eng, out, val, writes):
        self.P.op(eng, lambda h: h.memset(out, val), writes=writes)

    def dma(self, q, out, in_, reads, writes, store=None):
        if store is None:
            store = writes[0].name in self.dram_names
        self.P.dma(q, out, in_, reads=reads, writes=writes, store=store)

    def ld(self, q, dst, src_ap, src=None):
        self.P.dma(q, dst.t if isinstance(dst, Buf) else dst[0], src_ap,
                   reads=[src.r] if src is not None else [], writes=[dst.r if isinstance(dst, Buf) else dst[1]])

    def consts(self):
        self.ident = self.alloc("ident", [128, 128], BF16)
        self.ones = self.alloc("ones", [128, 128], BF16)
        self.epsb = self.alloc("epsb", [128, 1], F32)
        idin = self.inp("c_ident", [128, 128], BF16)
        self.dma("sp", self.ident.t, idin.t, [], [self.ident.r])
        self.memset("pool", self.ones.t, 1.0, [self.ones.r])
        self.memset("pool", self.epsb.t, EPS, [self.epsb.r])
        self.cmark = self.top

    def norm_pre(self, xt, A, SH, tmp, xn, junk, stat):
        ss = stat.t[:, 0:1]
        rs = stat.t[:, 1:2]
        self.act(junk.t, xt.t, AF.Square, [xt.r], [junk.r, stat.r], accum=ss)
        self.act(rs, ss, AF.Sqrt, [stat.r, self.epsb.r], [stat.r], bias=self.epsb.t[:, 0:1], scale=1.0 / D)
        self.recip(rs, rs, [stat.r], [stat.r])
        self.stt("dve", tmp.t, xt.t, rs, A.t, ALU.mult, ALU.mult, [xt.r, stat.r, A.r], [tmp.r])
        self.tt("pool", xn.t, tmp.t, SH.t, ALU.add, [tmp.r, SH.r], [xn.r])

    def norm_tr(self, xn, hT, col0, ntok=128):
        pT = self.pb[7]
        pTv = pT.t.bitcast(BF16)
        for kc in range(8):
            self.tr(pTv[:, kc * 128:kc * 128 + ntok], xn.t[:ntok, kc * 128:(kc + 1) * 128], self.ident.t[:ntok, :ntok],
                    [xn.r, self.ident.r], [pT.r], signal=(kc == 7))
        src = pTv.rearrange("p (a b) -> p a b", a=8)[:, :, :ntok]
        self.cp("act", hT.t[:, :, col0:col0 + ntok], src, [pT.r], [hT.r])

    def norm_T(self, xt, A, SH, tmp, xn, junk, stat, hT, col0, ntok=128, plain_g=None):
        self.norm_pre(xt, A, SH, tmp, xn, junk, stat)
        self.norm_tr(xn, hT, col0, ntok)

    def pool_of(self, name, n, shape, dt):
        return {"b": [self.alloc("%s%d" % (name, i), shape, dt) for i in range(n)], "i": 0}

    def nxt(self, pool):
        b = pool["b"][pool["i"] % len(pool["b"])]
        pool["i"] += 1
        return b

    def bank(self):
        b = self.pb[self._bk % 6]
        self._bk += 1
        return b

    def phase_mod(self):
        ccol = self.inp("ccol", [128, 8, 2])
        modw = self.inp("mod_w", [2, 1024, 6144])
        modb = self.inp("mod_b", [2, 6144])
        self.modv = self.scratch("modv", [2, 2, 6144], F32, debug=self.dbg)
        sc = self.alloc("scol", [128, 8, 2], F32)
        self.dma("sp", sc.t, ccol.t, [], [sc.r])
        self.act(sc.t, sc.t, AF.Silu, [sc.r], [sc.r])
        mrow = self.alloc("mrow", [2, 6144], F32)
        mb2 = self.alloc("mb2", [2, 6144], F32)
        wb = [self.alloc("mwb%d" % i, [128, 8, 512], F32) for i in range(2)]
        for i in range(2):
            self.dma("sp", mb2.t, modb.t[i].partition_broadcast(2), [], [mb2.r])
            for n in range(12):
                w = wb[n % 2]
                self.dma("sp", w.t, modw.t[i, :, n * 512:(n + 1) * 512].rearrange("(kc p) f -> p kc f", p=128), [], [w.r])
                pbk = self.pb[n % 2]
                for kc in range(8):
                    self.mm(pbk.t[0:2, :], sc.t[:, kc, :], w.t[:, kc, :], kc == 0, kc == 7, [sc.r, w.r], [pbk.r])
                self.tt("dve", mrow.t[:, n * 512:(n + 1) * 512], pbk.t[0:2, :], mb2.t[:, n * 512:(n + 1) * 512], ALU.add,
                        [pbk.r, mb2.r], [mrow.r])
            self.dma("sp", self.modv.t[i], mrow.t, [mrow.r], [self.modv.r])

    def mod_tiles(self, layer, which, i_sh, i_sc, g_ap):
        A = self.alloc("modA", [128, 1024], F32)
        SH = self.alloc("modSH", [128, 1024], F32)
        mv = self.modv.t[layer, which]
        self.dma("sp", A.t, mv[i_sc * D:(i_sc + 1) * D].partition_broadcast(128), [self.modv.r], [A.r])
        self.dma("sp", SH.t, g_ap.partition_broadcast(128), [], [SH.r])
        self.stt("dve", A.t, A.t, 1.0, SH.t, ALU.add, ALU.mult, [A.r, SH.r], [A.r])
        self.dma("sp", SH.t, mv[i_sh * D:(i_sh + 1) * D].partition_broadcast(128), [self.modv.r], [SH.r])
        return A, SH

    def load_w_bf16(self, dst, src_ap, kcn):
        for kc in range(kcn):
            self.dma("pool", dst.t[:, kc, :], src_ap[kc * 128:(kc + 1) * 128, :], [], [dst.r])

    def norm_bufs(self):
        nb = {}
        nb["x"] = self.pool_of("nx", 2, [128, 1024], F32)
        nb["tmp"] = self.alloc("ntmp", [128, 1024], F32)
        nb["xn"] = self.pool_of("nxn", 2, [128, 1024], BF16)
        nb["junk"] = self.alloc("njunk", [128, 1024], BF16)
        nb["stat"] = self.pool_of("nstat", 2, [128, 2], F32)
        return nb

    def proj_pass(self, xin, T, A, SH, w, fm, tm, nb, post=None):
        skip = os.environ.get("KSKIP", "")
        fm = [sp for sp in fm if sp["kind"] not in skip.split(",")]
        ng = T // GS
        hT = [self.alloc("hT%d" % i, [128, 8, GS + 32], BF16) for i in range(3)]
        for hb in hT:
            self.memset("pool", hb.t, 0.0, [hb.r])
        tmpc = self.pool_of("tmpc", 2, [128, GS], F32)
        tmpd = self.pool_of("tmpd", 6, [128, GS], F32)
        obf = self.pool_of("obf", 4 if not any("emit" in sp_ for sp_ in fm) else 1, [128, GS], BF16)
        tst = self.pool_of("tst", 2, [128, 512], BF16) if tm else None
        cs = self.pool_of("cs", 2, [128, 2, GS], F32) if any(sp_["kind"] == "rope" for sp_ in fm) else None
        pend_xn = {}

        def pre(k):
            xs = []
            for j in range(GS // 128):
                xt = self.nxt(nb["x"])
                t0 = k * GS + j * 128
                self.dma("sp", xt.t, xin.t[t0:t0 + 128, :], [xin.r], [xt.r])
                xn = self.nxt(nb["xn"])
                self.norm_pre(xt, A, SH, nb["tmp"], xn, nb["junk"], self.nxt(nb["stat"]))
                xs.append(xn)
            pend_xn[k] = xs

        def trn(k):
            h_ = hT[k % 3]
            for j, xn in enumerate(pend_xn.pop(k)):
                self.norm_tr(xn, h_, 16 + j * 128)
            if k == 0:
                self.memset("pool", h_.t[:, :, 15:16], 0.0, [h_.r])
            else:
                hp = hT[(k - 1) % 3]
                self.cp("pool", h_.t[:, :, 15:16], hp.t[:, :, GS + 15:GS + 16], [hp.r], [h_.r])

        for k0 in range(min(2, ng)):
            pre(k0)
            trn(k0)
        for gg in range(ng):
            if gg + 2 < ng:
                pre(gg + 2)
            h = hT[gg % 3]
            if gg == ng - 1:
                self.memset("pool", h.t[:, :, GS + 16:GS + 17], 0.0, [h.r])
            else:
                hn = hT[(gg + 1) % 3]
                self.cp("pool", h.t[:, :, GS + 16:GS + 17], hn.t[:, :, 16:17], [hn.r], [h.r])
            g0 = gg * GS
            for sp in fm:
                kind = sp["kind"]
                if kind == "rope":
                    c = self.nxt(cs)
                    if "nodma" in os.environ.get("ROPEVAR", ""):
                        self.memset("pool", c.t, 1.0, [c.r])
                    else:
                        self.dma("sp", c.t[:, 0, :], sp["cos"].t[:, g0:g0 + GS], [], [c.r])
                        self.dma("sp", c.t[:, 1, :], sp["sin"].t[:, g0:g0 + GS], [], [c.r])
                rope_pend = []
                for ci in sp.get("order", range(sp["n"])):
                    pbk = self.bank()
                    col = sp["col0"] + ci * 128
                    for kc in range(8):
                        self.mm(pbk.t[:, 0:GS + 4], w.t[:, kc, col:col + 128], h.t[:, kc, 14:GS + 18], kc == 0, kc == 7,
                                [w.r, h.r], [pbk.r])
                    if kind == "conv":
                        cw = sp["cw"]
                        k = sp["cwi0"] + ci
                        t1 = self.nxt(tmpc)
                        ob = self.nxt(obf)
                        self.act(t1.t, pbk.t[:, 2:GS + 2], AF.Identity, [pbk.r, cw.r], [t1.r],
                                 bias=cw.t[:, k, 3:4], scale=cw.t[:, k, 1:2])
                        self.stt("dve", t1.t, pbk.t[:, 1:GS + 1], cw.t[:, k, 0:1], t1.t, ALU.mult, ALU.add,
                                 [pbk.r, cw.r, t1.r], [t1.r])
                        if "emit" in sp:
                            t3 = self.nxt(tmpd)
                            self.stt("dve", t3.t, pbk.t[:, 3:GS + 3], cw.t[:, k, 2:3], t1.t, ALU.mult, ALU.add,
                                     [pbk.r, cw.r, t1.r], [t3.r])
                            sp["emit"](ci, t3, g0)
                        else:
                            self.stt("dve", ob.t, pbk.t[:, 3:GS + 3], cw.t[:, k, 2:3], t1.t, ALU.mult, ALU.add,
                                     [pbk.r, cw.r, t1.r], [ob.r])
                            r0 = sp["row0"] + ci * 128
                            self.dma("sp", sp["out"].t[r0:r0 + 128, g0:g0 + GS], ob.t, [ob.r], [sp["out"].r])
                    elif kind == "rope":
                        kb = self.nxt(obf)
                        self.cp("act", kb.t, pbk.t[:, 2:GS + 2], [pbk.r], [kb.r])

                        def rope_tail(pbk=pbk, kb=kb, ci=ci, c=c, sp=sp, to=sp["toff"] + g0):
                            ob = self.nxt(obf)
                            t1 = self.nxt(tmpc)
                            t2 = self.nxt(tmpd)
                            p2 = self.pb[6]
                            self.mm(p2.t[:, 0:GS], self.rperm.t, kb.t, True, True, [self.rperm.r, kb.r], [p2.r])
                            self.tt("dve", t1.t, pbk.t[:, 2:GS + 2], c.t[:, 0, :], ALU.mult, [pbk.r, c.r, kb.r], [t1.r])
                            self.tt("dve", t2.t, p2.t[:, 0:GS], c.t[:, 1, :], ALU.mult, [p2.r, c.r], [t2.r])
                            self.tt("pool", ob.t, t1.t, t2.t, ALU.add, [t1.r, t2.r], [ob.r])
                            self.dma("sp", sp["out"].t[ci, :, to:to + GS], ob.t, [ob.r], [sp["out"].r])

                        rope_pend.append(rope_tail)
                        if len(rope_pend) > 1:
                            rope_pend.pop(0)()
                    else:
                        ob = self.nxt(obf)
                        self.cp("act", ob.t, pbk.t[:, 2:GS + 2], [pbk.r], [ob.r])
                        to = sp["toff"] + g0
                        self.dma("sp", sp["out"].t[ci, :, to:to + GS], ob.t, [ob.r], [sp["out"].r])
                while rope_pend:
                    rope_pend.pop(0)()
            for sp in tm:
                for j in range(GS // 128):
                    pbk = self.bank()
                    for kc in range(8):
                        self.mm(pbk.t[:, 0:512], h.t[:, kc, 16 + j * 128:16 + (j + 1) * 128],
                                w.t[:, kc, sp["col0"]:sp["col0"] + 512], kc == 0, kc == 7, [w.r, h.r], [pbk.r])
                    st = self.nxt(tst)
                    self.cp("act", st.t, pbk.t[:, 0:512], [pbk.r], [st.r])
                    ro = sp["roff"] + g0 + j * 128
                    self.dma("sp", sp["out"].t[ro:ro + 128, :], st.t, [st.r], [sp["out"].r])
            if post is not None:
                post(gg, g0, h)
            if gg + 2 < ng:
                trn(gg + 2)

    def phase_inproj(self):
        dbg = self.dbg
        self.x_full = self.inp("x_full", [L, D])
        self.x_ext = self.inp("x_ext", [EXT, D])
        ctx = self.inp("ctx", [256, D])
        w_in = self.inp("w_in", [D, 3072])
        cwin = self.inp("hy_cw", [128, 12, 4])
        gmix = self.inp("norm_mix_g", [2, D])
        cosk = self.inp("ropek_cos", [128, L])
        sink = self.inp("ropek_sin", [128, L])
        cosq = self.inp("ropeq_cos", [128, EXT])
        sinq = self.inp("ropeq_sin", [128, EXT])
        rp = self.inp("c_rperm", [128, 128], BF16)
        self.VX1 = self.scratch("VX1", [1024, L], BF16, debug=dbg)
        self.X2 = self.scratch("X2", [512, EXT], BF16, debug=dbg)
        self.KT = self.scratch("KT", [4, 128, L + 256], BF16, debug=dbg)
        self.QT = self.scratch("QT", [4, 128, EXT], BF16, debug=dbg)
        self.VT = self.scratch("VT", [L + 256, 512], BF16, debug=dbg)
        self.top = self.cmark
        w = self.alloc("w_in", [128, 8, 3072], BF16)
        self.load_w_bf16(w, w_in.t, 8)
        cw = self.alloc("cw", [128, 12, 4], F32)
        self.dma("sp", cw.t, cwin.t, [], [cw.r])
        self.rperm = self.alloc("rperm", [128, 128], BF16)
        self.dma("sp", self.rperm.t, rp.t, [], [self.rperm.r])
        nb = self.norm_bufs()
        mark = self.top
        A, SH = self.mod_tiles(0, 1, 0, 1, gmix.t[0])
        import os
        self.proj_pass(ctx, 256, A, SH, w,
                       [dict(kind="rope", col0=2048, n=4, cos=cosk, sin=sink, out=self.KT, toff=0)] if os.environ.get("CTXROPE") else
                       [dict(kind="plain", col0=2048, n=4, out=self.KT, toff=0)],
                       [dict(col0=2560, out=self.VT, roff=0)], nb)
        self.P.barrier()
        self.top = mark
        if self.stop_after == "ctx":
            return
        A, SH = self.mod_tiles(0, 0, 0, 1, gmix.t[0])
        mark2 = self.top
        self.proj_pass(self.x_full, L, A, SH, w,
                       [dict(kind="conv", col0=0, n=8, cw=cw, cwi0=0, out=self.VX1, row0=0),
                        dict(kind="rope", col0=2048, n=4, cos=cosk, sin=sink, out=self.KT, toff=256)],
                       [dict(col0=2560, out=self.VT, roff=256)], nb)
        self.P.barrier()
        self.top = mark2
        self.proj_pass(self.x_ext, EXT, A, SH, w,
                       [dict(kind="conv", col0=1024, n=4, cw=cw, cwi0=8, out=self.X2, row0=0),
                        dict(kind="rope", col0=1536, n=4, cos=cosq, sin=sinq, out=self.QT, toff=0)],
                       [], nb)


    def phase_attn(self):
        dal = self.inp("da_lambda", [4, 64])
        subg = self.inp("subln_col", [128, 1])
        self.YT = self.scratch("YT", [1024, EXT], BF16, debug=self.dbg)
        self.top = self.cmark
        LAM_INIT = 0.8 - 0.6 * math.exp(0.0)
        lt = self.alloc("lt", [128, 256], F32)
        pr = self.alloc("lpr", [128, 128], F32)
        ls = self.alloc("ls", [128, 4], F32)
        negl = self.alloc("negl", [128, 1], F32)
        gsub = self.alloc("gsub", [128, 1], F32)
        self.dma("sp", lt.t, dal.t.rearrange("a b -> (a b)").partition_broadcast(128), [], [lt.r])
        self.tt("dve", pr.t[:, 0:64], lt.t[:, 0:64], lt.t[:, 64:128], ALU.mult, [lt.r], [pr.r])
        self.tt("dve", pr.t[:, 64:128], lt.t[:, 128:192], lt.t[:, 192:256], ALU.mult, [lt.r, pr.r], [pr.r])
        self.act(lt.t[:, 0:64], pr.t[:, 0:64], AF.Identity, [pr.r, lt.r], [lt.r, ls.r], accum=ls.t[:, 0:1])
        self.act(lt.t[:, 64:128], pr.t[:, 64:128], AF.Identity, [pr.r, lt.r, ls.r], [lt.r, ls.r], accum=ls.t[:, 1:2])
        self.act(ls.t[:, 2:4], ls.t[:, 0:2], AF.Exp, [ls.r], [ls.r])
        self.tt("dve", negl.t, ls.t[:, 3:4], ls.t[:, 2:3], ALU.subtract, [ls.r], [negl.r])
        self.ts("dve", negl.t, negl.t, -LAM_INIT, None, ALU.add, None, [negl.r], [negl.r])
        self.dma("sp", gsub.t, subg.t, [], [gsub.r])
        self.ts("dve", gsub.t, gsub.t, 1.0 - LAM_INIT, None, ALU.mult, None, [gsub.r], [gsub.r])
        NK = (L + 256) // 128
        KVQ = [(self.alloc("Kh%d" % i, [128, L + 256], BF16), self.alloc("Vh%d" % i, [128, NK, 128], BF16),
                self.alloc("Qh%d" % i, [128, EXT], BF16)) for i in range(2)]

        def load_head(hh):
            K_, V_, Q_ = KVQ[hh % 2]
            self.dma("sp", K_.t, self.KT.t[hh], [self.KT.r], [K_.r])
            self.dma("sp", V_.t, self.VT.t[:, hh * 128:(hh + 1) * 128].rearrange("(kt p) d -> p kt d", p=128), [self.VT.r], [V_.r])
            self.dma("sp", Q_.t, self.QT.t[hh], [self.QT.r], [Q_.r])

        Eb = [self.alloc("Eb%d" % i, [128, 2, 512], BF16) for i in range(2)]
        f = {n_: self.alloc("at_" + n_, [128, 512], F32) for n_ in ("r0", "r1", "t0", "t1", "o", "rs", "y")}
        osq = self.alloc("at_osq", [128, 512], BF16)
        acc0 = self.alloc("at_acc0", [128, 512], F32)
        ones32 = self.alloc("at_ones32", [128, 128], F32)
        self.memset("pool", ones32.t, 1.0, [ones32.r])
        yb = [self.alloc("at_yb%d" % i, [128, 512], BF16) for i in range(2)]
        pb = self.pb
        itc = [0]
        epi_pend = []
        load_head(0)
        for h in range(4):
            Kh, Vh, Qh = KVQ[h % 2]
            if h + 1 < 4:
                load_head(h + 1)
            groups = [(q0_, min(512, EXT - q0_)) for q0_ in range(0, EXT, 512)]

            def emit_qk(kt, q0, n):
                par = kt % 2
                for m in range(2):
                    sb_ = pb[2 * par + m]
                    self.mm(sb_.t[:, :n], Kh.t[64 * m:64 * m + 64, kt * 128:(kt + 1) * 128],
                            Qh.t[64 * m:64 * m + 64, q0:q0 + n], True, True, [Kh.r, Qh.r], [sb_.r])

            for gi_, (q0, n) in enumerate(groups):
                if gi_ == 0:
                    emit_qk(0, q0, n)
                for kt in range(NK):
                    par = kt % 2
                    if kt + 1 < NK:
                        emit_qk(kt + 1, q0, n)
                    E = Eb[par]
                    sv = self.ps[:, 1024 * par:1024 * par + 1024].rearrange("p (a b) -> p a b", a=2)[:, :, :n]
                    self.act(E.t[:, :, :n], sv, AF.Exp, [pb[2 * par].r, pb[2 * par + 1].r], [E.r], scale=0.125)
                    for m in range(2):
                        self.mm(pb[4 + m].t[:, :n], Vh.t[:, kt, :], E.t[:, m, :n], kt == 0, kt == NK - 1, [Vh.r, E.r], [pb[4 + m].r])
                    self.mm(pb[7].t[:, :n], self.ones.t, E.t[:, 1, :n], kt == 0, kt == NK - 1, [self.ones.r, E.r], [pb[7].r])
                    if kt == 2 and epi_pend:
                        epi_pend.pop(0)()
                    if kt == 0:
                        self.cp("dve", acc0.t[:, :n], E.t[:, 0, :n], [E.r], [acc0.r])
                    else:
                        self.tt("dve", acc0.t[:, :n], acc0.t[:, :n], E.t[:, 0, :n], ALU.add, [acc0.r, E.r], [acc0.r])
                self.mm(pb[6].t[:, :n], ones32.t, acc0.t[:, :n], True, True, [ones32.r, acc0.r], [pb[6].r])
                if gi_ + 1 < len(groups):
                    emit_qk(0, *groups[gi_ + 1])
                self.recip(f["r0"].t[:, :n], pb[6].t[:, :n], [pb[6].r], [f["r0"].r])
                self.recip(f["r1"].t[:, :n], pb[7].t[:, :n], [pb[7].r], [f["r1"].r])
                self.tt("dve", f["t0"].t[:, :n], pb[4].t[:, :n], f["r0"].t[:, :n], ALU.mult, [pb[4].r, f["r0"].r], [f["t0"].r])
                self.tt("dve", f["t1"].t[:, :n], pb[5].t[:, :n], f["r1"].t[:, :n], ALU.mult, [pb[5].r, f["r1"].r], [f["t1"].r])
                def epi_b(n=n, q0=q0, h=h):
                    self.stt("dve", f["o"].t[:, :n], f["t1"].t[:, :n], negl.t[:, 0:1], f["t0"].t[:, :n], ALU.mult, ALU.add,
                             [f["t1"].r, f["t0"].r, negl.r], [f["o"].r])
                    self.act(osq.t[:, :n], f["o"].t[:, :n], AF.Square, [f["o"].r], [osq.r])
                    self.mm(pb[6].t[:, :n], self.ones.t, osq.t[:, :n], True, True, [self.ones.r, osq.r], [pb[6].r])
                    self.act(f["rs"].t[:, :n], pb[6].t[:, :n], AF.Sqrt, [pb[6].r, self.epsb.r], [f["rs"].r],
                             bias=self.epsb.t[:, 0:1], scale=1.0 / 128)
                    self.recip(f["rs"].t[:, :n], f["rs"].t[:, :n], [f["rs"].r], [f["rs"].r])
                    self.tt("dve", f["y"].t[:, :n], f["o"].t[:, :n], f["rs"].t[:, :n], ALU.mult, [f["o"].r, f["rs"].r], [f["y"].r])
                    y2 = yb[itc[0] % 2]
                    itc[0] += 1
                    self.ts("dve", y2.t[:, :n], f["y"].t[:, :n], gsub.t[:, 0:1], None, ALU.mult, None, [f["y"].r, gsub.r], [y2.r])
                    r0 = 512 + h * 128
                    self.dma("sp", self.YT.t[r0:r0 + 128, q0:q0 + n], y2.t[:, :n], [y2.r], [self.YT.r])
                epi_pend.append(epi_b)
        while epi_pend:
            epi_pend.pop(0)()

    def load_w_gated(self, dst, src_ap, kcn, gate_ap):
        mark = self.top
        G = self.alloc("gateG", [128, 1024], F32)
        self.dma("sp", G.t, gate_ap.partition_broadcast(128), [self.modv.r], [G.r])
        stg = self.pool_of("wstg", 2, [128, 1024], F32)
        for kc in range(kcn):
            st = self.nxt(stg)
            self.dma("sp", st.t, src_ap[kc * 128:(kc + 1) * 128, :], [], [st.r])
            self.tt("dve", dst.t[:, kc, :], st.t, G.t, ALU.mult, [st.r, G.r], [dst.r])
        self.P.barrier()
        self.top = mark

    def resid_store(self, xin, xout, actT, kcn, Wd, g0, final_g=None):
        for j in range(GS // 128):
            xr = self.nxt(self.rx)
            r0 = g0 + j * 128
            self.dma("sp", xr.t, xin.t[r0:r0 + 128, :], [xin.r], [xr.r])
            xo = self.nxt(self.ro)
            for half in range(2):
                pbk = self.bank()
                for kc in range(kcn):
                    self.mm(pbk.t[:, 0:512], actT.t[:, kc, j * 128:(j + 1) * 128], Wd.t[:, kc, half * 512:(half + 1) * 512],
                            kc == 0, kc == kcn - 1, [actT.r, Wd.r], [pbk.r])
                self.tt("dve", xo.t[:, half * 512:(half + 1) * 512], pbk.t[:, 0:512], xr.t[:, half * 512:(half + 1) * 512],
                        ALU.add, [pbk.r, xr.r], [xo.r])
            if final_g is not None:
                st = self.nxt(self.fst)
                self.act(self.fjunk.t, xo.t, AF.Square, [xo.r], [self.fjunk.r, st.r], accum=st.t[:, 0:1])
                self.act(st.t[:, 1:2], st.t[:, 0:1], AF.Sqrt, [st.r, self.epsb.r], [st.r], bias=self.epsb.t[:, 0:1], scale=1.0 / D)
                self.recip(st.t[:, 1:2], st.t[:, 1:2], [st.r], [st.r])
                self.stt("dve", xo.t, xo.t, st.t[:, 1:2], final_g.t, ALU.mult, ALU.mult, [xo.r, st.r, final_g.r], [xo.r])
            self.dma("sp", xout.t[r0:r0 + 128, :], xo.t, [xo.r], [xout.r])

    def resid_bufs(self):
        self.rx = self.pool_of("rx", 2, [128, 1024], F32)
        self.ro = self.pool_of("ro", 2, [128, 1024], F32)

    def phase_outproj(self, name, yT_dram, xin, w_ap, layer):
        xout = self.scratch(name, [EXT, D], F32, debug=self.dbg)
        self.top = self.cmark
        Wo = self.alloc("Wo", [128, 8, 1024], BF16)
        self.load_w_gated(Wo, w_ap, 8, self.modv.t[layer, 0, 2 * D:3 * D])
        self.resid_bufs()
        yb = self.pool_of("yTb", 2, [128, 8, GS], BF16)
        for g in range(EXT // GS):
            g0 = g * GS
            y = self.nxt(yb)
            self.dma("sp", y.t, yT_dram.t[:, g0:g0 + GS].rearrange("(kc p) t -> p kc t", p=128), [yT_dram.r], [y.r])
            self.resid_store(xin, xout, y, 8, Wo, g0)
        return xout

    def phase_ffn(self, name, xin, layer, final=False):
        wup = self.inp("ffn_w_up%d" % layer, [D, 2 * DFF])
        wdn = self.inp("ffn_w_down%d" % layer, [DFF, D])
        cwin = self.inp("ffn_cw%d" % layer, [128, 44, 4])
        gffn = self.inp("norm_ffn_g", [2, D]) if "norm_ffn_g" not in self.ins else self.ins["norm_ffn_g"]
        xout = self.outp("out", [EXT, D], F32) if final else self.scratch(name, [EXT, D], F32, debug=self.dbg)
        self.top = self.cmark
        Wu = self.alloc("Wu", [128, 8, 2 * DFF], BF16)
        self.load_w_bf16(Wu, wup.t, 8)
        Wd = self.alloc("Wd", [128, 22, 1024], BF16)
        self.load_w_gated(Wd, wdn.t, 22, self.modv.t[layer, 0, 5 * D:6 * D])
        cw = self.alloc("cwf", [128, 44, 4], F32)
        self.dma("sp", cw.t, cwin.t, [], [cw.r])
        fg = None
        if final:
            fgi = self.inp("final_norm_g", [D])
            fg = self.alloc("fg", [128, 1024], F32)
            self.dma("sp", fg.t, fgi.t.partition_broadcast(128), [], [fg.r])
            self.fst = self.pool_of("fst", 2, [128, 2], F32)
            self.fjunk = self.alloc("fjunk", [128, 1024], BF16)
        A, SH = self.mod_tiles(layer, 0, 3, 4, gffn.t[layer])
        nb = self.norm_bufs_small()
        self.resid_bufs_small()
        gT = self.alloc("gT", [128, 22, GS], BF16)
        sil = self.pool_of("sil", 2, [128, GS], F32)
        hold = {}

        pend = []

        def flush(keep):
            while len(pend) > keep:
                j_, gb, ub = pend.pop(0)
                sb_ = self.nxt(sil)
                self.act(sb_.t, gb.t, AF.Silu, [gb.r], [sb_.r])
                self.tt("pool", gT.t[:, j_, :], sb_.t, ub.t, ALU.mult, [sb_.r, ub.r], [gT.r])

        def emit(ci, buf, g0):
            if ci < 22:
                hold["g"] = buf
            else:
                pend.append((ci - 22, hold["g"], buf))
                flush(1)

        def post(gg, g0, h):
            flush(0)
            self.resid_store(xin, xout, gT, 22, Wd, g0, final_g=fg)

        order = []
        for j in range(22):
            order += [j, 22 + j]
        self.proj_pass(xin, EXT, A, SH, Wu,
                       [dict(kind="conv", col0=0, n=44, cw=cw, cwi0=0, order=order, emit=emit)], [], nb, post=post)
        return xout

    def norm_bufs_small(self):
        nb = {}
        nb["x"] = self.pool_of("nx", 1, [128, 1024], F32)
        nb["tmp"] = self.alloc("ntmp", [128, 1024], F32)
        nb["xn"] = self.pool_of("nxn", 2, [128, 1024], BF16)
        nb["junk"] = self.alloc("njunk", [128, 1024], BF16)
        nb["stat"] = self.pool_of("nstat", 2, [128, 2], F32)
        return nb

    def resid_bufs_small(self):
        self.rx = self.pool_of("rx", 1, [128, 1024], F32)
        self.ro = self.pool_of("ro", 1, [128, 1024], F32)


    def phase_sgu(self, name, xin, layer=1):
        win = self.inp("sgu_w_in", [D, 2048])
        bcol_i = self.inp("sgu_bu_col", [128, 8])
        bv_i = self.inp("sgu_bv", [1024])
        lng_i = self.inp("sgu_ln_g", [1024])
        lnb_i = self.inp("sgu_ln_b", [1024])
        wsT_i = self.inp("sgu_wsT", [128, 8, 128])
        bs_i = self.inp("sgu_b_s", [1, 1024])
        wout = self.inp("sgu_w_out", [D, D])
        gmix = self.ins["norm_mix_g"]
        xout = self.scratch(name, [EXT, D], F32, debug=self.dbg)
        self.top = self.cmark
        Wi = self.alloc("Wi", [128, 8, 2048], BF16)
        self.load_w_bf16(Wi, win.t, 8)
        Wo = self.alloc("Wo", [128, 8, 1024], BF16)
        self.load_w_gated(Wo, wout.t, 8, self.modv.t[layer, 0, 2 * D:3 * D])
        wsT = self.alloc("wsT", [128, 8, 128], BF16)
        self.dma("pool", wsT.t, wsT_i.t, [], [wsT.r])
        bsr = self.alloc("bsr", [1, 1024], BF16)
        self.dma("pool", bsr.t, bs_i.t, [], [bsr.r])
        bcol = self.alloc("bcol", [128, 8], F32)
        self.dma("sp", bcol.t, bcol_i.t, [], [bcol.r])
        BV = self.alloc("BV", [128, 1024], F32)
        LNG = self.alloc("LNG", [128, 1024], F32)
        LNB = self.alloc("LNB", [128, 1024], F32)
        self.dma("sp", BV.t, bv_i.t.partition_broadcast(128), [], [BV.r])
        self.dma("sp", LNG.t, lng_i.t.partition_broadcast(128), [], [LNG.r])
        self.dma("sp", LNB.t, lnb_i.t.partition_broadcast(128), [], [LNB.r])
        A, SH = self.mod_tiles(layer, 0, 0, 1, gmix.t[layer])
        nb = self.norm_bufs()
        self.resid_bufs()
        hTb = self.pool_of("sg_hT", 2, [128, 8, GS], BF16)
        uT = self.alloc("sg_uT", [128, 8, GS], F32)
        guT = self.pool_of("sg_guT", 2, [128, 8, GS], BF16)
        vtp = self.pool_of("sg_vt", 2, [128, 1024], F32)
        vgp = self.pool_of("sg_vg", 2, [128, 1024], F32)
        vb = self.pool_of("sg_vb", 2, [128, 1024], BF16)
        st = self.pool_of("sg_st", 2, [128, 8], F32)
        NT = GS // 128
        ngr = EXT // GS
        pend = {}

        def s_pre(g):
            xs = []
            for j in range(NT):
                xt = self.nxt(nb["x"])
                self.dma("sp", xt.t, xin.t[g * GS + j * 128:g * GS + (j + 1) * 128, :], [xin.r], [xt.r])
                xn = self.nxt(nb["xn"])
                self.norm_pre(xt, A, SH, nb["tmp"], xn, nb["junk"], self.nxt(nb["stat"]))
                xs.append(xn)
            pend[g] = (self.nxt(hTb), xs)

        def s_tr(g):
            h_, xs = pend[g]
            for j, xn in enumerate(xs):
                self.norm_tr(xn, h_, j * 128)

        s_pre(0)
        s_tr(0)
        for g in range(ngr):
            g0 = g * GS
            hT = pend.pop(g)[0]
            gu = self.nxt(guT)
            ubanks = []
            for c in range(8):
                pbk = self.bank()
                for kc in range(8):
                    self.mm(pbk.t[:, 0:GS], Wi.t[:, kc, c * 128:(c + 1) * 128], hT.t[:, kc, :], kc == 0, kc == 7, [Wi.r, hT.r], [pbk.r])
                self.act(uT.t[:, c, :], pbk.t[:, 0:GS], AF.Gelu, [pbk.r, bcol.r], [uT.r], bias=bcol.t[:, c:c + 1])
            tl = []
            for j in range(NT):
                vt, vg, s_, v2 = self.nxt(vtp), self.nxt(vgp), self.nxt(st), self.nxt(vb)
                for half in range(2):
                    pbk = self.bank()
                    for kc in range(8):
                        self.mm(pbk.t[:, 0:512], hT.t[:, kc, j * 128:(j + 1) * 128],
                                Wi.t[:, kc, 1024 + half * 512:1024 + (half + 1) * 512], kc == 0, kc == 7, [Wi.r, hT.r], [pbk.r])
                    self.tt("dve", vt.t[:, half * 512:(half + 1) * 512], pbk.t[:, 0:512], BV.t[:, half * 512:(half + 1) * 512],
                            ALU.add, [pbk.r, BV.r], [vt.r])
                tl.append((vt, vg, s_, v2))
            if g + 1 < ngr:
                s_pre(g + 1)
            for vt, vg, s_, v2 in tl:
                self.act(vg.t, vt.t, AF.Gelu, [vt.r], [vg.r, s_.r], accum=s_.t[:, 0:1])
            for vt, vg, s_, v2 in tl:
                self.act(vt.t, vg.t, AF.Square, [vg.r, s_.r], [vt.r, s_.r], accum=s_.t[:, 1:2])
            for vt, vg, s_, v2 in tl:
                self.ts("dve", s_.t[:, 2:3], s_.t[:, 0:1], 1.0 / 1024, None, ALU.mult, None, [s_.r], [s_.r])
                self.tt("dve", s_.t[:, 3:4], s_.t[:, 2:3], s_.t[:, 2:3], ALU.mult, [s_.r], [s_.r])
                self.stt("dve", s_.t[:, 4:5], s_.t[:, 1:2], 1.0 / 1024, s_.t[:, 3:4], ALU.mult, ALU.subtract, [s_.r], [s_.r])
            for vt, vg, s_, v2 in tl:
                self.act(s_.t[:, 5:6], s_.t[:, 4:5], AF.Sqrt, [s_.r, self.epsb.r], [s_.r], bias=self.epsb.t[:, 0:1])
            for vt, vg, s_, v2 in tl:
                self.recip(s_.t[:, 5:6], s_.t[:, 5:6], [s_.r], [s_.r])
                self.ts("dve", vg.t, vg.t, s_.t[:, 2:3], s_.t[:, 5:6], ALU.subtract, ALU.mult, [vg.r, s_.r], [vg.r])
            for vt, vg, s_, v2 in tl:
                self.tt("pool", vg.t, vg.t, LNG.t, ALU.mult, [vg.r, LNG.r], [vg.r])
            for vt, vg, s_, v2 in tl:
                self.tt("dve", v2.t, vg.t, LNB.t, ALU.add, [vg.r, LNB.r], [v2.r])
            if g + 1 < ngr:
                s_tr(g + 1)
            for j, (vt, vg, s_, v2) in enumerate(tl):
                for a in range(2):
                    pbk = self.bank()
                    for q in range(4):
                        gi = 4 * a + q
                        self.P.op("pe", (lambda o_, l_, r_: (lambda h_: h_.matmul(o_, l_, r_, start=True, stop=False)))(
                            pbk.t[:, q * 128:(q + 1) * 128], v2.t[:, gi * 128:(gi + 1) * 128], wsT.t[:, gi, :]),
                            reads=[v2.r, wsT.r], writes=[pbk.r], signal=False)
                        self.P.op("pe", (lambda o_, l_, r_: (lambda h_: h_.matmul(o_, l_, r_, start=False, stop=True)))(
                            pbk.t[:, q * 128:(q + 1) * 128], self.ones.t[0:1, :], bsr.t[0:1, gi * 128:(gi + 1) * 128]),
                            reads=[self.ones.r, bsr.r], writes=[pbk.r], signal=(q == 3))
                    self.tt("dve", gu.t[:, 4 * a:4 * a + 4, j * 128:(j + 1) * 128],
                            pbk.t[:, 0:512].rearrange("p (a b) -> p a b", a=4), uT.t[:, 4 * a:4 * a + 4, j * 128:(j + 1) * 128],
                            ALU.mult, [pbk.r, uT.r], [gu.r])
            self.resid_store(xin, xout, gu, 8, Wo, g0)
        return xout


    def fft_stage1(self, X, Ad, f1):
        stg = self.pool_of("s1stg", 2, [128, 4, 512], BF16)
        for s_ in range(64):
            st = self.nxt(stg)
            for q in range(4):
                pbk = self.pb[(4 * (s_ % 2)) + q]
                mrows = 128 if q < 2 else 4
                self.mm(pbk.t[0:mrows, 0:512], f1.t[:, q * 128:q * 128 + mrows], X.t[:, s_, :], True, True, [f1.r, X.r], [pbk.r])
                self.cp("act" if q % 2 == 0 else "dve", st.t[0:mrows, q, :], pbk.t[0:mrows, 0:512], [pbk.r], [st.r])
            self.dma("sp", Ad.t[0:2, s_, :, :].rearrange("q p c -> p q c"), st.t[:, 0:2, :], [st.r], [Ad.r])
            self.dma("sp", Ad.t[2:4, s_, 0:4, :].rearrange("q p c -> p q c"), st.t[0:4, 2:4, :], [st.r], [Ad.r])

    def load_B(self, B, Ad, k1g, G):
        kc, p0 = k1g // 128, k1g % 128
        for ri in range(2):
            self.dma("sp", B.t[ri * 64:(ri + 1) * 64, :, :], Ad.t[2 * kc + ri, :, p0:p0 + G, :], [Ad.r], [B.r])

    def load_tab(self, Tb, tab, k1g, G):
        self.dma("sp", Tb.t, tab.t[k1g // G].rearrange("p (k m) -> p k m", k=G), [], [Tb.r])

    def phase_hyena(self):
        G = 4
        zT = self.inp("hy_zT", [33, L])
        w1i = self.inp("hy_w1", [33, 64])
        w2i = self.inp("hy_w2", [64, 64])
        prmi = self.inp("hy_prm", [64, 4])
        w3i = self.inp("hy_w3", [64, 2048])
        dsk = self.inp("hy_bias", [2, 512])
        win = self.inp("hy_win", [128, 64, 512])
        f1i = self.inp("t_F1", [128, 512], BF16)
        Htab = self.inp("t_H", [64, 128, 512], BF16)
        Hctab = self.inp("t_Hc", [64, 128, 512], BF16)
        M1tab = self.inp("t_M1", [64, 128, 512], BF16)
        M2tab = self.inp("t_M2", [64, 128, 512], BF16)
        fvi = self.inp("t_Finv", [128, 512], BF16)
        fei = self.inp("t_FinvE", [128, 4, 68], BF16)
        A0 = self.scratch("fftA0", [4, 64, 128, 512], BF16)
        A1 = self.scratch("fftA1", [4, 64, 128, 512], BF16)
        Cd = self.scratch("fftC", [128, 256, 512], BF16)
        Hf = self.scratch("fftHf", [2, 2, 64, 256, 512], BF16)
        X1s = self.scratch("X1s", [128, 64, 512], BF16)
        self.top = self.cmark
        f1 = self.alloc("f1", [128, 512], BF16)
        fv = self.alloc("fv", [128, 512], BF16)
        fe = self.alloc("fe", [128, 4, 68], BF16)
        self.dma("sp", f1.t, f1i.t, [], [f1.r])
        self.dma("sp", fv.t, fvi.t, [], [fv.r])
        self.dma("sp", fe.t, fei.t, [], [fe.r])
        base = self.top
        h2T = self.alloc("h2T", [128, L], BF16)
        w3 = self.alloc("w3sb", [64, 2048], BF16)
        self.dma("pool", w3.t, w3i.t, [], [w3.r])
        mlp_mark = self.top
        w1 = self.alloc("w1sb", [33, 64], F32)
        w2 = self.alloc("w2sb", [64, 64], F32)
        prm = self.alloc("prm", [64, 4], F32)
        pr2 = self.alloc("pr2", [64, 4], F32)
        self.dma("sp", w1.t, w1i.t, [], [w1.r])
        self.dma("sp", w2.t, w2i.t, [], [w2.r])
        self.dma("sp", prm.t, prmi.t, [], [prm.r])
        for a in range(2):
            self.ts("dve", pr2.t[:, 2 * a:2 * a + 1], prm.t[:, 2 * a + 1:2 * a + 2], 1.0 / 3, None, ALU.mult, None, [prm.r, pr2.r], [pr2.r])
            self.tt("dve", pr2.t[:, 2 * a + 1:2 * a + 2], pr2.t[:, 2 * a:2 * a + 1], prm.t[:, 2 * a:2 * a + 1], ALU.mult, [prm.r, pr2.r], [pr2.r])
        ztb = self.pool_of("ztb", 2, [33, 512], F32)
        s3 = self.alloc("s3", [64, 512], F32)
        qq = self.alloc("qq", [64, 512], F32)
        h1 = self.alloc("h1", [64, 512], F32)

        def sin3(dst, src_ps, a):
            self.act(s3.t, src_ps.t[0:64, 0:512], AF.Sin, [src_ps.r, pr2.r], [s3.r], bias=pr2.t[:, 2 * a + 1:2 * a + 2], scale=pr2.t[:, 2 * a:2 * a + 1])
            self.tt("dve", qq.t, s3.t, s3.t, ALU.mult, [s3.r], [qq.r])
            self.ts("dve", qq.t, qq.t, -4.0, 3.0, ALU.mult, ALU.add, [qq.r], [qq.r])
            self.tt("dve", dst[0], qq.t, s3.t, ALU.mult, [qq.r, s3.r], [dst[1]])

        for ch in range(L // 512):
            zt = self.nxt(ztb)
            self.dma("sp", zt.t, zT.t[:, ch * 512:(ch + 1) * 512], [], [zt.r])
            pa, pb2 = self.pb[ch % 2], self.pb[2 + ch % 2]
            self.mm(pa.t[0:64, 0:512], w1.t, zt.t, True, True, [w1.r, zt.r], [pa.r])
            sin3((h1.t, h1.r), pa, 0)
            self.mm(pb2.t[0:64, 0:512], w2.t, h1.t, True, True, [w2.r, h1.r], [pb2.r])
            sin3((h2T.t[0:64, ch * 512:(ch + 1) * 512], h2T.r), pb2, 1)
        self.P.barrier()
        self.top = mlp_mark
        h2v = h2T.t[0:64, :].rearrange("p (j s) -> p s j", s=64)
        Dz = self.alloc("Dz", [128, 512], F32)
        omark = self.top
        for o in range(2):
            self.P.barrier()
            self.top = omark
            self.memset("pool", Dz.t, 0.0, [Dz.r])
            self.dma("sp", Dz.t[0:64, :], dsk.t[o].partition_broadcast(64), [], [Dz.r])
            Xf = self.alloc("Xf", [128, 64, 512], BF16)
            Xb = self.alloc("Xb", [128, 64, 512], BF16)
            wt = self.pool_of("wint", 2, [128, 512], F32)
            for s_ in range(64):
                wn = self.nxt(wt)
                self.dma("sp", wn.t, win.t[:, s_, :], [], [wn.r])
                for dr, Xd in ((0, Xf), (1, Xb)):
                    pbk = self.bank()
                    c0 = (o * 2 + dr) * 512
                    self.mm(pbk.t[:, 0:512], h2v[:, s_, :], w3.t[:, c0:c0 + 512], True, True, [h2T.r, w3.r], [pbk.r])
                    self.tt("dve", Xd.t[:, s_, :], pbk.t[:, 0:512], wn.t, ALU.mult, [pbk.r, wn.r], [Xd.r])
            self.memset("pool", Xb.t[0:1, 0, :], 0.0, [Xb.r])
            self.fft_stage1(Xf, A0, f1)
            self.fft_stage1(Xb, A1, f1)
            self.P.barrier()
            self.top = omark
            Bf = self.pool_of("Bf", 2, [128, G, 512], BF16)
            Bb = self.pool_of("Bb", 2, [128, G, 512], BF16)
            Ht = self.pool_of("Ht", 2, [128, G, 128], BF16)
            Hct = self.pool_of("Hct", 2, [128, G, 128], BF16)
            Hst = self.pool_of("Hst", 2, [128, G, 512], BF16)
            Hfo = Hf.t[o].rearrange("r k a c -> (r k) a c")
            def f_loads(k1g):
                bf, bb, ht, hct = self.nxt(Bf), self.nxt(Bb), self.nxt(Ht), self.nxt(Hct)
                self.load_B(bf, A0, k1g, G)
                self.load_B(bb, A1, k1g, G)
                self.load_tab(ht, Htab, k1g, G)
                self.load_tab(hct, Hctab, k1g, G)
                return bf, bb, ht, hct

            nxt_l = f_loads(0)
            for k1g in range(0, 128 + G, G):
                bf, bb, ht, hct = nxt_l
                if k1g + G < 128 + G:
                    nxt_l = f_loads(k1g + G)
                hs = self.nxt(Hst)
                for g in range(G):
                    pbk = self.bank()
                    self.mm(pbk.t[:, 0:512], ht.t[:, g, :], bf.t[:, g, :], True, False, [ht.r, bf.r], [pbk.r])
                    self.mm(pbk.t[:, 0:512], hct.t[:, g, :], bb.t[:, g, :], False, True, [hct.r, bb.r], [pbk.r])
                    self.tt("dve", hs.t[:, g, :], pbk.t[:, 0:512], Dz.t, ALU.add, [pbk.r, Dz.r], [hs.r])
                self.dma("sp", Hfo[:, k1g:k1g + G, :], hs.t, [hs.r], [Hf.r])
        self.P.barrier()
        self.top = base
        X = self.alloc("Xc", [128, 64, 512], BF16)
        cmark2 = self.top
        Ub = self.pool_of("Ub", 2, [128, L], BF16)
        xst = self.pool_of("xst", 2, [128, 8, 128], BF16)
        nt = 0
        for part in range(2):
            for cc in range(4):
                U = self.nxt(Ub)
                r0 = part * 512 + cc * 128
                self.dma("sp", U.t, self.VX1.t[r0:r0 + 128, :], [self.VX1.r], [U.r])
                Uv = U.t.rearrange("p (j s) -> p s j", s=64)
                for s0 in range(0, 64, 8):
                    pT = self.pb[6 + nt % 2]
                    nt += 1
                    pTv = pT.t.bitcast(BF16)
                    for i in range(8):
                        self.tr(pTv[:, i * 128:(i + 1) * 128], Uv[:, s0 + i, :], self.ident.t, [U.r, self.ident.r], [pT.r], signal=(i == 7))
                    src = pTv.rearrange("p (a b) -> p a b", a=8)
                    if part == 0:
                        self.cp("act", X.t[:, s0:s0 + 8, cc * 128:(cc + 1) * 128], src, [pT.r], [X.r])
                    else:
                        st = self.nxt(xst)
                        self.cp("act", st.t, src, [pT.r], [st.r])
                        self.dma("sp", X1s.t[:, s0:s0 + 8, cc * 128:(cc + 1) * 128], st.t, [st.r], [X1s.r])
        for conv in range(2):
            self.P.barrier()
            self.top = cmark2
            if conv == 1:
                x2sb = self.alloc("x2sb", [128, 4, EXT], BF16)
                yaT = self.alloc("yaT", [128, 4, EXT], BF16)
                self.dma("sp", x2sb.t, self.X2.t.rearrange("(cc p) t -> p cc t", p=128), [self.X2.r], [x2sb.r])
            m2 = self.top
            self.fft_stage1(X, A0, f1)
            self.P.barrier()
            self.top = m2
            Bt = self.pool_of("Bt", 2, [128, G, 512], BF16)
            Ht = self.pool_of("cHt", 2, [128, G, 128], BF16)
            M1t = self.pool_of("cM1", 2, [128, G, 128], BF16)
            M2t = self.pool_of("cM2", 2, [128, G, 128], BF16)
            HH1 = self.pool_of("HH1", 2, [128, G, 512], BF16)
            HH2 = self.pool_of("HH2", 2, [128, G, 512], BF16)
            T1 = self.pool_of("T1", 2, [128, 512], BF16)
            T2 = self.pool_of("T2", 2, [128, 512], BF16)
            Cst = self.pool_of("Cst", 2, [128, G, 512], BF16)
            def c_loads(k1g):
                bt, ht, m1, m2_, h1_, h2_ = (self.nxt(Bt), self.nxt(Ht), self.nxt(M1t), self.nxt(M2t), self.nxt(HH1), self.nxt(HH2))
                self.load_B(bt, A0, k1g, G)
                self.load_tab(ht, Htab, k1g, G)
                self.load_tab(m1, M1tab, k1g, G)
                self.load_tab(m2_, M2tab, k1g, G)
                for half in range(2):
                    self.dma("sp", h1_.t[half * 64:(half + 1) * 64, :, :], Hf.t[conv, 0, :, k1g:k1g + G, :], [Hf.r], [h1_.r])
                    self.dma("sp", h2_.t[half * 64:(half + 1) * 64, :, :], Hf.t[conv, 1, :, k1g:k1g + G, :], [Hf.r], [h2_.r])
                return bt, ht, m1, m2_, h1_, h2_

            nxt_l = c_loads(0)
            for k1g in range(0, 128 + G, G):
                bt, ht, m1, m2_, h1_, h2_ = nxt_l
                if k1g + G < 128 + G:
                    nxt_l = c_loads(k1g + G)
                cs_ = self.nxt(Cst)
                for g in range(G):
                    pu = self.bank()
                    self.mm(pu.t[:, 0:512], ht.t[:, g, :], bt.t[:, g, :], True, True, [ht.r, bt.r], [pu.r])
                    t1, t2 = self.nxt(T1), self.nxt(T2)
                    self.tt("dve", t1.t, pu.t[:, 0:512], h1_.t[:, g, :], ALU.mult, [pu.r, h1_.r], [t1.r])
                    self.tt("dve", t2.t, pu.t[:, 0:512], h2_.t[:, g, :], ALU.mult, [pu.r, h2_.r], [t2.r])
                    pc = self.bank()
                    self.mm(pc.t[:, 0:512], m1.t[:, g, :], t1.t, True, False, [m1.r, t1.r], [pc.r])
                    self.mm(pc.t[:, 0:512], m2_.t[:, g, :], t2.t, False, True, [m2_.r, t2.r], [pc.r])
                    self.cp("act", cs_.t[:, g, :], pc.t[:, 0:512], [pc.r], [cs_.r])
                self.dma("sp", Cd.t[:, k1g:k1g + G, :], cs_.t, [cs_.r], [Cd.r])
            self.P.barrier()
            self.top = m2
            Dt = self.pool_of("Dt", 2, [128, 4, 512], BF16)
            x1t = self.pool_of("x1t", 2, [128, 512], BF16)
            for s_ in range(64):
                dt_ = self.nxt(Dt)
                for ri in range(2):
                    self.dma("sp", dt_.t[:, ri, :], Cd.t[ri * 64 + s_, 0:128, :], [Cd.r], [dt_.r])
                    self.dma("sp", dt_.t[0:4, 2 + ri, :], Cd.t[ri * 64 + s_, 128:132, :], [Cd.r], [dt_.r])
                if conv == 0:
                    xg = self.nxt(x1t)
                    self.dma("sp", xg.t, X1s.t[:, s_, :], [X1s.r], [xg.r])
                    pbk = self.bank()
                    for q in range(4):
                        kr = 128 if q < 2 else 4
                        self.mm(pbk.t[:, 0:512], fv.t[0:kr, q * 128:(q + 1) * 128], dt_.t[0:kr, q, :], q == 0, q == 3, [fv.r, dt_.r], [pbk.r])
                    self.tt("dve", X.t[:, s_, :], pbk.t[:, 0:512], xg.t, ALU.mult, [pbk.r, xg.r], [X.r])
                else:
                    pbk = self.bank()
                    for cc in range(4):
                        for q in range(4):
                            self.P.op("pe", (lambda o_, l_, r_, a_, b_: (lambda h_: h_.matmul(o_, l_, r_, start=a_, stop=b_)))(
                                pbk.t[:, cc * 68:(cc + 1) * 68], dt_.t[0:(128 if q < 2 else 4), q, cc * 128:(cc + 1) * 128],
                                fe.t[0:(128 if q < 2 else 4), q, :], q == 0, q == 3),
                                reads=[dt_.r, fe.r], writes=[pbk.r], signal=(cc == 3 and q == 3))
                    x2v = x2sb.t.rearrange("p c (j s) -> p c s j", s=64)[:, :, s_, :]
                    yav = yaT.t.rearrange("p c (j s) -> p c s j", s=64)[:, :, s_, :]
                    self.tt("dve", yav, pbk.t[:, 0:272].rearrange("p (c j) -> p c j", c=4), x2v, ALU.mult, [pbk.r, x2sb.r], [yaT.r])
            if conv == 1:
                self.dma("sp", self.YT.t[0:512, :].rearrange("(cc p) t -> p cc t", p=128), yaT.t, [yaT.r], [self.YT.r])


def build_program(dbg=False, stop_after=None):
    kb = KB(stop_after)
    kb.dbg = dbg
    kb._bk = 0
    kb.consts()
    kb.phase_mod()
    kb.P.barrier()
    if stop_after == "mod":
        kb.P.build()
        return kb
    kb.phase_inproj()
    kb.P.barrier()
    if stop_after == "inproj":
        kb.P.build()
        return kb
    kb.phase_attn()
    kb.P.barrier()
    if stop_after == "attn":
        kb.P.build()
        return kb
    if os.environ.get("FAKE_YA"):
        ya = kb.inp("dbg_yaT", [512, EXT], BF16)
        st = kb.alloc("yast", [128, EXT], BF16)
        for c in range(4):
            kb.dma("sp", st.t, ya.t[c * 128:(c + 1) * 128, :], [ya.r], [st.r])
            kb.dma("sp", kb.YT.t[c * 128:(c + 1) * 128, :], st.t, [st.r], [kb.YT.r])
        kb.P.barrier()
    else:
        kb.phase_hyena()
        kb.P.barrier()
    if stop_after == "hyena":
        kb.P.build()
        return kb
    wo = kb.inp("ab_w_out", [D, D])
    x1 = kb.phase_outproj("X1o", kb.YT, kb.x_ext, wo.t, 0)
    kb.P.barrier()
    x2 = kb.phase_ffn("X2o", x1, 0)
    kb.P.barrier()
    x3 = kb.phase_sgu("X3o", x2, 1)
    kb.P.barrier()
    kb.phase_ffn("X4o", x3, 1, final=True)
    kb.P.barrier()
    kb.P.build()
    return kb


def rope_tables(pos_row, pos_col):
    T = pos_row.shape[0]
    cos = np.zeros((128, T), np.float32)
    sin = np.zeros((128, T), np.float32)
    inv = (10000.0 ** (-np.arange(16, dtype=np.float32) / 16)).astype(np.float32)
    for f in range(128):
        d = f % 64
        pos = pos_row if d < 32 else pos_col
        dd = d % 32
        i = dd % 16
        ang = pos.astype(np.float32) * inv[i]
        cos[f] = np.cos(ang)
        sin[f] = -np.sin(ang) if dd < 16 else np.sin(ang)
    return cos, sin


def rperm_matrix():
    m = np.zeros((128, 128), np.float32)
    for f in range(128):
        dd = (f % 64) % 32
        partner = f + 16 if dd < 16 else f - 16
        m[partner, f] = 1.0
    return m.astype(ml_dtypes.bfloat16)


def fft_tables(lo):
    bf = ml_dtypes.bfloat16
    N = 16384
    j = np.arange(128, dtype=np.float64)[:, None]
    k1 = np.arange(256, dtype=np.float64)[None, :]
    th = 2 * np.pi * ((j * k1) % 256) / 256
    F1 = np.zeros((128, 4, 128))
    Fi = np.zeros((128, 4, 128))
    for kc in range(2):
        F1[:, 2 * kc, :] = np.cos(th[:, kc * 128:(kc + 1) * 128])
        F1[:, 2 * kc + 1, :] = -np.sin(th[:, kc * 128:(kc + 1) * 128])
        Fi[:, 2 * kc, :] = np.cos(th[:, kc * 128:(kc + 1) * 128]).T / N
        Fi[:, 2 * kc + 1, :] = -np.sin(th[:, kc * 128:(kc + 1) * 128]).T / N
    wgt = np.zeros((128, 4, 1))
    wgt[:, 0:2, 0] = 2.0
    wgt[0, 0:2, 0] = 1.0
    wgt[0, 2:4, 0] = 1.0
    Fi = Fi * wgt
    j0 = lo // 64
    FiE = Fi[:, :, j0:j0 + 68]
    s = np.arange(64, dtype=np.float64)
    k2 = np.arange(64, dtype=np.float64)
    kk = np.arange(256, dtype=np.float64)
    ang = 2 * np.pi * ((s[None, :, None] * kk[:, None, None]) / N + ((s[None, :, None] * k2[None, None, :]) % 64) / 64)
    Zr, Zi = np.cos(ang), -np.sin(ang)
    H = np.zeros((256, 128, 128))
    H[:, 0:64, 0:64] = Zr
    H[:, 64:128, 0:64] = -Zi
    H[:, 0:64, 64:128] = Zi
    H[:, 64:128, 64:128] = Zr
    Hc = H.copy()
    Hc[:, :, 64:128] *= -1
    ZrT, ZiT = Zr.transpose(0, 2, 1), Zi.transpose(0, 2, 1)
    M1 = np.zeros((256, 128, 128))
    M1[:, 0:64, 0:64] = ZrT
    M1[:, 64:128, 0:64] = ZiT
    M1[:, 0:64, 64:128] = -ZiT
    M1[:, 64:128, 64:128] = ZrT
    M2 = np.zeros((256, 128, 128))
    M2[:, 0:64, 0:64] = ZiT
    M2[:, 64:128, 0:64] = -ZrT
    M2[:, 0:64, 64:128] = ZrT
    M2[:, 64:128, 64:128] = ZiT
    c = lambda a: np.ascontiguousarray(a.astype(np.float32)).astype(bf)
    grp = lambda a: c(a.reshape(64, 4, 128, 128).transpose(0, 2, 1, 3).reshape(64, 128, 512))
    return {"t_F1": c(F1.reshape(128, 512)), "t_Finv": c(Fi.reshape(128, 512)), "t_FinvE": c(FiE),
            "t_H": grp(H), "t_Hc": grp(Hc), "t_M1": grp(M1), "t_M2": grp(M2)}


def hyena_consts():
    f32 = np.float32
    bands = 16
    pos = np.arange(L, dtype=f32)
    t = np.linspace(0.0, 1.0, L, dtype=f32)[:, None]
    ang = (f32(2.0 * math.pi / L) * pos[:, None] * np.linspace(1e-4, bands - 1, bands, dtype=f32)[None, :]).astype(f32)
    z = np.concatenate([t, np.cos(ang), -np.sin(ang)], axis=-1).astype(f32)
    deltas = np.abs(np.linspace(math.log(1e-2) / 1.5, math.log(1e-2) / 0.3, 512, dtype=f32))
    window = (np.exp(-t * deltas[None, :]) + f32(0.05)).astype(f32)
    win = np.ascontiguousarray(window.reshape(128, 64, 512))
    return np.ascontiguousarray(z.T), win


_TAB = {}


def host_inputs(inputs, cid):
    b, hf = cid // 2, cid % 2
    lo = 0 if hf == 0 else L - EXT
    f32 = np.float32
    m = {}
    m["c_ident"] = np.eye(128, dtype=f32).astype(ml_dtypes.bfloat16)
    cc = np.stack([inputs["c"][b], inputs["c_ctx"]], -1).astype(f32)
    m["ccol"] = np.ascontiguousarray(cc.reshape(8, 128, 2).transpose(1, 0, 2))
    m["mod_w"] = inputs["mod_w"]
    m["mod_b"] = inputs["mod_b"]
    m["x_full"] = np.ascontiguousarray(inputs["x"][b])
    m["x_ext"] = np.ascontiguousarray(inputs["x"][b, lo:lo + EXT])
    m["ctx"] = np.ascontiguousarray(inputs["ctx"][b])
    m["w_in"] = np.ascontiguousarray(inputs["ab_w_in"][0])
    cw = np.concatenate([inputs["hy_conv_w"][0], inputs["hy_conv_b"][0][None]], 0)
    m["hy_cw"] = np.ascontiguousarray(cw.reshape(4, 12, 128).transpose(2, 1, 0)).astype(f32)
    m["norm_mix_g"] = inputs["norm_mix_g"]
    t = np.arange(L)
    ck, sk = rope_tables(t // 64, t % 64)
    m["ropek_cos"], m["ropek_sin"] = ck, sk
    m["ropeq_cos"] = np.ascontiguousarray(ck[:, lo:lo + EXT])
    m["ropeq_sin"] = np.ascontiguousarray(sk[:, lo:lo + EXT])
    m["c_rperm"] = rperm_matrix()
    m["da_lambda"] = np.ascontiguousarray(inputs["da_lambda"][0]).astype(f32)
    m["ab_w_out"] = np.ascontiguousarray(inputs["ab_w_out"][0])
    m["norm_ffn_g"] = inputs["norm_ffn_g"]
    m["final_norm_g"] = inputs["final_norm_g"]
    for i in range(2):
        m["ffn_w_up%d" % i] = np.ascontiguousarray(inputs["ffn_w_up"][i])
        m["ffn_w_down%d" % i] = np.ascontiguousarray(inputs["ffn_w_down"][i])
        fcw = np.concatenate([inputs["ffn_conv_w"][i], inputs["ffn_conv_b"][i][None]], 0)
        m["ffn_cw%d" % i] = np.ascontiguousarray(fcw.reshape(4, 44, 128).transpose(2, 1, 0)).astype(f32)
    m["sgu_w_in"] = np.ascontiguousarray(inputs["sgu_w_in"][0])
    m["sgu_bu_col"] = np.ascontiguousarray(inputs["sgu_b_in"][0][:1024].reshape(8, 128).T).astype(f32)
    m["sgu_bv"] = np.ascontiguousarray(inputs["sgu_b_in"][0][1024:])
    m["sgu_ln_g"] = inputs["sgu_ln_g"][0]
    m["sgu_ln_b"] = inputs["sgu_ln_b"][0]
    m["sgu_wsT"] = np.ascontiguousarray(inputs["sgu_w_s"][0].transpose(2, 0, 1)).astype(f32)
    m["sgu_b_s"] = np.ascontiguousarray(inputs["sgu_b_s"][0].reshape(1, 1024)).astype(f32)
    m["sgu_w_out"] = np.ascontiguousarray(inputs["sgu_w_out"][0])
    if lo not in _TAB:
        _TAB[lo] = fft_tables(lo)
    if "hc" not in _TAB:
        _TAB["hc"] = hyena_consts()
    m.update(_TAB[lo])
    m["hy_zT"], m["hy_win"] = _TAB["hc"]
    m["hy_w1"] = np.ascontiguousarray(inputs["hy_w1"][0])
    m["hy_w2"] = np.ascontiguousarray(inputs["hy_w2"][0])
    m["hy_prm"] = np.ascontiguousarray(np.stack([inputs["hy_b1"][0], inputs["hy_freq"][0][0], inputs["hy_b2"][0], inputs["hy_freq"][0][1]], -1)).astype(f32)
    m["hy_w3"] = np.ascontiguousarray(inputs["hy_w3"][0])
    m["hy_bias"] = np.ascontiguousarray(inputs["hy_bias"][0])
    m["subln_col"] = np.ascontiguousarray(inputs["da_subln_g"][0].reshape(128, 1)).astype(f32)
    return m


_CACHE = {}


def kernel(**inputs):
    inputs = {k: np.asarray(v) for k, v in inputs.items()}
    if "kb" not in _CACHE:
        _CACHE["kb"] = build_program()
    kb = _CACHE["kb"]
    in_maps = []
    for cid in range(8):
        m = host_inputs(inputs, cid)
        in_maps.append({k: m[k] for k in kb.ins})
    res = run_bass_kernel_spmd(kb.nc, in_maps, core_ids=list(range(8)))
    out = np.zeros((4, L, D), np.float32)
    for cid in range(8):
        b, hf = cid // 2, cid % 2
        o = res.results[cid]["out"]
        if hf == 0:
            out[b, :4096] = o[:4096]
        else:
            out[b, 4096:] = o[EXT - 4096:]
    return out
```
